# Optimizing a Trainium2 kernel written in Bass

```python
import math
import jax, jax.numpy as jnp
from jax import lax
import numpy as np

D_MODEL = 2048
BATCH = 1
SEQ = 8192
DEPTH = 2
DEC_BATCH = 128
DEC_SEQ = 8
PAST_LEN = 2048
PAGE_SIZE = 128

GROUP_W = D_MODEL // 4
RET_HEADS = 4
RET_HD = GROUP_W // RET_HEADS
RET_CHUNK = 128
RET_ROPE_BASE = 10000.0
DIFF_HEADS = 4
DIFF_VD = GROUP_W // DIFF_HEADS
DIFF_QD = DIFF_VD // 2
ROPE_THETA = 500000.0
ROPE_DIM = DIFF_QD // 4
Q_BLOCK = 128
RWKV_HD = 64
RWKV_HEADS = GROUP_W // RWKV_HD
RWKV_DECAY_LORA = 64
RWKV_A_LORA = 64
RWKV_GATE_LORA = 128
RWKV_LN_EPS = 64e-5
S5_CH = 16
S5_GROUPS = GROUP_W // S5_CH
S5_N = 64
D_FF = 5632
CONV_W = 3
PLE_DIM = 256
EPS = 1e-6
NEG_INF = -1e30

RET_COLS = 4 * GROUP_W
DIFF_COLS = 3 * GROUP_W
RWKV_COLS = 3 * GROUP_W + RWKV_DECAY_LORA + RWKV_A_LORA + RWKV_GATE_LORA
S5_COLS = GROUP_W
IN_COLS = RET_COLS + DIFF_COLS + RWKV_COLS + S5_COLS
SPLITS = (RET_COLS, RET_COLS + DIFF_COLS, RET_COLS + DIFF_COLS + RWKV_COLS)

kernel_name = 'hybrid_parallel_head_groups_step'


def rmsnorm(x, g):
    xf = x.astype(jnp.float32)
    y = xf * lax.rsqrt(jnp.mean(xf * xf, axis=-1, keepdims=True) + EPS)
    return (y * g.astype(jnp.float32)).astype(x.dtype)


def group_norm(x, w, b, eps):
    xf = x.astype(jnp.float32)
    mu = jnp.mean(xf, axis=-1, keepdims=True)
    var = jnp.mean(jnp.square(xf - mu), axis=-1, keepdims=True)
    return (xf - mu) * lax.rsqrt(var + eps) * w.astype(jnp.float32) + b.astype(jnp.float32)


def rope(x, pos, base, rot_dim):
    half = rot_dim // 2
    inv = jnp.power(base, -jnp.arange(half, dtype=jnp.float32) / half)
    ang = pos.astype(jnp.float32)[:, None] * inv[None, :]
    cos = jnp.cos(ang)[:, None, :]
    sin = jnp.sin(ang)[:, None, :]
    xf = x.astype(jnp.float32)
    x1, x2, rest = xf[..., :half], xf[..., half:rot_dim], xf[..., rot_dim:]
    return jnp.concatenate([x1 * cos - x2 * sin, x1 * sin + x2 * cos, rest], axis=-1).astype(x.dtype)


def retention_scan(q, k, v, s0):
    B, T, H, d = q.shape
    L = RET_CHUNK if T % RET_CHUNK == 0 else T
    nc = T // L
    log_g = jnp.log1p(-jnp.exp2(-5.0 - jnp.arange(H, dtype=jnp.float32)))
    idx = jnp.arange(L, dtype=jnp.float32)
    rel = idx[:, None] - idx[None, :]
    dmask = jnp.where(rel >= 0, jnp.exp(log_g[:, None, None] * jnp.maximum(rel, 0.0)), 0.0)
    q_dec = jnp.exp(log_g[:, None] * (idx + 1.0))[:, :, None]
    k_dec = jnp.exp(log_g[:, None] * (L - 1.0 - idx))[:, :, None]
    c_dec = jnp.exp(log_g * L)[:, None, None]

    def to_chunks(t):
        return t.astype(jnp.float32).reshape(B, nc, L, H, d).transpose(1, 0, 3, 2, 4)

    def step(S, qkv):
        qc, kc, vc = qkv
        att = jnp.einsum('bhld,bhmd->bhlm', qc, kc) * dmask
        o = jnp.einsum('bhlm,bhme->bhle', att, vc) + jnp.einsum('bhld,bhde->bhle', qc * q_dec, S)
        S = S * c_dec + jnp.einsum('bhld,bhle->bhde', kc * k_dec, vc)
        return S, o

    S, o = lax.scan(step, s0.astype(jnp.float32), (to_chunks(q), to_chunks(k), to_chunks(v)))
    return o.transpose(1, 0, 3, 2, 4).reshape(B, T, H, d), S


def retention_mixer(cols, pos, s0, norm_w, norm_b):
    B, T, _ = cols.shape
    q, k, v, g = jnp.split(cols, 4, axis=-1)
    q = rope(q.reshape(B, T, RET_HEADS, RET_HD), pos, RET_ROPE_BASE, RET_HD)
    k = rope(k.reshape(B, T, RET_HEADS, RET_HD), pos, RET_ROPE_BASE, RET_HD) * (RET_HD ** -0.5)
    v = v.reshape(B, T, RET_HEADS, RET_HD)
    o, s_new = retention_scan(q, k, v, s0)
    o = group_norm(o, norm_w, norm_b, EPS) * jax.nn.silu(g.astype(jnp.float32)).reshape(B, T, RET_HEADS, RET_HD)
    return o.reshape(B, T, GROUP_W), s_new


def diff_block(qi, qpos, k_all, v_all, kpos, lam):
    s = jnp.einsum('bqhmd,bkhmd->bhmqk', qi, k_all, preferred_element_type=jnp.float32) * (DIFF_QD ** -0.5)
    s = jnp.where(kpos[None, :] <= qpos[:, None], s, NEG_INF)
    pr = jax.nn.softmax(s, axis=-1)
    pd = pr[:, :, 0] - lam * pr[:, :, 1]
    return jnp.einsum('bhqk,bkhd->bqhd', pd, v_all)


def diff_mixer(cols, pos, lam, lam_init, subln_w, k_past, v_past):
    B, T, _ = cols.shape
    q, k, v = jnp.split(cols, 3, axis=-1)
    q = rope(q.reshape(B, T, DIFF_HEADS * 2, DIFF_QD), pos, ROPE_THETA, ROPE_DIM).reshape(B, T, DIFF_HEADS, 2, DIFF_QD)
    k = rope(k.reshape(B, T, DIFF_HEADS * 2, DIFF_QD), pos, ROPE_THETA, ROPE_DIM).reshape(B, T, DIFF_HEADS, 2, DIFF_QD)
    v = v.reshape(B, T, DIFF_HEADS, DIFF_VD)
    if k_past is None:
        nb = T // Q_BLOCK
        qb = jnp.moveaxis(q.reshape(B, nb, Q_BLOCK, DIFF_HEADS, 2, DIFF_QD), 1, 0)
        kpos = jnp.arange(T)
        vf = v.astype(jnp.float32)

        def blk(args):
            qi, bi = args
            return diff_block(qi, bi * Q_BLOCK + jnp.arange(Q_BLOCK), k, vf, kpos, lam)

        o = lax.map(blk, (qb, jnp.arange(nb)))
        o = jnp.moveaxis(o, 0, 1).reshape(B, T, DIFF_HEADS, DIFF_VD)
    else:
        P = k_past.shape[1]
        k_all = jnp.concatenate([k_past.astype(k.dtype), k], axis=1)
        v_all = jnp.concatenate([v_past.astype(jnp.float32), v.astype(jnp.float32)], axis=1)
        o = diff_block(q, P + jnp.arange(T), k_all, v_all, jnp.arange(P + T), lam)
    o = rmsnorm(o, subln_w) * (1.0 - lam_init)
    return o.reshape(B, T, GROUP_W), k.reshape(B, T, DIFF_HEADS, 2 * DIFF_QD), v


def rwkv7_scan(r, w, k, v, a, b, s0):
    def step(S, inp):
        rt, wt, kt, vt, at, bt = inp
        sa = jnp.einsum('bhij,bhj->bhi', S, at)
        S = S * wt[:, :, None, :] + sa[..., None] * bt[:, :, None, :] + vt[..., None] * kt[:, :, None, :]
        return S, jnp.einsum('bhij,bhj->bhi', S, rt)

    seq = tuple(jnp.moveaxis(t, 1, 0) for t in (r, w, k, v, a, b))
    S, y = lax.scan(step, s0.astype(jnp.float32), seq)
    return jnp.moveaxis(y, 0, 1), S


def rwkv_mixer(cols, shift0, s0, mu, w0, w2, a0, a2, g2, k_k, k_a, r_k, ln_w, ln_b):
    B, T, _ = cols.shape
    f32 = jnp.float32
    prev = jnp.concatenate([shift0[:, None].astype(cols.dtype), cols[:, :-1]], axis=1)
    xm = (cols + (prev - cols) * mu).astype(f32)
    o1 = GROUP_W
    o4 = 3 * GROUP_W + RWKV_DECAY_LORA
    r, k, v, wl, al, gl = jnp.split(xm, [o1, 2 * o1, 3 * o1, o4, o4 + RWKV_A_LORA], axis=-1)
    w = -jax.nn.softplus(-(w0 + jnp.tanh(wl) @ w2)) - 0.5
    decay = jnp.exp(-jnp.exp(w.astype(f32)))
    a = jax.nn.sigmoid(a0 + al @ a2)
    g = jax.nn.sigmoid(gl) @ g2
    hs = lambda t: t.astype(f32).reshape(B, T, RWKV_HEADS, RWKV_HD)
    r, k, v, decay, a = hs(r), hs(k), hs(v), hs(decay), hs(a)
    kk = k * k_k.astype(f32).reshape(RWKV_HEADS, RWKV_HD)
    kk = kk * lax.rsqrt(jnp.maximum(jnp.sum(kk * kk, axis=-1, keepdims=True), 1e-24))
    k = k * (1.0 + (a - 1.0) * k_a.astype(f32).reshape(RWKV_HEADS, RWKV_HD))
    y, s_new = rwkv7_scan(r, decay, k, v, -kk, kk * a, s0)
    y = group_norm(y, ln_w, ln_b, RWKV_LN_EPS)
    y = y + jnp.sum(r * k * r_k.astype(f32), axis=-1, keepdims=True) * v
    return y.reshape(B, T, GROUP_W) * g, s_new, cols[:, -1]


def s5_combine(e1, e2):
    a1r, a1i, b1r, b1i = e1
    a2r, a2i, b2r, b2i = e2
    return (a2r * a1r - a2i * a1i, a2r * a1i + a2i * a1r,
            a2r * b1r - a2i * b1i + b2r, a2r * b1i + a2i * b1r + b2i)


def s5_mixer(u, s0_re, s0_im, lam_re, lam_im, log_step, b_re, b_im, c_re, c_im, d_skip, w_glu, b_glu, norm_w):
    B, T, _ = u.shape
    f32 = jnp.float32
    uf = u.astype(f32).reshape(B, T, S5_GROUPS, S5_CH)
    lr, li = lam_re.astype(f32), lam_im.astype(f32)
    dt = jnp.exp(log_step.astype(f32))[:, None]
    mag = jnp.exp(lr * dt)
    ab_re, ab_im = mag * jnp.cos(li * dt), mag * jnp.sin(li * dt)
    den = lr * lr + li * li
    cf_re = ((ab_re - 1.0) * lr + ab_im * li) / den
    cf_im = (ab_im * lr - (ab_re - 1.0) * li) / den
    br, bi = b_re.astype(f32), b_im.astype(f32)
    bb_re = cf_re[..., None] * br - cf_im[..., None] * bi
    bb_im = cf_re[..., None] * bi + cf_im[..., None] * br
    bu_re = jnp.einsum('btgc,gnc->btgn', uf, bb_re)
    bu_im = jnp.einsum('btgc,gnc->btgn', uf, bb_im)
    s0r, s0i = s0_re.astype(f32), s0_im.astype(f32)
    bu_re = bu_re.at[:, 0].add(ab_re * s0r - ab_im * s0i)
    bu_im = bu_im.at[:, 0].add(ab_re * s0i + ab_im * s0r)
    a_re = jnp.broadcast_to(ab_re, bu_re.shape)
    a_im = jnp.broadcast_to(ab_im, bu_im.shape)
    _, _, s_re, s_im = lax.associative_scan(s5_combine, (a_re, a_im, bu_re, bu_im), axis=1)
    y = jnp.einsum('gcn,btgn->btgc', c_re.astype(f32), s_re) - jnp.einsum('gcn,btgn->btgc', c_im.astype(f32), s_im)
    y = y.reshape(B, T, GROUP_W) + d_skip.astype(f32) * u.astype(f32)
    y = jax.nn.gelu(y)
    y = y * jax.nn.sigmoid(y @ w_glu + b_glu)
    return rmsnorm(y, norm_w), s_re[:, -1], s_im[:, -1]


def conv_ffn(h, conv0, w_up, conv_w, conv_b, w_down):
    up = h @ w_up
    T = up.shape[1]
    padded = jnp.concatenate([conv0.astype(up.dtype), up], axis=1)
    conv = conv_b
    for j in range(CONV_W):
        conv = conv + conv_w[j] * padded[:, j:j + T]
    gate, val = jnp.split(conv, 2, axis=-1)
    return (jax.nn.silu(gate) * val) @ w_down, padded[:, T:]


def run_group(x, p, pos, ret0, rwkv0, shift0, s5re0, s5im0, conv0, cache_k, cache_v, page_table, W):
    k_rows, v_rows, ret_s, rwkv_s, shift_s, s5r_s, s5i_s, conv_s = [], [], [], [], [], [], [], []
    for i in range(DEPTH):
        h = rmsnorm(x, W['norm_mix'][i])
        proj = h @ W['w_in'][i]
        c_ret, c_diff, c_rwkv, c_s5 = jnp.split(proj, SPLITS, axis=-1)
        o_ret, s_ret = retention_mixer(c_ret, pos, ret0[i], W['ret_norm_w'][i], W['ret_norm_b'][i])
        lam_init = 0.8 - 0.6 * math.exp(-0.3 * i)
        lam = (jnp.exp(jnp.sum(W['diff_lq1'][i].astype(jnp.float32) * W['diff_lk1'][i].astype(jnp.float32)))
               - jnp.exp(jnp.sum(W['diff_lq2'][i].astype(jnp.float32) * W['diff_lk2'][i].astype(jnp.float32)))
               + lam_init)
        if page_table is None:
            k_past, v_past = None, None
        else:
            nb, npg = page_table.shape
            k_past = cache_k[i][page_table].reshape(nb, npg * PAGE_SIZE, DIFF_HEADS, 2, DIFF_QD)
            v_past = cache_v[i][page_table].reshape(nb, npg * PAGE_SIZE, DIFF_HEADS, DIFF_VD)
        o_diff, k_new, v_new = diff_mixer(c_diff, pos, lam, lam_init, W['diff_subln'][i], k_past, v_past)
        o_rwkv, s_rwkv, shift_new = rwkv_mixer(c_rwkv, shift0[i], rwkv0[i], W['rwkv_mu'][i], W['rwkv_w0'][i],
                                               W['rwkv_w2'][i], W['rwkv_a0'][i], W['rwkv_a2'][i], W['rwkv_g2'][i],
                                               W['rwkv_kk'][i], W['rwkv_ka'][i], W['rwkv_rk'][i],
                                               W['rwkv_ln_w'][i], W['rwkv_ln_b'][i])
        o_s5, s_re, s_im = s5_mixer(c_s5, s5re0[i], s5im0[i], W['s5_lam_re'][i], W['s5_lam_im'][i],
                                    W['s5_log_step'][i], W['s5_b_re'][i], W['s5_b_im'][i], W['s5_c_re'][i],
                                    W['s5_c_im'][i], W['s5_d'][i], W['s5_w_glu'][i], W['s5_b_glu'][i],
                                    W['s5_norm'][i])
        x = x + jnp.concatenate([o_ret, o_diff, o_rwkv, o_s5], axis=-1) @ W['w_out'][i]
        f, conv_new = conv_ffn(rmsnorm(x, W['norm_ffn'][i]), conv0[i], W['ffn_w_up'][i], W['ffn_conv_w'][i],
                               W['ffn_conv_b'][i], W['ffn_w_down'][i])
        x = x + f
        e = rmsnorm(p[i] @ W['ple_w_proj'][i], W['ple_norm_e'][i])
        x = x + e * jax.nn.sigmoid(rmsnorm(x, W['norm_ple'][i]) @ W['ple_w_gate'][i])
        k_rows.append(k_new); v_rows.append(v_new); ret_s.append(s_ret); rwkv_s.append(s_rwkv)
        shift_s.append(shift_new); s5r_s.append(s_re); s5i_s.append(s_im); conv_s.append(conv_new)
    y = rmsnorm(x, W['norm_final'])
    return (y, jnp.stack(k_rows), jnp.stack(v_rows), jnp.stack(ret_s), jnp.stack(rwkv_s), jnp.stack(shift_s),
            jnp.stack(s5r_s), jnp.stack(s5i_s), jnp.stack(conv_s))


def setup_inputs(seed: int = 0) -> dict:
    key = jax.random.key(seed)
    ks = iter(jax.random.split(key, 96))
    f32 = jnp.float32

    def nrm(shape, scale=1.0):
        return jax.random.normal(next(ks), shape, f32) * scale

    def gain(shape):
        return 1.0 + nrm(shape, 0.02)

    n_pages = PAST_LEN // PAGE_SIZE
    n_pool = (DEC_BATCH * n_pages * 5) // 4
    page_table = jax.random.permutation(next(ks), n_pool)[:DEC_BATCH * n_pages].reshape(DEC_BATCH, n_pages).astype(jnp.int32)
    frac = jnp.arange(GROUP_W, dtype=f32) / (GROUP_W - 1)
    conv_id = jnp.zeros((CONV_W, 1), f32).at[CONV_W - 1].set(1.0)
    return {
        'x_prompt': nrm((BATCH, SEQ, D_MODEL)),
        'x_sample': nrm((DEC_BATCH, DEC_SEQ, D_MODEL)),
        'p_prompt': nrm((DEPTH, BATCH, SEQ, PLE_DIM)),
        'p_sample': nrm((DEPTH, DEC_BATCH, DEC_SEQ, PLE_DIM)),
        'cache_k': nrm((DEPTH, n_pool, PAGE_SIZE, DIFF_HEADS, 2 * DIFF_QD)),
        'cache_v': nrm((DEPTH, n_pool, PAGE_SIZE, DIFF_HEADS, DIFF_VD)),
        'page_table': page_table,
        'state_ret': nrm((DEPTH, DEC_BATCH, RET_HEADS, RET_HD, RET_HD), 0.5),
        'state_rwkv': nrm((DEPTH, DEC_BATCH, RWKV_HEADS, RWKV_HD, RWKV_HD), 0.5),
        'state_rwkv_shift': nrm((DEPTH, DEC_BATCH, RWKV_COLS)),
        'state_s5_re': nrm((DEPTH, DEC_BATCH, S5_GROUPS, S5_N), 0.5),
        'state_s5_im': nrm((DEPTH, DEC_BATCH, S5_GROUPS, S5_N), 0.5),
        'state_ffn_conv': nrm((DEPTH, DEC_BATCH, CONV_W - 1, 2 * D_FF)),
        'norm_mix': gain((DEPTH, D_MODEL)),
        'w_in': nrm((DEPTH, D_MODEL, IN_COLS), D_MODEL ** -0.5),
        'w_out': nrm((DEPTH, D_MODEL, D_MODEL), D_MODEL ** -0.5),
        'ret_norm_w': gain((DEPTH, RET_HEADS, RET_HD)),
        'ret_norm_b': nrm((DEPTH, RET_HEADS, RET_HD), 0.02),
        'diff_lq1': nrm((DEPTH, DIFF_QD), 0.1),
        'diff_lk1': nrm((DEPTH, DIFF_QD), 0.1),
        'diff_lq2': nrm((DEPTH, DIFF_QD), 0.1),
        'diff_lk2': nrm((DEPTH, DIFF_QD), 0.1),
        'diff_subln': gain((DEPTH, DIFF_VD)),
        'rwkv_mu': jax.random.uniform(next(ks), (DEPTH, RWKV_COLS), f32),
        'rwkv_w0': -6.0 + 5.0 * frac ** 0.7 + nrm((DEPTH, GROUP_W), 0.1),
        'rwkv_w2': nrm((DEPTH, RWKV_DECAY_LORA, GROUP_W), 0.1),
        'rwkv_a0': nrm((DEPTH, GROUP_W), 0.1),
        'rwkv_a2': nrm((DEPTH, RWKV_A_LORA, GROUP_W), 0.1),
        'rwkv_g2': nrm((DEPTH, RWKV_GATE_LORA, GROUP_W), RWKV_GATE_LORA ** -0.5),
        'rwkv_kk': 0.85 + nrm((DEPTH, GROUP_W), 0.02),
        'rwkv_ka': gain((DEPTH, GROUP_W)),
        'rwkv_rk': nrm((DEPTH, RWKV_HEADS, RWKV_HD), 0.1),
        'rwkv_ln_w': gain((DEPTH, RWKV_HEADS, RWKV_HD)),
        'rwkv_ln_b': nrm((DEPTH, RWKV_HEADS, RWKV_HD), 0.02),
        's5_lam_re': -0.5 + nrm((DEPTH, S5_GROUPS, S5_N), 0.01),
        's5_lam_im': math.pi * jnp.arange(S5_N, dtype=f32) + nrm((DEPTH, S5_GROUPS, S5_N), 0.01),
        's5_log_step': jax.random.uniform(next(ks), (DEPTH, S5_GROUPS), f32, math.log(1e-3), math.log(1e-1)),
        's5_b_re': nrm((DEPTH, S5_GROUPS, S5_N, S5_CH), (2 * S5_CH) ** -0.5),
        's5_b_im': nrm((DEPTH, S5_GROUPS, S5_N, S5_CH), (2 * S5_CH) ** -0.5),
        's5_c_re': nrm((DEPTH, S5_GROUPS, S5_CH, S5_N), (2 * S5_N) ** -0.5),
        's5_c_im': nrm((DEPTH, S5_GROUPS, S5_CH, S5_N), (2 * S5_N) ** -0.5),
        's5_d': nrm((DEPTH, GROUP_W), 0.5),
        's5_w_glu': nrm((DEPTH, GROUP_W, GROUP_W), GROUP_W ** -0.5),
        's5_b_glu': nrm((DEPTH, GROUP_W), 0.02),
        's5_norm': gain((DEPTH, GROUP_W)),
        'norm_ffn': gain((DEPTH, D_MODEL)),
        'ffn_w_up': nrm((DEPTH, D_MODEL, 2 * D_FF), D_MODEL ** -0.5),
        'ffn_conv_w': conv_id[None] + nrm((DEPTH, CONV_W, 2 * D_FF), 0.3),
        'ffn_conv_b': nrm((DEPTH, 2 * D_FF), 0.02),
        'ffn_w_down': nrm((DEPTH, D_FF, D_MODEL), D_FF ** -0.5),
        'norm_ple': gain((DEPTH, D_MODEL)),
        'ple_w_proj': nrm((DEPTH, PLE_DIM, D_MODEL), PLE_DIM ** -0.5),
        'ple_norm_e': gain((DEPTH, D_MODEL)),
        'ple_w_gate': nrm((DEPTH, D_MODEL, D_MODEL), D_MODEL ** -0.5),
        'norm_final': gain((D_MODEL,)),
    }


def reference(x_prompt, x_sample, p_prompt, p_sample, cache_k, cache_v, page_table,
              state_ret, state_rwkv, state_rwkv_shift, state_s5_re, state_s5_im, state_ffn_conv,
              norm_mix, w_in, w_out, ret_norm_w, ret_norm_b,
              diff_lq1, diff_lk1, diff_lq2, diff_lk2, diff_subln,
              rwkv_mu, rwkv_w0, rwkv_w2, rwkv_a0, rwkv_a2, rwkv_g2, rwkv_kk, rwkv_ka, rwkv_rk, rwkv_ln_w, rwkv_ln_b,
              s5_lam_re, s5_lam_im, s5_log_step, s5_b_re, s5_b_im, s5_c_re, s5_c_im, s5_d, s5_w_glu, s5_b_glu, s5_norm,
              norm_ffn, ffn_w_up, ffn_conv_w, ffn_conv_b, ffn_w_down,
              norm_ple, ple_w_proj, ple_norm_e, ple_w_gate, norm_final):
    W = dict(norm_mix=norm_mix, w_in=w_in, w_out=w_out, ret_norm_w=ret_norm_w, ret_norm_b=ret_norm_b,
             diff_lq1=diff_lq1, diff_lk1=diff_lk1, diff_lq2=diff_lq2, diff_lk2=diff_lk2, diff_subln=diff_subln,
             rwkv_mu=rwkv_mu, rwkv_w0=rwkv_w0, rwkv_w2=rwkv_w2, rwkv_a0=rwkv_a0, rwkv_a2=rwkv_a2, rwkv_g2=rwkv_g2,
             rwkv_kk=rwkv_kk, rwkv_ka=rwkv_ka, rwkv_rk=rwkv_rk, rwkv_ln_w=rwkv_ln_w, rwkv_ln_b=rwkv_ln_b,
             s5_lam_re=s5_lam_re, s5_lam_im=s5_lam_im, s5_log_step=s5_log_step, s5_b_re=s5_b_re, s5_b_im=s5_b_im,
             s5_c_re=s5_c_re, s5_c_im=s5_c_im, s5_d=s5_d, s5_w_glu=s5_w_glu, s5_b_glu=s5_b_glu, s5_norm=s5_norm,
             norm_ffn=norm_ffn, ffn_w_up=ffn_w_up, ffn_conv_w=ffn_conv_w, ffn_conv_b=ffn_conv_b, ffn_w_down=ffn_w_down,
             norm_ple=norm_ple, ple_w_proj=ple_w_proj, ple_norm_e=ple_norm_e, ple_w_gate=ple_w_gate,
             norm_final=norm_final)
    f32 = jnp.float32
    bp, tp = x_prompt.shape[0], x_prompt.shape[1]
    pos_p = jnp.arange(tp, dtype=jnp.int32)
    past_len = page_table.shape[1] * PAGE_SIZE
    pos_s = past_len + jnp.arange(x_sample.shape[1], dtype=jnp.int32)
    ret0 = jnp.zeros((DEPTH, bp, RET_HEADS, RET_HD, RET_HD), f32)
    rwkv0 = jnp.zeros((DEPTH, bp, RWKV_HEADS, RWKV_HD, RWKV_HD), f32)
    shift0 = jnp.zeros((DEPTH, bp, RWKV_COLS), f32)
    s5r0 = jnp.zeros((DEPTH, bp, S5_GROUPS, S5_N), f32)
    s5i0 = jnp.zeros((DEPTH, bp, S5_GROUPS, S5_N), f32)
    conv0 = jnp.zeros((DEPTH, bp, CONV_W - 1, 2 * D_FF), f32)
    (y_prompt, k_prompt, v_prompt, ret_prompt, rwkv_prompt, shift_prompt,
     s5_re_prompt, s5_im_prompt, conv_prompt) = run_group(
        x_prompt, p_prompt, pos_p, ret0, rwkv0, shift0, s5r0, s5i0, conv0, None, None, None, W)
    (y_sample, k_sample, v_sample, ret_sample, rwkv_sample, shift_sample,
     s5_re_sample, s5_im_sample, conv_sample) = run_group(
        x_sample, p_sample, pos_s, state_ret, state_rwkv, state_rwkv_shift, state_s5_re, state_s5_im,
        state_ffn_conv, cache_k, cache_v, page_table, W)
    return (y_prompt, y_sample, k_prompt, v_prompt, k_sample, v_sample, ret_prompt, ret_sample,
            rwkv_prompt, rwkv_sample, shift_prompt, shift_sample, s5_re_prompt, s5_im_prompt,
            s5_re_sample, s5_im_sample, conv_prompt, conv_sample)
```

```python
import math
import numpy as np
import concourse.bass as bass
import concourse.mybir as mybir
from concourse.bass_utils import run_bass_kernel_spmd

F32 = mybir.dt.float32
I32 = mybir.dt.int32
AF = mybir.ActivationFunctionType
ALU = mybir.AluOpType
AX = mybir.AxisListType

D = 2048
GW = 512
IN_COLS = 5888
RW_COLS = 1792
DFF = 5632
NCORES = 8


class TT:
    __slots__ = ("t", "lw", "rd")

    def __init__(self, t):
        self.t = t
        self.lw = None
        self.rd = {}

    def __getitem__(self, k):
        return self.t[k]


class Sched:
    SEM_LIMIT = 30000
    DK = 8

    def __init__(self, nc):
        self.nc = nc
        self.eng = {"pe": nc.tensor, "act": nc.scalar, "dve": nc.vector, "pool": nc.gpsimd, "sp": nc.sync}
        self.csem = {}
        self.ccnt = {}
        self.waited = {e: {} for e in self.eng}
        self.dsem = {}
        self.dcnt = {}
        self.semid = 0
        self.last = {}
        self.nins = 0

    def newsem(self):
        self.semid += 1
        return self.nc.alloc_semaphore(f"s{self.semid}")

    def _wait(self, e, ref):
        pe, sem, val = ref
        if e == "pe" and pe == "pe":
            return
        w = self.waited[e]
        k = id(sem)
        if w.get(k, (None, 0))[1] >= val:
            return
        w[k] = (sem, val)
        self.eng[e].wait_ge(sem, val)
        self.nins += 1

    def _deps(self, e, reads, writes):
        for t in reads:
            if t.lw is not None:
                self._wait(e, t.lw)
        for t in writes:
            if t.lw is not None:
                self._wait(e, t.lw)
            for r in t.rd.values():
                self._wait(e, r)

    def _mark(self, ref, reads, writes):
        for t in reads:
            t.rd[id(ref[1])] = ref
        for t in writes:
            t.lw = ref
            t.rd = {}

    def op(self, e, fn, reads=(), writes=()):
        self._deps(e, reads, writes)
        if e not in self.csem or self.ccnt[e] >= self.SEM_LIMIT:
            self.csem[e] = self.newsem()
            self.ccnt[e] = 0
        ins = fn(self.eng[e])
        self.ccnt[e] += 1
        ins.then_inc(self.csem[e], 1)
        ref = (e, self.csem[e], self.ccnt[e])
        self.last[e] = ref
        self._mark(ref, reads, writes)
        self.nins += 1
        return ref

    def dma(self, q, out, in_, reads=(), writes=(), **kw):
        self._deps(q, reads, writes)
        if q not in self.dsem or self.dcnt[q] >= self.DK * (self.SEM_LIMIT // 16):
            self.dsem[q] = [self.newsem() for _ in range(self.DK)]
            self.dcnt[q] = 0
            self.last.setdefault("dma", {})
        i = self.dcnt[q]
        sem = self.dsem[q][i % self.DK]
        prev = 16 * (i // self.DK)
        if prev > 0:
            self._wait(q, ("dma", sem, prev))
        ins = self.eng[q].dma_start(out=out, in_=in_, **kw)
        ins.then_inc(sem, 16)
        self.dcnt[q] = i + 1
        ref = ("dma", sem, prev + 16)
        self.last["dma"][id(sem)] = ref
        self._mark(ref, reads, writes)
        self.nins += 1
        return ref

    def barrier(self, engines=("pe", "act", "dve", "pool", "sp")):
        refs = [r for k, r in self.last.items() if k != "dma"]
        refs += list(self.last.get("dma", {}).values())
        for e in engines:
            for r in refs:
                if r[0] == e:
                    continue
                pe, sem, val = r
                w = self.waited[e]
                if w.get(id(sem), (None, 0))[1] >= val:
                    continue
                w[id(sem)] = (sem, val)
                self.eng[e].wait_ge(sem, val)
                self.nins += 1


class Cfg:
    def __init__(self, T=8192, NS=16, NPG=16, NPOOL=2560):
        self.T, self.NS, self.NPG, self.NPOOL = T, NS, NPG, NPOOL
        self.NTS = NS * 8
        self.NT = T + self.NTS
        self.tiles = [(i * 128, 128) for i in range(T // 128)] + [(T, self.NTS)]
        self.sts = [self.tiles[i:i + 4] for i in range(0, T // 128, 4)] + [[self.tiles[-1]]]
        self.sts2 = [self.tiles[i:i + 2] for i in range(0, T // 128, 2)] + [[self.tiles[-1]]]


class Ctx:
    pass


def build(cfg):
    from contextlib import ExitStack
    nc = bass.Bass("TRN2", target_bir_lowering=False)
    S = Sched(nc)
    T, NS, NT, NTS, NPG = cfg.T, cfg.NS, cfg.NT, cfg.NTS, cfg.NPG
    NSQ = 1 + NS
    dr = {}

    def din(name, shape, dt=F32):
        dr[name] = nc.dram_tensor(name, list(shape), dt, kind="ExternalInput").ap()

    def dout(name, shape):
        dr[name] = nc.dram_tensor(name, list(shape), F32, kind="ExternalOutput").ap()

    def dscr(name, shape):
        dr[name] = nc.dram_tensor(name, list(shape), F32).ap()

    din("x0", [NT, D]); din("p0", [2, NT, 256])
    din("ck", [2, cfg.NPOOL * 128, 512]); din("cv", [2, cfg.NPOOL * 128, 512])
    din("pt", [NS * NPG], I32)
    din("st_ret", [2, NS, 4, 128, 128]); din("st_rwkvT", [2, NS, 64, 8, 64]); din("st_shift", [2, NS, RW_COLS])
    din("st_s5re", [2, 128, NS, 16]); din("st_s5im", [2, 128, NS, 16]); din("st_convT", [2, 128, 88, NS, 2])
    din("norm_mix", [2, D]); din("w_in", [2, D, IN_COLS]); din("w_out", [2, D, D])
    din("ret_norm_w", [2, 512]); din("ret_norm_b", [2, 512])
    din("diff_l", [2, 4, 64]); din("diff_subln", [2, 128])
    din("rwkv_mu", [2, RW_COLS]); din("rwkv_w0", [2, 512]); din("rwkv_w2", [2, 64, 512]); din("rwkv_a0", [2, 512])
    din("rwkv_a2", [2, 64, 512]); din("rwkv_g2", [2, 128, 512]); din("rwkv_kk", [2, 512]); din("rwkv_ka", [2, 512])
    din("rwkv_rk", [2, 512]); din("rwkv_ln_w", [2, 512]); din("rwkv_ln_b", [2, 512])
    din("s5_lre", [2, 128, 16]); din("s5_lim", [2, 128, 16]); din("s5_ls", [2, 128, 16])
    din("s5_bre", [2, 128, 16, 128]); din("s5_bim", [2, 128, 16, 128])
    din("s5_cre", [2, 128, 16, 32]); din("s5_cim", [2, 128, 16, 32])
    din("s5_d", [2, 512]); din("s5_w_glu", [2, 512, 512]); din("s5_b_glu", [2, 512]); din("s5_norm", [2, 512])
    din("norm_ffn", [2, D]); din("ffn_w_up", [2, D, 2 * DFF]); din("ffn_cw", [2, 128, 88, 3]); din("ffn_cb", [2, 128, 88])
    din("ffn_w_down", [2, DFF, D]); din("norm_ple", [2, D]); din("ple_w_proj", [2, 256, D]); din("ple_norm_e", [2, D])
    din("ple_w_gate", [2, D, D]); din("norm_final", [D])
    din("ident", [128, 128]); din("rope_r", [NT, 2, 64]); din("rope_d", [NT, 2, 8])
    din("ret_dmT", [4, 128, 128]); din("ret_dmT8", [4, 8, 8]); din("ret_qd", [128, 4, 128]); din("ret_kd", [128, 4]);
    din("ret_qd8", [128, 4, 8]); din("ret_kd8", [8, 4]); din("cmask", [128, 128]); din("cmask8", [8, 8])
    din("mask8", [8, 512]); din("tidx", [128, 130]); din("iota", [128, 1], I32)
    dout("y", [NT, D]); dout("k_new", [2, NT, 512]); dout("v_new", [2, NT, 512])
    dout("o_ret", [2, NSQ, 4, 128, 128]); dout("o_rwkvT", [2, NSQ, 64, 8, 64]); dout("o_shift", [2, NSQ, RW_COLS])
    dout("o_s5re", [2, 128, NSQ, 16]); dout("o_s5im", [2, 128, NSQ, 16]); dout("o_convT", [2, 128, 88, NSQ, 2])
    dscr("X", [NT, D]); dscr("PROJ", [NT, IN_COLS]); dscr("OCAT", [NT, D]); dscr("QK", [NT, 1024])
    dscr("RWS", [6, NT, 512]); dscr("FMA", [64, NT, 40]); dscr("FMR", [64, NT, 8]); dscr("FMW", [64, NT, 8]); dscr("ERAW", [NT, D]); dscr("YS5", [NT, 512])

    C = Ctx()
    C.nc, C.S, C.cfg, C.dr = nc, S, cfg, dr
    C.uid = 0
    with ExitStack() as gs:
        def sb(name, shape, dt=F32, st=gs):
            C.uid += 1
            return TT(st.enter_context(nc.sbuf_tensor(f"t{C.uid}_" + name, list(shape), dt)))

        def ps(name, shape, st=gs):
            C.uid += 1
            return TT(st.enter_context(nc.psum_tensor(f"q{C.uid}_" + name, list(shape), F32)))
        C.sb, C.ps = sb, ps
        C.ident = sb("ident", [128, 128])
        S.dma("sp", C.ident[:], dr["ident"], writes=[C.ident])
        C.eps = sb("epsc", [128, 4])
        S.op("dve", lambda e: e.memset(C.eps[:, 0:1], 1e-6), writes=[C.eps])
        S.op("dve", lambda e: e.memset(C.eps[:, 1:2], -math.pi), writes=[C.eps])
        S.op("dve", lambda e: e.memset(C.eps[:, 2:3], 64e-5), writes=[C.eps])
        S.op("dve", lambda e: e.memset(C.eps[:, 3:4], 1.0), writes=[C.eps])
        for l in range(2):
            with ExitStack() as st:
                phase_proj(C, st, l)
            S.barrier()
            with ExitStack() as st:
                phase_prep(C, st, l)
            S.barrier()
            with ExitStack() as st:
                phase_ret(C, st, l)
            S.barrier()
            with ExitStack() as st:
                phase_diff(C, st, l)
            S.barrier()
            with ExitStack() as st:
                phase_rwkv(C, st, l)
            S.barrier()
            with ExitStack() as st:
                phase_s5(C, st, l)
            S.barrier()
            with ExitStack() as st:
                phase_wout(C, st, l)
            S.barrier()
            with ExitStack() as st:
                phase_ffn(C, st, l)
            S.barrier()
            with ExitStack() as st:
                phase_ple(C, st, l)
            S.barrier()
        S.barrier()
    return nc, S


def bcast_load(C, st, name, ap, n, q="sp"):
    t = C.sb(name, [128, n], st=st)
    C.S.dma(q, t[:], ap.partition_broadcast(128), writes=[t])
    return t


def rms_rows(C, x, P, n, g, out, ss, act_sq_out):
    S = C.S
    S.op("act", lambda e: e.activation(out=act_sq_out[:P, :n], in_=x[:P, :n], func=AF.Square, accum_out=ss[:P, 0:1]),
         reads=[x], writes=[act_sq_out, ss])
    S.op("dve", lambda e: e.tensor_scalar(out=ss[:P, 1:2], in0=ss[:P, 0:1], scalar1=1.0 / n, scalar2=1e-6,
                                          op0=ALU.mult, op1=ALU.add), reads=[ss], writes=[ss])
    S.op("act", lambda e: e.sqrt(out=ss[:P, 1:2], in_=ss[:P, 1:2]), reads=[ss], writes=[ss])
    S.op("dve", lambda e: e.reciprocal(out=ss[:P, 1:2], in_=ss[:P, 1:2]), reads=[ss], writes=[ss])
    S.op("dve", lambda e: e.scalar_tensor_tensor(out=out[:P, :n], in0=x[:P, :n], scalar=ss[:P, 1:2], in1=g[:P, :n],
                                                 op0=ALU.mult, op1=ALU.mult), reads=[x, ss, g], writes=[out])


def transpose_to(C, src, P, c0, ncols, dst_ap_fn, dst, ptp, i):
    S = C.S
    pt = ptp[i % len(ptp)]
    S.op("pe", lambda e: e.transpose(out=pt[:ncols, :P], in_=src[:P, c0:c0 + ncols], identity=C.ident[:P, :P]),
         reads=[src, C.ident], writes=[pt])
    eng = "act" if i % 2 == 0 else "dve"
    if eng == "act":
        S.op("act", lambda e: e.copy(out=dst_ap_fn(), in_=pt[:ncols, :P]), reads=[pt], writes=[dst])
    else:
        S.op("dve", lambda e: e.tensor_copy(out=dst_ap_fn(), in_=pt[:ncols, :P]), reads=[pt], writes=[dst])


def dense(C, hT, sizes, W, KC, N, wbufs, pbufs, epi, cbw=512):
    S = C.S
    cb = 0
    cnt = 0
    for c0 in range(0, N, cbw):
        ncol = min(cbw, N - c0)
        wb = wbufs[cb % len(wbufs)]
        S.dma("sp", wb[:, :KC, :ncol], W[:, c0:c0 + ncol].rearrange("(k p) c -> p k c", p=128), writes=[wb])
        for ti, (row0, P, off) in enumerate(sizes):
            po = pbufs[cnt % len(pbufs)]
            cnt += 1
            for k in range(KC):
                S.op("pe", lambda e: e.matmul(po[:P, :ncol], lhsT=hT[:, k, off:off + P], rhs=wb[:, k, :ncol],
                                              start=(k == 0), stop=(k == KC - 1)), reads=[hT, wb], writes=[po])
            epi(ti, row0, P, c0, ncol, po)
        cb += 1


def phase_proj(C, st, l):
    S, dr, cfg = C.S, C.dr, C.cfg
    g = bcast_load(C, st, "g_mix", dr["norm_mix"][l], D)
    xs = [C.sb(f"px{i}", [128, D], st=st) for i in range(2)]
    hs = C.sb("ph", [128, D], st=st)
    sq = C.sb("psq", [128, D], st=st)
    ss = C.sb("pss", [128, 2], st=st)
    hT = C.sb("phT", [128, 16, 512], st=st)
    wb = [C.sb(f"pw{i}", [128, 16, 512], st=st) for i in range(2)]
    ob = [C.sb(f"pob{i}", [128, 512], st=st) for i in range(4)]
    ptp = [C.ps(f"ppt{i}", [128, 128], st=st) for i in range(2)]
    pb = [C.ps(f"ppo{i}", [128, 512], st=st) for i in range(4)]
    src = dr["x0"] if l == 0 else dr["X"]
    n = 0
    for stl in cfg.sts:
        sizes = []
        off = 0
        for (r0, P) in stl:
            x = xs[n % 2]
            n += 1
            S.dma("sp", x[:P, :], src[r0:r0 + P, :], writes=[x])
            rms_rows(C, x, P, D, g, hs, ss, sq)
            for k in range(16):
                transpose_to(C, hs, P, k * 128, 128, lambda: hT[:, k, off:off + P], hT, ptp, k)
            sizes.append((r0, P, off))
            off += P
        cnt = [0]

        def epi(ti, r0, P, c0, ncol, po):
            o = ob[cnt[0] % 4]
            if cnt[0] % 2 == 0:
                S.op("act", lambda e: e.copy(out=o[:P, :ncol], in_=po[:P, :ncol]), reads=[po], writes=[o])
            else:
                S.op("dve", lambda e: e.tensor_copy(out=o[:P, :ncol], in_=po[:P, :ncol]), reads=[po], writes=[o])
            cnt[0] += 1
            S.dma("pool", dr["PROJ"][r0:r0 + P, c0:c0 + ncol], o[:P, :ncol], reads=[o])
        dense(C, hT, sizes, dr["w_in"][l], 16, IN_COLS, wb, pb, epi)


def v3(ap, h):
    return ap.rearrange("p (h d) -> p h d", h=h)


def rope_apply(C, src, dst, P, c0, nh, hd, half, cs, tmp, scale=None, sc0=None):
    S = C.S
    if sc0 is None:
        sc0 = c0
    s3 = v3(src[:P, sc0:sc0 + nh * hd], nh)
    d3 = v3(dst[:P, c0:c0 + nh * hd], nh)
    t3 = v3(tmp[:P, 0:nh * hd], nh)
    cosb = cs[:P, 0:1, :].to_broadcast([P, nh, half])
    sinb = cs[:P, 1:2, :].to_broadcast([P, nh, half])
    x1, x2 = s3[:, :, 0:half], s3[:, :, half:2 * half]
    if 2 * half < hd:
        S.op("pool", lambda e: e.tensor_copy(out=d3[:, :, 2 * half:hd], in_=s3[:, :, 2 * half:hd]), reads=[src], writes=[dst])
    S.op("dve", lambda e: e.tensor_tensor(out=t3[:, :, 0:half], in0=x1, in1=cosb, op=ALU.mult), reads=[src, cs], writes=[tmp])
    S.op("dve", lambda e: e.tensor_tensor(out=t3[:, :, half:2 * half], in0=x2, in1=sinb, op=ALU.mult), reads=[src, cs], writes=[tmp])
    S.op("dve", lambda e: e.tensor_tensor(out=d3[:, :, 0:half], in0=t3[:, :, 0:half], in1=t3[:, :, half:2 * half], op=ALU.subtract),
         reads=[tmp], writes=[dst])
    S.op("dve", lambda e: e.tensor_tensor(out=t3[:, :, 0:half], in0=x1, in1=sinb, op=ALU.mult), reads=[src, cs], writes=[tmp])
    S.op("dve", lambda e: e.tensor_tensor(out=t3[:, :, half:2 * half], in0=x2, in1=cosb, op=ALU.mult), reads=[src, cs], writes=[tmp])
    S.op("dve", lambda e: e.tensor_tensor(out=d3[:, :, half:2 * half], in0=t3[:, :, 0:half], in1=t3[:, :, half:2 * half], op=ALU.add),
         reads=[tmp], writes=[dst])
    if scale is not None:
        S.op("act", lambda e: e.mul(out=dst[:P, c0:c0 + nh * hd], in_=dst[:P, c0:c0 + nh * hd], mul=scale), reads=[dst], writes=[dst])


def phase_prep(C, st, l):
    S, dr, cfg = C.S, C.dr, C.cfg
    T, NS = cfg.T, cfg.NS
    pj = [C.sb(f"rpj{i}", [128, 3072], st=st) for i in range(2)]
    oo = [C.sb(f"roo{i}", [128, 2048], st=st) for i in range(2)]
    tmp = C.sb("rtmp", [128, 512], st=st)
    csr = [C.sb(f"rcsr{i}", [128, 2, 64], st=st) for i in range(2)]
    csd = [C.sb(f"rcsd{i}", [128, 2, 8], st=st) for i in range(2)]
    S.dma("pool", dr["v_new"][l], dr["PROJ"][:, 3072:3584])
    S.dma("pool", dr["o_shift"][l, 0:1, :], dr["PROJ"][T - 1:T, 3584:3584 + RW_COLS])
    for b in range(NS):
        S.dma("pool", dr["o_shift"][l, 1 + b:2 + b, :], dr["PROJ"][T + 8 * b + 7:T + 8 * b + 8, 3584:3584 + RW_COLS])
    for i, (r0, P) in enumerate(cfg.tiles):
        x, o, cr, cd = pj[i % 2], oo[i % 2], csr[i % 2], csd[i % 2]
        S.dma("sp", x[:P, :], dr["PROJ"][r0:r0 + P, 0:3072], writes=[x])
        S.dma("sp", cr[:P], dr["rope_r"][r0:r0 + P], writes=[cr])
        S.dma("sp", cd[:P], dr["rope_d"][r0:r0 + P], writes=[cd])
        rope_apply(C, x, o, P, 0, 4, 128, 64, cr, tmp)
        rope_apply(C, x, o, P, 512, 4, 128, 64, cr, tmp, scale=128 ** -0.5)
        S.dma("pool", dr["QK"][r0:r0 + P, :], o[:P, 0:1024], reads=[o])
        rope_apply(C, x, o, P, 1024, 8, 64, 8, cd, tmp, sc0=2048)
        rope_apply(C, x, o, P, 1536, 8, 64, 8, cd, tmp, sc0=2560)
        S.dma("pool", dr["PROJ"][r0:r0 + P, 2048:2560], o[:P, 1024:1536], reads=[o])
        S.dma("pool", dr["k_new"][l, r0:r0 + P, :], o[:P, 1536:2048], reads=[o])


def phase_ret(C, st, l):
    S, dr, cfg = C.S, C.dr, C.cfg
    T, NS = cfg.T, cfg.NS
    gw = bcast_load(C, st, "rgw", dr["ret_norm_w"][l], 512)
    gb = bcast_load(C, st, "rgb", dr["ret_norm_b"][l], 512)
    dmT = C.sb("rdmT", [128, 4, 128], st=st)
    S.dma("sp", dmT[:], dr["ret_dmT"].rearrange("h m l -> m h l"), writes=[dmT])
    dmT8 = C.sb("rdmT8", [8, 4, 8], st=st)
    S.dma("sp", dmT8[:], dr["ret_dmT8"].rearrange("h m l -> m h l"), writes=[dmT8])
    qd = C.sb("rqd", [128, 4, 128], st=st); S.dma("sp", qd[:], dr["ret_qd"], writes=[qd])
    kd = C.sb("rkd", [128, 4], st=st); S.dma("sp", kd[:], dr["ret_kd"], writes=[kd])
    qd8 = C.sb("rqd8", [128, 4, 8], st=st); S.dma("sp", qd8[:], dr["ret_qd8"], writes=[qd8])
    kd8 = C.sb("rkd8", [8, 4], st=st); S.dma("sp", kd8[:], dr["ret_kd8"], writes=[kd8])
    Sst = C.sb("rS", [128, 4, 128], st=st)
    qk = [C.sb(f"rqk{i}", [128, 1024], st=st) for i in range(2)]
    vg = [C.sb(f"rvg{i}", [128, 1024], st=st) for i in range(2)]
    qT = C.sb("rqT", [128, 4, 128], st=st)
    kT = C.sb("rkT", [128, 4, 128], st=st)
    qdT = C.sb("rqdT", [128, 4, 128], st=st)
    kdt = C.sb("rkdt", [128, 512], st=st)
    am = C.sb("ram", [128, 128], st=st)
    oh = C.sb("roh", [128, 512], st=st)
    sg = C.sb("rsg", [128, 512], st=st)
    stt = C.sb("rstt", [128, 16], st=st)
    ptp = [C.ps(f"rpt{i}", [128, 128], st=st) for i in range(2)]
    pa = C.ps("rpa", [128, 128], st=st)
    po = C.ps("rpo", [128, 128], st=st)
    pS = C.ps("rpS", [128, 128], st=st)

    def chunk(i, r0, L, dm, qdd, kdd, cdec):
        q, v = qk[i % 2], vg[i % 2]
        S.dma("sp", q[:L, :], dr["QK"][r0:r0 + L, :], writes=[q])
        S.dma("sp", v[:L, :], dr["PROJ"][r0:r0 + L, 1024:2048], writes=[v])
        for h in range(4):
            transpose_to(C, q, L, h * 128, 128, lambda: qT[:, h, :L], qT, ptp, 2 * h)
            transpose_to(C, q, L, 512 + h * 128, 128, lambda: kT[:, h, :L], kT, ptp, 2 * h + 1)
        S.op("pool", lambda e: e.tensor_tensor(out=qdT[:, :, :L], in0=qT[:, :, :L], in1=qdd[:, :, :L], op=ALU.mult), reads=[qT, qdd], writes=[qdT])
        S.op("pool", lambda e: e.tensor_tensor(out=v3(kdt[:L, :], 4), in0=v3(q[:L, 512:1024], 4),
                                               in1=kdd[:L, :].unsqueeze(2).to_broadcast([L, 4, 128]), op=ALU.mult), reads=[q, kdd], writes=[kdt])
        for h in range(4):
            S.op("pe", lambda e: e.matmul(pa[:L, :L], lhsT=kT[:, h, :L], rhs=qT[:, h, :L], start=True, stop=True), reads=[kT, qT], writes=[pa])
            S.op("dve", lambda e: e.tensor_tensor(out=am[:L, :L], in0=pa[:L, :L], in1=dm[:L, h, :L], op=ALU.mult), reads=[pa, dm], writes=[am])
            S.op("pe", lambda e: e.matmul(po[:L, :], lhsT=am[:L, :L], rhs=v[:L, h * 128:(h + 1) * 128], start=True, stop=False), reads=[am, v], writes=[po])
            S.op("pe", lambda e: e.matmul(po[:L, :], lhsT=qdT[:, h, :L], rhs=Sst[:, h, :], start=False, stop=True), reads=[qdT, Sst], writes=[po])
            S.op("act", lambda e: e.copy(out=oh[:L, h * 128:(h + 1) * 128], in_=po[:L, :]), reads=[po], writes=[oh])
            S.op("pe", lambda e: e.matmul(pS[:, :], lhsT=kdt[:L, h * 128:(h + 1) * 128], rhs=v[:L, h * 128:(h + 1) * 128], start=True, stop=True), reads=[kdt, v], writes=[pS])
            S.op("dve", lambda e: e.scalar_tensor_tensor(out=Sst[:, h, :], in0=Sst[:, h, :], scalar=float(cdec[h]), in1=pS[:, :],
                                                         op0=ALU.mult, op1=ALU.add), reads=[Sst, pS], writes=[Sst])
        o3 = v3(oh[:L, :], 4)
        S.op("dve", lambda e: e.tensor_reduce(out=stt[:L, 0:4], in_=o3, axis=AX.X, op=ALU.add), reads=[oh], writes=[stt])
        S.op("dve", lambda e: e.tensor_scalar(out=stt[:L, 0:4], in0=stt[:L, 0:4], scalar1=1.0 / 128, scalar2=None, op0=ALU.mult), reads=[stt], writes=[stt])
        S.op("dve", lambda e: e.tensor_tensor(out=o3, in0=o3, in1=stt[:L, 0:4].unsqueeze(2).to_broadcast([L, 4, 128]), op=ALU.subtract), reads=[oh, stt], writes=[oh])
        S.op("pool", lambda e: e.tensor_tensor(out=sg[:L, :], in0=oh[:L, :], in1=oh[:L, :], op=ALU.mult), reads=[oh], writes=[sg])
        S.op("dve", lambda e: e.tensor_reduce(out=stt[:L, 4:8], in_=v3(sg[:L, :], 4), axis=AX.X, op=ALU.add), reads=[sg], writes=[stt])
        S.op("dve", lambda e: e.tensor_scalar(out=stt[:L, 4:8], in0=stt[:L, 4:8], scalar1=1.0 / 128, scalar2=1e-6, op0=ALU.mult, op1=ALU.add), reads=[stt], writes=[stt])
        S.op("act", lambda e: e.sqrt(out=stt[:L, 4:8], in_=stt[:L, 4:8]), reads=[stt], writes=[stt])
        S.op("dve", lambda e: e.reciprocal(out=stt[:L, 4:8], in_=stt[:L, 4:8]), reads=[stt], writes=[stt])
        S.op("dve", lambda e: e.tensor_tensor(out=o3, in0=o3, in1=stt[:L, 4:8].unsqueeze(2).to_broadcast([L, 4, 128]), op=ALU.mult), reads=[oh, stt], writes=[oh])
        S.op("dve", lambda e: e.tensor_tensor(out=oh[:L, :], in0=oh[:L, :], in1=gw[:L, :], op=ALU.mult), reads=[oh, gw], writes=[oh])
        S.op("dve", lambda e: e.tensor_tensor(out=oh[:L, :], in0=oh[:L, :], in1=gb[:L, :], op=ALU.add), reads=[oh, gb], writes=[oh])
        S.op("act", lambda e: e.activation(out=sg[:L, :], in_=v[:L, 512:1024], func=AF.Silu), reads=[v], writes=[sg])
        S.op("dve", lambda e: e.tensor_tensor(out=oh[:L, :], in0=oh[:L, :], in1=sg[:L, :], op=ALU.mult), reads=[oh, sg], writes=[oh])
        S.dma("pool", dr["OCAT"][r0:r0 + L, 0:512], oh[:L, :], reads=[oh])

    gam = [1.0 - 2.0 ** (-5.0 - h) for h in range(4)]
    S.op("dve", lambda e: e.memset(Sst[:], 0.0), writes=[Sst])
    for i in range(T // 128):
        chunk(i, i * 128, 128, dmT, qd, kd, [g ** 128 for g in gam])
    S.dma("pool", dr["o_ret"][l, 0].rearrange("h d e -> d h e"), Sst[:], reads=[Sst])
    for b in range(NS):
        S.dma("sp", Sst[:], dr["st_ret"][l, b].rearrange("h d e -> d h e"), writes=[Sst])
        chunk(b, T + 8 * b, 8, dmT8, qd8, kd8, [g ** 8 for g in gam])
        S.dma("pool", dr["o_ret"][l, 1 + b].rearrange("h d e -> d h e"), Sst[:], reads=[Sst])


def _stub(C, st, l):
    pass


def _tables(cfg, past_len):
    T, NS, NT = cfg.T, cfg.NS, cfg.NT
    f = np.float32
    pos = np.concatenate([np.arange(T), np.tile(past_len + np.arange(8), NS)]).astype(f)
    inv_r = np.power(f(10000.0), -np.arange(64, dtype=f) / f(64)).astype(f)
    ang = pos[:, None] * inv_r[None, :]
    rope_r = np.stack([np.cos(ang), np.sin(ang)], 1).astype(f)
    inv_d = np.power(f(500000.0), -np.arange(8, dtype=f) / f(8)).astype(f)
    angd = pos[:, None] * inv_d[None, :]
    rope_d = np.stack([np.cos(angd), np.sin(angd)], 1).astype(f)
    log_g = np.log1p(-np.exp2(-5.0 - np.arange(4))).astype(np.float64)

    def dm(L):
        idx = np.arange(L)
        rel = idx[:, None] - idx[None, :]
        d = np.where(rel >= 0, np.exp(log_g[:, None, None] * np.maximum(rel, 0)), 0.0)
        return np.ascontiguousarray(d.transpose(0, 2, 1)).astype(f)

    def qd(L):
        v = np.exp(log_g[:, None] * (np.arange(L) + 1.0))
        return np.ascontiguousarray(np.broadcast_to(v[None], (128, 4, L))).astype(f)

    def kd(L):
        v = np.exp(log_g[:, None] * (L - 1.0 - np.arange(L)))
        return np.ascontiguousarray(v.T).astype(f)

    def cm(L):
        idx = np.arange(L)
        return np.where(idx[None, :] <= idx[:, None], 0.0, -1e30).astype(f)
    mask8 = np.zeros((8, 512), f)
    for h in range(8):
        mask8[h, h * 64:(h + 1) * 64] = 1.0
    return dict(ident=np.eye(128, dtype=f), rope_r=rope_r, rope_d=rope_d, ret_dmT=dm(128), ret_dmT8=dm(8),
                ret_qd=qd(128), ret_kd=kd(128), ret_qd8=qd(8), ret_kd8=kd(8), cmask=cm(128), cmask8=cm(8),
                mask8=mask8, tidx=np.ascontiguousarray(np.broadcast_to(np.arange(130, dtype=f)[None], (128, 130))),
                iota=np.arange(128, dtype=np.int32).reshape(128, 1))


def _gn(a):
    return np.ascontiguousarray(a.reshape(2, 16, 2, 64).transpose(0, 2, 3, 1).reshape(2, 128, 16))


def kernel(cfg=None, **inp):
    f = np.float32
    if cfg is None:
        cfg = Cfg()
    T, NS, NPG = cfg.T, cfg.NS, cfg.NPG
    past_len = NPG * 128
    A = {k: np.asarray(v) for k, v in inp.items()}
    shared = {}
    for k in ["norm_mix", "w_in", "w_out", "diff_subln", "rwkv_mu", "rwkv_w0", "rwkv_w2", "rwkv_a0", "rwkv_a2", "rwkv_g2",
              "rwkv_kk", "rwkv_ka", "s5_d", "s5_w_glu", "s5_b_glu", "s5_norm", "norm_ffn", "ffn_w_up", "ffn_w_down",
              "norm_ple", "ple_w_proj", "ple_norm_e", "ple_w_gate", "norm_final"]:
        shared[k] = np.ascontiguousarray(A[k], dtype=f)
    for k in ["ret_norm_w", "ret_norm_b", "rwkv_rk", "rwkv_ln_w", "rwkv_ln_b"]:
        shared[k] = np.ascontiguousarray(A[k].reshape(2, 512), dtype=f)
    shared["diff_l"] = np.ascontiguousarray(np.stack([A["diff_lq1"], A["diff_lk1"], A["diff_lq2"], A["diff_lk2"]], 1), dtype=f)
    shared["ck"] = np.ascontiguousarray(A["cache_k"].reshape(2, -1, 512), dtype=f)
    shared["cv"] = np.ascontiguousarray(A["cache_v"].reshape(2, -1, 512), dtype=f)
    shared["s5_lre"] = _gn(A["s5_lam_re"]); shared["s5_lim"] = _gn(A["s5_lam_im"])
    shared["s5_ls"] = _gn(np.broadcast_to(A["s5_log_step"][:, :, None], (2, 32, 64)))
    for nm, src in (("s5_bre", "s5_b_re"), ("s5_bim", "s5_b_im")):
        b = A[src].reshape(2, 16, 2, 64, 16)
        e = np.zeros((2, 128, 16, 128), f)
        for i in range(16):
            for gl in range(2):
                c0 = (i % 4) * 32 + gl * 16
                e[:, gl * 64:(gl + 1) * 64, i, c0:c0 + 16] = b[:, i, gl]
        shared[nm] = e
    for nm, src in (("s5_cre", "s5_c_re"), ("s5_cim", "s5_c_im")):
        c = A[src].reshape(2, 16, 2, 16, 64)
        e = np.zeros((2, 128, 16, 32), f)
        for i in range(16):
            for gl in range(2):
                e[:, gl * 64:(gl + 1) * 64, i, gl * 16:(gl + 1) * 16] = c[:, i, gl].transpose(0, 2, 1)
        shared[nm] = e
    shared["ffn_cw"] = np.ascontiguousarray(A["ffn_conv_w"].reshape(2, 3, 88, 128).transpose(0, 3, 2, 1), dtype=f)
    shared["ffn_cb"] = np.ascontiguousarray(A["ffn_conv_b"].reshape(2, 88, 128).transpose(0, 2, 1), dtype=f)
    shared.update(_tables(cfg, past_len))
    in_maps = []
    for c in range(NCORES):
        sl = slice(c * NS, (c + 1) * NS)
        m = dict(shared)
        m["x0"] = np.ascontiguousarray(np.concatenate([A["x_prompt"][0], A["x_sample"][sl].reshape(NS * 8, D)], 0), dtype=f)
        m["p0"] = np.ascontiguousarray(np.concatenate([A["p_prompt"][:, 0], A["p_sample"][:, sl].reshape(2, NS * 8, 256)], 1), dtype=f)
        m["pt"] = np.ascontiguousarray(A["page_table"][sl].reshape(-1), dtype=np.int32)
        m["st_ret"] = np.ascontiguousarray(A["state_ret"][:, sl], dtype=f)
        m["st_rwkvT"] = np.ascontiguousarray(A["state_rwkv"][:, sl].transpose(0, 1, 4, 2, 3), dtype=f)
        m["st_shift"] = np.ascontiguousarray(A["state_rwkv_shift"][:, sl], dtype=f)
        for nm, src in (("st_s5re", "state_s5_re"), ("st_s5im", "state_s5_im")):
            s = A[src][:, sl].reshape(2, NS, 16, 2, 64)
            m[nm] = np.ascontiguousarray(s.transpose(0, 3, 4, 1, 2).reshape(2, 128, NS, 16), dtype=f)
        cs = A["state_ffn_conv"][:, sl].reshape(2, NS, 2, 88, 128)
        m["st_convT"] = np.ascontiguousarray(cs.transpose(0, 4, 3, 1, 2), dtype=f)
        in_maps.append(m)
    nc, S = build(cfg)
    res = run_bass_kernel_spmd(nc, in_maps, core_ids=list(range(NCORES)))
    R = res.results
    kernel.last = R

    def cat_seq(name, fn):
        return fn(R[0][name], True), np.concatenate([fn(R[c][name], False) for c in range(NCORES)], axis=1)
    y_p = R[0]["y"][:T][None]
    y_s = np.concatenate([R[c]["y"][T:].reshape(NS, 8, D) for c in range(NCORES)], 0)
    k_p = R[0]["k_new"][:, :T].reshape(2, 1, T, 4, 128)
    v_p = R[0]["v_new"][:, :T].reshape(2, 1, T, 4, 128)
    k_s = np.concatenate([R[c]["k_new"][:, T:].reshape(2, NS, 8, 4, 128) for c in range(NCORES)], 1)
    v_s = np.concatenate([R[c]["v_new"][:, T:].reshape(2, NS, 8, 4, 128) for c in range(NCORES)], 1)
    ret_p = R[0]["o_ret"][:, 0:1]
    ret_s = np.concatenate([R[c]["o_ret"][:, 1:] for c in range(NCORES)], 1)
    rw = lambda a: a.transpose(0, 1, 3, 4, 2)
    rw_p = rw(R[0]["o_rwkvT"][:, 0:1])
    rw_s = np.concatenate([rw(R[c]["o_rwkvT"][:, 1:]) for c in range(NCORES)], 1)
    sh_p = R[0]["o_shift"][:, 0:1]
    sh_s = np.concatenate([R[c]["o_shift"][:, 1:] for c in range(NCORES)], 1)

    def s5(a):
        l_, _, b_, _ = a.shape
        return a.reshape(l_, 2, 64, b_, 16).transpose(0, 3, 4, 1, 2).reshape(l_, b_, 32, 64)
    s5r_p = s5(R[0]["o_s5re"][:, :, 0:1]); s5i_p = s5(R[0]["o_s5im"][:, :, 0:1])
    s5r_s = np.concatenate([s5(R[c]["o_s5re"][:, :, 1:]) for c in range(NCORES)], 1)
    s5i_s = np.concatenate([s5(R[c]["o_s5im"][:, :, 1:]) for c in range(NCORES)], 1)

    def cv(a):
        l_, _, _, b_, _ = a.shape
        return a.transpose(0, 3, 4, 2, 1).reshape(l_, b_, 2, 2 * DFF)
    cv_p = cv(R[0]["o_convT"][:, :, :, 0:1])
    cv_s = np.concatenate([cv(R[c]["o_convT"][:, :, :, 1:]) for c in range(NCORES)], 1)
    outs = (y_p, y_s, k_p, v_p, k_s, v_s, ret_p, ret_s, rw_p, rw_s, sh_p, sh_s, s5r_p, s5i_p, s5r_s, s5i_s, cv_p, cv_s)
    return tuple(np.ascontiguousarray(o, dtype=f) for o in outs)


def load_T(C, stl, src_ap_fn, ncols, xs, hT, ptp, norm=None):
    S = C.S
    sizes = []
    off = 0
    for n, (r0, P) in enumerate(stl):
        x = xs[n % len(xs)]
        S.dma("sp", x[:P, :ncols], src_ap_fn(r0, P), writes=[x])
        src = x
        if norm is not None:
            g, hs, ss, sq = norm
            rms_rows(C, x, P, ncols, g, hs, ss, sq)
            src = hs
        for k in range(ncols // 128):
            transpose_to(C, src, P, k * 128, 128, lambda: hT[:, k, off:off + P], hT, ptp, k)
        sizes.append((r0, P, off))
        off += P
    return sizes


def phase_wout(C, st, l):
    S, dr, cfg = C.S, C.dr, C.cfg
    xs = [C.sb(f"wx{i}", [128, D], st=st) for i in range(2)]
    hT = C.sb("whT", [128, 16, 512], st=st)
    wb = [C.sb(f"ww{i}", [128, 16, 512], st=st) for i in range(2)]
    xres = [C.sb(f"wxr{i}", [128, D], st=st) for i in range(4)]
    ptp = [C.ps(f"wpt{i}", [128, 128], st=st) for i in range(2)]
    pb = [C.ps(f"wpo{i}", [128, 512], st=st) for i in range(4)]
    xsrc = dr["x0"] if l == 0 else dr["X"]
    for stl in cfg.sts:
        sizes = load_T(C, stl, lambda r0, P: dr["OCAT"][r0:r0 + P, :], D, xs, hT, ptp)
        for ti, (r0, P, off) in enumerate(sizes):
            S.dma("sp", xres[ti][:P, :], xsrc[r0:r0 + P, :], writes=[xres[ti]])

        def epi(ti, r0, P, c0, ncol, po):
            xr = xres[ti]
            S.op("dve", lambda e: e.tensor_tensor(out=xr[:P, c0:c0 + ncol], in0=xr[:P, c0:c0 + ncol], in1=po[:P, :ncol], op=ALU.add),
                 reads=[xr, po], writes=[xr])
        dense(C, hT, sizes, dr["w_out"][l], 16, D, wb, pb, epi)
        for ti, (r0, P, off) in enumerate(sizes):
            S.dma("pool", dr["X"][r0:r0 + P, :], xres[ti][:P, :], reads=[xres[ti]])


def phase_ffn(C, st, l):
    S, dr, cfg = C.S, C.dr, C.cfg
    T, NS, NTS = cfg.T, cfg.NS, cfg.NTS
    g = bcast_load(C, st, "fg", dr["norm_ffn"][l], D)
    xs = [C.sb("fx0", [128, D], st=st)]
    hs = C.sb("fh", [128, D], st=st)
    sq = C.sb("fsq", [128, D], st=st)
    ss = C.sb("fss", [128, 2], st=st)
    hT = C.sb("fhT", [128, 16, 256], st=st)
    actT = C.sb("factT", [128, 44, 256], st=st)
    wu = [C.sb(f"fwu{i}", [128, 16, 128], st=st) for i in range(4)]
    wd = [C.sb(f"fwd{i}", [128, 11, 512], st=st) for i in range(2)]
    halo = C.sb("fhalo", [128, 88, 2], st=st)
    cw = C.sb("fcw", [128, 88, 3], st=st); S.dma("sp", cw[:], dr["ffn_cw"][l], writes=[cw])
    cbb = C.sb("fcb", [128, 88], st=st); S.dma("sp", cbb[:], dr["ffn_cb"][l], writes=[cbb])
    ext = [C.sb(f"fext{i}", [128, 320], st=st) for i in range(2)]
    cv_ = [C.sb(f"fcv{i}", [128, 256], st=st) for i in range(2)]
    sgl = C.sb("fsgl", [128, 256], st=st)
    xr = [C.sb(f"fxr{i}", [128, 512], st=st) for i in range(2)]
    ptp = [C.ps(f"fpt{i}", [128, 128], st=st) for i in range(2)]
    pu = [C.ps(f"fpu{i}", [128, 256], st=st) for i in range(2)]
    pd = [C.ps(f"fpd{i}", [128, 512], st=st) for i in range(2)]
    S.op("dve", lambda e: e.memset(halo[:], 0.0), writes=[halo])
    wup = dr["ffn_w_up"][l].rearrange("(k p) c -> p k c", p=128)
    wdn = dr["ffn_w_down"][l].rearrange("(j p) c -> p j c", p=128)
    cnt = 0
    xcnt = 0
    for si, stl in enumerate(cfg.sts2):
        sample = (stl[0][0] == T)
        sizes = load_T(C, stl, lambda r0, P: dr["X"][r0:r0 + P, :], D, xs, hT, ptp, norm=(g, hs, ss, sq))
        n = sum(P for _, P, _ in sizes)
        for j in range(44):
            for half in range(2):
                ct = j + 44 * half
                w = wu[cnt % 4]
                p_ = pu[cnt % 2]
                ex = ext[cnt % 2]
                cvt = cv_[half]
                cnt += 1
                S.dma("sp", w[:], wup[:, :, ct * 128:(ct + 1) * 128], writes=[w])
                for k in range(16):
                    S.op("pe", lambda e: e.matmul(p_[:, :n], lhsT=w[:, k, :], rhs=hT[:, k, :n], start=(k == 0), stop=(k == 15)),
                         reads=[w, hT], writes=[p_])
                if not sample:
                    e0 = lambda a, b: ex[:, a:b]
                    S.op("pool", lambda e: e.tensor_copy(out=ex[:, 0:2], in_=halo[:, ct, :]), reads=[halo], writes=[ex])
                    S.op("act", lambda e: e.copy(out=ex[:, 2:2 + n], in_=p_[:, :n]), reads=[p_], writes=[ex])
                    S.op("pool", lambda e: e.tensor_copy(out=halo[:, ct, :], in_=ex[:, n:n + 2]), reads=[ex], writes=[halo])
                    sl = [ex[:, 0:n], ex[:, 1:1 + n], ex[:, 2:2 + n]]
                    cvo = cvt[:, :n]
                else:
                    e3 = ex[:, 0:NS * 10].rearrange("p (b t) -> p b t", t=10)
                    S.dma("sp", e3[:, :, 0:2], dr["st_convT"][l, :, ct, :, :], writes=[ex])
                    S.op("act", lambda e: e.copy(out=e3[:, :, 2:10], in_=p_[:, :n].rearrange("p (b t) -> p b t", t=8)), reads=[p_], writes=[ex])
                    S.dma("pool", dr["o_convT"][l, :, ct, 1:, :], e3[:, :, 8:10], reads=[ex])
                    sl = [e3[:, :, 0:8], e3[:, :, 1:9], e3[:, :, 2:10]]
                    cvo = cvt[:, :n].rearrange("p (b t) -> p b t", t=8)
                S.op("act", lambda e: e.activation(out=cvo, in_=sl[2], func=AF.Identity, bias=cbb[:, ct:ct + 1], scale=cw[:, ct, 2:3]),
                     reads=[ex, cbb, cw], writes=[cvt])
                S.op("dve", lambda e: e.scalar_tensor_tensor(out=cvo, in0=sl[1], scalar=cw[:, ct, 1:2], in1=cvo, op0=ALU.mult, op1=ALU.add),
                     reads=[ex, cw, cvt], writes=[cvt])
                S.op("dve", lambda e: e.scalar_tensor_tensor(out=cvo, in0=sl[0], scalar=cw[:, ct, 0:1], in1=cvo, op0=ALU.mult, op1=ALU.add),
                     reads=[ex, cw, cvt], writes=[cvt])
            S.op("act", lambda e: e.activation(out=sgl[:, :n], in_=cv_[0][:, :n], func=AF.Silu), reads=[cv_[0]], writes=[sgl])
            S.op("dve", lambda e: e.tensor_tensor(out=actT[:, j, :n], in0=sgl[:, :n], in1=cv_[1][:, :n], op=ALU.mult),
                 reads=[sgl, cv_[1]], writes=[actT])
        if stl[-1][0] + stl[-1][1] == T:
            S.dma("pool", dr["o_convT"][l, :, :, 0, :], halo[:], reads=[halo])
        for c0 in range(0, D, 512):
            for jq in range(4):
                w = wd[xcnt % 2]
                xcnt += 1
                S.dma("sp", w[:], wdn[:, jq * 11:(jq + 1) * 11, c0:c0 + 512], writes=[w])
                for ti, (r0, P, off) in enumerate(sizes):
                    for jj in range(11):
                        j = jq * 11 + jj
                        S.op("pe", lambda e: e.matmul(pd[ti][:P, :], lhsT=actT[:, j, off:off + P], rhs=w[:, jj, :],
                                                      start=(j == 0), stop=(j == 43)), reads=[actT, w], writes=[pd[ti]])
            for ti, (r0, P, off) in enumerate(sizes):
                x = xr[ti]
                S.dma("sp", x[:P, :], dr["X"][r0:r0 + P, c0:c0 + 512], writes=[x])
                S.op("dve", lambda e: e.tensor_tensor(out=x[:P, :], in0=x[:P, :], in1=pd[ti][:P, :], op=ALU.add), reads=[x, pd[ti]], writes=[x])
                S.dma("pool", dr["X"][r0:r0 + P, c0:c0 + 512], x[:P, :], reads=[x])


def phase_ple(C, st0, l):
    from contextlib import ExitStack
    S, dr, cfg = C.S, C.dr, C.cfg
    with ExitStack() as st:
        xs = [C.sb(f"ep{i}", [128, 256], st=st) for i in range(2)]
        pT = C.sb("epT", [128, 2, 512], st=st)
        wb = [C.sb(f"ew{i}", [128, 2, 512], st=st) for i in range(2)]
        ob = [C.sb(f"eob{i}", [128, 512], st=st) for i in range(4)]
        ptp = [C.ps(f"ept{i}", [128, 128], st=st) for i in range(2)]
        pb = [C.ps(f"epo{i}", [128, 512], st=st) for i in range(4)]
        cnt = [0]
        for stl in cfg.sts:
            sizes = load_T(C, stl, lambda r0, P: dr["p0"][l, r0:r0 + P, :], 256, xs, pT, ptp)

            def epi(ti, r0, P, c0, ncol, po):
                o = ob[cnt[0] % 4]
                cnt[0] += 1
                S.op("act", lambda e: e.copy(out=o[:P, :ncol], in_=po[:P, :ncol]), reads=[po], writes=[o])
                S.dma("pool", dr["ERAW"][r0:r0 + P, c0:c0 + ncol], o[:P, :ncol], reads=[o])
            dense(C, pT, sizes, dr["ple_w_proj"][l], 2, D, wb, pb, epi)
    S.barrier()
    with ExitStack() as st:
        g = bcast_load(C, st, "gg", dr["norm_ple"][l], D)
        ge = bcast_load(C, st, "gge", dr["ple_norm_e"][l], D)
        gf = bcast_load(C, st, "ggf", dr["norm_final"], D)
        xres = [C.sb(f"gx{i}", [128, D], st=st) for i in range(2)]
        et = [C.sb(f"ge{i}", [128, D], st=st) for i in range(2)]
        hs = C.sb("gh", [128, D], st=st)
        sq = C.sb("gsq", [128, D], st=st)
        ss = C.sb("gss", [128, 2], st=st)
        hT = C.sb("ghT", [128, 16, 256], st=st)
        wb = [C.sb(f"gw{i}", [128, 16, 512], st=st) for i in range(2)]
        sg = [C.sb(f"gsg{i}", [128, 512], st=st) for i in range(2)]
        ptp = [C.ps(f"gpt{i}", [128, 128], st=st) for i in range(2)]
        pb = [C.ps(f"gpo{i}", [128, 512], st=st) for i in range(4)]
        cnt = [0]
        for stl in cfg.sts2:
            sizes = []
            off = 0
            for ti, (r0, P) in enumerate(stl):
                x = xres[ti]
                S.dma("sp", x[:P, :], dr["X"][r0:r0 + P, :], writes=[x])
                rms_rows(C, x, P, D, g, hs, ss, sq)
                for k in range(16):
                    transpose_to(C, hs, P, k * 128, 128, lambda: hT[:, k, off:off + P], hT, ptp, k)
                S.dma("sp", hs[:P, :], dr["ERAW"][r0:r0 + P, :], writes=[hs])
                rms_rows(C, hs, P, D, ge, et[ti], ss, sq)
                sizes.append((r0, P, off))
                off += P

            def epi(ti, r0, P, c0, ncol, po):
                s_ = sg[cnt[0] % 2]
                cnt[0] += 1
                S.op("act", lambda e: e.activation(out=s_[:P, :ncol], in_=po[:P, :ncol], func=AF.Sigmoid), reads=[po], writes=[s_])
                S.op("dve", lambda e: e.tensor_tensor(out=s_[:P, :ncol], in0=s_[:P, :ncol], in1=et[ti][:P, c0:c0 + ncol], op=ALU.mult),
                     reads=[s_, et[ti]], writes=[s_])
                S.op("pool", lambda e: e.tensor_tensor(out=xres[ti][:P, c0:c0 + ncol], in0=xres[ti][:P, c0:c0 + ncol], in1=s_[:P, :ncol], op=ALU.add),
                     reads=[s_, xres[ti]], writes=[xres[ti]])
            dense(C, hT, sizes, dr["ple_w_gate"][l], 16, D, wb, pb, epi)
            for ti, (r0, P, off) in enumerate(sizes):
                S.dma("pool", dr["X"][r0:r0 + P, :], xres[ti][:P, :], reads=[xres[ti]])
                if l == 1:
                    rms_rows(C, xres[ti], P, D, gf, hs, ss, sq)
                    S.dma("pool", dr["y"][r0:r0 + P, :], hs[:P, :], reads=[hs])


def phase_s5(C, st0, l):
    from contextlib import ExitStack
    S, dr, cfg = C.S, C.dr, C.cfg
    T, NS = cfg.T, cfg.NS
    PI = math.pi
    with ExitStack() as st:
        def ld(name, ap, shape):
            t = C.sb(name, shape, st=st)
            S.dma("sp", t[:], ap, writes=[t])
            return t
        lre = ld("slre", dr["s5_lre"][l], [128, 16]); lim = ld("slim", dr["s5_lim"][l], [128, 16]); ls = ld("sls", dr["s5_ls"][l], [128, 16])
        BRe = ld("sBRe", dr["s5_bre"][l], [128, 16, 128]); BIe = ld("sBIe", dr["s5_bim"][l], [128, 16, 128])
        CRe = ld("sCRe", dr["s5_cre"][l], [128, 16, 32]); CIm = ld("sCIm", dr["s5_cim"][l], [128, 16, 32])
        tidx = ld("stidx", dr["tidx"], [128, 130])
        sm = C.sb("ssm", [128, 16, 16], st=st)
        K = lambda k: sm[:, k, :]
        def tt(eng, out, a, b, op, rd, wr):
            S.op(eng, lambda e: e.tensor_tensor(out=out, in0=a, in1=b, op=op), reads=rd, writes=wr)
        def ts(eng, out, a, s1, s2, op0, op1, rd, wr):
            if op1 is None:
                S.op(eng, lambda e: e.tensor_scalar(out=out, in0=a, scalar1=s1, scalar2=None, op0=op0), reads=rd, writes=wr)
            else:
                S.op(eng, lambda e: e.tensor_scalar(out=out, in0=a, scalar1=s1, scalar2=s2, op0=op0, op1=op1), reads=rd, writes=wr)
        def act(out, a, fn, rd, wr, **kw):
            S.op("act", lambda e: e.activation(out=out, in_=a, func=fn, **kw), reads=rd, writes=wr)
        mpi = C.eps[:, 1:2]
        rti = C.sb("srti", [128, 129], I32, st=st); rtf = C.sb("srtf", [128, 129], st=st); rtx = C.sb("srtx", [128, 129], st=st)
        def sinr(out, x, n, shift, rd, wr):
            S.op("dve", lambda e: e.tensor_scalar(out=rtx[:, :n], in0=x, scalar1=shift, scalar2=None, op0=ALU.add), reads=rd, writes=[rtx])
            S.op("dve", lambda e: e.tensor_scalar(out=rti[:, :n], in0=rtx[:, :n], scalar1=1.0 / (2 * PI), scalar2=None, op0=ALU.mult), reads=[rtx], writes=[rti])
            S.op("dve", lambda e: e.tensor_copy(out=rtf[:, :n], in_=rti[:, :n]), reads=[rti], writes=[rtf])
            S.op("dve", lambda e: e.scalar_tensor_tensor(out=rtx[:, :n], in0=rtf[:, :n], scalar=-2 * PI, in1=rtx[:, :n], op0=ALU.mult, op1=ALU.add), reads=[rtf, rtx], writes=[rtx])
            S.op("dve", lambda e: e.tensor_scalar(out=rtf[:, :n], in0=rtx[:, :n], scalar1=PI, scalar2=2 * PI, op0=ALU.is_gt, op1=ALU.mult), reads=[rtx], writes=[rtf])
            S.op("dve", lambda e: e.tensor_tensor(out=rtx[:, :n], in0=rtx[:, :n], in1=rtf[:, :n], op=ALU.subtract), reads=[rtx, rtf], writes=[rtx])
            S.op("act", lambda e: e.activation(out=out, in_=rtx[:, :n], func=AF.Sin), reads=[rtx], writes=wr)
        act(K(0), ls[:], AF.Exp, [ls], [sm])
        tt("dve", K(1), lre[:], K(0), ALU.mult, [lre, sm], [sm])
        tt("dve", K(2), lim[:], K(0), ALU.mult, [lim, sm], [sm])
        ts("dve", K(3), K(1), -1.0, None, ALU.mult, None, [sm], [sm])
        act(K(9), K(1), AF.Exp, [sm], [sm])
        sinr(K(10), K(2), 16, 0.0, [sm], [sm])
        sinr(K(11), K(2), 16, 0.5 * PI, [sm], [sm])
        tt("dve", K(4), K(9), K(11), ALU.mult, [sm], [sm])
        tt("dve", K(5), K(9), K(10), ALU.mult, [sm], [sm])
        tt("dve", K(9), lre[:], lre[:], ALU.mult, [lre], [sm])
        tt("dve", K(10), lim[:], lim[:], ALU.mult, [lim], [sm])
        tt("dve", K(9), K(9), K(10), ALU.add, [sm], [sm])
        S.op("dve", lambda e: e.reciprocal(out=K(9), in_=K(9)), reads=[sm], writes=[sm])
        ts("dve", K(10), K(4), -1.0, None, ALU.add, None, [sm], [sm])
        tt("dve", K(11), K(10), lre[:], ALU.mult, [sm, lre], [sm])
        tt("dve", K(12), K(5), lim[:], ALU.mult, [sm, lim], [sm])
        tt("dve", K(11), K(11), K(12), ALU.add, [sm], [sm])
        tt("dve", K(6), K(11), K(9), ALU.mult, [sm], [sm])
        tt("dve", K(11), K(5), lre[:], ALU.mult, [sm, lre], [sm])
        tt("dve", K(12), K(10), lim[:], ALU.mult, [sm, lim], [sm])
        tt("dve", K(11), K(11), K(12), ALU.subtract, [sm], [sm])
        tt("dve", K(7), K(11), K(9), ALU.mult, [sm], [sm])
        ts("dve", K(8), K(7), -1.0, None, ALU.mult, None, [sm], [sm])
        S.op("dve", lambda e: e.tensor_scalar(out=CIm[:], in0=CIm[:], scalar1=-1.0, scalar2=None, op0=ALU.mult), reads=[CIm], writes=[CIm])
        BBTr = C.sb("sBBTr", [128, 16, 128], st=st); BBTi = C.sb("sBBTi", [128, 16, 128], st=st)
        t1 = C.sb("st1", [128, 128], st=st); t2 = C.sb("st2", [128, 128], st=st)
        ptp = [C.ps(f"spt{i}", [128, 128], st=st) for i in range(2)]
        for i in range(16):
            S.op("dve", lambda e: e.tensor_scalar(out=t1[:], in0=BRe[:, i, :], scalar1=sm[:, 6, i:i + 1], scalar2=None, op0=ALU.mult), reads=[BRe, sm], writes=[t1])
            S.op("dve", lambda e: e.scalar_tensor_tensor(out=t1[:], in0=BIe[:, i, :], scalar=sm[:, 8, i:i + 1], in1=t1[:], op0=ALU.mult, op1=ALU.add), reads=[BIe, sm, t1], writes=[t1])
            transpose_to(C, t1, 128, 0, 128, lambda: BBTr[:, i, :], BBTr, ptp, 2 * i)
            S.op("dve", lambda e: e.tensor_scalar(out=t2[:], in0=BIe[:, i, :], scalar1=sm[:, 6, i:i + 1], scalar2=None, op0=ALU.mult), reads=[BIe, sm], writes=[t2])
            S.op("dve", lambda e: e.scalar_tensor_tensor(out=t2[:], in0=BRe[:, i, :], scalar=sm[:, 7, i:i + 1], in1=t2[:], op0=ALU.mult, op1=ALU.add), reads=[BRe, sm, t2], writes=[t2])
            transpose_to(C, t2, 128, 0, 128, lambda: BBTi[:, i, :], BBTi, ptp, 2 * i + 1)
        PWr = C.sb("sPWr", [128, 16, 129], st=st); PWi = C.sb("sPWi", [128, 16, 129], st=st)
        PIr = C.sb("sPIr", [128, 16, 129], st=st); PIi = C.sb("sPIi", [128, 16, 129], st=st)
        ta = C.sb("sta", [128, 129], st=st); tb = C.sb("stb", [128, 129], st=st); tc = C.sb("stc", [128, 129], st=st); td = C.sb("std", [128, 129], st=st)
        for i in range(16):
            S.op("dve", lambda e: e.tensor_scalar(out=ta[:], in0=tidx[:, 0:129], scalar1=sm[:, 2, i:i + 1], scalar2=None, op0=ALU.mult), reads=[tidx, sm], writes=[ta])
            sinr(tb[:], ta[:], 129, 0.0, [ta], [tb])
            sinr(tc[:], ta[:], 129, 0.5 * PI, [ta], [tc])
            act(td[:], tidx[:, 0:129], AF.Exp, [tidx, sm], [td], scale=sm[:, 1, i:i + 1])
            tt("dve", PWr[:, i, :], td[:], tc[:], ALU.mult, [td, tc], [PWr])
            tt("dve", PWi[:, i, :], td[:], tb[:], ALU.mult, [td, tb], [PWi])
            act(td[:], tidx[:, 0:129], AF.Exp, [tidx, sm], [td], scale=sm[:, 3, i:i + 1])
            tt("dve", PIr[:, i, :], td[:], tc[:], ALU.mult, [td, tc], [PIr])
            S.op("dve", lambda e: e.scalar_tensor_tensor(out=PIi[:, i, :], in0=td[:], scalar=-1.0, in1=tb[:], op0=ALU.mult, op1=ALU.mult), reads=[td, tb], writes=[PIi])
        ones = C.sb("sones", [128, 128], st=st)
        S.op("dve", lambda e: e.memset(ones[:], 1.0), writes=[ones])
        ut = [C.sb(f"su{i}", [128, 512], st=st) for i in range(2)]
        uT = C.sb("suT", [128, 4, 128], st=st)
        SR = C.sb("sSR", [128, 16, 128], st=st); SI = C.sb("sSI", [128, 16, 128], st=st)
        br = C.sb("sbr", [128, 128], st=st); bi = C.sb("sbi", [128, 128], st=st)
        w1 = C.sb("sw1", [128, 128], st=st); w2 = C.sb("sw2", [128, 128], st=st); w3 = C.sb("sw3", [128, 128], st=st); w4 = C.sb("sw4", [128, 128], st=st)
        zr = C.sb("szr", [128, 128], st=st); zi = C.sb("szi", [128, 128], st=st)
        Z0 = C.sb("sZ0", [128, 2, 16], st=st); ZL = C.sb("sZL", [128, 2, 16], st=st); zt = C.sb("szt", [128, 4, 16], st=st)
        sin_ = C.sb("ssin", [128, 2, 16], st=st)
        SL = C.sb("sSL", [128, 2, 16], st=st)
        yo = [C.sb(f"syo{i}", [128, 512], st=st) for i in range(2)]
        pbr = C.ps("spbr", [128, 128], st=st); pbi = C.ps("spbi", [128, 128], st=st)
        py = [C.ps(f"spy{i}", [128, 512], st=st) for i in range(2)]
        ycnt = [0]

        def chunk(r0, c0, L, last_seq_idx):
            for i in range(16):
                q, s_ = i // 4, i % 4
                S.op("pe", lambda e: e.matmul(pbr[:, :L], lhsT=BBTr[:, i, :], rhs=uT[:, q, c0:c0 + L], start=True, stop=True), reads=[BBTr, uT], writes=[pbr])
                S.op("pe", lambda e: e.matmul(pbi[:, :L], lhsT=BBTi[:, i, :], rhs=uT[:, q, c0:c0 + L], start=True, stop=True), reads=[BBTi, uT], writes=[pbi])
                S.op("act", lambda e: e.copy(out=br[:, :L], in_=pbr[:, :L]), reads=[pbr], writes=[br])
                S.op("act", lambda e: e.copy(out=bi[:, :L], in_=pbi[:, :L]), reads=[pbi], writes=[bi])
                tt("dve", w1[:, :L], PIr[:, i, :L], br[:, :L], ALU.mult, [PIr, br], [w1])
                tt("pool", w2[:, :L], PIi[:, i, :L], bi[:, :L], ALU.mult, [PIi, bi], [w2])
                tt("dve", w1[:, :L], w1[:, :L], w2[:, :L], ALU.subtract, [w1, w2], [w1])
                tt("pool", w3[:, :L], PIr[:, i, :L], bi[:, :L], ALU.mult, [PIr, bi], [w3])
                tt("dve", w4[:, :L], PIi[:, i, :L], br[:, :L], ALU.mult, [PIi, br], [w4])
                tt("dve", w3[:, :L], w3[:, :L], w4[:, :L], ALU.add, [w3, w4], [w3])
                S.op("dve", lambda e: e.tensor_tensor_scan(out=zr[:, :L], data0=ones[:, :L], data1=w1[:, :L], initial=Z0[:, 0, i:i + 1], op0=ALU.mult, op1=ALU.add), reads=[ones, w1, Z0], writes=[zr])
                S.op("dve", lambda e: e.tensor_tensor_scan(out=zi[:, :L], data0=ones[:, :L], data1=w3[:, :L], initial=Z0[:, 1, i:i + 1], op0=ALU.mult, op1=ALU.add), reads=[ones, w3, Z0], writes=[zi])
                S.op("act", lambda e: e.copy(out=ZL[:, 0, i:i + 1], in_=zr[:, L - 1:L]), reads=[zr], writes=[ZL])
                S.op("act", lambda e: e.copy(out=ZL[:, 1, i:i + 1], in_=zi[:, L - 1:L]), reads=[zi], writes=[ZL])
                tt("dve", w1[:, :L], PWr[:, i, :L], zr[:, :L], ALU.mult, [PWr, zr], [w1])
                tt("pool", w2[:, :L], PWi[:, i, :L], zi[:, :L], ALU.mult, [PWi, zi], [w2])
                tt("dve", SR[:, i, :L], w1[:, :L], w2[:, :L], ALU.subtract, [w1, w2], [SR])
                tt("pool", w3[:, :L], PWr[:, i, :L], zi[:, :L], ALU.mult, [PWr, zi], [w3])
                tt("dve", w4[:, :L], PWi[:, i, :L], zr[:, :L], ALU.mult, [PWi, zr], [w4])
                tt("dve", SI[:, i, :L], w3[:, :L], w4[:, :L], ALU.add, [w3, w4], [SI])
            p = py[ycnt[0] % 2]; o = yo[ycnt[0] % 2]; ycnt[0] += 1
            for i in range(16):
                S.op("pe", lambda e: e.matmul(p[:L, i * 32:(i + 1) * 32], lhsT=SR[:, i, :L], rhs=CRe[:, i, :], start=True, stop=False), reads=[SR, CRe], writes=[p])
                S.op("pe", lambda e: e.matmul(p[:L, i * 32:(i + 1) * 32], lhsT=SI[:, i, :L], rhs=CIm[:, i, :], start=False, stop=True), reads=[SI, CIm], writes=[p])
            S.op("act", lambda e: e.copy(out=o[:L, :], in_=p[:L, :]), reads=[p], writes=[o])
            S.dma("pool", dr["YS5"][r0:r0 + L, :], o[:L, :], reads=[o])
            tt("dve", zt[:, 0, :], PWr[:, :, L], ZL[:, 0, :], ALU.mult, [PWr, ZL], [zt])
            tt("dve", zt[:, 1, :], PWi[:, :, L], ZL[:, 1, :], ALU.mult, [PWi, ZL], [zt])
            tt("dve", zt[:, 2, :], PWr[:, :, L], ZL[:, 1, :], ALU.mult, [PWr, ZL], [zt])
            tt("dve", zt[:, 3, :], PWi[:, :, L], ZL[:, 0, :], ALU.mult, [PWi, ZL], [zt])
            tt("dve", Z0[:, 0, :], zt[:, 0, :], zt[:, 1, :], ALU.subtract, [zt], [Z0])
            tt("dve", Z0[:, 1, :], zt[:, 2, :], zt[:, 3, :], ALU.add, [zt], [Z0])
            if last_seq_idx is not None:
                S.op("act", lambda e: e.copy(out=SL[:, 0, :], in_=SR[:, :, L - 1]), reads=[SR], writes=[SL])
                S.op("act", lambda e: e.copy(out=SL[:, 1, :], in_=SI[:, :, L - 1]), reads=[SI], writes=[SL])
                S.dma("pool", dr["o_s5re"][l, :, last_seq_idx, :], SL[:, 0, :], reads=[SL])
                S.dma("pool", dr["o_s5im"][l, :, last_seq_idx, :], SL[:, 1, :], reads=[SL])

        S.op("dve", lambda e: e.memset(Z0[:], 0.0), writes=[Z0])
        for ti, (r0, P) in enumerate(cfg.tiles):
            u = ut[ti % 2]
            S.dma("sp", u[:P, :], dr["PROJ"][r0:r0 + P, 5376:5888], writes=[u])
            for k in range(4):
                transpose_to(C, u, P, k * 128, 128, lambda: uT[:, k, :P], uT, ptp, k)
            if r0 < T:
                chunk(r0, 0, 128, 0 if r0 + 128 == T else None)
            else:
                for b in range(NS):
                    S.dma("sp", sin_[:, 0, :], dr["st_s5re"][l, :, b, :], writes=[sin_])
                    S.dma("sp", sin_[:, 1, :], dr["st_s5im"][l, :, b, :], writes=[sin_])
                    tt("dve", zt[:, 0, :], K(4), sin_[:, 0, :], ALU.mult, [sm, sin_], [zt])
                    tt("dve", zt[:, 1, :], K(5), sin_[:, 1, :], ALU.mult, [sm, sin_], [zt])
                    tt("dve", zt[:, 2, :], K(4), sin_[:, 1, :], ALU.mult, [sm, sin_], [zt])
                    tt("dve", zt[:, 3, :], K(5), sin_[:, 0, :], ALU.mult, [sm, sin_], [zt])
                    tt("dve", Z0[:, 0, :], zt[:, 0, :], zt[:, 1, :], ALU.subtract, [zt], [Z0])
                    tt("dve", Z0[:, 1, :], zt[:, 2, :], zt[:, 3, :], ALU.add, [zt], [Z0])
                    chunk(r0 + 8 * b, 8 * b, 8, 1 + b)
    S.barrier()
    with ExitStack() as st:
        dsk = bcast_load(C, st, "pd", dr["s5_d"][l], 512)
        bgl = bcast_load(C, st, "pbg", dr["s5_b_glu"][l], 512)
        gn = bcast_load(C, st, "pgn", dr["s5_norm"][l], 512)
        wg = C.sb("pwg", [128, 4, 512], st=st)
        S.dma("sp", wg[:], dr["s5_w_glu"][l].rearrange("(k p) c -> p k c", p=128), writes=[wg])
        yt = [C.sb(f"py{i}", [128, 512], st=st) for i in range(2)]
        ut = [C.sb(f"pu{i}", [128, 512], st=st) for i in range(2)]
        a1 = C.sb("pa1", [128, 512], st=st); a2 = C.sb("pa2", [128, 512], st=st); a3 = C.sb("pa3", [128, 512], st=st)
        yT = C.sb("pyT", [128, 4, 128], st=st)
        ss = C.sb("pss", [128, 2], st=st)
        ptp = [C.ps(f"ppt{i}", [128, 128], st=st) for i in range(2)]
        pg = C.ps("ppg", [128, 512], st=st)
        for ti, (r0, P) in enumerate(cfg.tiles):
            y, u = yt[ti % 2], ut[ti % 2]
            S.dma("sp", y[:P, :], dr["YS5"][r0:r0 + P, :], writes=[y])
            S.dma("sp", u[:P, :], dr["PROJ"][r0:r0 + P, 5376:5888], writes=[u])
            S.op("dve", lambda e: e.tensor_tensor(out=u[:P, :], in0=u[:P, :], in1=dsk[:P, :], op=ALU.mult), reads=[u, dsk], writes=[u])
            S.op("dve", lambda e: e.tensor_tensor(out=y[:P, :], in0=y[:P, :], in1=u[:P, :], op=ALU.add), reads=[y, u], writes=[y])
            S.op("pool", lambda e: e.tensor_tensor(out=a1[:P, :], in0=y[:P, :], in1=y[:P, :], op=ALU.mult), reads=[y], writes=[a1])
            S.op("dve", lambda e: e.tensor_scalar(out=a1[:P, :], in0=a1[:P, :], scalar1=0.044715, scalar2=1.0, op0=ALU.mult, op1=ALU.add), reads=[a1], writes=[a1])
            S.op("dve", lambda e: e.tensor_tensor(out=a1[:P, :], in0=a1[:P, :], in1=y[:P, :], op=ALU.mult), reads=[a1, y], writes=[a1])
            S.op("act", lambda e: e.activation(out=a1[:P, :], in_=a1[:P, :], func=AF.Sigmoid, scale=2.0 * math.sqrt(2.0 / math.pi)), reads=[a1], writes=[a1])
            S.op("dve", lambda e: e.tensor_tensor(out=a2[:P, :], in0=a1[:P, :], in1=y[:P, :], op=ALU.mult), reads=[a1, y], writes=[a2])
            for k in range(4):
                transpose_to(C, a2, P, k * 128, 128, lambda: yT[:, k, :P], yT, ptp, k)
            for k in range(4):
                S.op("pe", lambda e: e.matmul(pg[:P, :], lhsT=yT[:, k, :P], rhs=wg[:, k, :], start=(k == 0), stop=(k == 3)), reads=[yT, wg], writes=[pg])
            S.op("dve", lambda e: e.tensor_tensor(out=a3[:P, :], in0=pg[:P, :], in1=bgl[:P, :], op=ALU.add), reads=[pg, bgl], writes=[a3])
            S.op("act", lambda e: e.activation(out=a3[:P, :], in_=a3[:P, :], func=AF.Sigmoid), reads=[a3], writes=[a3])
            S.op("dve", lambda e: e.tensor_tensor(out=a2[:P, :], in0=a2[:P, :], in1=a3[:P, :], op=ALU.mult), reads=[a2, a3], writes=[a2])
            rms_rows(C, a2, P, 512, gn, a3, ss, a1)
            S.dma("pool", dr["OCAT"][r0:r0 + P, 1536:2048], a3[:P, :], reads=[a3])


def phase_rwkv(C, st0, l):
    from contextlib import ExitStack
    S, dr, cfg = C.S, C.dr, C.cfg
    T, NS, NTS = cfg.T, cfg.NS, cfg.NTS
    R0 = 3584

    def tt(eng, out, a, b, op, rd, wr):
        S.op(eng, lambda e: e.tensor_tensor(out=out, in0=a, in1=b, op=op), reads=rd, writes=wr)
    with ExitStack() as st:
        mu = bcast_load(C, st, "kmu", dr["rwkv_mu"][l], RW_COLS)
        w0 = bcast_load(C, st, "kw0", dr["rwkv_w0"][l], 512)
        a0 = bcast_load(C, st, "ka0", dr["rwkv_a0"][l], 512)
        kkw = bcast_load(C, st, "kkk", dr["rwkv_kk"][l], 512)
        kaw = bcast_load(C, st, "kka", dr["rwkv_ka"][l], 512)
        w2 = C.sb("kw2", [64, 512], st=st); S.dma("sp", w2[:], dr["rwkv_w2"][l], writes=[w2])
        a2 = C.sb("ka2", [64, 512], st=st); S.dma("sp", a2[:], dr["rwkv_a2"][l], writes=[a2])
        g2 = C.sb("kg2", [128, 512], st=st); S.dma("sp", g2[:], dr["rwkv_g2"][l], writes=[g2])
        cur = [C.sb(f"kc{i}", [128, RW_COLS], st=st) for i in range(2)]
        prv = [C.sb(f"kp{i}", [128, RW_COLS], st=st) for i in range(2)]
        lo = C.sb("klo", [128, 256], st=st)
        loT = C.sb("kloT", [128, 3, 128], st=st)
        dec = C.sb("kdec", [128, 512], st=st); aa = C.sb("kaa", [128, 512], st=st); gg = C.sb("kgg", [128, 512], st=st)
        kk = C.sb("kkkv", [128, 512], st=st); k2 = C.sb("kk2", [128, 512], st=st); bb = C.sb("kbb", [128, 512], st=st)
        ain = C.sb("kain", [128, 512], st=st); t5 = C.sb("kt5", [128, 512], st=st)
        s8 = C.sb("ks8", [128, 8], st=st)
        fa = [C.sb(f"kfa{i}", [64, 128, 40], st=st) for i in range(2)]
        fr = [C.sb(f"kfr{i}", [64, 128, 8], st=st) for i in range(2)]
        fw = [C.sb(f"kfw{i}", [64, 128, 8], st=st) for i in range(2)]
        ptp = [C.ps(f"kpt{i}", [128, 128], st=st) for i in range(2)]
        pl = [C.ps(f"kpl{i}", [128, 512], st=st) for i in range(3)]
        for i in range(2):
            S.op("pool", lambda e: e.memset(fa[i][:], 0.0), writes=[fa[i]])
        for ti, (r0, P) in enumerate(cfg.tiles):
            c, p = cur[ti % 2], prv[ti % 2]
            S.dma("sp", c[:P, :], dr["PROJ"][r0:r0 + P, R0:R0 + RW_COLS], writes=[c])
            if r0 == 0:
                S.op("pool", lambda e: e.memset(p[0:32, :], 0.0), writes=[p])
                S.dma("sp", p[1:P, :], dr["PROJ"][0:P - 1, R0:R0 + RW_COLS], writes=[p])
            else:
                S.dma("sp", p[:P, :], dr["PROJ"][r0 - 1:r0 + P - 1, R0:R0 + RW_COLS], writes=[p])
                if r0 >= T:
                    for b in range(NS):
                        S.dma("sp", p[8 * b:8 * b + 1, :], dr["st_shift"][l, b:b + 1, :], writes=[p])
            tt("dve", p[:P, :], p[:P, :], c[:P, :], ALU.subtract, [p, c], [p])
            tt("pool", p[:P, :], p[:P, :], mu[:P, :], ALU.mult, [p, mu], [p])
            tt("dve", p[:P, :], p[:P, :], c[:P, :], ALU.add, [p, c], [p])
            x = p
            S.op("act", lambda e: e.activation(out=lo[:P, 0:64], in_=x[:P, 1536:1600], func=AF.Tanh), reads=[x], writes=[lo])
            S.op("act", lambda e: e.copy(out=lo[:P, 64:128], in_=x[:P, 1600:1664]), reads=[x], writes=[lo])
            S.op("act", lambda e: e.activation(out=lo[:P, 128:256], in_=x[:P, 1664:1792], func=AF.Sigmoid), reads=[x], writes=[lo])
            transpose_to(C, lo, P, 0, 64, lambda: loT[0:64, 0, :P], loT, ptp, 0)
            transpose_to(C, lo, P, 64, 64, lambda: loT[0:64, 1, :P], loT, ptp, 1)
            transpose_to(C, lo, P, 128, 128, lambda: loT[:, 2, :P], loT, ptp, 2)
            S.op("pe", lambda e: e.matmul(pl[0][:P, :], lhsT=loT[0:64, 0, :P], rhs=w2[:, :], start=True, stop=True), reads=[loT, w2], writes=[pl[0]])
            S.op("pe", lambda e: e.matmul(pl[1][:P, :], lhsT=loT[0:64, 1, :P], rhs=a2[:, :], start=True, stop=True), reads=[loT, a2], writes=[pl[1]])
            S.op("pe", lambda e: e.matmul(pl[2][:P, :], lhsT=loT[:, 2, :P], rhs=g2[:, :], start=True, stop=True), reads=[loT, g2], writes=[pl[2]])
            tt("dve", dec[:P, :], pl[0][:P, :], w0[:P, :], ALU.add, [pl[0], w0], [dec])
            S.op("act", lambda e: e.activation(out=dec[:P, :], in_=dec[:P, :], func=AF.Sigmoid), reads=[dec], writes=[dec])
            S.op("act", lambda e: e.activation(out=dec[:P, :], in_=dec[:P, :], func=AF.Exp, scale=-math.exp(-0.5)), reads=[dec], writes=[dec])
            tt("dve", aa[:P, :], pl[1][:P, :], a0[:P, :], ALU.add, [pl[1], a0], [aa])
            S.op("act", lambda e: e.activation(out=aa[:P, :], in_=aa[:P, :], func=AF.Sigmoid), reads=[aa], writes=[aa])
            S.op("act", lambda e: e.copy(out=gg[:P, :], in_=pl[2][:P, :]), reads=[pl[2]], writes=[gg])
            tt("dve", kk[:P, :], x[:P, 512:1024], kkw[:P, :], ALU.mult, [x, kkw], [kk])
            tt("pool", t5[:P, :], kk[:P, :], kk[:P, :], ALU.mult, [kk], [t5])
            S.op("dve", lambda e: e.tensor_reduce(out=s8[:P, :], in_=v3(t5[:P, :], 8), axis=AX.X, op=ALU.add), reads=[t5], writes=[s8])
            S.op("dve", lambda e: e.tensor_scalar(out=s8[:P, :], in0=s8[:P, :], scalar1=1e-24, scalar2=None, op0=ALU.max), reads=[s8], writes=[s8])
            S.op("act", lambda e: e.sqrt(out=s8[:P, :], in_=s8[:P, :]), reads=[s8], writes=[s8])
            S.op("dve", lambda e: e.reciprocal(out=s8[:P, :], in_=s8[:P, :]), reads=[s8], writes=[s8])
            tt("dve", v3(kk[:P, :], 8), v3(kk[:P, :], 8), s8[:P, :].unsqueeze(2).to_broadcast([P, 8, 64]), ALU.mult, [kk, s8], [kk])
            S.op("dve", lambda e: e.scalar_tensor_tensor(out=t5[:P, :], in0=aa[:P, :], scalar=-1.0, in1=kaw[:P, :], op0=ALU.add, op1=ALU.mult), reads=[aa, kaw], writes=[t5])
            S.op("dve", lambda e: e.scalar_tensor_tensor(out=k2[:P, :], in0=t5[:P, :], scalar=1.0, in1=x[:P, 512:1024], op0=ALU.add, op1=ALU.mult), reads=[t5, x], writes=[k2])
            tt("pool", bb[:P, :], kk[:P, :], aa[:P, :], ALU.mult, [kk, aa], [bb])
            S.op("act", lambda e: e.mul(out=ain[:P, :], in_=kk[:P, :], mul=-1.0), reads=[kk], writes=[ain])
            S.dma("pool", dr["RWS"][0, r0:r0 + P, :], bb[:P, :], reads=[bb])
            S.dma("pool", dr["RWS"][1, r0:r0 + P, :], k2[:P, :], reads=[k2])
            S.dma("pool", dr["RWS"][2, r0:r0 + P, :], x[:P, 1024:1536], reads=[x])
            S.dma("pool", dr["RWS"][3, r0:r0 + P, :], x[:P, 0:512], reads=[x])
            S.dma("pool", dr["RWS"][4, r0:r0 + P, :], gg[:P, :], reads=[gg])
            A_, R_, W_ = fa[ti % 2], fr[ti % 2], fw[ti % 2]
            for h in range(8):
                transpose_to(C, ain, P, h * 64, 64, lambda: A_[:, :P, 32 + h], A_, ptp, 3 * h)
                transpose_to(C, x, P, h * 64, 64, lambda: R_[:, :P, h], R_, ptp, 3 * h + 1)
                transpose_to(C, dec, P, h * 64, 64, lambda: W_[:, :P, h], W_, ptp, 3 * h + 2)
            S.dma("pool", dr["FMA"][:, r0:r0 + P, :], A_[:, :P, :], reads=[A_])
            S.dma("pool", dr["FMR"][:, r0:r0 + P, :], R_[:, :P, :], reads=[R_])
            S.dma("pool", dr["FMW"][:, r0:r0 + P, :], W_[:, :P, :], reads=[W_])
    S.barrier()
    TC = 16
    with ExitStack() as st:
        ST = C.sb("nST", [64, 512], st=st)
        TMP = C.sb("nTMP", [64, 512], st=st)
        m40 = C.sb("nm40", [40, 512], st=st)
        S.op("dve", lambda e: e.memset(m40[:], 0.0), writes=[m40])
        S.dma("sp", m40[0:8, :], dr["mask8"], writes=[m40])
        S.dma("sp", m40[32:40, :], dr["mask8"], writes=[m40])
        BK = [C.sb(f"nBK{i}", [40, TC, 64], st=st) for i in range(2)]
        SAV = [C.sb(f"nSAV{i}", [40, TC, 512], st=st) for i in range(2)]
        AT = [C.sb(f"nAT{i}", [64, TC, 40], st=st) for i in range(2)]
        RT = [C.sb(f"nRT{i}", [64, TC, 8], st=st) for i in range(2)]
        WT = [C.sb(f"nWT{i}", [64, TC, 8], st=st) for i in range(2)]
        YB = [C.sb(f"nYB{i}", [8, TC, 512], st=st) for i in range(2)]
        psa = [C.ps(f"npsa{i}", [40, 512], st=st) for i in range(2)]
        psu = [C.ps(f"npsu{i}", [64, 512], st=st) for i in range(2)]
        psy = [C.ps(f"npsy{i}", [8, 512], st=st) for i in range(2)]
        for i in range(2):
            S.op("pool", lambda e: e.memset(BK[i][:], 0.0), writes=[BK[i]])
            S.op("pool", lambda e: e.memset(SAV[i][:], 0.0), writes=[SAV[i]])
        cc = [0]
        stp = [0]

        def seq(t0, n):
            for c0 in range(0, n, TC):
                L = min(TC, n - c0)
                r0 = t0 + c0
                k = cc[0] % 2
                cc[0] += 1
                bk, sav, at, rt, wt, yb = BK[k], SAV[k], AT[k], RT[k], WT[k], YB[k]
                S.dma("sp", bk[0:8, :L, :], dr["RWS"][1, r0:r0 + L, :].rearrange("t (h j) -> h t j", h=8), writes=[bk])
                S.dma("sp", bk[32:40, :L, :], dr["RWS"][0, r0:r0 + L, :].rearrange("t (h j) -> h t j", h=8), writes=[bk])
                S.dma("sp", sav[0:8, :L, :], dr["RWS"][2, r0:r0 + L, :].partition_broadcast(8), writes=[sav])
                S.op("pool", lambda e: e.tensor_tensor(out=sav[0:8, :L, :], in0=sav[0:8, :L, :],
                                                       in1=m40[0:8, :].unsqueeze(1).to_broadcast([8, L, 512]), op=ALU.mult), reads=[sav, m40], writes=[sav])
                S.dma("sp", at[:, :L, :], dr["FMA"][:, r0:r0 + L, :], writes=[at])
                S.dma("sp", rt[:, :L, :], dr["FMR"][:, r0:r0 + L, :], writes=[rt])
                S.dma("sp", wt[:, :L, :], dr["FMW"][:, r0:r0 + L, :], writes=[wt])
                for t in range(L):
                    pa, pu, py_ = psa[stp[0] % 2], psu[stp[0] % 2], psy[stp[0] % 2]
                    stp[0] += 1
                    S.op("pe", lambda e: e.matmul(pa[:, :], lhsT=at[:, t, :], rhs=ST[:, :], start=True, stop=True), reads=[at, ST], writes=[pa])
                    S.op("dve", lambda e: e.tensor_tensor(out=sav[32:40, t, :], in0=pa[32:40, :], in1=m40[32:40, :], op=ALU.mult), reads=[pa, m40], writes=[sav])
                    S.op("pool", lambda e: e.tensor_tensor(out=v3(TMP[:, :], 8), in0=v3(ST[:, :], 8),
                                                           in1=wt[:, t, :].unsqueeze(2).to_broadcast([64, 8, 64]), op=ALU.mult), reads=[ST, wt], writes=[TMP])
                    S.op("pe", lambda e: e.matmul(pu[:, :], lhsT=bk[:, t, :], rhs=sav[:, t, :], start=True, stop=True), reads=[bk, sav], writes=[pu])
                    S.op("dve", lambda e: e.tensor_tensor(out=ST[:, :], in0=TMP[:, :], in1=pu[:, :], op=ALU.add), reads=[TMP, pu], writes=[ST])
                    S.op("pe", lambda e: e.matmul(py_[:, :], lhsT=rt[:, t, :], rhs=ST[:, :], start=True, stop=True), reads=[rt, ST], writes=[py_])
                    S.op("act", lambda e: e.copy(out=yb[:, t, :], in_=py_[:, :]), reads=[py_], writes=[yb])
                for h in range(8):
                    S.dma("pool", dr["RWS"][5, r0:r0 + L, h * 64:(h + 1) * 64], yb[h:h + 1, :L, h * 64:(h + 1) * 64], reads=[yb])

        S.op("dve", lambda e: e.memset(ST[:], 0.0), writes=[ST])
        seq(0, T)
        S.dma("pool", dr["o_rwkvT"][l, 0].rearrange("j h i -> j (h i)"), ST[:, :], reads=[ST])
        for b in range(NS):
            S.dma("sp", ST[:, :], dr["st_rwkvT"][l, b].rearrange("j h i -> j (h i)"), writes=[ST])
            seq(T + 8 * b, 8)
            S.dma("pool", dr["o_rwkvT"][l, 1 + b].rearrange("j h i -> j (h i)"), ST[:, :], reads=[ST])
    S.barrier()
    with ExitStack() as st:
        lw = bcast_load(C, st, "olw", dr["rwkv_ln_w"][l], 512)
        lb = bcast_load(C, st, "olb", dr["rwkv_ln_b"][l], 512)
        rk = bcast_load(C, st, "ork", dr["rwkv_rk"][l], 512)
        ins = [[C.sb(f"oi{j}_{i}", [128, 512], st=st) for j in range(5)] for i in range(2)]
        t1 = C.sb("ot1", [128, 512], st=st); t2 = C.sb("ot2", [128, 512], st=st)
        s8 = C.sb("os8", [128, 16], st=st)
        for ti, (r0, P) in enumerate(cfg.tiles):
            k2, v, r, g, y = ins[ti % 2]
            for j, tl in zip((1, 2, 3, 4, 5), (k2, v, r, g, y)):
                S.dma("sp", tl[:P, :], dr["RWS"][j, r0:r0 + P, :], writes=[tl])
            y3 = v3(y[:P, :], 8)
            S.op("dve", lambda e: e.tensor_reduce(out=s8[:P, 0:8], in_=y3, axis=AX.X, op=ALU.add), reads=[y], writes=[s8])
            S.op("dve", lambda e: e.tensor_scalar(out=s8[:P, 0:8], in0=s8[:P, 0:8], scalar1=1.0 / 64, scalar2=None, op0=ALU.mult), reads=[s8], writes=[s8])
            tt("dve", y3, y3, s8[:P, 0:8].unsqueeze(2).to_broadcast([P, 8, 64]), ALU.subtract, [y, s8], [y])
            tt("pool", t1[:P, :], y[:P, :], y[:P, :], ALU.mult, [y], [t1])
            S.op("dve", lambda e: e.tensor_reduce(out=s8[:P, 8:16], in_=v3(t1[:P, :], 8), axis=AX.X, op=ALU.add), reads=[t1], writes=[s8])
            S.op("dve", lambda e: e.tensor_scalar(out=s8[:P, 8:16], in0=s8[:P, 8:16], scalar1=1.0 / 64, scalar2=64e-5, op0=ALU.mult, op1=ALU.add), reads=[s8], writes=[s8])
            S.op("act", lambda e: e.sqrt(out=s8[:P, 8:16], in_=s8[:P, 8:16]), reads=[s8], writes=[s8])
            S.op("dve", lambda e: e.reciprocal(out=s8[:P, 8:16], in_=s8[:P, 8:16]), reads=[s8], writes=[s8])
            tt("dve", y3, y3, s8[:P, 8:16].unsqueeze(2).to_broadcast([P, 8, 64]), ALU.mult, [y, s8], [y])
            tt("dve", y[:P, :], y[:P, :], lw[:P, :], ALU.mult, [y, lw], [y])
            tt("dve", y[:P, :], y[:P, :], lb[:P, :], ALU.add, [y, lb], [y])
            tt("pool", t1[:P, :], r[:P, :], k2[:P, :], ALU.mult, [r, k2], [t1])
            tt("pool", t1[:P, :], t1[:P, :], rk[:P, :], ALU.mult, [t1, rk], [t1])
            S.op("dve", lambda e: e.tensor_reduce(out=s8[:P, 0:8], in_=v3(t1[:P, :], 8), axis=AX.X, op=ALU.add), reads=[t1], writes=[s8])
            tt("dve", v3(t2[:P, :], 8), v3(v[:P, :], 8), s8[:P, 0:8].unsqueeze(2).to_broadcast([P, 8, 64]), ALU.mult, [v, s8], [t2])
            tt("dve", y[:P, :], y[:P, :], t2[:P, :], ALU.add, [y, t2], [y])
            tt("dve", y[:P, :], y[:P, :], g[:P, :], ALU.mult, [y, g], [y])
            S.dma("pool", dr["OCAT"][r0:r0 + P, 1024:1536], y[:P, :], reads=[y])


def phase_diff(C, st0, l):
    from contextlib import ExitStack
    S, dr, cfg = C.S, C.dr, C.cfg
    T, NS, NTS, NPG = cfg.T, cfg.NS, cfg.NTS, cfg.NPG
    NQB = T // 128
    lam_init = 0.8 - 0.6 * math.exp(-0.3 * l)

    def tt(eng, out, a, b, op, rd, wr):
        S.op(eng, lambda e: e.tensor_tensor(out=out, in0=a, in1=b, op=op), reads=rd, writes=wr)
    with ExitStack() as st:
        dl = C.sb("dl", [128, 4, 64], st=st)
        S.dma("sp", dl[:], dr["diff_l"][l].partition_broadcast(128), writes=[dl])
        lt = C.sb("dlt", [128, 2, 64], st=st)
        lam = C.sb("dlam", [128, 4], st=st)
        tt("dve", lt[:, 0, :], dl[:, 0, :], dl[:, 1, :], ALU.mult, [dl], [lt])
        tt("dve", lt[:, 1, :], dl[:, 2, :], dl[:, 3, :], ALU.mult, [dl], [lt])
        S.op("dve", lambda e: e.tensor_reduce(out=lam[:, 0:2], in_=lt[:], axis=AX.X, op=ALU.add), reads=[lt], writes=[lam])
        S.op("act", lambda e: e.activation(out=lam[:, 0:2], in_=lam[:, 0:2], func=AF.Exp), reads=[lam], writes=[lam])
        tt("dve", lam[:, 2:3], lam[:, 0:1], lam[:, 1:2], ALU.subtract, [lam], [lam])
        S.op("dve", lambda e: e.tensor_scalar(out=lam[:, 2:3], in0=lam[:, 2:3], scalar1=lam_init, scalar2=-1.0, op0=ALU.add, op1=ALU.mult), reads=[lam], writes=[lam])
        sub = bcast_load(C, st, "dsub", dr["diff_subln"][l], 128)
        S.op("act", lambda e: e.mul(out=sub[:], in_=sub[:], mul=(1.0 - lam_init)), reads=[sub], writes=[sub])
        cm = C.sb("dcm", [128, 128], st=st); S.dma("sp", cm[:], dr["cmask"], writes=[cm])
        cm8 = C.sb("dcm8", [8, 8], st=st); S.dma("sp", cm8[:], dr["cmask8"], writes=[cm8])
        KW = max(T, NPG * 128 + 8)
        KT = C.sb("dKT", [128, KW], st=st)
        SC = [C.sb(f"dSC{m}", [128, KW], st=st) for m in range(2)]
        QT = C.sb("dQT", [128, 128], st=st)
        xq = [C.sb(f"dxq{i}", [128, 128], st=st) for i in range(2)]
        pTs = [C.sb(f"dpT{i}", [128, 128], st=st) for i in range(3)]
        oo = [C.sb(f"doo{i}", [128, 128], st=st) for i in range(2)]
        sq = C.sb("dsq", [128, 128], st=st)
        sm = C.sb("dsm", [128, 8], st=st)
        ss = C.sb("dss", [128, 2], st=st)
        vn = C.sb("dvn", [8, 128], st=st)
        ptp = [C.ps(f"dpt{i}", [128, 128], st=st) for i in range(2)]
        psc = [C.ps(f"dps{i}", [128, 512], st=st) for i in range(3)]
        po = [C.ps(f"dpo{i}", [128, 128], st=st) for i in range(2)]
        cnt = [0]

        def attn(P, nk, vblocks, mask, msz, r0, h):
            for m in range(2):
                for k0 in range(0, nk, 512):
                    n = min(512, nk - k0)
                    p_ = psc[cnt[0] % 3]
                    cnt[0] += 1
                    S.op("pe", lambda e: e.matmul(p_[:P, :n], lhsT=QT[m * 64:(m + 1) * 64, :P], rhs=KT[m * 64:(m + 1) * 64, k0:k0 + n],
                                                  start=True, stop=True), reads=[QT, KT], writes=[p_])
                    if cnt[0] % 2 == 0:
                        S.op("act", lambda e: e.mul(out=SC[m][:P, k0:k0 + n], in_=p_[:P, :n], mul=0.125), reads=[p_], writes=[SC[m]])
                    else:
                        S.op("dve", lambda e: e.tensor_scalar(out=SC[m][:P, k0:k0 + n], in0=p_[:P, :n], scalar1=0.125, scalar2=None, op0=ALU.mult), reads=[p_], writes=[SC[m]])
                tt("pool", SC[m][:P, nk - msz:nk], SC[m][:P, nk - msz:nk], mask[:P, :msz], ALU.add, [SC[m], mask], [SC[m]])
                S.op("dve", lambda e: e.tensor_reduce(out=sm[:P, m:m + 1], in_=SC[m][:P, :nk], axis=AX.X, op=ALU.max), reads=[SC[m]], writes=[sm])
                S.op("dve", lambda e: e.tensor_scalar(out=sm[:P, 2 + m:3 + m], in0=sm[:P, m:m + 1], scalar1=-1.0, scalar2=None, op0=ALU.mult), reads=[sm], writes=[sm])
                S.op("act", lambda e: e.activation(out=SC[m][:P, :nk], in_=SC[m][:P, :nk], func=AF.Exp, bias=sm[:P, 2 + m:3 + m], scale=1.0,
                                                   accum_out=sm[:P, 4 + m:5 + m]), reads=[SC[m], sm], writes=[SC[m], sm])
            S.op("dve", lambda e: e.reciprocal(out=sm[:P, 6:8], in_=sm[:P, 4:6]), reads=[sm], writes=[sm])
            tt("dve", sm[:P, 7:8], sm[:P, 7:8], lam[:P, 2:3], ALU.mult, [sm, lam], [sm])
            S.op("dve", lambda e: e.tensor_scalar(out=SC[0][:P, :nk], in0=SC[0][:P, :nk], scalar1=sm[:P, 6:7], scalar2=None, op0=ALU.mult), reads=[SC[0], sm], writes=[SC[0]])
            S.op("dve", lambda e: e.scalar_tensor_tensor(out=SC[0][:P, :nk], in0=SC[1][:P, :nk], scalar=sm[:P, 7:8], in1=SC[0][:P, :nk],
                                                         op0=ALU.mult, op1=ALU.add), reads=[SC[0], SC[1], sm], writes=[SC[0]])
            pO = po[cnt[0] % 2]
            for bi, (k0, ksz, v_ap, vt) in enumerate(vblocks):
                pt = ptp[bi % 2]
                pT = pTs[bi % 3]
                S.op("pe", lambda e: e.transpose(out=pt[:ksz, :P], in_=SC[0][:P, k0:k0 + ksz], identity=C.ident[:P, :P]), reads=[SC[0], C.ident], writes=[pt])
                if bi % 2 == 0:
                    S.op("act", lambda e: e.copy(out=pT[:ksz, :P], in_=pt[:ksz, :P]), reads=[pt], writes=[pT])
                else:
                    S.op("dve", lambda e: e.tensor_copy(out=pT[:ksz, :P], in_=pt[:ksz, :P]), reads=[pt], writes=[pT])
                S.op("pe", lambda e: e.matmul(pO[:P, :], lhsT=pT[:ksz, :P], rhs=v_ap, start=(bi == 0), stop=(bi == len(vblocks) - 1)), reads=[pT, vt], writes=[pO])
            o = oo[cnt[0] % 2]
            S.op("act", lambda e: e.copy(out=o[:P, :], in_=pO[:P, :]), reads=[pO], writes=[o])
            rms_rows(C, o, P, 128, sub, o, ss, sq)
            S.dma("pool", dr["OCAT"][r0:r0 + P, 512 + h * 128:512 + (h + 1) * 128], o[:P, :], reads=[o])

        n = 0
        st2 = ExitStack()
        st2.__enter__()
        Vh = C.sb("dVh", [128, NQB, 128], st=st2)
        for h in range(4):
            for kb in range(NQB):
                x = xq[n % 2]; n += 1
                S.dma("sp", x[:, :], dr["k_new"][l, kb * 128:(kb + 1) * 128, h * 128:(h + 1) * 128], writes=[x])
                transpose_to(C, x, 128, 0, 128, lambda: KT[:, kb * 128:(kb + 1) * 128], KT, ptp, kb)
            S.dma("sp", Vh[:, :NQB, :], dr["PROJ"][0:T, 3072 + h * 128:3072 + (h + 1) * 128].rearrange("(k p) c -> p k c", p=128), writes=[Vh])
            for qb in range(NQB):
                x = xq[n % 2]; n += 1
                S.dma("sp", x[:, :], dr["PROJ"][qb * 128:(qb + 1) * 128, 2048 + h * 128:2048 + (h + 1) * 128], writes=[x])
                transpose_to(C, x, 128, 0, 128, lambda: QT[:, :], QT, ptp, qb)
                vb = [(kb * 128, 128, Vh[:, kb, :], Vh) for kb in range(qb + 1)]
                attn(128, (qb + 1) * 128, vb, cm, 128, qb * 128, h)
        S.barrier()
        st2.__exit__(None, None, None)
        ptb = C.sb("dptb", [128, NS * NPG], I32, st=st)
        S.dma("sp", ptb[:], dr["pt"].partition_broadcast(128), writes=[ptb])
        ptf = C.sb("dptf", [128, NS * NPG], st=st)
        io = C.sb("dio", [128, 1], I32, st=st); S.dma("sp", io[:], dr["iota"], writes=[io])
        iof = C.sb("diof", [128, 1], st=st)
        S.op("dve", lambda e: e.tensor_copy(out=ptf[:], in_=ptb[:]), reads=[ptb], writes=[ptf])
        S.op("dve", lambda e: e.tensor_copy(out=iof[:], in_=io[:]), reads=[io], writes=[iof])
        S.op("dve", lambda e: e.tensor_scalar(out=ptf[:], in0=ptf[:], scalar1=128.0, scalar2=iof[:, 0:1], op0=ALU.mult, op1=ALU.add), reads=[ptf, iof], writes=[ptf])
        S.op("dve", lambda e: e.tensor_scalar(out=ptf[:], in0=ptf[:], scalar1=float(l * cfg.NPOOL * 128), scalar2=None, op0=ALU.add), reads=[ptf], writes=[ptf])
        idx = C.sb("didx", [128, NS * NPG], I32, st=st)
        S.op("dve", lambda e: e.tensor_copy(out=idx[:], in_=ptf[:]), reads=[ptf], writes=[idx])
        KP = [C.sb(f"dKP{i}", [128, 512], st=st) for i in range(2)]
        VP = [C.sb(f"dVP{i}", [128, 512], st=st) for i in range(NPG)]
        ckf = dr["ck"].rearrange("l r c -> (l r) c")
        cvf = dr["cv"].rearrange("l r c -> (l r) c")
        KTs = C.sb("dKTs", [128, 4, NPG * 128 + 8], st=st)
        QTs = C.sb("dQTs", [128, 4, 8], st=st)
        q8 = C.sb("dq8", [8, 512], st=st); k8 = C.sb("dk8", [8, 512], st=st); v8 = C.sb("dv8", [8, 512], st=st)
        for b in range(NS):
            r0 = T + 8 * b
            for pg in range(NPG):
                kp = KP[pg % 2]
                col = b * NPG + pg
                gather(C, kp, kp[:, :], ckf, idx, col)
                gather(C, VP[pg], VP[pg][:, :], cvf, idx, col)
                for h in range(4):
                    transpose_to(C, kp, 128, h * 128, 128, lambda: KTs[:, h, pg * 128:(pg + 1) * 128], KTs, ptp, h)
            S.dma("sp", q8[:, :], dr["PROJ"][r0:r0 + 8, 2048:2560], writes=[q8])
            S.dma("sp", k8[:, :], dr["k_new"][l, r0:r0 + 8, :], writes=[k8])
            S.dma("sp", v8[:, :], dr["PROJ"][r0:r0 + 8, 3072:3584], writes=[v8])
            for h in range(4):
                transpose_to(C, k8, 8, h * 128, 128, lambda: KTs[:, h, NPG * 128:NPG * 128 + 8], KTs, ptp, h)
                transpose_to(C, q8, 8, h * 128, 128, lambda: QTs[:, h, :], QTs, ptp, h + 1)
            for h in range(4):
                S.op("act", lambda e: e.copy(out=QT[:, 0:8], in_=QTs[:, h, :]), reads=[QTs], writes=[QT])
                S.op("pool", lambda e: e.tensor_copy(out=KT[:, 0:NPG * 128 + 8], in_=KTs[:, h, :]), reads=[KTs], writes=[KT])
                vb = [(pg * 128, 128, VP[pg][:, h * 128:(h + 1) * 128], VP[pg]) for pg in range(NPG)]
                vb.append((NPG * 128, 8, v8[:, h * 128:(h + 1) * 128], v8))
                attn(8, NPG * 128 + 8, vb, cm8, 8, r0, h)


def gather(C, tile, out_ap, table, idx, col):
    S = C.S
    q = "pool"
    S._deps(q, [idx], [tile])
    if q not in S.dsem or S.dcnt[q] >= S.DK * (S.SEM_LIMIT // 16):
        S.dsem[q] = [S.newsem() for _ in range(S.DK)]
        S.dcnt[q] = 0
        S.last.setdefault("dma", {})
    i = S.dcnt[q]
    sem = S.dsem[q][i % S.DK]
    prev = 16 * (i // S.DK)
    if prev > 0:
        S._wait(q, ("dma", sem, prev))
    ins = S.eng[q].indirect_dma_start(out=out_ap, out_offset=None, in_=table,
                                      in_offset=bass.IndirectOffsetOnAxis(ap=idx[:, col:col + 1], axis=0))
    ins.then_inc(sem, 16)
    S.dcnt[q] = i + 1
    ref = ("dma", sem, prev + 16)
    S.last.setdefault("dma", {})[id(sem)] = ref
    S._mark(ref, [idx], [tile])
    S.nins += 1
```

```python
import math
import numpy as np
import concourse.bass as bass
import concourse.mybir as mybir
from concourse.bass_utils import run_bass_kernel_spmd

F32 = mybir.dt.float32
I32 = mybir.dt.int32
AF = mybir.ActivationFunctionType
ALU = mybir.AluOpType
AX = mybir.AxisListType

D = 2048
GW = 512
IN_COLS = 5888
RW_COLS = 1792
DFF = 5632
NCORES = 8


class TT:
    __slots__ = ("t", "lw", "rd")

    def __init__(self, t):
        self.t = t
        self.lw = None
        self.rd = {}

    def __getitem__(self, k):
        return self.t[k]


class Sched:
    SEM_LIMIT = 30000
    DK = 8

    def __init__(self, nc):
        self.nc = nc
        self.eng = {"pe": nc.tensor, "act": nc.scalar, "dve": nc.vector, "pool": nc.gpsimd, "sp": nc.sync}
        self.csem = {}
        self.ccnt = {}
        self.waited = {e: {} for e in self.eng}
        self.dsem = {}
        self.dcnt = {}
        self.semid = 0
        self.last = {}
        self.nins = 0

    def newsem(self):
        self.semid += 1
        return self.nc.alloc_semaphore(f"s{self.semid}")

    def _need(self, e, ref, lst):
        pe, sem, val = ref
        if e == "pe" and pe == "pe":
            return
        w = self.waited[e]
        k = id(sem)
        if w.get(k, (None, 0))[1] >= val:
            return
        w[k] = (sem, val)
        lst.append((sem, val))

    def _wait(self, e, ref):
        lst = []
        self._need(e, ref, lst)
        for sem, val in lst:
            self.eng[e].wait_ge(sem, val)
            self.nins += 1

    def _deps(self, e, reads, writes, lst=None):
        own = lst is None
        if own:
            lst = []
        for t in reads:
            if t.lw is not None:
                self._need(e, t.lw, lst)
        for t in writes:
            if t.lw is not None and t.lw[0] != e:
                self._need(e, t.lw, lst)
            for r in t.rd.values():
                if r[0] != e:
                    self._need(e, r, lst)
        if own:
            for sem, val in lst:
                self.eng[e].wait_ge(sem, val)
                self.nins += 1
        return lst

    def _emit_waits(self, e, lst, ins):
        for sem, val in lst[:-1]:
            pass
        if lst:
            sem, val = lst[-1]
            ins._wait_ge(sem, val)

    def _mark(self, ref, reads, writes):
        for t in reads:
            t.rd[id(ref[1])] = ref
        for t in writes:
            t.lw = ref
            t.rd = {}

    def op(self, e, fn, reads=(), writes=()):
        lst = self._deps(e, reads, writes, [])
        for sem, val in lst[:-1]:
            self.eng[e].wait_ge(sem, val)
            self.nins += 1
        if e not in self.csem or self.ccnt[e] >= self.SEM_LIMIT:
            self.csem[e] = self.newsem()
            self.ccnt[e] = 0
        ins = fn(self.eng[e])
        if lst:
            ins._wait_ge(lst[-1][0], lst[-1][1])
        self.ccnt[e] += 1
        ins.then_inc(self.csem[e], 1)
        ref = (e, self.csem[e], self.ccnt[e])
        self.last[e] = ref
        self._mark(ref, reads, writes)
        self.nins += 1
        return ref

    def dma(self, q, out, in_, reads=(), writes=(), **kw):
        lst = self._deps(q, reads, writes, [])
        if q not in self.dsem or self.dcnt[q] >= self.DK * (self.SEM_LIMIT // 16):
            self.dsem[q] = [self.newsem() for _ in range(self.DK)]
            self.dcnt[q] = 0
            self.last.setdefault("dma", {})
        i = self.dcnt[q]
        sem = self.dsem[q][i % self.DK]
        prev = 16 * (i // self.DK)
        if prev > 0:
            self._need(q, ("dma", sem, prev), lst)
        for sm_, val in lst[:-1]:
            self.eng[q].wait_ge(sm_, val)
            self.nins += 1
        ins = self.eng[q].dma_start(out=out, in_=in_, **kw)
        if lst:
            ins._wait_ge(lst[-1][0], lst[-1][1])
        ins.then_inc(sem, 16)
        self.dcnt[q] = i + 1
        ref = ("dma", sem, prev + 16)
        self.last["dma"][id(sem)] = ref
        self._mark(ref, reads, writes)
        self.nins += 1
        return ref

    def barrier(self, engines=("pe", "act", "dve", "pool", "sp")):
        refs = [r for k, r in self.last.items() if k != "dma"]
        refs += list(self.last.get("dma", {}).values())
        for e in engines:
            for r in refs:
                if r[0] == e:
                    continue
                pe, sem, val = r
                w = self.waited[e]
                if w.get(id(sem), (None, 0))[1] >= val:
                    continue
                w[id(sem)] = (sem, val)
                self.eng[e].wait_ge(sem, val)
                self.nins += 1


class Cfg:
    def __init__(self, T=8192, NS=16, NPG=16, NPOOL=2560):
        self.T, self.NS, self.NPG, self.NPOOL = T, NS, NPG, NPOOL
        self.NTS = NS * 8
        self.NT = T + self.NTS
        self.tiles = [(i * 128, 128) for i in range(T // 128)] + [(T, self.NTS)]
        self.sts = [self.tiles[i:i + 4] for i in range(0, T // 128, 4)] + [[self.tiles[-1]]]
        self.sts2 = [self.tiles[i:i + 2] for i in range(0, T // 128, 2)] + [[self.tiles[-1]]]


class Ctx:
    pass


def build(cfg):
    from contextlib import ExitStack
    nc = bass.Bass("TRN2", target_bir_lowering=False)
    S = Sched(nc)
    T, NS, NT, NTS, NPG = cfg.T, cfg.NS, cfg.NT, cfg.NTS, cfg.NPG
    NSQ = 1 + NS
    dr = {}

    def din(name, shape, dt=F32):
        dr[name] = nc.dram_tensor(name, list(shape), dt, kind="ExternalInput").ap()

    def dout(name, shape):
        dr[name] = nc.dram_tensor(name, list(shape), F32, kind="ExternalOutput").ap()

    def dscr(name, shape):
        dr[name] = nc.dram_tensor(name, list(shape), F32).ap()

    din("x0", [NT, D]); din("p0", [2, NT, 256])
    din("ck", [2, cfg.NPOOL * 128, 512]); din("cv", [2, cfg.NPOOL * 128, 512])
    din("pt", [NS * NPG], I32)
    din("st_ret", [2, NS, 4, 128, 128]); din("st_rwkvT", [2, NS, 64, 8, 64]); din("st_shift", [2, NS, RW_COLS])
    din("st_s5re", [2, 128, NS, 16]); din("st_s5im", [2, 128, NS, 16]); din("st_convT", [2, 128, 88, NS, 2])
    din("norm_mix", [2, D]); din("w_in", [2, D, IN_COLS]); din("w_out", [2, D, D])
    din("ret_norm_w", [2, 512]); din("ret_norm_b", [2, 512])
    din("diff_l", [2, 4, 64]); din("diff_subln", [2, 128])
    din("rwkv_mu", [2, RW_COLS]); din("rwkv_w0", [2, 512]); din("rwkv_w2", [2, 64, 512]); din("rwkv_a0", [2, 512])
    din("rwkv_a2", [2, 64, 512]); din("rwkv_g2", [2, 128, 512]); din("rwkv_kk", [2, 512]); din("rwkv_ka", [2, 512])
    din("rwkv_rk", [2, 512]); din("rwkv_ln_w", [2, 512]); din("rwkv_ln_b", [2, 512])
    din("s5_lre", [2, 128, 16]); din("s5_lim", [2, 128, 16]); din("s5_ls", [2, 128, 16])
    din("s5_bre", [2, 128, 16, 128]); din("s5_bim", [2, 128, 16, 128])
    din("s5_cre", [2, 128, 16, 32]); din("s5_cim", [2, 128, 16, 32])
    din("s5_d", [2, 512]); din("s5_w_glu", [2, 512, 512]); din("s5_b_glu", [2, 512]); din("s5_norm", [2, 512])
    din("norm_ffn", [2, D]); din("ffn_w_up", [2, D, 2 * DFF]); din("ffn_cw", [2, 128, 88, 3]); din("ffn_cb", [2, 128, 88])
    din("ffn_w_down", [2, DFF, D]); din("norm_ple", [2, D]); din("ple_w_proj", [2, 256, D]); din("ple_norm_e", [2, D])
    din("ple_w_gate", [2, D, D]); din("norm_final", [D])
    din("ident", [128, 128]); din("rope_r", [NT, 2, 64]); din("rope_d", [NT, 2, 8])
    din("ret_dmT", [4, 128, 128]); din("ret_dmT8", [4, 8, 8]); din("ret_qd", [128, 4, 128]); din("ret_kd", [128, 4]);
    din("ret_qd8", [128, 4, 8]); din("ret_kd8", [8, 4]); din("cmask", [128, 128]); din("cmask8", [8, 8])
    din("mask8", [8, 512]); din("tidx", [128, 130]); din("iota", [128, 1], I32)
    dout("y", [NT, D]); dout("k_new", [2, NT, 512]); dout("v_new", [2, NT, 512])
    dout("o_ret", [2, NSQ, 4, 128, 128]); dout("o_rwkvT", [2, NSQ, 64, 8, 64]); dout("o_shift", [2, NSQ, RW_COLS])
    dout("o_s5re", [2, 128, NSQ, 16]); dout("o_s5im", [2, 128, NSQ, 16]); dout("o_convT", [2, 128, 88, NSQ, 2])
    dscr("X", [NT, D]); dscr("PROJ", [NT, IN_COLS]); dscr("OCAT", [NT, D]); dscr("QK", [NT, 1024])
    dscr("RWS", [6, NT, 512]); dscr("FMA", [64, NT, 40]); dscr("FMR", [64, NT, 8]); dscr("FMW", [64, NT, 8]); dscr("ERAW", [NT, D]); dscr("YS5", [NT, 512])

    C = Ctx()
    C.nc, C.S, C.cfg, C.dr = nc, S, cfg, dr
    C.uid = 0
    with ExitStack() as gs:
        def sb(name, shape, dt=F32, st=gs):
            C.uid += 1
            return TT(st.enter_context(nc.sbuf_tensor(f"t{C.uid}_" + name, list(shape), dt)))

        def ps(name, shape, st=gs):
            C.uid += 1
            return TT(st.enter_context(nc.psum_tensor(f"q{C.uid}_" + name, list(shape), F32)))
        C.sb, C.ps = sb, ps
        C.ident = sb("ident", [128, 128])
        S.dma("sp", C.ident[:], dr["ident"], writes=[C.ident])
        C.eps = sb("epsc", [128, 4])
        S.op("dve", lambda e: e.memset(C.eps[:, 0:1], 1e-6), writes=[C.eps])
        S.op("dve", lambda e: e.memset(C.eps[:, 1:2], -math.pi), writes=[C.eps])
        S.op("dve", lambda e: e.memset(C.eps[:, 2:3], 64e-5), writes=[C.eps])
        S.op("dve", lambda e: e.memset(C.eps[:, 3:4], 1.0), writes=[C.eps])
        for l in range(2):
            with ExitStack() as st:
                phase_proj(C, st, l)
            S.barrier()
            with ExitStack() as st:
                phase_prep(C, st, l)
            S.barrier()
            with ExitStack() as st:
                phase_ret(C, st, l)
            S.barrier()
            with ExitStack() as st:
                phase_diff(C, st, l)
            S.barrier()
            with ExitStack() as st:
                phase_rwkv(C, st, l)
            S.barrier()
            with ExitStack() as st:
                phase_s5(C, st, l)
            S.barrier()
            with ExitStack() as st:
                phase_wout(C, st, l)
            S.barrier()
            with ExitStack() as st:
                phase_ffn(C, st, l)
            S.barrier()
            with ExitStack() as st:
                phase_ple(C, st, l)
            S.barrier()
        S.barrier()
    return nc, S


def bcast_load(C, st, name, ap, n, q="sp"):
    t = C.sb(name, [128, n], st=st)
    C.S.dma(q, t[:], ap.partition_broadcast(128), writes=[t])
    return t


def rms_rows(C, x, P, n, g, out, ss, act_sq_out):
    S = C.S
    S.op("act", lambda e: e.activation(out=act_sq_out[:P, :n], in_=x[:P, :n], func=AF.Square, accum_out=ss[:P, 0:1]),
         reads=[x], writes=[act_sq_out, ss])
    S.op("dve", lambda e: e.tensor_scalar(out=ss[:P, 1:2], in0=ss[:P, 0:1], scalar1=1.0 / n, scalar2=1e-6,
                                          op0=ALU.mult, op1=ALU.add), reads=[ss], writes=[ss])
    S.op("act", lambda e: e.sqrt(out=ss[:P, 1:2], in_=ss[:P, 1:2]), reads=[ss], writes=[ss])
    S.op("dve", lambda e: e.reciprocal(out=ss[:P, 1:2], in_=ss[:P, 1:2]), reads=[ss], writes=[ss])
    S.op("dve", lambda e: e.scalar_tensor_tensor(out=out[:P, :n], in0=x[:P, :n], scalar=ss[:P, 1:2], in1=g[:P, :n],
                                                 op0=ALU.mult, op1=ALU.mult), reads=[x, ss, g], writes=[out])


def transpose_to(C, src, P, c0, ncols, dst_ap_fn, dst, ptp, i):
    S = C.S
    pt = ptp[i % len(ptp)]
    S.op("pe", lambda e: e.transpose(out=pt[:ncols, :P], in_=src[:P, c0:c0 + ncols], identity=C.ident[:P, :P]),
         reads=[src, C.ident], writes=[pt])
    eng = "act" if i % 2 == 0 else "dve"
    if eng == "act":
        S.op("act", lambda e: e.copy(out=dst_ap_fn(), in_=pt[:ncols, :P]), reads=[pt], writes=[dst])
    else:
        S.op("dve", lambda e: e.tensor_copy(out=dst_ap_fn(), in_=pt[:ncols, :P]), reads=[pt], writes=[dst])


def dense(C, hT, sizes, W, KC, N, wbufs, pbufs, epi, cbw=512):
    S = C.S
    cb = 0
    cnt = 0
    for c0 in range(0, N, cbw):
        ncol = min(cbw, N - c0)
        wb = wbufs[cb % len(wbufs)]
        S.dma("sp", wb[:, :KC, :ncol], W[:, c0:c0 + ncol].rearrange("(k p) c -> p k c", p=128), writes=[wb])
        for ti, (row0, P, off) in enumerate(sizes):
            po = pbufs[cnt % len(pbufs)]
            cnt += 1
            for k in range(KC):
                S.op("pe", lambda e: e.matmul(po[:P, :ncol], lhsT=hT[:, k, off:off + P], rhs=wb[:, k, :ncol],
                                              start=(k == 0), stop=(k == KC - 1)), reads=[hT, wb], writes=[po])
            epi(ti, row0, P, c0, ncol, po)
        cb += 1


def phase_proj(C, st, l):
    S, dr, cfg = C.S, C.dr, C.cfg
    g = bcast_load(C, st, "g_mix", dr["norm_mix"][l], D)
    xs = [C.sb(f"px{i}", [128, D], st=st) for i in range(2)]
    hs = C.sb("ph", [128, D], st=st)
    sq = C.sb("psq", [128, D], st=st)
    ss = C.sb("pss", [128, 2], st=st)
    hT = C.sb("phT", [128, 16, 512], st=st)
    wb = [C.sb(f"pw{i}", [128, 16, 512], st=st) for i in range(2)]
    ob = [C.sb(f"pob{i}", [128, 512], st=st) for i in range(4)]
    ptp = [C.ps(f"ppt{i}", [128, 128], st=st) for i in range(2)]
    pb = [C.ps(f"ppo{i}", [128, 512], st=st) for i in range(4)]
    src = dr["x0"] if l == 0 else dr["X"]
    n = 0
    for stl in cfg.sts:
        sizes = []
        off = 0
        for (r0, P) in stl:
            x = xs[n % 2]
            n += 1
            S.dma("sp", x[:P, :], src[r0:r0 + P, :], writes=[x])
            rms_rows(C, x, P, D, g, hs, ss, sq)
            for k in range(16):
                transpose_to(C, hs, P, k * 128, 128, lambda: hT[:, k, off:off + P], hT, ptp, k)
            sizes.append((r0, P, off))
            off += P
        cnt = [0]

        def epi(ti, r0, P, c0, ncol, po):
            o = ob[cnt[0] % 4]
            if cnt[0] % 2 == 0:
                S.op("act", lambda e: e.copy(out=o[:P, :ncol], in_=po[:P, :ncol]), reads=[po], writes=[o])
            else:
                S.op("dve", lambda e: e.tensor_copy(out=o[:P, :ncol], in_=po[:P, :ncol]), reads=[po], writes=[o])
            cnt[0] += 1
            S.dma("pool", dr["PROJ"][r0:r0 + P, c0:c0 + ncol], o[:P, :ncol], reads=[o])
        dense(C, hT, sizes, dr["w_in"][l], 16, IN_COLS, wb, pb, epi)


def v3(ap, h):
    return ap.rearrange("p (h d) -> p h d", h=h)


def rope_apply(C, src, dst, P, c0, nh, hd, half, cs, tmp, scale=None, sc0=None):
    S = C.S
    if sc0 is None:
        sc0 = c0
    s3 = v3(src[:P, sc0:sc0 + nh * hd], nh)
    d3 = v3(dst[:P, c0:c0 + nh * hd], nh)
    t3 = v3(tmp[:P, 0:nh * hd], nh)
    cosb = cs[:P, 0:1, :].to_broadcast([P, nh, half])
    sinb = cs[:P, 1:2, :].to_broadcast([P, nh, half])
    x1, x2 = s3[:, :, 0:half], s3[:, :, half:2 * half]
    if 2 * half < hd:
        S.op("pool", lambda e: e.tensor_copy(out=d3[:, :, 2 * half:hd], in_=s3[:, :, 2 * half:hd]), reads=[src], writes=[dst])
    S.op("dve", lambda e: e.tensor_tensor(out=t3[:, :, 0:half], in0=x1, in1=cosb, op=ALU.mult), reads=[src, cs], writes=[tmp])
    S.op("dve", lambda e: e.tensor_tensor(out=t3[:, :, half:2 * half], in0=x2, in1=sinb, op=ALU.mult), reads=[src, cs], writes=[tmp])
    S.op("dve", lambda e: e.tensor_tensor(out=d3[:, :, 0:half], in0=t3[:, :, 0:half], in1=t3[:, :, half:2 * half], op=ALU.subtract),
         reads=[tmp], writes=[dst])
    S.op("dve", lambda e: e.tensor_tensor(out=t3[:, :, 0:half], in0=x1, in1=sinb, op=ALU.mult), reads=[src, cs], writes=[tmp])
    S.op("dve", lambda e: e.tensor_tensor(out=t3[:, :, half:2 * half], in0=x2, in1=cosb, op=ALU.mult), reads=[src, cs], writes=[tmp])
    S.op("dve", lambda e: e.tensor_tensor(out=d3[:, :, half:2 * half], in0=t3[:, :, 0:half], in1=t3[:, :, half:2 * half], op=ALU.add),
         reads=[tmp], writes=[dst])
    if scale is not None:
        S.op("act", lambda e: e.mul(out=dst[:P, c0:c0 + nh * hd], in_=dst[:P, c0:c0 + nh * hd], mul=scale), reads=[dst], writes=[dst])


def phase_prep(C, st, l):
    S, dr, cfg = C.S, C.dr, C.cfg
    T, NS = cfg.T, cfg.NS
    pj = [C.sb(f"rpj{i}", [128, 3072], st=st) for i in range(2)]
    oo = [C.sb(f"roo{i}", [128, 2048], st=st) for i in range(2)]
    tmp = C.sb("rtmp", [128, 512], st=st)
    csr = [C.sb(f"rcsr{i}", [128, 2, 64], st=st) for i in range(2)]
    csd = [C.sb(f"rcsd{i}", [128, 2, 8], st=st) for i in range(2)]
    S.dma("pool", dr["v_new"][l], dr["PROJ"][:, 3072:3584])
    S.dma("pool", dr["o_shift"][l, 0:1, :], dr["PROJ"][T - 1:T, 3584:3584 + RW_COLS])
    for b in range(NS):
        S.dma("pool", dr["o_shift"][l, 1 + b:2 + b, :], dr["PROJ"][T + 8 * b + 7:T + 8 * b + 8, 3584:3584 + RW_COLS])
    for i, (r0, P) in enumerate(cfg.tiles):
        x, o, cr, cd = pj[i % 2], oo[i % 2], csr[i % 2], csd[i % 2]
        S.dma("sp", x[:P, :], dr["PROJ"][r0:r0 + P, 0:3072], writes=[x])
        S.dma("sp", cr[:P], dr["rope_r"][r0:r0 + P], writes=[cr])
        S.dma("sp", cd[:P], dr["rope_d"][r0:r0 + P], writes=[cd])
        rope_apply(C, x, o, P, 0, 4, 128, 64, cr, tmp)
        rope_apply(C, x, o, P, 512, 4, 128, 64, cr, tmp, scale=128 ** -0.5)
        S.dma("pool", dr["QK"][r0:r0 + P, :], o[:P, 0:1024], reads=[o])
        rope_apply(C, x, o, P, 1024, 8, 64, 8, cd, tmp, sc0=2048)
        rope_apply(C, x, o, P, 1536, 8, 64, 8, cd, tmp, sc0=2560)
        S.dma("pool", dr["PROJ"][r0:r0 + P, 2048:2560], o[:P, 1024:1536], reads=[o])
        S.dma("pool", dr["k_new"][l, r0:r0 + P, :], o[:P, 1536:2048], reads=[o])


def phase_ret(C, st, l):
    S, dr, cfg = C.S, C.dr, C.cfg
    T, NS = cfg.T, cfg.NS
    gw = bcast_load(C, st, "rgw", dr["ret_norm_w"][l], 512)
    gb = bcast_load(C, st, "rgb", dr["ret_norm_b"][l], 512)
    dmT = C.sb("rdmT", [128, 4, 128], st=st)
    S.dma("sp", dmT[:], dr["ret_dmT"].rearrange("h m l -> m h l"), writes=[dmT])
    dmT8 = C.sb("rdmT8", [8, 4, 8], st=st)
    S.dma("sp", dmT8[:], dr["ret_dmT8"].rearrange("h m l -> m h l"), writes=[dmT8])
    qd = C.sb("rqd", [128, 4, 128], st=st); S.dma("sp", qd[:], dr["ret_qd"], writes=[qd])
    kd = C.sb("rkd", [128, 4], st=st); S.dma("sp", kd[:], dr["ret_kd"], writes=[kd])
    qd8 = C.sb("rqd8", [128, 4, 8], st=st); S.dma("sp", qd8[:], dr["ret_qd8"], writes=[qd8])
    kd8 = C.sb("rkd8", [8, 4], st=st); S.dma("sp", kd8[:], dr["ret_kd8"], writes=[kd8])
    Sst = C.sb("rS", [128, 4, 128], st=st)
    qk = [C.sb(f"rqk{i}", [128, 1024], st=st) for i in range(2)]
    vg = [C.sb(f"rvg{i}", [128, 1024], st=st) for i in range(2)]
    qT = C.sb("rqT", [128, 4, 128], st=st)
    kT = C.sb("rkT", [128, 4, 128], st=st)
    qdT = C.sb("rqdT", [128, 4, 128], st=st)
    kdt = C.sb("rkdt", [128, 512], st=st)
    am = C.sb("ram", [128, 128], st=st)
    oh = C.sb("roh", [128, 512], st=st)
    sg = C.sb("rsg", [128, 512], st=st)
    stt = C.sb("rstt", [128, 16], st=st)
    ptp = [C.ps(f"rpt{i}", [128, 128], st=st) for i in range(2)]
    pa = C.ps("rpa", [128, 128], st=st)
    po = C.ps("rpo", [128, 128], st=st)
    pS = C.ps("rpS", [128, 128], st=st)

    def chunk(i, r0, L, dm, qdd, kdd, cdec):
        q, v = qk[i % 2], vg[i % 2]
        S.dma("sp", q[:L, :], dr["QK"][r0:r0 + L, :], writes=[q])
        S.dma("sp", v[:L, :], dr["PROJ"][r0:r0 + L, 1024:2048], writes=[v])
        for h in range(4):
            transpose_to(C, q, L, h * 128, 128, lambda: qT[:, h, :L], qT, ptp, 2 * h)
            transpose_to(C, q, L, 512 + h * 128, 128, lambda: kT[:, h, :L], kT, ptp, 2 * h + 1)
        S.op("pool", lambda e: e.tensor_tensor(out=qdT[:, :, :L], in0=qT[:, :, :L], in1=qdd[:, :, :L], op=ALU.mult), reads=[qT, qdd], writes=[qdT])
        S.op("pool", lambda e: e.tensor_tensor(out=v3(kdt[:L, :], 4), in0=v3(q[:L, 512:1024], 4),
                                               in1=kdd[:L, :].unsqueeze(2).to_broadcast([L, 4, 128]), op=ALU.mult), reads=[q, kdd], writes=[kdt])
        for h in range(4):
            S.op("pe", lambda e: e.matmul(pa[:L, :L], lhsT=kT[:, h, :L], rhs=qT[:, h, :L], start=True, stop=True), reads=[kT, qT], writes=[pa])
            S.op("dve", lambda e: e.tensor_tensor(out=am[:L, :L], in0=pa[:L, :L], in1=dm[:L, h, :L], op=ALU.mult), reads=[pa, dm], writes=[am])
            S.op("pe", lambda e: e.matmul(po[:L, :], lhsT=am[:L, :L], rhs=v[:L, h * 128:(h + 1) * 128], start=True, stop=False), reads=[am, v], writes=[po])
            S.op("pe", lambda e: e.matmul(po[:L, :], lhsT=qdT[:, h, :L], rhs=Sst[:, h, :], start=False, stop=True), reads=[qdT, Sst], writes=[po])
            S.op("act", lambda e: e.copy(out=oh[:L, h * 128:(h + 1) * 128], in_=po[:L, :]), reads=[po], writes=[oh])
            S.op("pe", lambda e: e.matmul(pS[:, :], lhsT=kdt[:L, h * 128:(h + 1) * 128], rhs=v[:L, h * 128:(h + 1) * 128], start=True, stop=True), reads=[kdt, v], writes=[pS])
            S.op("dve", lambda e: e.scalar_tensor_tensor(out=Sst[:, h, :], in0=Sst[:, h, :], scalar=float(cdec[h]), in1=pS[:, :],
                                                         op0=ALU.mult, op1=ALU.add), reads=[Sst, pS], writes=[Sst])
        o3 = v3(oh[:L, :], 4)
        S.op("dve", lambda e: e.tensor_reduce(out=stt[:L, 0:4], in_=o3, axis=AX.X, op=ALU.add), reads=[oh], writes=[stt])
        S.op("dve", lambda e: e.tensor_scalar(out=stt[:L, 0:4], in0=stt[:L, 0:4], scalar1=1.0 / 128, scalar2=None, op0=ALU.mult), reads=[stt], writes=[stt])
        S.op("dve", lambda e: e.tensor_tensor(out=o3, in0=o3, in1=stt[:L, 0:4].unsqueeze(2).to_broadcast([L, 4, 128]), op=ALU.subtract), reads=[oh, stt], writes=[oh])
        S.op("pool", lambda e: e.tensor_tensor(out=sg[:L, :], in0=oh[:L, :], in1=oh[:L, :], op=ALU.mult), reads=[oh], writes=[sg])
        S.op("dve", lambda e: e.tensor_reduce(out=stt[:L, 4:8], in_=v3(sg[:L, :], 4), axis=AX.X, op=ALU.add), reads=[sg], writes=[stt])
        S.op("dve", lambda e: e.tensor_scalar(out=stt[:L, 4:8], in0=stt[:L, 4:8], scalar1=1.0 / 128, scalar2=1e-6, op0=ALU.mult, op1=ALU.add), reads=[stt], writes=[stt])
        S.op("act", lambda e: e.sqrt(out=stt[:L, 4:8], in_=stt[:L, 4:8]), reads=[stt], writes=[stt])
        S.op("dve", lambda e: e.reciprocal(out=stt[:L, 4:8], in_=stt[:L, 4:8]), reads=[stt], writes=[stt])
        S.op("dve", lambda e: e.tensor_tensor(out=o3, in0=o3, in1=stt[:L, 4:8].unsqueeze(2).to_broadcast([L, 4, 128]), op=ALU.mult), reads=[oh, stt], writes=[oh])
        S.op("dve", lambda e: e.tensor_tensor(out=oh[:L, :], in0=oh[:L, :], in1=gw[:L, :], op=ALU.mult), reads=[oh, gw], writes=[oh])
        S.op("dve", lambda e: e.tensor_tensor(out=oh[:L, :], in0=oh[:L, :], in1=gb[:L, :], op=ALU.add), reads=[oh, gb], writes=[oh])
        S.op("act", lambda e: e.activation(out=sg[:L, :], in_=v[:L, 512:1024], func=AF.Silu), reads=[v], writes=[sg])
        S.op("dve", lambda e: e.tensor_tensor(out=oh[:L, :], in0=oh[:L, :], in1=sg[:L, :], op=ALU.mult), reads=[oh, sg], writes=[oh])
        S.dma("pool", dr["OCAT"][r0:r0 + L, 0:512], oh[:L, :], reads=[oh])

    gam = [1.0 - 2.0 ** (-5.0 - h) for h in range(4)]
    S.op("dve", lambda e: e.memset(Sst[:], 0.0), writes=[Sst])
    for i in range(T // 128):
        chunk(i, i * 128, 128, dmT, qd, kd, [g ** 128 for g in gam])
    S.dma("pool", dr["o_ret"][l, 0].rearrange("h d e -> d h e"), Sst[:], reads=[Sst])
    for b in range(NS):
        S.dma("sp", Sst[:], dr["st_ret"][l, b].rearrange("h d e -> d h e"), writes=[Sst])
        chunk(b, T + 8 * b, 8, dmT8, qd8, kd8, [g ** 8 for g in gam])
        S.dma("pool", dr["o_ret"][l, 1 + b].rearrange("h d e -> d h e"), Sst[:], reads=[Sst])


def _stub(C, st, l):
    pass


def _tables(cfg, past_len):
    T, NS, NT = cfg.T, cfg.NS, cfg.NT
    f = np.float32
    pos = np.concatenate([np.arange(T), np.tile(past_len + np.arange(8), NS)]).astype(f)
    inv_r = np.power(f(10000.0), -np.arange(64, dtype=f) / f(64)).astype(f)
    ang = pos[:, None] * inv_r[None, :]
    rope_r = np.stack([np.cos(ang), np.sin(ang)], 1).astype(f)
    inv_d = np.power(f(500000.0), -np.arange(8, dtype=f) / f(8)).astype(f)
    angd = pos[:, None] * inv_d[None, :]
    rope_d = np.stack([np.cos(angd), np.sin(angd)], 1).astype(f)
    log_g = np.log1p(-np.exp2(-5.0 - np.arange(4))).astype(np.float64)

    def dm(L):
        idx = np.arange(L)
        rel = idx[:, None] - idx[None, :]
        d = np.where(rel >= 0, np.exp(log_g[:, None, None] * np.maximum(rel, 0)), 0.0)
        return np.ascontiguousarray(d.transpose(0, 2, 1)).astype(f)

    def qd(L):
        v = np.exp(log_g[:, None] * (np.arange(L) + 1.0))
        return np.ascontiguousarray(np.broadcast_to(v[None], (128, 4, L))).astype(f)

    def kd(L):
        v = np.exp(log_g[:, None] * (L - 1.0 - np.arange(L)))
        return np.ascontiguousarray(v.T).astype(f)

    def cm(L):
        idx = np.arange(L)
        return np.where(idx[None, :] <= idx[:, None], 0.0, -1e30).astype(f)
    mask8 = np.zeros((8, 512), f)
    for h in range(8):
        mask8[h, h * 64:(h + 1) * 64] = 1.0
    return dict(ident=np.eye(128, dtype=f), rope_r=rope_r, rope_d=rope_d, ret_dmT=dm(128), ret_dmT8=dm(8),
                ret_qd=qd(128), ret_kd=kd(128), ret_qd8=qd(8), ret_kd8=kd(8), cmask=cm(128), cmask8=cm(8),
                mask8=mask8, tidx=np.ascontiguousarray(np.broadcast_to(np.arange(130, dtype=f)[None], (128, 130))),
                iota=np.arange(128, dtype=np.int32).reshape(128, 1))


def _gn(a):
    return np.ascontiguousarray(a.reshape(2, 16, 2, 64).transpose(0, 2, 3, 1).reshape(2, 128, 16))


def kernel(cfg=None, **inp):
    f = np.float32
    if cfg is None:
        cfg = Cfg()
    T, NS, NPG = cfg.T, cfg.NS, cfg.NPG
    past_len = NPG * 128
    A = {k: np.asarray(v) for k, v in inp.items()}
    shared = {}
    for k in ["norm_mix", "w_in", "w_out", "diff_subln", "rwkv_mu", "rwkv_w0", "rwkv_w2", "rwkv_a0", "rwkv_a2", "rwkv_g2",
              "rwkv_kk", "rwkv_ka", "s5_d", "s5_w_glu", "s5_b_glu", "s5_norm", "norm_ffn", "ffn_w_up", "ffn_w_down",
              "norm_ple", "ple_w_proj", "ple_norm_e", "ple_w_gate", "norm_final"]:
        shared[k] = np.ascontiguousarray(A[k], dtype=f)
    for k in ["ret_norm_w", "ret_norm_b", "rwkv_rk", "rwkv_ln_w", "rwkv_ln_b"]:
        shared[k] = np.ascontiguousarray(A[k].reshape(2, 512), dtype=f)
    shared["diff_l"] = np.ascontiguousarray(np.stack([A["diff_lq1"], A["diff_lk1"], A["diff_lq2"], A["diff_lk2"]], 1), dtype=f)
    shared["ck"] = np.ascontiguousarray(A["cache_k"].reshape(2, -1, 512), dtype=f)
    shared["cv"] = np.ascontiguousarray(A["cache_v"].reshape(2, -1, 512), dtype=f)
    shared["s5_lre"] = _gn(A["s5_lam_re"]); shared["s5_lim"] = _gn(A["s5_lam_im"])
    shared["s5_ls"] = _gn(np.broadcast_to(A["s5_log_step"][:, :, None], (2, 32, 64)))
    for nm, src in (("s5_bre", "s5_b_re"), ("s5_bim", "s5_b_im")):
        b = A[src].reshape(2, 16, 2, 64, 16)
        e = np.zeros((2, 128, 16, 128), f)
        for i in range(16):
            for gl in range(2):
                c0 = (i % 4) * 32 + gl * 16
                e[:, gl * 64:(gl + 1) * 64, i, c0:c0 + 16] = b[:, i, gl]
        shared[nm] = e
    for nm, src in (("s5_cre", "s5_c_re"), ("s5_cim", "s5_c_im")):
        c = A[src].reshape(2, 16, 2, 16, 64)
        e = np.zeros((2, 128, 16, 32), f)
        for i in range(16):
            for gl in range(2):
                e[:, gl * 64:(gl + 1) * 64, i, gl * 16:(gl + 1) * 16] = c[:, i, gl].transpose(0, 2, 1)
        shared[nm] = e
    shared["ffn_cw"] = np.ascontiguousarray(A["ffn_conv_w"].reshape(2, 3, 88, 128).transpose(0, 3, 2, 1), dtype=f)
    shared["ffn_cb"] = np.ascontiguousarray(A["ffn_conv_b"].reshape(2, 88, 128).transpose(0, 2, 1), dtype=f)
    shared.update(_tables(cfg, past_len))
    in_maps = []
    for c in range(NCORES):
        sl = slice(c * NS, (c + 1) * NS)
        m = dict(shared)
        m["x0"] = np.ascontiguousarray(np.concatenate([A["x_prompt"][0], A["x_sample"][sl].reshape(NS * 8, D)], 0), dtype=f)
        m["p0"] = np.ascontiguousarray(np.concatenate([A["p_prompt"][:, 0], A["p_sample"][:, sl].reshape(2, NS * 8, 256)], 1), dtype=f)
        m["pt"] = np.ascontiguousarray(A["page_table"][sl].reshape(-1), dtype=np.int32)
        m["st_ret"] = np.ascontiguousarray(A["state_ret"][:, sl], dtype=f)
        m["st_rwkvT"] = np.ascontiguousarray(A["state_rwkv"][:, sl].transpose(0, 1, 4, 2, 3), dtype=f)
        m["st_shift"] = np.ascontiguousarray(A["state_rwkv_shift"][:, sl], dtype=f)
        for nm, src in (("st_s5re", "state_s5_re"), ("st_s5im", "state_s5_im")):
            s = A[src][:, sl].reshape(2, NS, 16, 2, 64)
            m[nm] = np.ascontiguousarray(s.transpose(0, 3, 4, 1, 2).reshape(2, 128, NS, 16), dtype=f)
        cs = A["state_ffn_conv"][:, sl].reshape(2, NS, 2, 88, 128)
        m["st_convT"] = np.ascontiguousarray(cs.transpose(0, 4, 3, 1, 2), dtype=f)
        in_maps.append(m)
    nc, S = build(cfg)
    res = run_bass_kernel_spmd(nc, in_maps, core_ids=list(range(NCORES)))
    R = res.results
    kernel.last_res = res

    def cat_seq(name, fn):
        return fn(R[0][name], True), np.concatenate([fn(R[c][name], False) for c in range(NCORES)], axis=1)
    y_p = R[0]["y"][:T][None]
    y_s = np.concatenate([R[c]["y"][T:].reshape(NS, 8, D) for c in range(NCORES)], 0)
    k_p = R[0]["k_new"][:, :T].reshape(2, 1, T, 4, 128)
    v_p = R[0]["v_new"][:, :T].reshape(2, 1, T, 4, 128)
    k_s = np.concatenate([R[c]["k_new"][:, T:].reshape(2, NS, 8, 4, 128) for c in range(NCORES)], 1)
    v_s = np.concatenate([R[c]["v_new"][:, T:].reshape(2, NS, 8, 4, 128) for c in range(NCORES)], 1)
    ret_p = R[0]["o_ret"][:, 0:1]
    ret_s = np.concatenate([R[c]["o_ret"][:, 1:] for c in range(NCORES)], 1)
    rw = lambda a: a.transpose(0, 1, 3, 4, 2)
    rw_p = rw(R[0]["o_rwkvT"][:, 0:1])
    rw_s = np.concatenate([rw(R[c]["o_rwkvT"][:, 1:]) for c in range(NCORES)], 1)
    sh_p = R[0]["o_shift"][:, 0:1]
    sh_s = np.concatenate([R[c]["o_shift"][:, 1:] for c in range(NCORES)], 1)

    def s5(a):
        l_, _, b_, _ = a.shape
        return a.reshape(l_, 2, 64, b_, 16).transpose(0, 3, 4, 1, 2).reshape(l_, b_, 32, 64)
    s5r_p = s5(R[0]["o_s5re"][:, :, 0:1]); s5i_p = s5(R[0]["o_s5im"][:, :, 0:1])
    s5r_s = np.concatenate([s5(R[c]["o_s5re"][:, :, 1:]) for c in range(NCORES)], 1)
    s5i_s = np.concatenate([s5(R[c]["o_s5im"][:, :, 1:]) for c in range(NCORES)], 1)

    def cv(a):
        l_, _, _, b_, _ = a.shape
        return a.transpose(0, 3, 4, 2, 1).reshape(l_, b_, 2, 2 * DFF)
    cv_p = cv(R[0]["o_convT"][:, :, :, 0:1])
    cv_s = np.concatenate([cv(R[c]["o_convT"][:, :, :, 1:]) for c in range(NCORES)], 1)
    outs = (y_p, y_s, k_p, v_p, k_s, v_s, ret_p, ret_s, rw_p, rw_s, sh_p, sh_s, s5r_p, s5i_p, s5r_s, s5i_s, cv_p, cv_s)
    return tuple(np.ascontiguousarray(o, dtype=f) for o in outs)


def load_T(C, stl, src_ap_fn, ncols, xs, hT, ptp, norm=None):
    S = C.S
    sizes = []
    off = 0
    for n, (r0, P) in enumerate(stl):
        x = xs[n % len(xs)]
        S.dma("sp", x[:P, :ncols], src_ap_fn(r0, P), writes=[x])
        src = x
        if norm is not None:
            g, hs, ss, sq = norm
            rms_rows(C, x, P, ncols, g, hs, ss, sq)
            src = hs
        for k in range(ncols // 128):
            transpose_to(C, src, P, k * 128, 128, lambda: hT[:, k, off:off + P], hT, ptp, k)
        sizes.append((r0, P, off))
        off += P
    return sizes


def phase_wout(C, st, l):
    S, dr, cfg = C.S, C.dr, C.cfg
    xs = [C.sb(f"wx{i}", [128, D], st=st) for i in range(2)]
    hT = C.sb("whT", [128, 16, 512], st=st)
    wb = [C.sb(f"ww{i}", [128, 16, 512], st=st) for i in range(2)]
    xres = [C.sb(f"wxr{i}", [128, D], st=st) for i in range(4)]
    ptp = [C.ps(f"wpt{i}", [128, 128], st=st) for i in range(2)]
    pb = [C.ps(f"wpo{i}", [128, 512], st=st) for i in range(4)]
    xsrc = dr["x0"] if l == 0 else dr["X"]
    for stl in cfg.sts:
        sizes = load_T(C, stl, lambda r0, P: dr["OCAT"][r0:r0 + P, :], D, xs, hT, ptp)
        for ti, (r0, P, off) in enumerate(sizes):
            S.dma("sp", xres[ti][:P, :], xsrc[r0:r0 + P, :], writes=[xres[ti]])

        def epi(ti, r0, P, c0, ncol, po):
            xr = xres[ti]
            S.op("dve", lambda e: e.tensor_tensor(out=xr[:P, c0:c0 + ncol], in0=xr[:P, c0:c0 + ncol], in1=po[:P, :ncol], op=ALU.add),
                 reads=[xr, po], writes=[xr])
        dense(C, hT, sizes, dr["w_out"][l], 16, D, wb, pb, epi)
        for ti, (r0, P, off) in enumerate(sizes):
            S.dma("pool", dr["X"][r0:r0 + P, :], xres[ti][:P, :], reads=[xres[ti]])


def phase_ffn(C, st, l):
    S, dr, cfg = C.S, C.dr, C.cfg
    T, NS, NTS = cfg.T, cfg.NS, cfg.NTS
    g = bcast_load(C, st, "fg", dr["norm_ffn"][l], D)
    xs = [C.sb("fx0", [128, D], st=st)]
    hs = C.sb("fh", [128, D], st=st)
    ss = C.sb("fss", [128, 2], st=st)
    hT = C.sb("fhT", [128, 16, 512], st=st)
    actT = C.sb("factT", [128, 44, 512], st=st)
    wu = [C.sb(f"fwu{i}", [128, 16, 128], st=st) for i in range(2)]
    wd = [C.sb(f"fwd{i}", [128, 4, 512], st=st) for i in range(2)]
    halo = C.sb("fhalo", [128, 88, 2], st=st)
    cw = C.sb("fcw", [128, 88, 3], st=st); S.dma("sp", cw[:], dr["ffn_cw"][l], writes=[cw])
    cbb = C.sb("fcb", [128, 88], st=st); S.dma("sp", cbb[:], dr["ffn_cb"][l], writes=[cbb])
    ext = [C.sb(f"fext{i}", [128, 516], st=st) for i in range(2)]
    cv_ = [C.sb(f"fcv{i}", [128, 512], st=st) for i in range(2)]
    sgl = C.sb("fsgl", [128, 512], st=st)
    xr = [C.sb(f"fxr{i}", [128, 512], st=st) for i in range(4)]
    ptp = [C.ps(f"fpt{i}", [128, 128], st=st) for i in range(2)]
    pu = [C.ps(f"fpu{i}", [128, 512], st=st) for i in range(2)]
    pd = [C.ps(f"fpd{i}", [128, 512], st=st) for i in range(4)]
    S.op("dve", lambda e: e.memset(halo[:], 0.0), writes=[halo])
    wup = dr["ffn_w_up"][l].rearrange("(k p) c -> p k c", p=128)
    wdn = dr["ffn_w_down"][l].rearrange("(j p) c -> p j c", p=128)
    cnt = 0
    xcnt = 0
    for si, stl in enumerate(cfg.sts):
        sample = (stl[0][0] == T)
        sizes = load_T(C, stl, lambda r0, P: dr["X"][r0:r0 + P, :], D, xs, hT, ptp, norm=(g, hs, ss, hs))
        n = sum(P for _, P, _ in sizes)
        for j in range(44):
            for half in range(2):
                ct = j + 44 * half
                w = wu[cnt % 2]
                p_ = pu[cnt % 2]
                ex = ext[cnt % 2]
                cvt = cv_[half]
                cnt += 1
                S.dma("sp", w[:], wup[:, :, ct * 128:(ct + 1) * 128], writes=[w])
                for k in range(16):
                    S.op("pe", lambda e: e.matmul(p_[:, :n], lhsT=w[:, k, :], rhs=hT[:, k, :n], start=(k == 0), stop=(k == 15)),
                         reads=[w, hT], writes=[p_])
                if not sample:
                    e0 = lambda a, b: ex[:, a:b]
                    S.op("pool", lambda e: e.tensor_copy(out=ex[:, 0:2], in_=halo[:, ct, :]), reads=[halo], writes=[ex])
                    S.op("act", lambda e: e.copy(out=ex[:, 2:2 + n], in_=p_[:, :n]), reads=[p_], writes=[ex])
                    S.op("pool", lambda e: e.tensor_copy(out=halo[:, ct, :], in_=ex[:, n:n + 2]), reads=[ex], writes=[halo])
                    sl = [ex[:, 0:n], ex[:, 1:1 + n], ex[:, 2:2 + n]]
                    cvo = cvt[:, :n]
                else:
                    e3 = ex[:, 0:NS * 10].rearrange("p (b t) -> p b t", t=10)
                    S.dma("sp", e3[:, :, 0:2], dr["st_convT"][l, :, ct, :, :], writes=[ex])
                    S.op("act", lambda e: e.copy(out=e3[:, :, 2:10], in_=p_[:, :n].rearrange("p (b t) -> p b t", t=8)), reads=[p_], writes=[ex])
                    S.dma("pool", dr["o_convT"][l, :, ct, 1:, :], e3[:, :, 8:10], reads=[ex])
                    sl = [e3[:, :, 0:8], e3[:, :, 1:9], e3[:, :, 2:10]]
                    cvo = cvt[:, :n].rearrange("p (b t) -> p b t", t=8)
                S.op("act", lambda e: e.activation(out=cvo, in_=sl[2], func=AF.Identity, bias=cbb[:, ct:ct + 1], scale=cw[:, ct, 2:3]),
                     reads=[ex, cbb, cw], writes=[cvt])
                S.op("dve", lambda e: e.scalar_tensor_tensor(out=cvo, in0=sl[1], scalar=cw[:, ct, 1:2], in1=cvo, op0=ALU.mult, op1=ALU.add),
                     reads=[ex, cw, cvt], writes=[cvt])
                S.op("dve", lambda e: e.scalar_tensor_tensor(out=cvo, in0=sl[0], scalar=cw[:, ct, 0:1], in1=cvo, op0=ALU.mult, op1=ALU.add),
                     reads=[ex, cw, cvt], writes=[cvt])
            S.op("act", lambda e: e.activation(out=sgl[:, :n], in_=cv_[0][:, :n], func=AF.Silu), reads=[cv_[0]], writes=[sgl])
            S.op("dve", lambda e: e.tensor_tensor(out=actT[:, j, :n], in0=sgl[:, :n], in1=cv_[1][:, :n], op=ALU.mult),
                 reads=[sgl, cv_[1]], writes=[actT])
        if stl[-1][0] + stl[-1][1] == T:
            S.dma("pool", dr["o_convT"][l, :, :, 0, :], halo[:], reads=[halo])
        for c0 in range(0, D, 512):
            for jq in range(11):
                w = wd[xcnt % 2]
                xcnt += 1
                S.dma("sp", w[:], wdn[:, jq * 4:(jq + 1) * 4, c0:c0 + 512], writes=[w])
                for ti, (r0, P, off) in enumerate(sizes):
                    for jj in range(4):
                        j = jq * 4 + jj
                        S.op("pe", lambda e: e.matmul(pd[ti][:P, :], lhsT=actT[:, j, off:off + P], rhs=w[:, jj, :],
                                                      start=(j == 0), stop=(j == 43)), reads=[actT, w], writes=[pd[ti]])
            for ti, (r0, P, off) in enumerate(sizes):
                x = xr[ti]
                S.dma("sp", x[:P, :], dr["X"][r0:r0 + P, c0:c0 + 512], writes=[x])
                S.op("dve", lambda e: e.tensor_tensor(out=x[:P, :], in0=x[:P, :], in1=pd[ti][:P, :], op=ALU.add), reads=[x, pd[ti]], writes=[x])
                S.dma("pool", dr["X"][r0:r0 + P, c0:c0 + 512], x[:P, :], reads=[x])


def phase_ple(C, st0, l):
    from contextlib import ExitStack
    S, dr, cfg = C.S, C.dr, C.cfg
    with ExitStack() as st:
        xs = [C.sb(f"ep{i}", [128, 256], st=st) for i in range(2)]
        pT = C.sb("epT", [128, 2, 512], st=st)
        wb = [C.sb(f"ew{i}", [128, 2, 512], st=st) for i in range(2)]
        ob = [C.sb(f"eob{i}", [128, 512], st=st) for i in range(4)]
        ptp = [C.ps(f"ept{i}", [128, 128], st=st) for i in range(2)]
        pb = [C.ps(f"epo{i}", [128, 512], st=st) for i in range(4)]
        cnt = [0]
        for stl in cfg.sts:
            sizes = load_T(C, stl, lambda r0, P: dr["p0"][l, r0:r0 + P, :], 256, xs, pT, ptp)

            def epi(ti, r0, P, c0, ncol, po):
                o = ob[cnt[0] % 4]
                cnt[0] += 1
                S.op("act", lambda e: e.copy(out=o[:P, :ncol], in_=po[:P, :ncol]), reads=[po], writes=[o])
                S.dma("pool", dr["ERAW"][r0:r0 + P, c0:c0 + ncol], o[:P, :ncol], reads=[o])
            dense(C, pT, sizes, dr["ple_w_proj"][l], 2, D, wb, pb, epi)
    S.barrier()
    with ExitStack() as st:
        g = bcast_load(C, st, "gg", dr["norm_ple"][l], D)
        ge = bcast_load(C, st, "gge", dr["ple_norm_e"][l], D)
        gf = bcast_load(C, st, "ggf", dr["norm_final"], D)
        xres = [C.sb(f"gx{i}", [128, D], st=st) for i in range(2)]
        et = [C.sb(f"ge{i}", [128, D], st=st) for i in range(2)]
        hs = C.sb("gh", [128, D], st=st)
        sq = C.sb("gsq", [128, D], st=st)
        ss = C.sb("gss", [128, 2], st=st)
        hT = C.sb("ghT", [128, 16, 256], st=st)
        wb = [C.sb(f"gw{i}", [128, 16, 512], st=st) for i in range(2)]
        sg = [C.sb(f"gsg{i}", [128, 512], st=st) for i in range(2)]
        ptp = [C.ps(f"gpt{i}", [128, 128], st=st) for i in range(2)]
        pb = [C.ps(f"gpo{i}", [128, 512], st=st) for i in range(4)]
        cnt = [0]
        for stl in cfg.sts2:
            sizes = []
            off = 0
            for ti, (r0, P) in enumerate(stl):
                x = xres[ti]
                S.dma("sp", x[:P, :], dr["X"][r0:r0 + P, :], writes=[x])
                rms_rows(C, x, P, D, g, hs, ss, sq)
                for k in range(16):
                    transpose_to(C, hs, P, k * 128, 128, lambda: hT[:, k, off:off + P], hT, ptp, k)
                S.dma("sp", hs[:P, :], dr["ERAW"][r0:r0 + P, :], writes=[hs])
                rms_rows(C, hs, P, D, ge, et[ti], ss, sq)
                sizes.append((r0, P, off))
                off += P

            def epi(ti, r0, P, c0, ncol, po):
                s_ = sg[cnt[0] % 2]
                cnt[0] += 1
                S.op("act", lambda e: e.activation(out=s_[:P, :ncol], in_=po[:P, :ncol], func=AF.Sigmoid), reads=[po], writes=[s_])
                S.op("dve", lambda e: e.tensor_tensor(out=s_[:P, :ncol], in0=s_[:P, :ncol], in1=et[ti][:P, c0:c0 + ncol], op=ALU.mult),
                     reads=[s_, et[ti]], writes=[s_])
                S.op("pool", lambda e: e.tensor_tensor(out=xres[ti][:P, c0:c0 + ncol], in0=xres[ti][:P, c0:c0 + ncol], in1=s_[:P, :ncol], op=ALU.add),
                     reads=[s_, xres[ti]], writes=[xres[ti]])
            dense(C, hT, sizes, dr["ple_w_gate"][l], 16, D, wb, pb, epi)
            for ti, (r0, P, off) in enumerate(sizes):
                S.dma("pool", dr["X"][r0:r0 + P, :], xres[ti][:P, :], reads=[xres[ti]])
                if l == 1:
                    rms_rows(C, xres[ti], P, D, gf, hs, ss, sq)
                    S.dma("pool", dr["y"][r0:r0 + P, :], hs[:P, :], reads=[hs])


def phase_s5(C, st0, l):
    from contextlib import ExitStack
    S, dr, cfg = C.S, C.dr, C.cfg
    T, NS = cfg.T, cfg.NS
    PI = math.pi
    with ExitStack() as st:
        def ld(name, ap, shape):
            t = C.sb(name, shape, st=st)
            S.dma("sp", t[:], ap, writes=[t])
            return t
        lre = ld("slre", dr["s5_lre"][l], [128, 16]); lim = ld("slim", dr["s5_lim"][l], [128, 16]); ls = ld("sls", dr["s5_ls"][l], [128, 16])
        BRe = ld("sBRe", dr["s5_bre"][l], [128, 16, 128]); BIe = ld("sBIe", dr["s5_bim"][l], [128, 16, 128])
        CRe = ld("sCRe", dr["s5_cre"][l], [128, 16, 32]); CIm = ld("sCIm", dr["s5_cim"][l], [128, 16, 32])
        tidx = ld("stidx", dr["tidx"], [128, 130])
        sm = C.sb("ssm", [128, 16, 16], st=st)
        K = lambda k: sm[:, k, :]
        def tt(eng, out, a, b, op, rd, wr):
            S.op(eng, lambda e: e.tensor_tensor(out=out, in0=a, in1=b, op=op), reads=rd, writes=wr)
        def ts(eng, out, a, s1, s2, op0, op1, rd, wr):
            if op1 is None:
                S.op(eng, lambda e: e.tensor_scalar(out=out, in0=a, scalar1=s1, scalar2=None, op0=op0), reads=rd, writes=wr)
            else:
                S.op(eng, lambda e: e.tensor_scalar(out=out, in0=a, scalar1=s1, scalar2=s2, op0=op0, op1=op1), reads=rd, writes=wr)
        def act(out, a, fn, rd, wr, **kw):
            S.op("act", lambda e: e.activation(out=out, in_=a, func=fn, **kw), reads=rd, writes=wr)
        mpi = C.eps[:, 1:2]
        rti = C.sb("srti", [128, 129], I32, st=st); rtf = C.sb("srtf", [128, 129], st=st); rtx = C.sb("srtx", [128, 129], st=st)
        def sinr(out, x, n, shift, rd, wr):
            S.op("dve", lambda e: e.tensor_scalar(out=rtx[:, :n], in0=x, scalar1=shift, scalar2=None, op0=ALU.add), reads=rd, writes=[rtx])
            S.op("dve", lambda e: e.tensor_scalar(out=rti[:, :n], in0=rtx[:, :n], scalar1=1.0 / (2 * PI), scalar2=None, op0=ALU.mult), reads=[rtx], writes=[rti])
            S.op("dve", lambda e: e.tensor_copy(out=rtf[:, :n], in_=rti[:, :n]), reads=[rti], writes=[rtf])
            S.op("dve", lambda e: e.scalar_tensor_tensor(out=rtx[:, :n], in0=rtf[:, :n], scalar=-2 * PI, in1=rtx[:, :n], op0=ALU.mult, op1=ALU.add), reads=[rtf, rtx], writes=[rtx])
            S.op("dve", lambda e: e.tensor_scalar(out=rtf[:, :n], in0=rtx[:, :n], scalar1=PI, scalar2=2 * PI, op0=ALU.is_gt, op1=ALU.mult), reads=[rtx], writes=[rtf])
            S.op("dve", lambda e: e.tensor_tensor(out=rtx[:, :n], in0=rtx[:, :n], in1=rtf[:, :n], op=ALU.subtract), reads=[rtx, rtf], writes=[rtx])
            S.op("act", lambda e: e.activation(out=out, in_=rtx[:, :n], func=AF.Sin), reads=[rtx], writes=wr)
        act(K(0), ls[:], AF.Exp, [ls], [sm])
        tt("dve", K(1), lre[:], K(0), ALU.mult, [lre, sm], [sm])
        tt("dve", K(2), lim[:], K(0), ALU.mult, [lim, sm], [sm])
        ts("dve", K(3), K(1), -1.0, None, ALU.mult, None, [sm], [sm])
        act(K(9), K(1), AF.Exp, [sm], [sm])
        sinr(K(10), K(2), 16, 0.0, [sm], [sm])
        sinr(K(11), K(2), 16, 0.5 * PI, [sm], [sm])
        tt("dve", K(4), K(9), K(11), ALU.mult, [sm], [sm])
        tt("dve", K(5), K(9), K(10), ALU.mult, [sm], [sm])
        tt("dve", K(9), lre[:], lre[:], ALU.mult, [lre], [sm])
        tt("dve", K(10), lim[:], lim[:], ALU.mult, [lim], [sm])
        tt("dve", K(9), K(9), K(10), ALU.add, [sm], [sm])
        S.op("dve", lambda e: e.reciprocal(out=K(9), in_=K(9)), reads=[sm], writes=[sm])
        ts("dve", K(10), K(4), -1.0, None, ALU.add, None, [sm], [sm])
        tt("dve", K(11), K(10), lre[:], ALU.mult, [sm, lre], [sm])
        tt("dve", K(12), K(5), lim[:], ALU.mult, [sm, lim], [sm])
        tt("dve", K(11), K(11), K(12), ALU.add, [sm], [sm])
        tt("dve", K(6), K(11), K(9), ALU.mult, [sm], [sm])
        tt("dve", K(11), K(5), lre[:], ALU.mult, [sm, lre], [sm])
        tt("dve", K(12), K(10), lim[:], ALU.mult, [sm, lim], [sm])
        tt("dve", K(11), K(11), K(12), ALU.subtract, [sm], [sm])
        tt("dve", K(7), K(11), K(9), ALU.mult, [sm], [sm])
        ts("dve", K(8), K(7), -1.0, None, ALU.mult, None, [sm], [sm])
        S.op("dve", lambda e: e.tensor_scalar(out=CIm[:], in0=CIm[:], scalar1=-1.0, scalar2=None, op0=ALU.mult), reads=[CIm], writes=[CIm])
        BBTr = C.sb("sBBTr", [128, 16, 128], st=st); BBTi = C.sb("sBBTi", [128, 16, 128], st=st)
        t1 = C.sb("st1", [128, 128], st=st); t2 = C.sb("st2", [128, 128], st=st)
        ptp = [C.ps(f"spt{i}", [128, 128], st=st) for i in range(2)]
        for i in range(16):
            S.op("dve", lambda e: e.tensor_scalar(out=t1[:], in0=BRe[:, i, :], scalar1=sm[:, 6, i:i + 1], scalar2=None, op0=ALU.mult), reads=[BRe, sm], writes=[t1])
            S.op("dve", lambda e: e.scalar_tensor_tensor(out=t1[:], in0=BIe[:, i, :], scalar=sm[:, 8, i:i + 1], in1=t1[:], op0=ALU.mult, op1=ALU.add), reads=[BIe, sm, t1], writes=[t1])
            transpose_to(C, t1, 128, 0, 128, lambda: BBTr[:, i, :], BBTr, ptp, 2 * i)
            S.op("dve", lambda e: e.tensor_scalar(out=t2[:], in0=BIe[:, i, :], scalar1=sm[:, 6, i:i + 1], scalar2=None, op0=ALU.mult), reads=[BIe, sm], writes=[t2])
            S.op("dve", lambda e: e.scalar_tensor_tensor(out=t2[:], in0=BRe[:, i, :], scalar=sm[:, 7, i:i + 1], in1=t2[:], op0=ALU.mult, op1=ALU.add), reads=[BRe, sm, t2], writes=[t2])
            transpose_to(C, t2, 128, 0, 128, lambda: BBTi[:, i, :], BBTi, ptp, 2 * i + 1)
        PWr = C.sb("sPWr", [128, 16, 129], st=st); PWi = C.sb("sPWi", [128, 16, 129], st=st)
        PIr = C.sb("sPIr", [128, 16, 129], st=st); PIi = C.sb("sPIi", [128, 16, 129], st=st)
        ta = C.sb("sta", [128, 129], st=st); tb = C.sb("stb", [128, 129], st=st); tc = C.sb("stc", [128, 129], st=st); td = C.sb("std", [128, 129], st=st)
        for i in range(16):
            S.op("dve", lambda e: e.tensor_scalar(out=ta[:], in0=tidx[:, 0:129], scalar1=sm[:, 2, i:i + 1], scalar2=None, op0=ALU.mult), reads=[tidx, sm], writes=[ta])
            sinr(tb[:], ta[:], 129, 0.0, [ta], [tb])
            sinr(tc[:], ta[:], 129, 0.5 * PI, [ta], [tc])
            act(td[:], tidx[:, 0:129], AF.Exp, [tidx, sm], [td], scale=sm[:, 1, i:i + 1])
            tt("dve", PWr[:, i, :], td[:], tc[:], ALU.mult, [td, tc], [PWr])
            tt("dve", PWi[:, i, :], td[:], tb[:], ALU.mult, [td, tb], [PWi])
            act(td[:], tidx[:, 0:129], AF.Exp, [tidx, sm], [td], scale=sm[:, 3, i:i + 1])
            tt("dve", PIr[:, i, :], td[:], tc[:], ALU.mult, [td, tc], [PIr])
            S.op("dve", lambda e: e.scalar_tensor_tensor(out=PIi[:, i, :], in0=td[:], scalar=-1.0, in1=tb[:], op0=ALU.mult, op1=ALU.mult), reads=[td, tb], writes=[PIi])
        ones = C.sb("sones", [128, 128], st=st)
        S.op("dve", lambda e: e.memset(ones[:], 1.0), writes=[ones])
        ut = [C.sb(f"su{i}", [128, 512], st=st) for i in range(2)]
        uT = C.sb("suT", [128, 4, 128], st=st)
        SR = C.sb("sSR", [128, 16, 128], st=st); SI = C.sb("sSI", [128, 16, 128], st=st)
        br = C.sb("sbr", [128, 16, 128], st=st); bi = C.sb("sbi", [128, 16, 128], st=st)
        w1 = C.sb("sw1", [128, 16, 128], st=st); w2 = C.sb("sw2", [128, 16, 128], st=st); w3 = C.sb("sw3", [128, 16, 128], st=st); w4 = C.sb("sw4", [128, 16, 128], st=st)
        zr = C.sb("szr", [128, 16, 128], st=st); zi = C.sb("szi", [128, 16, 128], st=st)
        Z0 = C.sb("sZ0", [128, 2, 16], st=st); ZL = C.sb("sZL", [128, 2, 16], st=st); zt = C.sb("szt", [128, 4, 16], st=st)
        sin_ = C.sb("ssin", [128, 2, 16], st=st)
        SL = C.sb("sSL", [128, 2, 16], st=st)
        yo = [C.sb(f"syo{i}", [128, 512], st=st) for i in range(2)]
        pbr = [C.ps(f"spbr{i}", [128, 128], st=st) for i in range(2)]; pbi = [C.ps(f"spbi{i}", [128, 128], st=st) for i in range(2)]
        py = [C.ps(f"spy{i}", [128, 512], st=st) for i in range(2)]
        ycnt = [0]

        def chunk(r0, c0, L, last_seq_idx):
            for i in range(16):
                q = i // 4
                pr_, pi_ = pbr[i % 2], pbi[i % 2]
                S.op("pe", lambda e: e.matmul(pr_[:, :L], lhsT=BBTr[:, i, :], rhs=uT[:, q, c0:c0 + L], start=True, stop=True), reads=[BBTr, uT], writes=[pr_])
                S.op("pe", lambda e: e.matmul(pi_[:, :L], lhsT=BBTi[:, i, :], rhs=uT[:, q, c0:c0 + L], start=True, stop=True), reads=[BBTi, uT], writes=[pi_])
                S.op("act", lambda e: e.copy(out=br[:, i, :L], in_=pr_[:, :L]), reads=[pr_], writes=[br])
                S.op("dve", lambda e: e.tensor_copy(out=bi[:, i, :L], in_=pi_[:, :L]), reads=[pi_], writes=[bi])
            tt("dve", w1[:, :, :L], PIr[:, :, :L], br[:, :, :L], ALU.mult, [PIr, br], [w1])
            tt("pool", w2[:, :, :L], PIi[:, :, :L], bi[:, :, :L], ALU.mult, [PIi, bi], [w2])
            tt("dve", w1[:, :, :L], w1[:, :, :L], w2[:, :, :L], ALU.subtract, [w1, w2], [w1])
            tt("pool", w3[:, :, :L], PIr[:, :, :L], bi[:, :, :L], ALU.mult, [PIr, bi], [w3])
            tt("dve", w4[:, :, :L], PIi[:, :, :L], br[:, :, :L], ALU.mult, [PIi, br], [w4])
            tt("dve", w3[:, :, :L], w3[:, :, :L], w4[:, :, :L], ALU.add, [w3, w4], [w3])
            for i in range(16):
                S.op("dve", lambda e: e.tensor_tensor_scan(out=zr[:, i, :L], data0=ones[:, :L], data1=w1[:, i, :L], initial=Z0[:, 0, i:i + 1], op0=ALU.mult, op1=ALU.add), reads=[ones, w1, Z0], writes=[zr])
                S.op("dve", lambda e: e.tensor_tensor_scan(out=zi[:, i, :L], data0=ones[:, :L], data1=w3[:, i, :L], initial=Z0[:, 1, i:i + 1], op0=ALU.mult, op1=ALU.add), reads=[ones, w3, Z0], writes=[zi])
            S.op("act", lambda e: e.copy(out=ZL[:, 0, :], in_=zr[:, :, L - 1]), reads=[zr], writes=[ZL])
            S.op("act", lambda e: e.copy(out=ZL[:, 1, :], in_=zi[:, :, L - 1]), reads=[zi], writes=[ZL])
            tt("dve", w1[:, :, :L], PWr[:, :, :L], zr[:, :, :L], ALU.mult, [PWr, zr], [w1])
            tt("pool", w2[:, :, :L], PWi[:, :, :L], zi[:, :, :L], ALU.mult, [PWi, zi], [w2])
            tt("dve", SR[:, :, :L], w1[:, :, :L], w2[:, :, :L], ALU.subtract, [w1, w2], [SR])
            tt("pool", w3[:, :, :L], PWr[:, :, :L], zi[:, :, :L], ALU.mult, [PWr, zi], [w3])
            tt("dve", w4[:, :, :L], PWi[:, :, :L], zr[:, :, :L], ALU.mult, [PWi, zr], [w4])
            tt("dve", SI[:, :, :L], w3[:, :, :L], w4[:, :, :L], ALU.add, [w3, w4], [SI])
            p = py[ycnt[0] % 2]; o = yo[ycnt[0] % 2]; ycnt[0] += 1
            for i in range(16):
                S.op("pe", lambda e: e.matmul(p[:L, i * 32:(i + 1) * 32], lhsT=SR[:, i, :L], rhs=CRe[:, i, :], start=True, stop=False), reads=[SR, CRe], writes=[p])
                S.op("pe", lambda e: e.matmul(p[:L, i * 32:(i + 1) * 32], lhsT=SI[:, i, :L], rhs=CIm[:, i, :], start=False, stop=True), reads=[SI, CIm], writes=[p])
            S.op("act", lambda e: e.copy(out=o[:L, :], in_=p[:L, :]), reads=[p], writes=[o])
            S.dma("pool", dr["YS5"][r0:r0 + L, :], o[:L, :], reads=[o])
            tt("dve", zt[:, 0, :], PWr[:, :, L], ZL[:, 0, :], ALU.mult, [PWr, ZL], [zt])
            tt("dve", zt[:, 1, :], PWi[:, :, L], ZL[:, 1, :], ALU.mult, [PWi, ZL], [zt])
            tt("dve", zt[:, 2, :], PWr[:, :, L], ZL[:, 1, :], ALU.mult, [PWr, ZL], [zt])
            tt("dve", zt[:, 3, :], PWi[:, :, L], ZL[:, 0, :], ALU.mult, [PWi, ZL], [zt])
            tt("dve", Z0[:, 0, :], zt[:, 0, :], zt[:, 1, :], ALU.subtract, [zt], [Z0])
            tt("dve", Z0[:, 1, :], zt[:, 2, :], zt[:, 3, :], ALU.add, [zt], [Z0])
            if last_seq_idx is not None:
                S.op("act", lambda e: e.copy(out=SL[:, 0, :], in_=SR[:, :, L - 1]), reads=[SR], writes=[SL])
                S.op("act", lambda e: e.copy(out=SL[:, 1, :], in_=SI[:, :, L - 1]), reads=[SI], writes=[SL])
                S.dma("pool", dr["o_s5re"][l, :, last_seq_idx, :], SL[:, 0, :], reads=[SL])
                S.dma("pool", dr["o_s5im"][l, :, last_seq_idx, :], SL[:, 1, :], reads=[SL])

        S.op("dve", lambda e: e.memset(Z0[:], 0.0), writes=[Z0])
        for ti, (r0, P) in enumerate(cfg.tiles):
            u = ut[ti % 2]
            S.dma("sp", u[:P, :], dr["PROJ"][r0:r0 + P, 5376:5888], writes=[u])
            for k in range(4):
                transpose_to(C, u, P, k * 128, 128, lambda: uT[:, k, :P], uT, ptp, k)
            if r0 < T:
                chunk(r0, 0, 128, 0 if r0 + 128 == T else None)
            else:
                for b in range(NS):
                    S.dma("sp", sin_[:, 0, :], dr["st_s5re"][l, :, b, :], writes=[sin_])
                    S.dma("sp", sin_[:, 1, :], dr["st_s5im"][l, :, b, :], writes=[sin_])
                    tt("dve", zt[:, 0, :], K(4), sin_[:, 0, :], ALU.mult, [sm, sin_], [zt])
                    tt("dve", zt[:, 1, :], K(5), sin_[:, 1, :], ALU.mult, [sm, sin_], [zt])
                    tt("dve", zt[:, 2, :], K(4), sin_[:, 1, :], ALU.mult, [sm, sin_], [zt])
                    tt("dve", zt[:, 3, :], K(5), sin_[:, 0, :], ALU.mult, [sm, sin_], [zt])
                    tt("dve", Z0[:, 0, :], zt[:, 0, :], zt[:, 1, :], ALU.subtract, [zt], [Z0])
                    tt("dve", Z0[:, 1, :], zt[:, 2, :], zt[:, 3, :], ALU.add, [zt], [Z0])
                    chunk(r0 + 8 * b, 8 * b, 8, 1 + b)
    S.barrier()
    with ExitStack() as st:
        dsk = bcast_load(C, st, "pd", dr["s5_d"][l], 512)
        bgl = bcast_load(C, st, "pbg", dr["s5_b_glu"][l], 512)
        gn = bcast_load(C, st, "pgn", dr["s5_norm"][l], 512)
        wg = C.sb("pwg", [128, 4, 512], st=st)
        S.dma("sp", wg[:], dr["s5_w_glu"][l].rearrange("(k p) c -> p k c", p=128), writes=[wg])
        yt = [C.sb(f"py{i}", [128, 512], st=st) for i in range(2)]
        ut = [C.sb(f"pu{i}", [128, 512], st=st) for i in range(2)]
        a1 = C.sb("pa1", [128, 512], st=st); a2 = C.sb("pa2", [128, 512], st=st); a3 = C.sb("pa3", [128, 512], st=st)
        yT = C.sb("pyT", [128, 4, 128], st=st)
        ss = C.sb("pss", [128, 2], st=st)
        ptp = [C.ps(f"ppt{i}", [128, 128], st=st) for i in range(2)]
        pg = C.ps("ppg", [128, 512], st=st)
        for ti, (r0, P) in enumerate(cfg.tiles):
            y, u = yt[ti % 2], ut[ti % 2]
            S.dma("sp", y[:P, :], dr["YS5"][r0:r0 + P, :], writes=[y])
            S.dma("sp", u[:P, :], dr["PROJ"][r0:r0 + P, 5376:5888], writes=[u])
            S.op("dve", lambda e: e.tensor_tensor(out=u[:P, :], in0=u[:P, :], in1=dsk[:P, :], op=ALU.mult), reads=[u, dsk], writes=[u])
            S.op("dve", lambda e: e.tensor_tensor(out=y[:P, :], in0=y[:P, :], in1=u[:P, :], op=ALU.add), reads=[y, u], writes=[y])
            S.op("pool", lambda e: e.tensor_tensor(out=a1[:P, :], in0=y[:P, :], in1=y[:P, :], op=ALU.mult), reads=[y], writes=[a1])
            S.op("dve", lambda e: e.tensor_scalar(out=a1[:P, :], in0=a1[:P, :], scalar1=0.044715, scalar2=1.0, op0=ALU.mult, op1=ALU.add), reads=[a1], writes=[a1])
            S.op("dve", lambda e: e.tensor_tensor(out=a1[:P, :], in0=a1[:P, :], in1=y[:P, :], op=ALU.mult), reads=[a1, y], writes=[a1])
            S.op("act", lambda e: e.activation(out=a1[:P, :], in_=a1[:P, :], func=AF.Sigmoid, scale=2.0 * math.sqrt(2.0 / math.pi)), reads=[a1], writes=[a1])
            S.op("dve", lambda e: e.tensor_tensor(out=a2[:P, :], in0=a1[:P, :], in1=y[:P, :], op=ALU.mult), reads=[a1, y], writes=[a2])
            for k in range(4):
                transpose_to(C, a2, P, k * 128, 128, lambda: yT[:, k, :P], yT, ptp, k)
            for k in range(4):
                S.op("pe", lambda e: e.matmul(pg[:P, :], lhsT=yT[:, k, :P], rhs=wg[:, k, :], start=(k == 0), stop=(k == 3)), reads=[yT, wg], writes=[pg])
            S.op("dve", lambda e: e.tensor_tensor(out=a3[:P, :], in0=pg[:P, :], in1=bgl[:P, :], op=ALU.add), reads=[pg, bgl], writes=[a3])
            S.op("act", lambda e: e.activation(out=a3[:P, :], in_=a3[:P, :], func=AF.Sigmoid), reads=[a3], writes=[a3])
            S.op("dve", lambda e: e.tensor_tensor(out=a2[:P, :], in0=a2[:P, :], in1=a3[:P, :], op=ALU.mult), reads=[a2, a3], writes=[a2])
            rms_rows(C, a2, P, 512, gn, a3, ss, a1)
            S.dma("pool", dr["OCAT"][r0:r0 + P, 1536:2048], a3[:P, :], reads=[a3])


def phase_rwkv(C, st0, l):
    from contextlib import ExitStack
    S, dr, cfg = C.S, C.dr, C.cfg
    T, NS, NTS = cfg.T, cfg.NS, cfg.NTS
    R0 = 3584

    def tt(eng, out, a, b, op, rd, wr):
        S.op(eng, lambda e: e.tensor_tensor(out=out, in0=a, in1=b, op=op), reads=rd, writes=wr)
    with ExitStack() as st:
        mu = bcast_load(C, st, "kmu", dr["rwkv_mu"][l], RW_COLS)
        w0 = bcast_load(C, st, "kw0", dr["rwkv_w0"][l], 512)
        a0 = bcast_load(C, st, "ka0", dr["rwkv_a0"][l], 512)
        kkw = bcast_load(C, st, "kkk", dr["rwkv_kk"][l], 512)
        kaw = bcast_load(C, st, "kka", dr["rwkv_ka"][l], 512)
        w2 = C.sb("kw2", [64, 512], st=st); S.dma("sp", w2[:], dr["rwkv_w2"][l], writes=[w2])
        a2 = C.sb("ka2", [64, 512], st=st); S.dma("sp", a2[:], dr["rwkv_a2"][l], writes=[a2])
        g2 = C.sb("kg2", [128, 512], st=st); S.dma("sp", g2[:], dr["rwkv_g2"][l], writes=[g2])
        cur = [C.sb(f"kc{i}", [128, RW_COLS], st=st) for i in range(2)]
        prv = [C.sb(f"kp{i}", [128, RW_COLS], st=st) for i in range(2)]
        lo = C.sb("klo", [128, 256], st=st)
        loT = C.sb("kloT", [128, 3, 128], st=st)
        dec = C.sb("kdec", [128, 512], st=st); aa = C.sb("kaa", [128, 512], st=st); gg = C.sb("kgg", [128, 512], st=st)
        kk = C.sb("kkkv", [128, 512], st=st); k2 = C.sb("kk2", [128, 512], st=st); bb = C.sb("kbb", [128, 512], st=st)
        ain = C.sb("kain", [128, 512], st=st); t5 = C.sb("kt5", [128, 512], st=st)
        s8 = C.sb("ks8", [128, 8], st=st)
        fa = [C.sb(f"kfa{i}", [64, 128, 40], st=st) for i in range(2)]
        fr = [C.sb(f"kfr{i}", [64, 128, 8], st=st) for i in range(2)]
        fw = [C.sb(f"kfw{i}", [64, 128, 8], st=st) for i in range(2)]
        ptp = [C.ps(f"kpt{i}", [128, 128], st=st) for i in range(2)]
        pl = [C.ps(f"kpl{i}", [128, 512], st=st) for i in range(3)]
        for i in range(2):
            S.op("pool", lambda e: e.memset(fa[i][:], 0.0), writes=[fa[i]])
        for ti, (r0, P) in enumerate(cfg.tiles):
            c, p = cur[ti % 2], prv[ti % 2]
            S.dma("sp", c[:P, :], dr["PROJ"][r0:r0 + P, R0:R0 + RW_COLS], writes=[c])
            if r0 == 0:
                S.op("pool", lambda e: e.memset(p[0:32, :], 0.0), writes=[p])
                S.dma("sp", p[1:P, :], dr["PROJ"][0:P - 1, R0:R0 + RW_COLS], writes=[p])
            else:
                S.dma("sp", p[:P, :], dr["PROJ"][r0 - 1:r0 + P - 1, R0:R0 + RW_COLS], writes=[p])
                if r0 >= T:
                    for b in range(NS):
                        S.dma("sp", p[8 * b:8 * b + 1, :], dr["st_shift"][l, b:b + 1, :], writes=[p])
            tt("dve", p[:P, :], p[:P, :], c[:P, :], ALU.subtract, [p, c], [p])
            tt("pool", p[:P, :], p[:P, :], mu[:P, :], ALU.mult, [p, mu], [p])
            tt("dve", p[:P, :], p[:P, :], c[:P, :], ALU.add, [p, c], [p])
            x = p
            S.op("act", lambda e: e.activation(out=lo[:P, 0:64], in_=x[:P, 1536:1600], func=AF.Tanh), reads=[x], writes=[lo])
            S.op("act", lambda e: e.copy(out=lo[:P, 64:128], in_=x[:P, 1600:1664]), reads=[x], writes=[lo])
            S.op("act", lambda e: e.activation(out=lo[:P, 128:256], in_=x[:P, 1664:1792], func=AF.Sigmoid), reads=[x], writes=[lo])
            transpose_to(C, lo, P, 0, 64, lambda: loT[0:64, 0, :P], loT, ptp, 0)
            transpose_to(C, lo, P, 64, 64, lambda: loT[0:64, 1, :P], loT, ptp, 1)
            transpose_to(C, lo, P, 128, 128, lambda: loT[:, 2, :P], loT, ptp, 2)
            S.op("pe", lambda e: e.matmul(pl[0][:P, :], lhsT=loT[0:64, 0, :P], rhs=w2[:, :], start=True, stop=True), reads=[loT, w2], writes=[pl[0]])
            S.op("pe", lambda e: e.matmul(pl[1][:P, :], lhsT=loT[0:64, 1, :P], rhs=a2[:, :], start=True, stop=True), reads=[loT, a2], writes=[pl[1]])
            S.op("pe", lambda e: e.matmul(pl[2][:P, :], lhsT=loT[:, 2, :P], rhs=g2[:, :], start=True, stop=True), reads=[loT, g2], writes=[pl[2]])
            tt("dve", dec[:P, :], pl[0][:P, :], w0[:P, :], ALU.add, [pl[0], w0], [dec])
            S.op("act", lambda e: e.activation(out=dec[:P, :], in_=dec[:P, :], func=AF.Sigmoid), reads=[dec], writes=[dec])
            S.op("act", lambda e: e.activation(out=dec[:P, :], in_=dec[:P, :], func=AF.Exp, scale=-math.exp(-0.5)), reads=[dec], writes=[dec])
            tt("dve", aa[:P, :], pl[1][:P, :], a0[:P, :], ALU.add, [pl[1], a0], [aa])
            S.op("act", lambda e: e.activation(out=aa[:P, :], in_=aa[:P, :], func=AF.Sigmoid), reads=[aa], writes=[aa])
            S.op("act", lambda e: e.copy(out=gg[:P, :], in_=pl[2][:P, :]), reads=[pl[2]], writes=[gg])
            tt("dve", kk[:P, :], x[:P, 512:1024], kkw[:P, :], ALU.mult, [x, kkw], [kk])
            tt("pool", t5[:P, :], kk[:P, :], kk[:P, :], ALU.mult, [kk], [t5])
            S.op("dve", lambda e: e.tensor_reduce(out=s8[:P, :], in_=v3(t5[:P, :], 8), axis=AX.X, op=ALU.add), reads=[t5], writes=[s8])
            S.op("dve", lambda e: e.tensor_scalar(out=s8[:P, :], in0=s8[:P, :], scalar1=1e-24, scalar2=None, op0=ALU.max), reads=[s8], writes=[s8])
            S.op("act", lambda e: e.sqrt(out=s8[:P, :], in_=s8[:P, :]), reads=[s8], writes=[s8])
            S.op("dve", lambda e: e.reciprocal(out=s8[:P, :], in_=s8[:P, :]), reads=[s8], writes=[s8])
            tt("dve", v3(kk[:P, :], 8), v3(kk[:P, :], 8), s8[:P, :].unsqueeze(2).to_broadcast([P, 8, 64]), ALU.mult, [kk, s8], [kk])
            S.op("dve", lambda e: e.scalar_tensor_tensor(out=t5[:P, :], in0=aa[:P, :], scalar=-1.0, in1=kaw[:P, :], op0=ALU.add, op1=ALU.mult), reads=[aa, kaw], writes=[t5])
            S.op("dve", lambda e: e.scalar_tensor_tensor(out=k2[:P, :], in0=t5[:P, :], scalar=1.0, in1=x[:P, 512:1024], op0=ALU.add, op1=ALU.mult), reads=[t5, x], writes=[k2])
            tt("pool", bb[:P, :], kk[:P, :], aa[:P, :], ALU.mult, [kk, aa], [bb])
            S.op("act", lambda e: e.mul(out=ain[:P, :], in_=kk[:P, :], mul=-1.0), reads=[kk], writes=[ain])
            S.dma("pool", dr["RWS"][0, r0:r0 + P, :], bb[:P, :], reads=[bb])
            S.dma("pool", dr["RWS"][1, r0:r0 + P, :], k2[:P, :], reads=[k2])
            S.dma("pool", dr["RWS"][2, r0:r0 + P, :], x[:P, 1024:1536], reads=[x])
            S.dma("pool", dr["RWS"][3, r0:r0 + P, :], x[:P, 0:512], reads=[x])
            S.dma("pool", dr["RWS"][4, r0:r0 + P, :], gg[:P, :], reads=[gg])
            A_, R_, W_ = fa[ti % 2], fr[ti % 2], fw[ti % 2]
            for h in range(8):
                transpose_to(C, ain, P, h * 64, 64, lambda: A_[:, :P, 32 + h], A_, ptp, 3 * h)
                transpose_to(C, x, P, h * 64, 64, lambda: R_[:, :P, h], R_, ptp, 3 * h + 1)
                transpose_to(C, dec, P, h * 64, 64, lambda: W_[:, :P, h], W_, ptp, 3 * h + 2)
            S.dma("pool", dr["FMA"][:, r0:r0 + P, :], A_[:, :P, :], reads=[A_])
            S.dma("pool", dr["FMR"][:, r0:r0 + P, :], R_[:, :P, :], reads=[R_])
            S.dma("pool", dr["FMW"][:, r0:r0 + P, :], W_[:, :P, :], reads=[W_])
    S.barrier()
    TC = 16
    with ExitStack() as st:
        ST = C.sb("nST", [64, 512], st=st)
        TMP = C.sb("nTMP", [64, 512], st=st)
        m40 = C.sb("nm40", [40, 512], st=st)
        S.op("dve", lambda e: e.memset(m40[:], 0.0), writes=[m40])
        S.dma("sp", m40[0:8, :], dr["mask8"], writes=[m40])
        S.dma("sp", m40[32:40, :], dr["mask8"], writes=[m40])
        BK = [C.sb(f"nBK{i}", [40, TC, 64], st=st) for i in range(2)]
        SAV = [C.sb(f"nSAV{i}", [40, TC, 512], st=st) for i in range(2)]
        AT = [C.sb(f"nAT{i}", [64, TC, 40], st=st) for i in range(2)]
        RT = [C.sb(f"nRT{i}", [64, TC, 8], st=st) for i in range(2)]
        WT = [C.sb(f"nWT{i}", [64, TC, 8], st=st) for i in range(2)]
        YB = [C.sb(f"nYB{i}", [8, TC, 512], st=st) for i in range(2)]
        psa = [C.ps(f"npsa{i}", [40, 512], st=st) for i in range(2)]
        psu = [C.ps(f"npsu{i}", [64, 512], st=st) for i in range(2)]
        psy = [C.ps(f"npsy{i}", [8, 512], st=st) for i in range(2)]
        for i in range(2):
            S.op("pool", lambda e: e.memset(BK[i][:], 0.0), writes=[BK[i]])
            S.op("pool", lambda e: e.memset(SAV[i][:], 0.0), writes=[SAV[i]])
        cc = [0]
        stp = [0]

        def seq(t0, n):
            for c0 in range(0, n, TC):
                L = min(TC, n - c0)
                r0 = t0 + c0
                k = cc[0] % 2
                cc[0] += 1
                bk, sav, at, rt, wt, yb = BK[k], SAV[k], AT[k], RT[k], WT[k], YB[k]
                S.dma("sp", bk[0:8, :L, :], dr["RWS"][1, r0:r0 + L, :].rearrange("t (h j) -> h t j", h=8), writes=[bk])
                S.dma("sp", bk[32:40, :L, :], dr["RWS"][0, r0:r0 + L, :].rearrange("t (h j) -> h t j", h=8), writes=[bk])
                S.dma("sp", sav[0:8, :L, :], dr["RWS"][2, r0:r0 + L, :].partition_broadcast(8), writes=[sav])
                S.op("pool", lambda e: e.tensor_tensor(out=sav[0:8, :L, :], in0=sav[0:8, :L, :],
                                                       in1=m40[0:8, :].unsqueeze(1).to_broadcast([8, L, 512]), op=ALU.mult), reads=[sav, m40], writes=[sav])
                S.dma("sp", at[:, :L, :], dr["FMA"][:, r0:r0 + L, :], writes=[at])
                S.dma("sp", rt[:, :L, :], dr["FMR"][:, r0:r0 + L, :], writes=[rt])
                S.dma("sp", wt[:, :L, :], dr["FMW"][:, r0:r0 + L, :], writes=[wt])
                for t in range(L):
                    pa, pu, py_ = psa[stp[0] % 2], psu[stp[0] % 2], psy[stp[0] % 2]
                    stp[0] += 1
                    S.op("pe", lambda e: e.matmul(pa[:, :], lhsT=at[:, t, :], rhs=ST[:, :], start=True, stop=True), reads=[at, ST], writes=[pa])
                    S.op("dve", lambda e: e.tensor_tensor(out=sav[32:40, t, :], in0=pa[32:40, :], in1=m40[32:40, :], op=ALU.mult), reads=[pa, m40], writes=[sav])
                    S.op("pool", lambda e: e.tensor_tensor(out=v3(TMP[:, :], 8), in0=v3(ST[:, :], 8),
                                                           in1=wt[:, t, :].unsqueeze(2).to_broadcast([64, 8, 64]), op=ALU.mult), reads=[ST, wt], writes=[TMP])
                    S.op("pe", lambda e: e.matmul(pu[:, :], lhsT=bk[:, t, :], rhs=sav[:, t, :], start=True, stop=True), reads=[bk, sav], writes=[pu])
                    S.op("dve", lambda e: e.tensor_tensor(out=ST[:, :], in0=TMP[:, :], in1=pu[:, :], op=ALU.add), reads=[TMP, pu], writes=[ST])
                    S.op("pe", lambda e: e.matmul(py_[:, :], lhsT=rt[:, t, :], rhs=ST[:, :], start=True, stop=True), reads=[rt, ST], writes=[py_])
                    S.op("act", lambda e: e.copy(out=yb[:, t, :], in_=py_[:, :]), reads=[py_], writes=[yb])
                for h in range(8):
                    S.dma("pool", dr["RWS"][5, r0:r0 + L, h * 64:(h + 1) * 64], yb[h:h + 1, :L, h * 64:(h + 1) * 64], reads=[yb])

        S.op("dve", lambda e: e.memset(ST[:], 0.0), writes=[ST])
        seq(0, T)
        S.dma("pool", dr["o_rwkvT"][l, 0].rearrange("j h i -> j (h i)"), ST[:, :], reads=[ST])
        for b in range(NS):
            S.dma("sp", ST[:, :], dr["st_rwkvT"][l, b].rearrange("j h i -> j (h i)"), writes=[ST])
            seq(T + 8 * b, 8)
            S.dma("pool", dr["o_rwkvT"][l, 1 + b].rearrange("j h i -> j (h i)"), ST[:, :], reads=[ST])
    S.barrier()
    with ExitStack() as st:
        lw = bcast_load(C, st, "olw", dr["rwkv_ln_w"][l], 512)
        lb = bcast_load(C, st, "olb", dr["rwkv_ln_b"][l], 512)
        rk = bcast_load(C, st, "ork", dr["rwkv_rk"][l], 512)
        ins = [[C.sb(f"oi{j}_{i}", [128, 512], st=st) for j in range(5)] for i in range(2)]
        t1 = C.sb("ot1", [128, 512], st=st); t2 = C.sb("ot2", [128, 512], st=st)
        s8 = C.sb("os8", [128, 16], st=st)
        for ti, (r0, P) in enumerate(cfg.tiles):
            k2, v, r, g, y = ins[ti % 2]
            for j, tl in zip((1, 2, 3, 4, 5), (k2, v, r, g, y)):
                S.dma("sp", tl[:P, :], dr["RWS"][j, r0:r0 + P, :], writes=[tl])
            y3 = v3(y[:P, :], 8)
            S.op("dve", lambda e: e.tensor_reduce(out=s8[:P, 0:8], in_=y3, axis=AX.X, op=ALU.add), reads=[y], writes=[s8])
            S.op("dve", lambda e: e.tensor_scalar(out=s8[:P, 0:8], in0=s8[:P, 0:8], scalar1=1.0 / 64, scalar2=None, op0=ALU.mult), reads=[s8], writes=[s8])
            tt("dve", y3, y3, s8[:P, 0:8].unsqueeze(2).to_broadcast([P, 8, 64]), ALU.subtract, [y, s8], [y])
            tt("pool", t1[:P, :], y[:P, :], y[:P, :], ALU.mult, [y], [t1])
            S.op("dve", lambda e: e.tensor_reduce(out=s8[:P, 8:16], in_=v3(t1[:P, :], 8), axis=AX.X, op=ALU.add), reads=[t1], writes=[s8])
            S.op("dve", lambda e: e.tensor_scalar(out=s8[:P, 8:16], in0=s8[:P, 8:16], scalar1=1.0 / 64, scalar2=64e-5, op0=ALU.mult, op1=ALU.add), reads=[s8], writes=[s8])
            S.op("act", lambda e: e.sqrt(out=s8[:P, 8:16], in_=s8[:P, 8:16]), reads=[s8], writes=[s8])
            S.op("dve", lambda e: e.reciprocal(out=s8[:P, 8:16], in_=s8[:P, 8:16]), reads=[s8], writes=[s8])
            tt("dve", y3, y3, s8[:P, 8:16].unsqueeze(2).to_broadcast([P, 8, 64]), ALU.mult, [y, s8], [y])
            tt("dve", y[:P, :], y[:P, :], lw[:P, :], ALU.mult, [y, lw], [y])
            tt("dve", y[:P, :], y[:P, :], lb[:P, :], ALU.add, [y, lb], [y])
            tt("pool", t1[:P, :], r[:P, :], k2[:P, :], ALU.mult, [r, k2], [t1])
            tt("pool", t1[:P, :], t1[:P, :], rk[:P, :], ALU.mult, [t1, rk], [t1])
            S.op("dve", lambda e: e.tensor_reduce(out=s8[:P, 0:8], in_=v3(t1[:P, :], 8), axis=AX.X, op=ALU.add), reads=[t1], writes=[s8])
            tt("dve", v3(t2[:P, :], 8), v3(v[:P, :], 8), s8[:P, 0:8].unsqueeze(2).to_broadcast([P, 8, 64]), ALU.mult, [v, s8], [t2])
            tt("dve", y[:P, :], y[:P, :], t2[:P, :], ALU.add, [y, t2], [y])
            tt("dve", y[:P, :], y[:P, :], g[:P, :], ALU.mult, [y, g], [y])
            S.dma("pool", dr["OCAT"][r0:r0 + P, 1024:1536], y[:P, :], reads=[y])


def phase_diff(C, st0, l):
    from contextlib import ExitStack
    S, dr, cfg = C.S, C.dr, C.cfg
    T, NS, NTS, NPG = cfg.T, cfg.NS, cfg.NTS, cfg.NPG
    NQB = T // 128
    lam_init = 0.8 - 0.6 * math.exp(-0.3 * l)

    def tt(eng, out, a, b, op, rd, wr):
        S.op(eng, lambda e: e.tensor_tensor(out=out, in0=a, in1=b, op=op), reads=rd, writes=wr)
    with ExitStack() as st:
        dl = C.sb("dl", [128, 4, 64], st=st)
        S.dma("sp", dl[:], dr["diff_l"][l].partition_broadcast(128), writes=[dl])
        lt = C.sb("dlt", [128, 2, 64], st=st)
        lam = C.sb("dlam", [128, 4], st=st)
        tt("dve", lt[:, 0, :], dl[:, 0, :], dl[:, 1, :], ALU.mult, [dl], [lt])
        tt("dve", lt[:, 1, :], dl[:, 2, :], dl[:, 3, :], ALU.mult, [dl], [lt])
        S.op("dve", lambda e: e.tensor_reduce(out=lam[:, 0:2], in_=lt[:], axis=AX.X, op=ALU.add), reads=[lt], writes=[lam])
        S.op("act", lambda e: e.activation(out=lam[:, 0:2], in_=lam[:, 0:2], func=AF.Exp), reads=[lam], writes=[lam])
        tt("dve", lam[:, 2:3], lam[:, 0:1], lam[:, 1:2], ALU.subtract, [lam], [lam])
        S.op("dve", lambda e: e.tensor_scalar(out=lam[:, 2:3], in0=lam[:, 2:3], scalar1=lam_init, scalar2=-1.0, op0=ALU.add, op1=ALU.mult), reads=[lam], writes=[lam])
        sub = bcast_load(C, st, "dsub", dr["diff_subln"][l], 128)
        S.op("act", lambda e: e.mul(out=sub[:], in_=sub[:], mul=(1.0 - lam_init)), reads=[sub], writes=[sub])
        cm = C.sb("dcm", [128, 128], st=st); S.dma("sp", cm[:], dr["cmask"], writes=[cm])
        cm8 = C.sb("dcm8", [8, 8], st=st); S.dma("sp", cm8[:], dr["cmask8"], writes=[cm8])
        KW = max(T, NPG * 128 + 8)
        KT = C.sb("dKT", [128, KW], st=st)
        SC = [C.sb(f"dSC{m}", [128, KW], st=st) for m in range(2)]
        QT = C.sb("dQT", [128, 128], st=st)
        xq = [C.sb(f"dxq{i}", [128, 128], st=st) for i in range(2)]
        pTs = [C.sb(f"dpT{i}", [128, 128], st=st) for i in range(3)]
        oo = [C.sb(f"doo{i}", [128, 128], st=st) for i in range(2)]
        sq = C.sb("dsq", [128, 128], st=st)
        sm = C.sb("dsm", [128, 8], st=st)
        ss = C.sb("dss", [128, 2], st=st)
        vn = C.sb("dvn", [8, 128], st=st)
        ptp = [C.ps(f"dpt{i}", [128, 128], st=st) for i in range(2)]
        psc = [C.ps(f"dps{i}", [128, 512], st=st) for i in range(3)]
        po = [C.ps(f"dpo{i}", [128, 128], st=st) for i in range(2)]
        cnt = [0]

        def attn(P, nk, vblocks, mask, msz, r0, h):
            for m in range(2):
                for k0 in range(0, nk, 512):
                    n = min(512, nk - k0)
                    p_ = psc[cnt[0] % 3]
                    cnt[0] += 1
                    S.op("pe", lambda e: e.matmul(p_[:P, :n], lhsT=QT[m * 64:(m + 1) * 64, :P], rhs=KT[m * 64:(m + 1) * 64, k0:k0 + n],
                                                  start=True, stop=True), reads=[QT, KT], writes=[p_])
                    if cnt[0] % 2 == 0:
                        S.op("act", lambda e: e.mul(out=SC[m][:P, k0:k0 + n], in_=p_[:P, :n], mul=0.125), reads=[p_], writes=[SC[m]])
                    else:
                        S.op("dve", lambda e: e.tensor_scalar(out=SC[m][:P, k0:k0 + n], in0=p_[:P, :n], scalar1=0.125, scalar2=None, op0=ALU.mult), reads=[p_], writes=[SC[m]])
                tt("pool", SC[m][:P, nk - msz:nk], SC[m][:P, nk - msz:nk], mask[:P, :msz], ALU.add, [SC[m], mask], [SC[m]])
                S.op("dve", lambda e: e.tensor_reduce(out=sm[:P, m:m + 1], in_=SC[m][:P, :nk], axis=AX.X, op=ALU.max), reads=[SC[m]], writes=[sm])
                S.op("dve", lambda e: e.tensor_scalar(out=sm[:P, 2 + m:3 + m], in0=sm[:P, m:m + 1], scalar1=-1.0, scalar2=None, op0=ALU.mult), reads=[sm], writes=[sm])
                S.op("act", lambda e: e.activation(out=SC[m][:P, :nk], in_=SC[m][:P, :nk], func=AF.Exp, bias=sm[:P, 2 + m:3 + m], scale=1.0,
                                                   accum_out=sm[:P, 4 + m:5 + m]), reads=[SC[m], sm], writes=[SC[m], sm])
            S.op("dve", lambda e: e.reciprocal(out=sm[:P, 6:8], in_=sm[:P, 4:6]), reads=[sm], writes=[sm])
            tt("dve", sm[:P, 7:8], sm[:P, 7:8], lam[:P, 2:3], ALU.mult, [sm, lam], [sm])
            S.op("dve", lambda e: e.tensor_scalar(out=SC[0][:P, :nk], in0=SC[0][:P, :nk], scalar1=sm[:P, 6:7], scalar2=None, op0=ALU.mult), reads=[SC[0], sm], writes=[SC[0]])
            S.op("dve", lambda e: e.scalar_tensor_tensor(out=SC[0][:P, :nk], in0=SC[1][:P, :nk], scalar=sm[:P, 7:8], in1=SC[0][:P, :nk],
                                                         op0=ALU.mult, op1=ALU.add), reads=[SC[0], SC[1], sm], writes=[SC[0]])
            pO = po[cnt[0] % 2]
            for bi, (k0, ksz, v_ap, vt) in enumerate(vblocks):
                pt = ptp[bi % 2]
                pT = pTs[bi % 3]
                S.op("pe", lambda e: e.transpose(out=pt[:ksz, :P], in_=SC[0][:P, k0:k0 + ksz], identity=C.ident[:P, :P]), reads=[SC[0], C.ident], writes=[pt])
                if bi % 2 == 0:
                    S.op("act", lambda e: e.copy(out=pT[:ksz, :P], in_=pt[:ksz, :P]), reads=[pt], writes=[pT])
                else:
                    S.op("dve", lambda e: e.tensor_copy(out=pT[:ksz, :P], in_=pt[:ksz, :P]), reads=[pt], writes=[pT])
                S.op("pe", lambda e: e.matmul(pO[:P, :], lhsT=pT[:ksz, :P], rhs=v_ap, start=(bi == 0), stop=(bi == len(vblocks) - 1)), reads=[pT, vt], writes=[pO])
            o = oo[cnt[0] % 2]
            S.op("act", lambda e: e.copy(out=o[:P, :], in_=pO[:P, :]), reads=[pO], writes=[o])
            rms_rows(C, o, P, 128, sub, o, ss, sq)
            S.dma("pool", dr["OCAT"][r0:r0 + P, 512 + h * 128:512 + (h + 1) * 128], o[:P, :], reads=[o])

        n = 0
        st2 = ExitStack()
        st2.__enter__()
        Vh = C.sb("dVh", [128, NQB, 128], st=st2)
        for h in range(4):
            for kb in range(NQB):
                x = xq[n % 2]; n += 1
                S.dma("sp", x[:, :], dr["k_new"][l, kb * 128:(kb + 1) * 128, h * 128:(h + 1) * 128], writes=[x])
                transpose_to(C, x, 128, 0, 128, lambda: KT[:, kb * 128:(kb + 1) * 128], KT, ptp, kb)
            S.dma("sp", Vh[:, :NQB, :], dr["PROJ"][0:T, 3072 + h * 128:3072 + (h + 1) * 128].rearrange("(k p) c -> p k c", p=128), writes=[Vh])
            for qb in range(NQB):
                x = xq[n % 2]; n += 1
                S.dma("sp", x[:, :], dr["PROJ"][qb * 128:(qb + 1) * 128, 2048 + h * 128:2048 + (h + 1) * 128], writes=[x])
                transpose_to(C, x, 128, 0, 128, lambda: QT[:, :], QT, ptp, qb)
                vb = [(kb * 128, 128, Vh[:, kb, :], Vh) for kb in range(qb + 1)]
                attn(128, (qb + 1) * 128, vb, cm, 128, qb * 128, h)
        S.barrier()
        st2.__exit__(None, None, None)
        ptb = C.sb("dptb", [128, NS * NPG], I32, st=st)
        S.dma("sp", ptb[:], dr["pt"].partition_broadcast(128), writes=[ptb])
        ptf = C.sb("dptf", [128, NS * NPG], st=st)
        io = C.sb("dio", [128, 1], I32, st=st); S.dma("sp", io[:], dr["iota"], writes=[io])
        iof = C.sb("diof", [128, 1], st=st)
        S.op("dve", lambda e: e.tensor_copy(out=ptf[:], in_=ptb[:]), reads=[ptb], writes=[ptf])
        S.op("dve", lambda e: e.tensor_copy(out=iof[:], in_=io[:]), reads=[io], writes=[iof])
        S.op("dve", lambda e: e.tensor_scalar(out=ptf[:], in0=ptf[:], scalar1=128.0, scalar2=iof[:, 0:1], op0=ALU.mult, op1=ALU.add), reads=[ptf, iof], writes=[ptf])
        S.op("dve", lambda e: e.tensor_scalar(out=ptf[:], in0=ptf[:], scalar1=float(l * cfg.NPOOL * 128), scalar2=None, op0=ALU.add), reads=[ptf], writes=[ptf])
        idx = C.sb("didx", [128, NS * NPG], I32, st=st)
        S.op("dve", lambda e: e.tensor_copy(out=idx[:], in_=ptf[:]), reads=[ptf], writes=[idx])
        KP = [C.sb(f"dKP{i}", [128, 512], st=st) for i in range(2)]
        VP = [C.sb(f"dVP{i}", [128, 512], st=st) for i in range(NPG)]
        ckf = dr["ck"].rearrange("l r c -> (l r) c")
        cvf = dr["cv"].rearrange("l r c -> (l r) c")
        KTs = C.sb("dKTs", [128, 4, NPG * 128 + 8], st=st)
        QTs = C.sb("dQTs", [128, 4, 8], st=st)
        q8 = C.sb("dq8", [8, 512], st=st); k8 = C.sb("dk8", [8, 512], st=st); v8 = C.sb("dv8", [8, 512], st=st)
        for b in range(NS):
            r0 = T + 8 * b
            for pg in range(NPG):
                kp = KP[pg % 2]
                col = b * NPG + pg
                gather(C, kp, kp[:, :], ckf, idx, col)
                gather(C, VP[pg], VP[pg][:, :], cvf, idx, col)
                for h in range(4):
                    transpose_to(C, kp, 128, h * 128, 128, lambda: KTs[:, h, pg * 128:(pg + 1) * 128], KTs, ptp, h)
            S.dma("sp", q8[:, :], dr["PROJ"][r0:r0 + 8, 2048:2560], writes=[q8])
            S.dma("sp", k8[:, :], dr["k_new"][l, r0:r0 + 8, :], writes=[k8])
            S.dma("sp", v8[:, :], dr["PROJ"][r0:r0 + 8, 3072:3584], writes=[v8])
            for h in range(4):
                transpose_to(C, k8, 8, h * 128, 128, lambda: KTs[:, h, NPG * 128:NPG * 128 + 8], KTs, ptp, h)
                transpose_to(C, q8, 8, h * 128, 128, lambda: QTs[:, h, :], QTs, ptp, h + 1)
            for h in range(4):
                S.op("act", lambda e: e.copy(out=QT[:, 0:8], in_=QTs[:, h, :]), reads=[QTs], writes=[QT])
                S.op("pool", lambda e: e.tensor_copy(out=KT[:, 0:NPG * 128 + 8], in_=KTs[:, h, :]), reads=[KTs], writes=[KT])
                vb = [(pg * 128, 128, VP[pg][:, h * 128:(h + 1) * 128], VP[pg]) for pg in range(NPG)]
                vb.append((NPG * 128, 8, v8[:, h * 128:(h + 1) * 128], v8))
                attn(8, NPG * 128 + 8, vb, cm8, 8, r0, h)


def gather(C, tile, out_ap, table, idx, col):
    S = C.S
    q = "pool"
    S._deps(q, [idx], [tile])
    if q not in S.dsem or S.dcnt[q] >= S.DK * (S.SEM_LIMIT // 16):
        S.dsem[q] = [S.newsem() for _ in range(S.DK)]
        S.dcnt[q] = 0
        S.last.setdefault("dma", {})
    i = S.dcnt[q]
    sem = S.dsem[q][i % S.DK]
    prev = 16 * (i // S.DK)
    if prev > 0:
        S._wait(q, ("dma", sem, prev))
    ins = S.eng[q].indirect_dma_start(out=out_ap, out_offset=None, in_=table,
                                      in_offset=bass.IndirectOffsetOnAxis(ap=idx[:, col:col + 1], axis=0))
    ins.then_inc(sem, 16)
    S.dcnt[q] = i + 1
    ref = ("dma", sem, prev + 16)
    S.last.setdefault("dma", {})[id(sem)] = ref
    S._mark(ref, [idx], [tile])
    S.nins += 1
```

```python
import math
import numpy as np
import concourse.bass as bass
import concourse.mybir as mybir
from concourse.bass_utils import run_bass_kernel_spmd

F32 = mybir.dt.float32
I32 = mybir.dt.int32
F32R = mybir.dt.float32r


def R_(ap):
    return ap.bitcast(F32R)
AF = mybir.ActivationFunctionType
ALU = mybir.AluOpType
AX = mybir.AxisListType

D = 2048
GW = 512
IN_COLS = 5888
RW_COLS = 1792
DFF = 5632
NCORES = 8


class TT:
    __slots__ = ("t", "lw", "rd")

    def __init__(self, t):
        self.t = t
        self.lw = None
        self.rd = {}

    def __getitem__(self, k):
        return self.t[k]


class Sched:
    SEM_LIMIT = 30000
    DK = 8

    def __init__(self, nc):
        self.nc = nc
        self.eng = {"pe": nc.tensor, "act": nc.scalar, "dve": nc.vector, "pool": nc.gpsimd, "sp": nc.sync}
        self.csem = {}
        self.ccnt = {}
        self.waited = {e: {} for e in self.eng}
        self.dsem = {}
        self.dcnt = {}
        self.semid = 0
        self.last = {}
        self.nins = 0

    def newsem(self):
        self.semid += 1
        return self.nc.alloc_semaphore(f"s{self.semid}")

    def _need(self, e, ref, lst):
        pe, sem, val = ref
        if e == "pe" and pe == "pe":
            return
        w = self.waited[e]
        k = id(sem)
        if w.get(k, (None, 0))[1] >= val:
            return
        w[k] = (sem, val)
        lst.append((sem, val))

    def _wait(self, e, ref):
        lst = []
        self._need(e, ref, lst)
        for sem, val in lst:
            self.eng[e].wait_ge(sem, val)
            self.nins += 1

    def _deps(self, e, reads, writes, lst=None):
        own = lst is None
        if own:
            lst = []
        for t in reads:
            if t.lw is not None:
                self._need(e, t.lw, lst)
        for t in writes:
            if t.lw is not None and t.lw[0] != e:
                self._need(e, t.lw, lst)
            for r in t.rd.values():
                if r[0] != e:
                    self._need(e, r, lst)
        if own:
            for sem, val in lst:
                self.eng[e].wait_ge(sem, val)
                self.nins += 1
        return lst

    def _emit_waits(self, e, lst, ins):
        for sem, val in lst[:-1]:
            pass
        if lst:
            sem, val = lst[-1]
            ins._wait_ge(sem, val)

    def _mark(self, ref, reads, writes):
        for t in reads:
            t.rd[id(ref[1])] = ref
        for t in writes:
            t.lw = ref
            t.rd = {}

    def op(self, e, fn, reads=(), writes=()):
        lst = self._deps(e, reads, writes, [])
        for sem, val in lst[:-1]:
            self.eng[e].wait_ge(sem, val)
            self.nins += 1
        if e not in self.csem or self.ccnt[e] >= self.SEM_LIMIT:
            self.csem[e] = self.newsem()
            self.ccnt[e] = 0
        ins = fn(self.eng[e])
        if lst:
            ins._wait_ge(lst[-1][0], lst[-1][1])
        self.ccnt[e] += 1
        ins.then_inc(self.csem[e], 1)
        ref = (e, self.csem[e], self.ccnt[e])
        self.last[e] = ref
        self._mark(ref, reads, writes)
        self.nins += 1
        return ref

    def dma(self, q, out, in_, reads=(), writes=(), **kw):
        lst = self._deps(q, reads, writes, [])
        if q not in self.dsem or self.dcnt[q] >= self.DK * (self.SEM_LIMIT // 16):
            self.dsem[q] = [self.newsem() for _ in range(self.DK)]
            self.dcnt[q] = 0
            self.last.setdefault("dma", {})
        i = self.dcnt[q]
        sem = self.dsem[q][i % self.DK]
        prev = 16 * (i // self.DK)
        if prev > 0:
            self._need(q, ("dma", sem, prev), lst)
        for sm_, val in lst[:-1]:
            self.eng[q].wait_ge(sm_, val)
            self.nins += 1
        ins = self.eng[q].dma_start(out=out, in_=in_, **kw)
        if lst:
            ins._wait_ge(lst[-1][0], lst[-1][1])
        ins.then_inc(sem, 16)
        self.dcnt[q] = i + 1
        ref = ("dma", sem, prev + 16)
        self.last["dma"][id(sem)] = ref
        self._mark(ref, reads, writes)
        self.nins += 1
        return ref

    def barrier(self, engines=("pe", "act", "dve", "pool", "sp")):
        refs = [r for k, r in self.last.items() if k != "dma"]
        refs += list(self.last.get("dma", {}).values())
        for e in engines:
            for r in refs:
                if r[0] == e:
                    continue
                pe, sem, val = r
                w = self.waited[e]
                if w.get(id(sem), (None, 0))[1] >= val:
                    continue
                w[id(sem)] = (sem, val)
                self.eng[e].wait_ge(sem, val)
                self.nins += 1


class Cfg:
    def __init__(self, T=8192, NS=16, NPG=16, NPOOL=2560):
        self.T, self.NS, self.NPG, self.NPOOL = T, NS, NPG, NPOOL
        self.NTS = NS * 8
        self.NT = T + self.NTS
        self.tiles = [(i * 128, 128) for i in range(T // 128)] + [(T, self.NTS)]
        self.sts = [self.tiles[i:i + 4] for i in range(0, T // 128, 4)] + [[self.tiles[-1]]]
        self.sts2 = [self.tiles[i:i + 2] for i in range(0, T // 128, 2)] + [[self.tiles[-1]]]


class Ctx:
    pass


def build(cfg):
    from contextlib import ExitStack
    nc = bass.Bass("TRN2", target_bir_lowering=False)
    S = Sched(nc)
    T, NS, NT, NTS, NPG = cfg.T, cfg.NS, cfg.NT, cfg.NTS, cfg.NPG
    NSQ = 1 + NS
    dr = {}

    def din(name, shape, dt=F32):
        dr[name] = nc.dram_tensor(name, list(shape), dt, kind="ExternalInput").ap()

    def dout(name, shape):
        dr[name] = nc.dram_tensor(name, list(shape), F32, kind="ExternalOutput").ap()

    def dscr(name, shape):
        dr[name] = nc.dram_tensor(name, list(shape), F32).ap()

    din("x0", [NT, D]); din("p0", [2, NT, 256])
    din("ck", [2, cfg.NPOOL * 128, 512]); din("cv", [2, cfg.NPOOL * 128, 512])
    din("pt", [NS * NPG], I32)
    din("st_ret", [2, NS, 4, 128, 128]); din("st_rwkvT", [2, NS, 64, 8, 64]); din("st_shift", [2, NS, RW_COLS])
    din("st_s5re", [2, 128, NS, 16]); din("st_s5im", [2, 128, NS, 16]); din("st_convT", [2, 128, 88, NS, 2])
    din("norm_mix", [2, D]); din("w_in", [2, D, IN_COLS]); din("w_out", [2, D, D])
    din("ret_norm_w", [2, 512]); din("ret_norm_b", [2, 512])
    din("diff_l", [2, 4, 64]); din("diff_subln", [2, 128])
    din("rwkv_mu", [2, RW_COLS]); din("rwkv_w0", [2, 512]); din("rwkv_w2", [2, 64, 512]); din("rwkv_a0", [2, 512])
    din("rwkv_a2", [2, 64, 512]); din("rwkv_g2", [2, 128, 512]); din("rwkv_kk", [2, 512]); din("rwkv_ka", [2, 512])
    din("rwkv_rk", [2, 512]); din("rwkv_ln_w", [2, 512]); din("rwkv_ln_b", [2, 512])
    din("s5_lre", [2, 128, 16]); din("s5_lim", [2, 128, 16]); din("s5_ls", [2, 128, 16])
    din("s5_bre", [2, 128, 16, 128]); din("s5_bim", [2, 128, 16, 128])
    din("s5_cre", [2, 128, 16, 32]); din("s5_cim", [2, 128, 16, 32])
    din("s5_d", [2, 512]); din("s5_w_glu", [2, 512, 512]); din("s5_b_glu", [2, 512]); din("s5_norm", [2, 512])
    din("norm_ffn", [2, D]); din("ffn_w_up", [2, D, 2 * DFF]); din("ffn_cw", [2, 128, 88, 3]); din("ffn_cb", [2, 128, 88])
    din("ffn_w_down", [2, DFF, D]); din("norm_ple", [2, D]); din("ple_w_proj", [2, 256, D]); din("ple_norm_e", [2, D])
    din("ple_w_gate", [2, D, D]); din("norm_final", [D])
    din("ident", [128, 128]); din("rope_r", [NT, 2, 64]); din("rope_d", [NT, 2, 8])
    din("ret_dmT", [4, 128, 128]); din("ret_dmT8", [4, 8, 8]); din("ret_qd", [128, 4, 128]); din("ret_kd", [128, 4]);
    din("ret_qd8", [128, 4, 8]); din("ret_kd8", [8, 4]); din("cmask", [128, 128]); din("cmask8", [8, 8])
    din("mask8", [8, 512]); din("tidx", [128, 130]); din("iota", [128, 1], I32)
    dout("y", [NT, D]); dout("k_new", [2, NT, 512]); dout("v_new", [2, NT, 512])
    dout("o_ret", [2, NSQ, 4, 128, 128]); dout("o_rwkvT", [2, NSQ, 64, 8, 64]); dout("o_shift", [2, NSQ, RW_COLS])
    dout("o_s5re", [2, 128, NSQ, 16]); dout("o_s5im", [2, 128, NSQ, 16]); dout("o_convT", [2, 128, 88, NSQ, 2])
    dscr("X", [NT, D]); dscr("PROJ", [NT, IN_COLS]); dscr("OCAT", [NT, D]); dscr("QK", [NT, 1024])
    dscr("RWS", [6, NT, 512]); dscr("FMA", [64, NT, 40]); dscr("FMR", [64, NT, 8]); dscr("FMW", [64, NT, 8]); dscr("ERAW", [NT, D]); dscr("YS5", [NT, 512])

    C = Ctx()
    C.nc, C.S, C.cfg, C.dr = nc, S, cfg, dr
    C.uid = 0
    with ExitStack() as gs:
        def sb(name, shape, dt=F32, st=gs):
            C.uid += 1
            return TT(st.enter_context(nc.sbuf_tensor(f"t{C.uid}_" + name, list(shape), dt)))

        def ps(name, shape, st=gs):
            C.uid += 1
            return TT(st.enter_context(nc.psum_tensor(f"q{C.uid}_" + name, list(shape), F32)))
        C.sb, C.ps = sb, ps
        C.ident = sb("ident", [128, 128])
        S.dma("sp", C.ident[:], dr["ident"], writes=[C.ident])
        C.eps = sb("epsc", [128, 4])
        S.op("dve", lambda e: e.memset(C.eps[:, 0:1], 1e-6), writes=[C.eps])
        S.op("dve", lambda e: e.memset(C.eps[:, 1:2], -math.pi), writes=[C.eps])
        S.op("dve", lambda e: e.memset(C.eps[:, 2:3], 64e-5), writes=[C.eps])
        S.op("dve", lambda e: e.memset(C.eps[:, 3:4], 1.0), writes=[C.eps])
        for l in range(2):
            with ExitStack() as st:
                phase_proj(C, st, l)
            S.barrier()
            with ExitStack() as st:
                phase_prep(C, st, l)
            S.barrier()
            with ExitStack() as st:
                phase_ret(C, st, l)
            S.barrier()
            with ExitStack() as st:
                phase_diff(C, st, l)
            S.barrier()
            with ExitStack() as st:
                phase_rwkv(C, st, l)
            S.barrier()
            with ExitStack() as st:
                phase_s5(C, st, l)
            S.barrier()
            with ExitStack() as st:
                phase_wout(C, st, l)
            S.barrier()
            with ExitStack() as st:
                phase_ffn(C, st, l)
            S.barrier()
            with ExitStack() as st:
                phase_ple(C, st, l)
            S.barrier()
        S.barrier()
    return nc, S


def bcast_load(C, st, name, ap, n, q="sp"):
    t = C.sb(name, [128, n], st=st)
    C.S.dma(q, t[:], ap.partition_broadcast(128), writes=[t])
    return t


def rms_rows(C, x, P, n, g, out, ss, act_sq_out):
    S = C.S
    if isinstance(act_sq_out, tuple):
        jt, jap = act_sq_out
        jap = R_(jap)
    else:
        jt, jap = act_sq_out, act_sq_out
    S.op("act", lambda e: e.activation(out=jap[:P, :n], in_=x[:P, :n], func=AF.Square, accum_out=ss[:P, 0:1]),
         reads=[x], writes=[jt, ss])
    S.op("dve", lambda e: e.tensor_scalar(out=ss[:P, 1:2], in0=ss[:P, 0:1], scalar1=1.0 / n, scalar2=1e-6,
                                          op0=ALU.mult, op1=ALU.add), reads=[ss], writes=[ss])
    S.op("act", lambda e: e.sqrt(out=ss[:P, 1:2], in_=ss[:P, 1:2]), reads=[ss], writes=[ss])
    S.op("dve", lambda e: e.reciprocal(out=ss[:P, 1:2], in_=ss[:P, 1:2]), reads=[ss], writes=[ss])
    S.op("dve", lambda e: e.scalar_tensor_tensor(out=out[:P, :n], in0=x[:P, :n], scalar=ss[:P, 1:2], in1=g[:P, :n],
                                                 op0=ALU.mult, op1=ALU.mult), reads=[x, ss, g], writes=[out])


def transpose_to(C, src, P, c0, ncols, dst_ap_fn, dst, ptp, i, r=False):
    S = C.S
    pt = ptp[i % len(ptp)]
    S.op("pe", lambda e: e.transpose(out=pt[:ncols, :P], in_=src[:P, c0:c0 + ncols], identity=C.ident[:P, :P]),
         reads=[src, C.ident], writes=[pt])
    eng = "act" if i % 2 == 0 else "dve"
    cast = R_ if r else (lambda a: a)
    if eng == "act":
        S.op("act", lambda e: e.copy(out=cast(dst_ap_fn()), in_=pt[:ncols, :P]), reads=[pt], writes=[dst])
    else:
        S.op("dve", lambda e: e.tensor_copy(out=cast(dst_ap_fn()), in_=pt[:ncols, :P]), reads=[pt], writes=[dst])


def dense(C, hT, sizes, W, KC, N, wbufs, pbufs, epi, cbw=512, wst=None):
    S = C.S
    cb = 0
    cnt = 0
    for c0 in range(0, N, cbw):
        ncol = min(cbw, N - c0)
        wb = wbufs[cb % len(wbufs)]
        S.dma("sp", wst[:, :KC, :ncol], W[:, c0:c0 + ncol].rearrange("(k p) c -> p k c", p=128), writes=[wst])
        S.op("pool", lambda e: e.tensor_copy(out=R_(wb[:, :KC, :ncol]), in_=wst[:, :KC, :ncol]), reads=[wst], writes=[wb])
        for ti, (row0, P, off) in enumerate(sizes):
            po = pbufs[cnt % len(pbufs)]
            cnt += 1
            for k in range(KC):
                S.op("pe", lambda e: e.matmul(po[:P, :ncol], lhsT=R_(hT[:, k, off:off + P]), rhs=R_(wb[:, k, :ncol]),
                                              start=(k == 0), stop=(k == KC - 1)), reads=[hT, wb], writes=[po])
            epi(ti, row0, P, c0, ncol, po)
        cb += 1


def phase_proj(C, st, l):
    S, dr, cfg = C.S, C.dr, C.cfg
    g = bcast_load(C, st, "g_mix", dr["norm_mix"][l], D)
    xs = [C.sb(f"px{i}", [128, D], st=st) for i in range(2)]
    hs = C.sb("ph", [128, D], st=st)
    sq = C.sb("psq", [128, D], st=st)
    ss = C.sb("pss", [128, 2], st=st)
    hT = C.sb("phT", [128, 16, 512], st=st)
    wb = [C.sb(f"pw{i}", [128, 16, 512], st=st) for i in range(2)]
    wst = C.sb("pwst", [128, 16, 512], st=st)
    ob = [C.sb(f"pob{i}", [128, 512], st=st) for i in range(4)]
    ptp = [C.ps(f"ppt{i}", [128, 128], st=st) for i in range(2)]
    pb = [C.ps(f"ppo{i}", [128, 512], st=st) for i in range(4)]
    src = dr["x0"] if l == 0 else dr["X"]
    n = 0
    for stl in cfg.sts:
        sizes = []
        off = 0
        for (r0, P) in stl:
            x = xs[n % 2]
            n += 1
            S.dma("sp", x[:P, :], src[r0:r0 + P, :], writes=[x])
            rms_rows(C, x, P, D, g, hs, ss, sq)
            for k in range(16):
                transpose_to(C, hs, P, k * 128, 128, lambda: hT[:, k, off:off + P], hT, ptp, k, r=True)
            sizes.append((r0, P, off))
            off += P
        cnt = [0]

        def epi(ti, r0, P, c0, ncol, po):
            o = ob[cnt[0] % 4]
            if cnt[0] % 2 == 0:
                S.op("act", lambda e: e.copy(out=o[:P, :ncol], in_=po[:P, :ncol]), reads=[po], writes=[o])
            else:
                S.op("dve", lambda e: e.tensor_copy(out=o[:P, :ncol], in_=po[:P, :ncol]), reads=[po], writes=[o])
            cnt[0] += 1
            S.dma("pool", dr["PROJ"][r0:r0 + P, c0:c0 + ncol], o[:P, :ncol], reads=[o])
        dense(C, hT, sizes, dr["w_in"][l], 16, IN_COLS, wb, pb, epi, wst=wst)


def v3(ap, h):
    return ap.rearrange("p (h d) -> p h d", h=h)


def rope_apply(C, src, dst, P, c0, nh, hd, half, cs, tmp, scale=None, sc0=None):
    S = C.S
    if sc0 is None:
        sc0 = c0
    s3 = v3(src[:P, sc0:sc0 + nh * hd], nh)
    d3 = v3(dst[:P, c0:c0 + nh * hd], nh)
    t3 = v3(tmp[:P, 0:nh * hd], nh)
    cosb = cs[:P, 0:1, :].to_broadcast([P, nh, half])
    sinb = cs[:P, 1:2, :].to_broadcast([P, nh, half])
    x1, x2 = s3[:, :, 0:half], s3[:, :, half:2 * half]
    if 2 * half < hd:
        S.op("pool", lambda e: e.tensor_copy(out=d3[:, :, 2 * half:hd], in_=s3[:, :, 2 * half:hd]), reads=[src], writes=[dst])
    S.op("dve", lambda e: e.tensor_tensor(out=t3[:, :, 0:half], in0=x1, in1=cosb, op=ALU.mult), reads=[src, cs], writes=[tmp])
    S.op("dve", lambda e: e.tensor_tensor(out=t3[:, :, half:2 * half], in0=x2, in1=sinb, op=ALU.mult), reads=[src, cs], writes=[tmp])
    S.op("dve", lambda e: e.tensor_tensor(out=d3[:, :, 0:half], in0=t3[:, :, 0:half], in1=t3[:, :, half:2 * half], op=ALU.subtract),
         reads=[tmp], writes=[dst])
    S.op("dve", lambda e: e.tensor_tensor(out=t3[:, :, 0:half], in0=x1, in1=sinb, op=ALU.mult), reads=[src, cs], writes=[tmp])
    S.op("dve", lambda e: e.tensor_tensor(out=t3[:, :, half:2 * half], in0=x2, in1=cosb, op=ALU.mult), reads=[src, cs], writes=[tmp])
    S.op("dve", lambda e: e.tensor_tensor(out=d3[:, :, half:2 * half], in0=t3[:, :, 0:half], in1=t3[:, :, half:2 * half], op=ALU.add),
         reads=[tmp], writes=[dst])
    if scale is not None:
        S.op("act", lambda e: e.mul(out=dst[:P, c0:c0 + nh * hd], in_=dst[:P, c0:c0 + nh * hd], mul=scale), reads=[dst], writes=[dst])


def phase_prep(C, st, l):
    S, dr, cfg = C.S, C.dr, C.cfg
    T, NS = cfg.T, cfg.NS
    pj = [C.sb(f"rpj{i}", [128, 3072], st=st) for i in range(2)]
    oo = [C.sb(f"roo{i}", [128, 2048], st=st) for i in range(2)]
    tmp = C.sb("rtmp", [128, 512], st=st)
    csr = [C.sb(f"rcsr{i}", [128, 2, 64], st=st) for i in range(2)]
    csd = [C.sb(f"rcsd{i}", [128, 2, 8], st=st) for i in range(2)]
    S.dma("pool", dr["v_new"][l], dr["PROJ"][:, 3072:3584])
    S.dma("pool", dr["o_shift"][l, 0:1, :], dr["PROJ"][T - 1:T, 3584:3584 + RW_COLS])
    for b in range(NS):
        S.dma("pool", dr["o_shift"][l, 1 + b:2 + b, :], dr["PROJ"][T + 8 * b + 7:T + 8 * b + 8, 3584:3584 + RW_COLS])
    for i, (r0, P) in enumerate(cfg.tiles):
        x, o, cr, cd = pj[i % 2], oo[i % 2], csr[i % 2], csd[i % 2]
        S.dma("sp", x[:P, :], dr["PROJ"][r0:r0 + P, 0:3072], writes=[x])
        S.dma("sp", cr[:P], dr["rope_r"][r0:r0 + P], writes=[cr])
        S.dma("sp", cd[:P], dr["rope_d"][r0:r0 + P], writes=[cd])
        rope_apply(C, x, o, P, 0, 4, 128, 64, cr, tmp)
        rope_apply(C, x, o, P, 512, 4, 128, 64, cr, tmp, scale=128 ** -0.5)
        S.dma("pool", dr["QK"][r0:r0 + P, :], o[:P, 0:1024], reads=[o])
        rope_apply(C, x, o, P, 1024, 8, 64, 8, cd, tmp, sc0=2048)
        rope_apply(C, x, o, P, 1536, 8, 64, 8, cd, tmp, sc0=2560)
        S.dma("pool", dr["PROJ"][r0:r0 + P, 2048:2560], o[:P, 1024:1536], reads=[o])
        S.dma("pool", dr["k_new"][l, r0:r0 + P, :], o[:P, 1536:2048], reads=[o])


def phase_ret(C, st, l):
    S, dr, cfg = C.S, C.dr, C.cfg
    T, NS = cfg.T, cfg.NS
    gw = bcast_load(C, st, "rgw", dr["ret_norm_w"][l], 512)
    gb = bcast_load(C, st, "rgb", dr["ret_norm_b"][l], 512)
    dmT = C.sb("rdmT", [128, 4, 128], st=st)
    S.dma("sp", dmT[:], dr["ret_dmT"].rearrange("h m l -> m h l"), writes=[dmT])
    dmT8 = C.sb("rdmT8", [8, 4, 8], st=st)
    S.dma("sp", dmT8[:], dr["ret_dmT8"].rearrange("h m l -> m h l"), writes=[dmT8])
    qd = C.sb("rqd", [128, 4, 128], st=st); S.dma("sp", qd[:], dr["ret_qd"], writes=[qd])
    kd = C.sb("rkd", [128, 4], st=st); S.dma("sp", kd[:], dr["ret_kd"], writes=[kd])
    qd8 = C.sb("rqd8", [128, 4, 8], st=st); S.dma("sp", qd8[:], dr["ret_qd8"], writes=[qd8])
    kd8 = C.sb("rkd8", [8, 4], st=st); S.dma("sp", kd8[:], dr["ret_kd8"], writes=[kd8])
    Sst = C.sb("rS", [128, 4, 128], st=st)
    qk = [C.sb(f"rqk{i}", [128, 1024], st=st) for i in range(2)]
    vg = [C.sb(f"rvg{i}", [128, 1024], st=st) for i in range(2)]
    qT = C.sb("rqT", [128, 4, 128], st=st)
    kT = C.sb("rkT", [128, 4, 128], st=st)
    qdT = C.sb("rqdT", [128, 4, 128], st=st)
    kdt = C.sb("rkdt", [128, 512], st=st)
    am = C.sb("ram", [128, 128], st=st)
    oh = C.sb("roh", [128, 512], st=st)
    sg = C.sb("rsg", [128, 512], st=st)
    stt = C.sb("rstt", [128, 16], st=st)
    ptp = [C.ps(f"rpt{i}", [128, 128], st=st) for i in range(2)]
    pa = C.ps("rpa", [128, 128], st=st)
    po = C.ps("rpo", [128, 128], st=st)
    pS = C.ps("rpS", [128, 128], st=st)

    def chunk(i, r0, L, dm, qdd, kdd, cdec):
        q, v = qk[i % 2], vg[i % 2]
        S.dma("sp", q[:L, :], dr["QK"][r0:r0 + L, :], writes=[q])
        S.dma("sp", v[:L, :], dr["PROJ"][r0:r0 + L, 1024:2048], writes=[v])
        for h in range(4):
            transpose_to(C, q, L, h * 128, 128, lambda: qT[:, h, :L], qT, ptp, 2 * h)
            transpose_to(C, q, L, 512 + h * 128, 128, lambda: kT[:, h, :L], kT, ptp, 2 * h + 1)
        S.op("pool", lambda e: e.tensor_tensor(out=qdT[:, :, :L], in0=qT[:, :, :L], in1=qdd[:, :, :L], op=ALU.mult), reads=[qT, qdd], writes=[qdT])
        S.op("pool", lambda e: e.tensor_tensor(out=v3(kdt[:L, :], 4), in0=v3(q[:L, 512:1024], 4),
                                               in1=kdd[:L, :].unsqueeze(2).to_broadcast([L, 4, 128]), op=ALU.mult), reads=[q, kdd], writes=[kdt])
        for h in range(4):
            S.op("pe", lambda e: e.matmul(pa[:L, :L], lhsT=kT[:, h, :L], rhs=qT[:, h, :L], start=True, stop=True), reads=[kT, qT], writes=[pa])
            S.op("dve", lambda e: e.tensor_tensor(out=am[:L, :L], in0=pa[:L, :L], in1=dm[:L, h, :L], op=ALU.mult), reads=[pa, dm], writes=[am])
            S.op("pe", lambda e: e.matmul(po[:L, :], lhsT=am[:L, :L], rhs=v[:L, h * 128:(h + 1) * 128], start=True, stop=False), reads=[am, v], writes=[po])
            S.op("pe", lambda e: e.matmul(po[:L, :], lhsT=qdT[:, h, :L], rhs=Sst[:, h, :], start=False, stop=True), reads=[qdT, Sst], writes=[po])
            S.op("act", lambda e: e.copy(out=oh[:L, h * 128:(h + 1) * 128], in_=po[:L, :]), reads=[po], writes=[oh])
            S.op("pe", lambda e: e.matmul(pS[:, :], lhsT=kdt[:L, h * 128:(h + 1) * 128], rhs=v[:L, h * 128:(h + 1) * 128], start=True, stop=True), reads=[kdt, v], writes=[pS])
            S.op("dve", lambda e: e.scalar_tensor_tensor(out=Sst[:, h, :], in0=Sst[:, h, :], scalar=float(cdec[h]), in1=pS[:, :],
                                                         op0=ALU.mult, op1=ALU.add), reads=[Sst, pS], writes=[Sst])
        o3 = v3(oh[:L, :], 4)
        S.op("dve", lambda e: e.tensor_reduce(out=stt[:L, 0:4], in_=o3, axis=AX.X, op=ALU.add), reads=[oh], writes=[stt])
        S.op("dve", lambda e: e.tensor_scalar(out=stt[:L, 0:4], in0=stt[:L, 0:4], scalar1=1.0 / 128, scalar2=None, op0=ALU.mult), reads=[stt], writes=[stt])
        S.op("dve", lambda e: e.tensor_tensor(out=o3, in0=o3, in1=stt[:L, 0:4].unsqueeze(2).to_broadcast([L, 4, 128]), op=ALU.subtract), reads=[oh, stt], writes=[oh])
        S.op("pool", lambda e: e.tensor_tensor(out=sg[:L, :], in0=oh[:L, :], in1=oh[:L, :], op=ALU.mult), reads=[oh], writes=[sg])
        S.op("dve", lambda e: e.tensor_reduce(out=stt[:L, 4:8], in_=v3(sg[:L, :], 4), axis=AX.X, op=ALU.add), reads=[sg], writes=[stt])
        S.op("dve", lambda e: e.tensor_scalar(out=stt[:L, 4:8], in0=stt[:L, 4:8], scalar1=1.0 / 128, scalar2=1e-6, op0=ALU.mult, op1=ALU.add), reads=[stt], writes=[stt])
        S.op("act", lambda e: e.sqrt(out=stt[:L, 4:8], in_=stt[:L, 4:8]), reads=[stt], writes=[stt])
        S.op("dve", lambda e: e.reciprocal(out=stt[:L, 4:8], in_=stt[:L, 4:8]), reads=[stt], writes=[stt])
        S.op("dve", lambda e: e.tensor_tensor(out=o3, in0=o3, in1=stt[:L, 4:8].unsqueeze(2).to_broadcast([L, 4, 128]), op=ALU.mult), reads=[oh, stt], writes=[oh])
        S.op("dve", lambda e: e.tensor_tensor(out=oh[:L, :], in0=oh[:L, :], in1=gw[:L, :], op=ALU.mult), reads=[oh, gw], writes=[oh])
        S.op("dve", lambda e: e.tensor_tensor(out=oh[:L, :], in0=oh[:L, :], in1=gb[:L, :], op=ALU.add), reads=[oh, gb], writes=[oh])
        S.op("act", lambda e: e.activation(out=sg[:L, :], in_=v[:L, 512:1024], func=AF.Silu), reads=[v], writes=[sg])
        S.op("dve", lambda e: e.tensor_tensor(out=oh[:L, :], in0=oh[:L, :], in1=sg[:L, :], op=ALU.mult), reads=[oh, sg], writes=[oh])
        S.dma("pool", dr["OCAT"][r0:r0 + L, 0:512], oh[:L, :], reads=[oh])

    gam = [1.0 - 2.0 ** (-5.0 - h) for h in range(4)]
    S.op("dve", lambda e: e.memset(Sst[:], 0.0), writes=[Sst])
    for i in range(T // 128):
        chunk(i, i * 128, 128, dmT, qd, kd, [g ** 128 for g in gam])
    S.dma("pool", dr["o_ret"][l, 0].rearrange("h d e -> d h e"), Sst[:], reads=[Sst])
    for b in range(NS):
        S.dma("sp", Sst[:], dr["st_ret"][l, b].rearrange("h d e -> d h e"), writes=[Sst])
        chunk(b, T + 8 * b, 8, dmT8, qd8, kd8, [g ** 8 for g in gam])
        S.dma("pool", dr["o_ret"][l, 1 + b].rearrange("h d e -> d h e"), Sst[:], reads=[Sst])


def _stub(C, st, l):
    pass


def _tables(cfg, past_len):
    T, NS, NT = cfg.T, cfg.NS, cfg.NT
    f = np.float32
    pos = np.concatenate([np.arange(T), np.tile(past_len + np.arange(8), NS)]).astype(f)
    inv_r = np.power(f(10000.0), -np.arange(64, dtype=f) / f(64)).astype(f)
    ang = pos[:, None] * inv_r[None, :]
    rope_r = np.stack([np.cos(ang), np.sin(ang)], 1).astype(f)
    inv_d = np.power(f(500000.0), -np.arange(8, dtype=f) / f(8)).astype(f)
    angd = pos[:, None] * inv_d[None, :]
    rope_d = np.stack([np.cos(angd), np.sin(angd)], 1).astype(f)
    log_g = np.log1p(-np.exp2(-5.0 - np.arange(4))).astype(np.float64)

    def dm(L):
        idx = np.arange(L)
        rel = idx[:, None] - idx[None, :]
        d = np.where(rel >= 0, np.exp(log_g[:, None, None] * np.maximum(rel, 0)), 0.0)
        return np.ascontiguousarray(d.transpose(0, 2, 1)).astype(f)

    def qd(L):
        v = np.exp(log_g[:, None] * (np.arange(L) + 1.0))
        return np.ascontiguousarray(np.broadcast_to(v[None], (128, 4, L))).astype(f)

    def kd(L):
        v = np.exp(log_g[:, None] * (L - 1.0 - np.arange(L)))
        return np.ascontiguousarray(v.T).astype(f)

    def cm(L):
        idx = np.arange(L)
        return np.where(idx[None, :] <= idx[:, None], 0.0, -1e30).astype(f)
    mask8 = np.zeros((8, 512), f)
    for h in range(8):
        mask8[h, h * 64:(h + 1) * 64] = 1.0
    return dict(ident=np.eye(128, dtype=f), rope_r=rope_r, rope_d=rope_d, ret_dmT=dm(128), ret_dmT8=dm(8),
                ret_qd=qd(128), ret_kd=kd(128), ret_qd8=qd(8), ret_kd8=kd(8), cmask=cm(128), cmask8=cm(8),
                mask8=mask8, tidx=np.ascontiguousarray(np.broadcast_to(np.arange(130, dtype=f)[None], (128, 130))),
                iota=np.arange(128, dtype=np.int32).reshape(128, 1))


def _gn(a):
    return np.ascontiguousarray(a.reshape(2, 16, 2, 64).transpose(0, 2, 3, 1).reshape(2, 128, 16))


def kernel(cfg=None, **inp):
    f = np.float32
    if cfg is None:
        cfg = Cfg()
    T, NS, NPG = cfg.T, cfg.NS, cfg.NPG
    past_len = NPG * 128
    A = {k: np.asarray(v) for k, v in inp.items()}
    shared = {}
    for k in ["norm_mix", "w_in", "w_out", "diff_subln", "rwkv_mu", "rwkv_w0", "rwkv_w2", "rwkv_a0", "rwkv_a2", "rwkv_g2",
              "rwkv_kk", "rwkv_ka", "s5_d", "s5_w_glu", "s5_b_glu", "s5_norm", "norm_ffn", "ffn_w_up", "ffn_w_down",
              "norm_ple", "ple_w_proj", "ple_norm_e", "ple_w_gate", "norm_final"]:
        shared[k] = np.ascontiguousarray(A[k], dtype=f)
    for k in ["ret_norm_w", "ret_norm_b", "rwkv_rk", "rwkv_ln_w", "rwkv_ln_b"]:
        shared[k] = np.ascontiguousarray(A[k].reshape(2, 512), dtype=f)
    shared["diff_l"] = np.ascontiguousarray(np.stack([A["diff_lq1"], A["diff_lk1"], A["diff_lq2"], A["diff_lk2"]], 1), dtype=f)
    shared["ck"] = np.ascontiguousarray(A["cache_k"].reshape(2, -1, 512), dtype=f)
    shared["cv"] = np.ascontiguousarray(A["cache_v"].reshape(2, -1, 512), dtype=f)
    shared["s5_lre"] = _gn(A["s5_lam_re"]); shared["s5_lim"] = _gn(A["s5_lam_im"])
    shared["s5_ls"] = _gn(np.broadcast_to(A["s5_log_step"][:, :, None], (2, 32, 64)))
    for nm, src in (("s5_bre", "s5_b_re"), ("s5_bim", "s5_b_im")):
        b = A[src].reshape(2, 16, 2, 64, 16)
        e = np.zeros((2, 128, 16, 128), f)
        for i in range(16):
            for gl in range(2):
                c0 = (i % 4) * 32 + gl * 16
                e[:, gl * 64:(gl + 1) * 64, i, c0:c0 + 16] = b[:, i, gl]
        shared[nm] = e
    for nm, src in (("s5_cre", "s5_c_re"), ("s5_cim", "s5_c_im")):
        c = A[src].reshape(2, 16, 2, 16, 64)
        e = np.zeros((2, 128, 16, 32), f)
        for i in range(16):
            for gl in range(2):
                e[:, gl * 64:(gl + 1) * 64, i, gl * 16:(gl + 1) * 16] = c[:, i, gl].transpose(0, 2, 1)
        shared[nm] = e
    shared["ffn_cw"] = np.ascontiguousarray(A["ffn_conv_w"].reshape(2, 3, 88, 128).transpose(0, 3, 2, 1), dtype=f)
    shared["ffn_cb"] = np.ascontiguousarray(A["ffn_conv_b"].reshape(2, 88, 128).transpose(0, 2, 1), dtype=f)
    shared.update(_tables(cfg, past_len))
    in_maps = []
    for c in range(NCORES):
        sl = slice(c * NS, (c + 1) * NS)
        m = dict(shared)
        m["x0"] = np.ascontiguousarray(np.concatenate([A["x_prompt"][0], A["x_sample"][sl].reshape(NS * 8, D)], 0), dtype=f)
        m["p0"] = np.ascontiguousarray(np.concatenate([A["p_prompt"][:, 0], A["p_sample"][:, sl].reshape(2, NS * 8, 256)], 1), dtype=f)
        m["pt"] = np.ascontiguousarray(A["page_table"][sl].reshape(-1), dtype=np.int32)
        m["st_ret"] = np.ascontiguousarray(A["state_ret"][:, sl], dtype=f)
        m["st_rwkvT"] = np.ascontiguousarray(A["state_rwkv"][:, sl].transpose(0, 1, 4, 2, 3), dtype=f)
        m["st_shift"] = np.ascontiguousarray(A["state_rwkv_shift"][:, sl], dtype=f)
        for nm, src in (("st_s5re", "state_s5_re"), ("st_s5im", "state_s5_im")):
            s = A[src][:, sl].reshape(2, NS, 16, 2, 64)
            m[nm] = np.ascontiguousarray(s.transpose(0, 3, 4, 1, 2).reshape(2, 128, NS, 16), dtype=f)
        cs = A["state_ffn_conv"][:, sl].reshape(2, NS, 2, 88, 128)
        m["st_convT"] = np.ascontiguousarray(cs.transpose(0, 4, 3, 1, 2), dtype=f)
        in_maps.append(m)
    nc, S = build(cfg)
    res = run_bass_kernel_spmd(nc, in_maps, core_ids=list(range(NCORES)))
    R = res.results
    kernel.last_res = res

    def cat_seq(name, fn):
        return fn(R[0][name], True), np.concatenate([fn(R[c][name], False) for c in range(NCORES)], axis=1)
    y_p = R[0]["y"][:T][None]
    y_s = np.concatenate([R[c]["y"][T:].reshape(NS, 8, D) for c in range(NCORES)], 0)
    k_p = R[0]["k_new"][:, :T].reshape(2, 1, T, 4, 128)
    v_p = R[0]["v_new"][:, :T].reshape(2, 1, T, 4, 128)
    k_s = np.concatenate([R[c]["k_new"][:, T:].reshape(2, NS, 8, 4, 128) for c in range(NCORES)], 1)
    v_s = np.concatenate([R[c]["v_new"][:, T:].reshape(2, NS, 8, 4, 128) for c in range(NCORES)], 1)
    ret_p = R[0]["o_ret"][:, 0:1]
    ret_s = np.concatenate([R[c]["o_ret"][:, 1:] for c in range(NCORES)], 1)
    rw = lambda a: a.transpose(0, 1, 3, 4, 2)
    rw_p = rw(R[0]["o_rwkvT"][:, 0:1])
    rw_s = np.concatenate([rw(R[c]["o_rwkvT"][:, 1:]) for c in range(NCORES)], 1)
    sh_p = R[0]["o_shift"][:, 0:1]
    sh_s = np.concatenate([R[c]["o_shift"][:, 1:] for c in range(NCORES)], 1)

    def s5(a):
        l_, _, b_, _ = a.shape
        return a.reshape(l_, 2, 64, b_, 16).transpose(0, 3, 4, 1, 2).reshape(l_, b_, 32, 64)
    s5r_p = s5(R[0]["o_s5re"][:, :, 0:1]); s5i_p = s5(R[0]["o_s5im"][:, :, 0:1])
    s5r_s = np.concatenate([s5(R[c]["o_s5re"][:, :, 1:]) for c in range(NCORES)], 1)
    s5i_s = np.concatenate([s5(R[c]["o_s5im"][:, :, 1:]) for c in range(NCORES)], 1)

    def cv(a):
        l_, _, _, b_, _ = a.shape
        return a.transpose(0, 3, 4, 2, 1).reshape(l_, b_, 2, 2 * DFF)
    cv_p = cv(R[0]["o_convT"][:, :, :, 0:1])
    cv_s = np.concatenate([cv(R[c]["o_convT"][:, :, :, 1:]) for c in range(NCORES)], 1)
    outs = (y_p, y_s, k_p, v_p, k_s, v_s, ret_p, ret_s, rw_p, rw_s, sh_p, sh_s, s5r_p, s5i_p, s5r_s, s5i_s, cv_p, cv_s)
    return tuple(np.ascontiguousarray(o, dtype=f) for o in outs)


def load_T(C, stl, src_ap_fn, ncols, xs, hT, ptp, norm=None):
    S = C.S
    sizes = []
    off = 0
    for n, (r0, P) in enumerate(stl):
        x = xs[n % len(xs)]
        S.dma("sp", x[:P, :ncols], src_ap_fn(r0, P), writes=[x])
        src = x
        if norm is not None:
            g, hs, ss, sq = norm
            rms_rows(C, x, P, ncols, g, hs, ss, sq)
            src = hs
        for k in range(ncols // 128):
            transpose_to(C, src, P, k * 128, 128, lambda: hT[:, k, off:off + P], hT, ptp, k, r=True)
        sizes.append((r0, P, off))
        off += P
    return sizes


def phase_wout(C, st, l):
    S, dr, cfg = C.S, C.dr, C.cfg
    xs = [C.sb(f"wx{i}", [128, D], st=st) for i in range(2)]
    hT = C.sb("whT", [128, 16, 512], st=st)
    wb = [C.sb(f"ww{i}", [128, 16, 512], st=st) for i in range(2)]
    wst = C.sb("wwst", [128, 16, 512], st=st)
    xres = [C.sb(f"wxr{i}", [128, D], st=st) for i in range(4)]
    ptp = [C.ps(f"wpt{i}", [128, 128], st=st) for i in range(2)]
    pb = [C.ps(f"wpo{i}", [128, 512], st=st) for i in range(4)]
    xsrc = dr["x0"] if l == 0 else dr["X"]
    for stl in cfg.sts:
        sizes = load_T(C, stl, lambda r0, P: dr["OCAT"][r0:r0 + P, :], D, xs, hT, ptp)
        for ti, (r0, P, off) in enumerate(sizes):
            S.dma("sp", xres[ti][:P, :], xsrc[r0:r0 + P, :], writes=[xres[ti]])

        def epi(ti, r0, P, c0, ncol, po):
            xr = xres[ti]
            S.op("dve", lambda e: e.tensor_tensor(out=xr[:P, c0:c0 + ncol], in0=xr[:P, c0:c0 + ncol], in1=po[:P, :ncol], op=ALU.add),
                 reads=[xr, po], writes=[xr])
        dense(C, hT, sizes, dr["w_out"][l], 16, D, wb, pb, epi, wst=wst)
        for ti, (r0, P, off) in enumerate(sizes):
            S.dma("pool", dr["X"][r0:r0 + P, :], xres[ti][:P, :], reads=[xres[ti]])


def phase_ffn(C, st, l):
    S, dr, cfg = C.S, C.dr, C.cfg
    T, NS, NTS = cfg.T, cfg.NS, cfg.NTS
    g = bcast_load(C, st, "fg", dr["norm_ffn"][l], D)
    xs = [C.sb("fx0", [128, D], st=st)]
    ss = C.sb("fss", [128, 2], st=st)
    hT = C.sb("fhT", [128, 16, 512], st=st)
    actT = C.sb("factT", [128, 44, 512], st=st)
    wu = [C.sb(f"fwu{i}", [128, 16, 128], st=st) for i in range(2)]
    wd = [C.sb(f"fwd{i}", [128, 4, 512], st=st) for i in range(2)]
    halo = C.sb("fhalo", [128, 88, 2], st=st)
    cw = C.sb("fcw", [128, 88, 3], st=st); S.dma("sp", cw[:], dr["ffn_cw"][l], writes=[cw])
    cbb = C.sb("fcb", [128, 88], st=st); S.dma("sp", cbb[:], dr["ffn_cb"][l], writes=[cbb])
    ext = [C.sb(f"fext{i}", [128, 516], st=st) for i in range(2)]
    cv_ = [C.sb(f"fcv{i}", [128, 512], st=st) for i in range(2)]
    sgl = C.sb("fsgl", [128, 512], st=st)
    xr = [C.sb(f"fxr{i}", [128, 512], st=st) for i in range(2)]
    wus = C.sb("fwus", [128, 16, 128], st=st)
    wds = C.sb("fwds", [128, 4, 512], st=st)
    ptp = [C.ps(f"fpt{i}", [128, 128], st=st) for i in range(2)]
    pu = [C.ps(f"fpu{i}", [128, 512], st=st) for i in range(2)]
    pd = [C.ps(f"fpd{i}", [128, 512], st=st) for i in range(4)]
    S.op("dve", lambda e: e.memset(halo[:], 0.0), writes=[halo])
    wup = dr["ffn_w_up"][l].rearrange("(k p) c -> p k c", p=128)
    wdn = dr["ffn_w_down"][l].rearrange("(j p) c -> p j c", p=128)
    cnt = 0
    xcnt = 0
    for si, stl in enumerate(cfg.sts):
        sample = (stl[0][0] == T)
        sizes = load_T(C, stl, lambda r0, P: dr["X"][r0:r0 + P, :], D, xs, hT, ptp, norm=(g, xs[0], ss, (actT, actT.t[:, 0:4, :].rearrange("p a b -> p (a b)"))))
        n = sum(P for _, P, _ in sizes)
        for j in range(44):
            for half in range(2):
                ct = j + 44 * half
                w = wu[cnt % 2]
                p_ = pu[cnt % 2]
                ex = ext[cnt % 2]
                cvt = cv_[half]
                cnt += 1
                S.dma("sp", wus[:], wup[:, :, ct * 128:(ct + 1) * 128], writes=[wus])
                S.op("pool", lambda e: e.tensor_copy(out=R_(w[:]), in_=wus[:]), reads=[wus], writes=[w])
                for k in range(16):
                    S.op("pe", lambda e: e.matmul(p_[:, :n], lhsT=R_(w[:, k, :]), rhs=R_(hT[:, k, :n]), start=(k == 0), stop=(k == 15)),
                         reads=[w, hT], writes=[p_])
                if not sample:
                    e0 = lambda a, b: ex[:, a:b]
                    S.op("pool", lambda e: e.tensor_copy(out=ex[:, 0:2], in_=halo[:, ct, :]), reads=[halo], writes=[ex])
                    S.op("act", lambda e: e.copy(out=ex[:, 2:2 + n], in_=p_[:, :n]), reads=[p_], writes=[ex])
                    S.op("pool", lambda e: e.tensor_copy(out=halo[:, ct, :], in_=ex[:, n:n + 2]), reads=[ex], writes=[halo])
                    sl = [ex[:, 0:n], ex[:, 1:1 + n], ex[:, 2:2 + n]]
                    cvo = cvt[:, :n]
                else:
                    e3 = ex[:, 0:NS * 10].rearrange("p (b t) -> p b t", t=10)
                    S.dma("sp", e3[:, :, 0:2], dr["st_convT"][l, :, ct, :, :], writes=[ex])
                    S.op("act", lambda e: e.copy(out=e3[:, :, 2:10], in_=p_[:, :n].rearrange("p (b t) -> p b t", t=8)), reads=[p_], writes=[ex])
                    S.dma("pool", dr["o_convT"][l, :, ct, 1:, :], e3[:, :, 8:10], reads=[ex])
                    sl = [e3[:, :, 0:8], e3[:, :, 1:9], e3[:, :, 2:10]]
                    cvo = cvt[:, :n].rearrange("p (b t) -> p b t", t=8)
                S.op("act", lambda e: e.activation(out=cvo, in_=sl[2], func=AF.Identity, bias=cbb[:, ct:ct + 1], scale=cw[:, ct, 2:3]),
                     reads=[ex, cbb, cw], writes=[cvt])
                S.op("dve", lambda e: e.scalar_tensor_tensor(out=cvo, in0=sl[1], scalar=cw[:, ct, 1:2], in1=cvo, op0=ALU.mult, op1=ALU.add),
                     reads=[ex, cw, cvt], writes=[cvt])
                S.op("dve", lambda e: e.scalar_tensor_tensor(out=cvo, in0=sl[0], scalar=cw[:, ct, 0:1], in1=cvo, op0=ALU.mult, op1=ALU.add),
                     reads=[ex, cw, cvt], writes=[cvt])
            S.op("act", lambda e: e.activation(out=sgl[:, :n], in_=cv_[0][:, :n], func=AF.Silu), reads=[cv_[0]], writes=[sgl])
            S.op("dve", lambda e: e.tensor_tensor(out=R_(actT[:, j, :n]), in0=sgl[:, :n], in1=cv_[1][:, :n], op=ALU.mult),
                 reads=[sgl, cv_[1]], writes=[actT])
        if stl[-1][0] + stl[-1][1] == T:
            S.dma("pool", dr["o_convT"][l, :, :, 0, :], halo[:], reads=[halo])
        for c0 in range(0, D, 512):
            for jq in range(11):
                w = wd[xcnt % 2]
                xcnt += 1
                S.dma("sp", wds[:], wdn[:, jq * 4:(jq + 1) * 4, c0:c0 + 512], writes=[wds])
                S.op("pool", lambda e: e.tensor_copy(out=R_(w[:]), in_=wds[:]), reads=[wds], writes=[w])
                for ti, (r0, P, off) in enumerate(sizes):
                    for jj in range(4):
                        j = jq * 4 + jj
                        S.op("pe", lambda e: e.matmul(pd[ti][:P, :], lhsT=R_(actT[:, j, off:off + P]), rhs=R_(w[:, jj, :]),
                                                      start=(j == 0), stop=(j == 43)), reads=[actT, w], writes=[pd[ti]])
            for ti, (r0, P, off) in enumerate(sizes):
                x = xr[ti % 2]
                S.dma("sp", x[:P, :], dr["X"][r0:r0 + P, c0:c0 + 512], writes=[x])
                S.op("dve", lambda e: e.tensor_tensor(out=x[:P, :], in0=x[:P, :], in1=pd[ti][:P, :], op=ALU.add), reads=[x, pd[ti]], writes=[x])
                S.dma("pool", dr["X"][r0:r0 + P, c0:c0 + 512], x[:P, :], reads=[x])


def phase_ple(C, st0, l):
    from contextlib import ExitStack
    S, dr, cfg = C.S, C.dr, C.cfg
    with ExitStack() as st:
        xs = [C.sb(f"ep{i}", [128, 256], st=st) for i in range(2)]
        pT = C.sb("epT", [128, 2, 512], st=st)
        wb = [C.sb(f"ew{i}", [128, 2, 512], st=st) for i in range(2)]
        wst = C.sb("ewst", [128, 2, 512], st=st)
        ob = [C.sb(f"eob{i}", [128, 512], st=st) for i in range(4)]
        ptp = [C.ps(f"ept{i}", [128, 128], st=st) for i in range(2)]
        pb = [C.ps(f"epo{i}", [128, 512], st=st) for i in range(4)]
        cnt = [0]
        for stl in cfg.sts:
            sizes = load_T(C, stl, lambda r0, P: dr["p0"][l, r0:r0 + P, :], 256, xs, pT, ptp)

            def epi(ti, r0, P, c0, ncol, po):
                o = ob[cnt[0] % 4]
                cnt[0] += 1
                S.op("act", lambda e: e.copy(out=o[:P, :ncol], in_=po[:P, :ncol]), reads=[po], writes=[o])
                S.dma("pool", dr["ERAW"][r0:r0 + P, c0:c0 + ncol], o[:P, :ncol], reads=[o])
            dense(C, pT, sizes, dr["ple_w_proj"][l], 2, D, wb, pb, epi, wst=wst)
    S.barrier()
    with ExitStack() as st:
        g = bcast_load(C, st, "gg", dr["norm_ple"][l], D)
        ge = bcast_load(C, st, "gge", dr["ple_norm_e"][l], D)
        gf = bcast_load(C, st, "ggf", dr["norm_final"], D)
        xres = [C.sb(f"gx{i}", [128, D], st=st) for i in range(2)]
        et = [C.sb(f"ge{i}", [128, D], st=st) for i in range(2)]
        hs = C.sb("gh", [128, D], st=st)
        sq = C.sb("gsq", [128, D], st=st)
        ss = C.sb("gss", [128, 2], st=st)
        hT = C.sb("ghT", [128, 16, 256], st=st)
        wb = [C.sb(f"gw{i}", [128, 16, 512], st=st) for i in range(2)]
        wst = C.sb("gwst", [128, 16, 512], st=st)
        sg = [C.sb(f"gsg{i}", [128, 512], st=st) for i in range(2)]
        ptp = [C.ps(f"gpt{i}", [128, 128], st=st) for i in range(2)]
        pb = [C.ps(f"gpo{i}", [128, 512], st=st) for i in range(4)]
        cnt = [0]
        for stl in cfg.sts2:
            sizes = []
            off = 0
            for ti, (r0, P) in enumerate(stl):
                x = xres[ti]
                S.dma("sp", x[:P, :], dr["X"][r0:r0 + P, :], writes=[x])
                rms_rows(C, x, P, D, g, hs, ss, sq)
                for k in range(16):
                    transpose_to(C, hs, P, k * 128, 128, lambda: hT[:, k, off:off + P], hT, ptp, k, r=True)
                S.dma("sp", hs[:P, :], dr["ERAW"][r0:r0 + P, :], writes=[hs])
                rms_rows(C, hs, P, D, ge, et[ti], ss, sq)
                sizes.append((r0, P, off))
                off += P

            def epi(ti, r0, P, c0, ncol, po):
                s_ = sg[cnt[0] % 2]
                cnt[0] += 1
                S.op("act", lambda e: e.activation(out=s_[:P, :ncol], in_=po[:P, :ncol], func=AF.Sigmoid), reads=[po], writes=[s_])
                S.op("dve", lambda e: e.tensor_tensor(out=s_[:P, :ncol], in0=s_[:P, :ncol], in1=et[ti][:P, c0:c0 + ncol], op=ALU.mult),
                     reads=[s_, et[ti]], writes=[s_])
                S.op("pool", lambda e: e.tensor_tensor(out=xres[ti][:P, c0:c0 + ncol], in0=xres[ti][:P, c0:c0 + ncol], in1=s_[:P, :ncol], op=ALU.add),
                     reads=[s_, xres[ti]], writes=[xres[ti]])
            dense(C, hT, sizes, dr["ple_w_gate"][l], 16, D, wb, pb, epi, wst=wst)
            for ti, (r0, P, off) in enumerate(sizes):
                S.dma("pool", dr["X"][r0:r0 + P, :], xres[ti][:P, :], reads=[xres[ti]])
                if l == 1:
                    rms_rows(C, xres[ti], P, D, gf, hs, ss, sq)
                    S.dma("pool", dr["y"][r0:r0 + P, :], hs[:P, :], reads=[hs])


def phase_s5(C, st0, l):
    from contextlib import ExitStack
    S, dr, cfg = C.S, C.dr, C.cfg
    T, NS = cfg.T, cfg.NS
    PI = math.pi
    with ExitStack() as st:
        def ld(name, ap, shape):
            t = C.sb(name, shape, st=st)
            S.dma("sp", t[:], ap, writes=[t])
            return t
        lre = ld("slre", dr["s5_lre"][l], [128, 16]); lim = ld("slim", dr["s5_lim"][l], [128, 16]); ls = ld("sls", dr["s5_ls"][l], [128, 16])
        BRe = ld("sBRe", dr["s5_bre"][l], [128, 16, 128]); BIe = ld("sBIe", dr["s5_bim"][l], [128, 16, 128])
        CRe = ld("sCRe", dr["s5_cre"][l], [128, 16, 32]); CIm = ld("sCIm", dr["s5_cim"][l], [128, 16, 32])
        tidx = ld("stidx", dr["tidx"], [128, 130])
        sm = C.sb("ssm", [128, 16, 16], st=st)
        K = lambda k: sm[:, k, :]
        def tt(eng, out, a, b, op, rd, wr):
            S.op(eng, lambda e: e.tensor_tensor(out=out, in0=a, in1=b, op=op), reads=rd, writes=wr)
        def ts(eng, out, a, s1, s2, op0, op1, rd, wr):
            if op1 is None:
                S.op(eng, lambda e: e.tensor_scalar(out=out, in0=a, scalar1=s1, scalar2=None, op0=op0), reads=rd, writes=wr)
            else:
                S.op(eng, lambda e: e.tensor_scalar(out=out, in0=a, scalar1=s1, scalar2=s2, op0=op0, op1=op1), reads=rd, writes=wr)
        def act(out, a, fn, rd, wr, **kw):
            S.op("act", lambda e: e.activation(out=out, in_=a, func=fn, **kw), reads=rd, writes=wr)
        mpi = C.eps[:, 1:2]
        rti = C.sb("srti", [128, 129], I32, st=st); rtf = C.sb("srtf", [128, 129], st=st); rtx = C.sb("srtx", [128, 129], st=st)
        def sinr(out, x, n, shift, rd, wr):
            S.op("dve", lambda e: e.tensor_scalar(out=rtx[:, :n], in0=x, scalar1=shift, scalar2=None, op0=ALU.add), reads=rd, writes=[rtx])
            S.op("dve", lambda e: e.tensor_scalar(out=rti[:, :n], in0=rtx[:, :n], scalar1=1.0 / (2 * PI), scalar2=None, op0=ALU.mult), reads=[rtx], writes=[rti])
            S.op("dve", lambda e: e.tensor_copy(out=rtf[:, :n], in_=rti[:, :n]), reads=[rti], writes=[rtf])
            S.op("dve", lambda e: e.scalar_tensor_tensor(out=rtx[:, :n], in0=rtf[:, :n], scalar=-2 * PI, in1=rtx[:, :n], op0=ALU.mult, op1=ALU.add), reads=[rtf, rtx], writes=[rtx])
            S.op("dve", lambda e: e.tensor_scalar(out=rtf[:, :n], in0=rtx[:, :n], scalar1=PI, scalar2=2 * PI, op0=ALU.is_gt, op1=ALU.mult), reads=[rtx], writes=[rtf])
            S.op("dve", lambda e: e.tensor_tensor(out=rtx[:, :n], in0=rtx[:, :n], in1=rtf[:, :n], op=ALU.subtract), reads=[rtx, rtf], writes=[rtx])
            S.op("act", lambda e: e.activation(out=out, in_=rtx[:, :n], func=AF.Sin), reads=[rtx], writes=wr)
        act(K(0), ls[:], AF.Exp, [ls], [sm])
        tt("dve", K(1), lre[:], K(0), ALU.mult, [lre, sm], [sm])
        tt("dve", K(2), lim[:], K(0), ALU.mult, [lim, sm], [sm])
        ts("dve", K(3), K(1), -1.0, None, ALU.mult, None, [sm], [sm])
        act(K(9), K(1), AF.Exp, [sm], [sm])
        sinr(K(10), K(2), 16, 0.0, [sm], [sm])
        sinr(K(11), K(2), 16, 0.5 * PI, [sm], [sm])
        tt("dve", K(4), K(9), K(11), ALU.mult, [sm], [sm])
        tt("dve", K(5), K(9), K(10), ALU.mult, [sm], [sm])
        tt("dve", K(9), lre[:], lre[:], ALU.mult, [lre], [sm])
        tt("dve", K(10), lim[:], lim[:], ALU.mult, [lim], [sm])
        tt("dve", K(9), K(9), K(10), ALU.add, [sm], [sm])
        S.op("dve", lambda e: e.reciprocal(out=K(9), in_=K(9)), reads=[sm], writes=[sm])
        ts("dve", K(10), K(4), -1.0, None, ALU.add, None, [sm], [sm])
        tt("dve", K(11), K(10), lre[:], ALU.mult, [sm, lre], [sm])
        tt("dve", K(12), K(5), lim[:], ALU.mult, [sm, lim], [sm])
        tt("dve", K(11), K(11), K(12), ALU.add, [sm], [sm])
        tt("dve", K(6), K(11), K(9), ALU.mult, [sm], [sm])
        tt("dve", K(11), K(5), lre[:], ALU.mult, [sm, lre], [sm])
        tt("dve", K(12), K(10), lim[:], ALU.mult, [sm, lim], [sm])
        tt("dve", K(11), K(11), K(12), ALU.subtract, [sm], [sm])
        tt("dve", K(7), K(11), K(9), ALU.mult, [sm], [sm])
        ts("dve", K(8), K(7), -1.0, None, ALU.mult, None, [sm], [sm])
        S.op("dve", lambda e: e.tensor_scalar(out=CIm[:], in0=CIm[:], scalar1=-1.0, scalar2=None, op0=ALU.mult), reads=[CIm], writes=[CIm])
        BBTr = C.sb("sBBTr", [128, 16, 128], st=st); BBTi = C.sb("sBBTi", [128, 16, 128], st=st)
        t1 = C.sb("st1", [128, 128], st=st); t2 = C.sb("st2", [128, 128], st=st)
        ptp = [C.ps(f"spt{i}", [128, 128], st=st) for i in range(2)]
        for i in range(16):
            S.op("dve", lambda e: e.tensor_scalar(out=t1[:], in0=BRe[:, i, :], scalar1=sm[:, 6, i:i + 1], scalar2=None, op0=ALU.mult), reads=[BRe, sm], writes=[t1])
            S.op("dve", lambda e: e.scalar_tensor_tensor(out=t1[:], in0=BIe[:, i, :], scalar=sm[:, 8, i:i + 1], in1=t1[:], op0=ALU.mult, op1=ALU.add), reads=[BIe, sm, t1], writes=[t1])
            transpose_to(C, t1, 128, 0, 128, lambda: BBTr[:, i, :], BBTr, ptp, 2 * i)
            S.op("dve", lambda e: e.tensor_scalar(out=t2[:], in0=BIe[:, i, :], scalar1=sm[:, 6, i:i + 1], scalar2=None, op0=ALU.mult), reads=[BIe, sm], writes=[t2])
            S.op("dve", lambda e: e.scalar_tensor_tensor(out=t2[:], in0=BRe[:, i, :], scalar=sm[:, 7, i:i + 1], in1=t2[:], op0=ALU.mult, op1=ALU.add), reads=[BRe, sm, t2], writes=[t2])
            transpose_to(C, t2, 128, 0, 128, lambda: BBTi[:, i, :], BBTi, ptp, 2 * i + 1)
        PWr = C.sb("sPWr", [128, 16, 129], st=st); PWi = C.sb("sPWi", [128, 16, 129], st=st)
        PIr = C.sb("sPIr", [128, 16, 129], st=st); PIi = C.sb("sPIi", [128, 16, 129], st=st)
        ta = C.sb("sta", [128, 129], st=st); tb = C.sb("stb", [128, 129], st=st); tc = C.sb("stc", [128, 129], st=st); td = C.sb("std", [128, 129], st=st)
        for i in range(16):
            S.op("dve", lambda e: e.tensor_scalar(out=ta[:], in0=tidx[:, 0:129], scalar1=sm[:, 2, i:i + 1], scalar2=None, op0=ALU.mult), reads=[tidx, sm], writes=[ta])
            sinr(tb[:], ta[:], 129, 0.0, [ta], [tb])
            sinr(tc[:], ta[:], 129, 0.5 * PI, [ta], [tc])
            act(td[:], tidx[:, 0:129], AF.Exp, [tidx, sm], [td], scale=sm[:, 1, i:i + 1])
            tt("dve", PWr[:, i, :], td[:], tc[:], ALU.mult, [td, tc], [PWr])
            tt("dve", PWi[:, i, :], td[:], tb[:], ALU.mult, [td, tb], [PWi])
            act(td[:], tidx[:, 0:129], AF.Exp, [tidx, sm], [td], scale=sm[:, 3, i:i + 1])
            tt("dve", PIr[:, i, :], td[:], tc[:], ALU.mult, [td, tc], [PIr])
            S.op("dve", lambda e: e.scalar_tensor_tensor(out=PIi[:, i, :], in0=td[:], scalar=-1.0, in1=tb[:], op0=ALU.mult, op1=ALU.mult), reads=[td, tb], writes=[PIi])
        ones = C.sb("sones", [128, 128], st=st)
        S.op("dve", lambda e: e.memset(ones[:], 1.0), writes=[ones])
        ut = [C.sb(f"su{i}", [128, 512], st=st) for i in range(2)]
        uT = C.sb("suT", [128, 4, 128], st=st)
        SR = C.sb("sSR", [128, 16, 128], st=st); SI = C.sb("sSI", [128, 16, 128], st=st)
        br = C.sb("sbr", [128, 16, 128], st=st); bi = C.sb("sbi", [128, 16, 128], st=st)
        w1 = C.sb("sw1", [128, 16, 128], st=st); w2 = C.sb("sw2", [128, 16, 128], st=st); w3 = C.sb("sw3", [128, 16, 128], st=st); w4 = C.sb("sw4", [128, 16, 128], st=st)
        zr = C.sb("szr", [128, 16, 128], st=st); zi = C.sb("szi", [128, 16, 128], st=st)
        Z0 = C.sb("sZ0", [128, 2, 16], st=st); ZL = C.sb("sZL", [128, 2, 16], st=st); zt = C.sb("szt", [128, 4, 16], st=st)
        sin_ = C.sb("ssin", [128, 2, 16], st=st)
        SL = C.sb("sSL", [128, 2, 16], st=st)
        yo = [C.sb(f"syo{i}", [128, 512], st=st) for i in range(2)]
        pbr = [C.ps(f"spbr{i}", [128, 128], st=st) for i in range(2)]; pbi = [C.ps(f"spbi{i}", [128, 128], st=st) for i in range(2)]
        py = [C.ps(f"spy{i}", [128, 512], st=st) for i in range(2)]
        ycnt = [0]

        def chunk(r0, c0, L, last_seq_idx):
            for i in range(16):
                q = i // 4
                pr_, pi_ = pbr[i % 2], pbi[i % 2]
                S.op("pe", lambda e: e.matmul(pr_[:, :L], lhsT=BBTr[:, i, :], rhs=uT[:, q, c0:c0 + L], start=True, stop=True), reads=[BBTr, uT], writes=[pr_])
                S.op("pe", lambda e: e.matmul(pi_[:, :L], lhsT=BBTi[:, i, :], rhs=uT[:, q, c0:c0 + L], start=True, stop=True), reads=[BBTi, uT], writes=[pi_])
                S.op("act", lambda e: e.copy(out=br[:, i, :L], in_=pr_[:, :L]), reads=[pr_], writes=[br])
                S.op("dve", lambda e: e.tensor_copy(out=bi[:, i, :L], in_=pi_[:, :L]), reads=[pi_], writes=[bi])
            tt("dve", w1[:, :, :L], PIr[:, :, :L], br[:, :, :L], ALU.mult, [PIr, br], [w1])
            tt("pool", w2[:, :, :L], PIi[:, :, :L], bi[:, :, :L], ALU.mult, [PIi, bi], [w2])
            tt("dve", w1[:, :, :L], w1[:, :, :L], w2[:, :, :L], ALU.subtract, [w1, w2], [w1])
            tt("pool", w3[:, :, :L], PIr[:, :, :L], bi[:, :, :L], ALU.mult, [PIr, bi], [w3])
            tt("dve", w4[:, :, :L], PIi[:, :, :L], br[:, :, :L], ALU.mult, [PIi, br], [w4])
            tt("dve", w3[:, :, :L], w3[:, :, :L], w4[:, :, :L], ALU.add, [w3, w4], [w3])
            for i in range(16):
                S.op("dve", lambda e: e.tensor_tensor_scan(out=zr[:, i, :L], data0=ones[:, :L], data1=w1[:, i, :L], initial=Z0[:, 0, i:i + 1], op0=ALU.mult, op1=ALU.add), reads=[ones, w1, Z0], writes=[zr])
                S.op("dve", lambda e: e.tensor_tensor_scan(out=zi[:, i, :L], data0=ones[:, :L], data1=w3[:, i, :L], initial=Z0[:, 1, i:i + 1], op0=ALU.mult, op1=ALU.add), reads=[ones, w3, Z0], writes=[zi])
            S.op("act", lambda e: e.copy(out=ZL[:, 0, :], in_=zr[:, :, L - 1]), reads=[zr], writes=[ZL])
            S.op("act", lambda e: e.copy(out=ZL[:, 1, :], in_=zi[:, :, L - 1]), reads=[zi], writes=[ZL])
            tt("dve", w1[:, :, :L], PWr[:, :, :L], zr[:, :, :L], ALU.mult, [PWr, zr], [w1])
            tt("pool", w2[:, :, :L], PWi[:, :, :L], zi[:, :, :L], ALU.mult, [PWi, zi], [w2])
            tt("dve", SR[:, :, :L], w1[:, :, :L], w2[:, :, :L], ALU.subtract, [w1, w2], [SR])
            tt("pool", w3[:, :, :L], PWr[:, :, :L], zi[:, :, :L], ALU.mult, [PWr, zi], [w3])
            tt("dve", w4[:, :, :L], PWi[:, :, :L], zr[:, :, :L], ALU.mult, [PWi, zr], [w4])
            tt("dve", SI[:, :, :L], w3[:, :, :L], w4[:, :, :L], ALU.add, [w3, w4], [SI])
            p = py[ycnt[0] % 2]; o = yo[ycnt[0] % 2]; ycnt[0] += 1
            for i in range(16):
                S.op("pe", lambda e: e.matmul(p[:L, i * 32:(i + 1) * 32], lhsT=SR[:, i, :L], rhs=CRe[:, i, :], start=True, stop=False), reads=[SR, CRe], writes=[p])
                S.op("pe", lambda e: e.matmul(p[:L, i * 32:(i + 1) * 32], lhsT=SI[:, i, :L], rhs=CIm[:, i, :], start=False, stop=True), reads=[SI, CIm], writes=[p])
            S.op("act", lambda e: e.copy(out=o[:L, :], in_=p[:L, :]), reads=[p], writes=[o])
            S.dma("pool", dr["YS5"][r0:r0 + L, :], o[:L, :], reads=[o])
            tt("dve", zt[:, 0, :], PWr[:, :, L], ZL[:, 0, :], ALU.mult, [PWr, ZL], [zt])
            tt("dve", zt[:, 1, :], PWi[:, :, L], ZL[:, 1, :], ALU.mult, [PWi, ZL], [zt])
            tt("dve", zt[:, 2, :], PWr[:, :, L], ZL[:, 1, :], ALU.mult, [PWr, ZL], [zt])
            tt("dve", zt[:, 3, :], PWi[:, :, L], ZL[:, 0, :], ALU.mult, [PWi, ZL], [zt])
            tt("dve", Z0[:, 0, :], zt[:, 0, :], zt[:, 1, :], ALU.subtract, [zt], [Z0])
            tt("dve", Z0[:, 1, :], zt[:, 2, :], zt[:, 3, :], ALU.add, [zt], [Z0])
            if last_seq_idx is not None:
                S.op("act", lambda e: e.copy(out=SL[:, 0, :], in_=SR[:, :, L - 1]), reads=[SR], writes=[SL])
                S.op("act", lambda e: e.copy(out=SL[:, 1, :], in_=SI[:, :, L - 1]), reads=[SI], writes=[SL])
                S.dma("pool", dr["o_s5re"][l, :, last_seq_idx, :], SL[:, 0, :], reads=[SL])
                S.dma("pool", dr["o_s5im"][l, :, last_seq_idx, :], SL[:, 1, :], reads=[SL])

        S.op("dve", lambda e: e.memset(Z0[:], 0.0), writes=[Z0])
        for ti, (r0, P) in enumerate(cfg.tiles):
            u = ut[ti % 2]
            S.dma("sp", u[:P, :], dr["PROJ"][r0:r0 + P, 5376:5888], writes=[u])
            for k in range(4):
                transpose_to(C, u, P, k * 128, 128, lambda: uT[:, k, :P], uT, ptp, k)
            if r0 < T:
                chunk(r0, 0, 128, 0 if r0 + 128 == T else None)
            else:
                for b in range(NS):
                    S.dma("sp", sin_[:, 0, :], dr["st_s5re"][l, :, b, :], writes=[sin_])
                    S.dma("sp", sin_[:, 1, :], dr["st_s5im"][l, :, b, :], writes=[sin_])
                    tt("dve", zt[:, 0, :], K(4), sin_[:, 0, :], ALU.mult, [sm, sin_], [zt])
                    tt("dve", zt[:, 1, :], K(5), sin_[:, 1, :], ALU.mult, [sm, sin_], [zt])
                    tt("dve", zt[:, 2, :], K(4), sin_[:, 1, :], ALU.mult, [sm, sin_], [zt])
                    tt("dve", zt[:, 3, :], K(5), sin_[:, 0, :], ALU.mult, [sm, sin_], [zt])
                    tt("dve", Z0[:, 0, :], zt[:, 0, :], zt[:, 1, :], ALU.subtract, [zt], [Z0])
                    tt("dve", Z0[:, 1, :], zt[:, 2, :], zt[:, 3, :], ALU.add, [zt], [Z0])
                    chunk(r0 + 8 * b, 8 * b, 8, 1 + b)
    S.barrier()
    with ExitStack() as st:
        dsk = bcast_load(C, st, "pd", dr["s5_d"][l], 512)
        bgl = bcast_load(C, st, "pbg", dr["s5_b_glu"][l], 512)
        gn = bcast_load(C, st, "pgn", dr["s5_norm"][l], 512)
        wg = C.sb("pwg", [128, 4, 512], st=st)
        S.dma("sp", wg[:], dr["s5_w_glu"][l].rearrange("(k p) c -> p k c", p=128), writes=[wg])
        yt = [C.sb(f"py{i}", [128, 512], st=st) for i in range(2)]
        ut = [C.sb(f"pu{i}", [128, 512], st=st) for i in range(2)]
        a1 = C.sb("pa1", [128, 512], st=st); a2 = C.sb("pa2", [128, 512], st=st); a3 = C.sb("pa3", [128, 512], st=st)
        yT = C.sb("pyT", [128, 4, 128], st=st)
        ss = C.sb("pss", [128, 2], st=st)
        ptp = [C.ps(f"ppt{i}", [128, 128], st=st) for i in range(2)]
        pg = C.ps("ppg", [128, 512], st=st)
        for ti, (r0, P) in enumerate(cfg.tiles):
            y, u = yt[ti % 2], ut[ti % 2]
            S.dma("sp", y[:P, :], dr["YS5"][r0:r0 + P, :], writes=[y])
            S.dma("sp", u[:P, :], dr["PROJ"][r0:r0 + P, 5376:5888], writes=[u])
            S.op("dve", lambda e: e.tensor_tensor(out=u[:P, :], in0=u[:P, :], in1=dsk[:P, :], op=ALU.mult), reads=[u, dsk], writes=[u])
            S.op("dve", lambda e: e.tensor_tensor(out=y[:P, :], in0=y[:P, :], in1=u[:P, :], op=ALU.add), reads=[y, u], writes=[y])
            S.op("pool", lambda e: e.tensor_tensor(out=a1[:P, :], in0=y[:P, :], in1=y[:P, :], op=ALU.mult), reads=[y], writes=[a1])
            S.op("dve", lambda e: e.tensor_scalar(out=a1[:P, :], in0=a1[:P, :], scalar1=0.044715, scalar2=1.0, op0=ALU.mult, op1=ALU.add), reads=[a1], writes=[a1])
            S.op("dve", lambda e: e.tensor_tensor(out=a1[:P, :], in0=a1[:P, :], in1=y[:P, :], op=ALU.mult), reads=[a1, y], writes=[a1])
            S.op("act", lambda e: e.activation(out=a1[:P, :], in_=a1[:P, :], func=AF.Sigmoid, scale=2.0 * math.sqrt(2.0 / math.pi)), reads=[a1], writes=[a1])
            S.op("dve", lambda e: e.tensor_tensor(out=a2[:P, :], in0=a1[:P, :], in1=y[:P, :], op=ALU.mult), reads=[a1, y], writes=[a2])
            for k in range(4):
                transpose_to(C, a2, P, k * 128, 128, lambda: yT[:, k, :P], yT, ptp, k)
            for k in range(4):
                S.op("pe", lambda e: e.matmul(pg[:P, :], lhsT=yT[:, k, :P], rhs=wg[:, k, :], start=(k == 0), stop=(k == 3)), reads=[yT, wg], writes=[pg])
            S.op("dve", lambda e: e.tensor_tensor(out=a3[:P, :], in0=pg[:P, :], in1=bgl[:P, :], op=ALU.add), reads=[pg, bgl], writes=[a3])
            S.op("act", lambda e: e.activation(out=a3[:P, :], in_=a3[:P, :], func=AF.Sigmoid), reads=[a3], writes=[a3])
            S.op("dve", lambda e: e.tensor_tensor(out=a2[:P, :], in0=a2[:P, :], in1=a3[:P, :], op=ALU.mult), reads=[a2, a3], writes=[a2])
            rms_rows(C, a2, P, 512, gn, a3, ss, a1)
            S.dma("pool", dr["OCAT"][r0:r0 + P, 1536:2048], a3[:P, :], reads=[a3])


def phase_rwkv(C, st0, l):
    from contextlib import ExitStack
    S, dr, cfg = C.S, C.dr, C.cfg
    T, NS, NTS = cfg.T, cfg.NS, cfg.NTS
    R0 = 3584

    def tt(eng, out, a, b, op, rd, wr):
        S.op(eng, lambda e: e.tensor_tensor(out=out, in0=a, in1=b, op=op), reads=rd, writes=wr)
    with ExitStack() as st:
        mu = bcast_load(C, st, "kmu", dr["rwkv_mu"][l], RW_COLS)
        w0 = bcast_load(C, st, "kw0", dr["rwkv_w0"][l], 512)
        a0 = bcast_load(C, st, "ka0", dr["rwkv_a0"][l], 512)
        kkw = bcast_load(C, st, "kkk", dr["rwkv_kk"][l], 512)
        kaw = bcast_load(C, st, "kka", dr["rwkv_ka"][l], 512)
        w2 = C.sb("kw2", [64, 512], st=st); S.dma("sp", w2[:], dr["rwkv_w2"][l], writes=[w2])
        a2 = C.sb("ka2", [64, 512], st=st); S.dma("sp", a2[:], dr["rwkv_a2"][l], writes=[a2])
        g2 = C.sb("kg2", [128, 512], st=st); S.dma("sp", g2[:], dr["rwkv_g2"][l], writes=[g2])
        cur = [C.sb(f"kc{i}", [128, RW_COLS], st=st) for i in range(2)]
        prv = [C.sb(f"kp{i}", [128, RW_COLS], st=st) for i in range(2)]
        lo = C.sb("klo", [128, 256], st=st)
        loT = C.sb("kloT", [128, 3, 128], st=st)
        dec = C.sb("kdec", [128, 512], st=st); aa = C.sb("kaa", [128, 512], st=st); gg = C.sb("kgg", [128, 512], st=st)
        kk = C.sb("kkkv", [128, 512], st=st); k2 = C.sb("kk2", [128, 512], st=st); bb = C.sb("kbb", [128, 512], st=st)
        ain = C.sb("kain", [128, 512], st=st); t5 = C.sb("kt5", [128, 512], st=st)
        s8 = C.sb("ks8", [128, 8], st=st)
        fa = [C.sb(f"kfa{i}", [64, 128, 40], st=st) for i in range(2)]
        fr = [C.sb(f"kfr{i}", [64, 128, 8], st=st) for i in range(2)]
        fw = [C.sb(f"kfw{i}", [64, 128, 8], st=st) for i in range(2)]
        ptp = [C.ps(f"kpt{i}", [128, 128], st=st) for i in range(2)]
        pl = [C.ps(f"kpl{i}", [128, 512], st=st) for i in range(3)]
        for i in range(2):
            S.op("pool", lambda e: e.memset(fa[i][:], 0.0), writes=[fa[i]])
        for ti, (r0, P) in enumerate(cfg.tiles):
            c, p = cur[ti % 2], prv[ti % 2]
            S.dma("sp", c[:P, :], dr["PROJ"][r0:r0 + P, R0:R0 + RW_COLS], writes=[c])
            if r0 == 0:
                S.op("pool", lambda e: e.memset(p[0:32, :], 0.0), writes=[p])
                S.dma("sp", p[1:P, :], dr["PROJ"][0:P - 1, R0:R0 + RW_COLS], writes=[p])
            else:
                S.dma("sp", p[:P, :], dr["PROJ"][r0 - 1:r0 + P - 1, R0:R0 + RW_COLS], writes=[p])
                if r0 >= T:
                    for b in range(NS):
                        S.dma("sp", p[8 * b:8 * b + 1, :], dr["st_shift"][l, b:b + 1, :], writes=[p])
            tt("dve", p[:P, :], p[:P, :], c[:P, :], ALU.subtract, [p, c], [p])
            tt("pool", p[:P, :], p[:P, :], mu[:P, :], ALU.mult, [p, mu], [p])
            tt("dve", p[:P, :], p[:P, :], c[:P, :], ALU.add, [p, c], [p])
            x = p
            S.op("act", lambda e: e.activation(out=lo[:P, 0:64], in_=x[:P, 1536:1600], func=AF.Tanh), reads=[x], writes=[lo])
            S.op("act", lambda e: e.copy(out=lo[:P, 64:128], in_=x[:P, 1600:1664]), reads=[x], writes=[lo])
            S.op("act", lambda e: e.activation(out=lo[:P, 128:256], in_=x[:P, 1664:1792], func=AF.Sigmoid), reads=[x], writes=[lo])
            transpose_to(C, lo, P, 0, 64, lambda: loT[0:64, 0, :P], loT, ptp, 0)
            transpose_to(C, lo, P, 64, 64, lambda: loT[0:64, 1, :P], loT, ptp, 1)
            transpose_to(C, lo, P, 128, 128, lambda: loT[:, 2, :P], loT, ptp, 2)
            S.op("pe", lambda e: e.matmul(pl[0][:P, :], lhsT=loT[0:64, 0, :P], rhs=w2[:, :], start=True, stop=True), reads=[loT, w2], writes=[pl[0]])
            S.op("pe", lambda e: e.matmul(pl[1][:P, :], lhsT=loT[0:64, 1, :P], rhs=a2[:, :], start=True, stop=True), reads=[loT, a2], writes=[pl[1]])
            S.op("pe", lambda e: e.matmul(pl[2][:P, :], lhsT=loT[:, 2, :P], rhs=g2[:, :], start=True, stop=True), reads=[loT, g2], writes=[pl[2]])
            tt("dve", dec[:P, :], pl[0][:P, :], w0[:P, :], ALU.add, [pl[0], w0], [dec])
            S.op("act", lambda e: e.activation(out=dec[:P, :], in_=dec[:P, :], func=AF.Sigmoid), reads=[dec], writes=[dec])
            S.op("act", lambda e: e.activation(out=dec[:P, :], in_=dec[:P, :], func=AF.Exp, scale=-math.exp(-0.5)), reads=[dec], writes=[dec])
            tt("dve", aa[:P, :], pl[1][:P, :], a0[:P, :], ALU.add, [pl[1], a0], [aa])
            S.op("act", lambda e: e.activation(out=aa[:P, :], in_=aa[:P, :], func=AF.Sigmoid), reads=[aa], writes=[aa])
            S.op("act", lambda e: e.copy(out=gg[:P, :], in_=pl[2][:P, :]), reads=[pl[2]], writes=[gg])
            tt("dve", kk[:P, :], x[:P, 512:1024], kkw[:P, :], ALU.mult, [x, kkw], [kk])
            tt("pool", t5[:P, :], kk[:P, :], kk[:P, :], ALU.mult, [kk], [t5])
            S.op("dve", lambda e: e.tensor_reduce(out=s8[:P, :], in_=v3(t5[:P, :], 8), axis=AX.X, op=ALU.add), reads=[t5], writes=[s8])
            S.op("dve", lambda e: e.tensor_scalar(out=s8[:P, :], in0=s8[:P, :], scalar1=1e-24, scalar2=None, op0=ALU.max), reads=[s8], writes=[s8])
            S.op("act", lambda e: e.sqrt(out=s8[:P, :], in_=s8[:P, :]), reads=[s8], writes=[s8])
            S.op("dve", lambda e: e.reciprocal(out=s8[:P, :], in_=s8[:P, :]), reads=[s8], writes=[s8])
            tt("dve", v3(kk[:P, :], 8), v3(kk[:P, :], 8), s8[:P, :].unsqueeze(2).to_broadcast([P, 8, 64]), ALU.mult, [kk, s8], [kk])
            S.op("dve", lambda e: e.scalar_tensor_tensor(out=t5[:P, :], in0=aa[:P, :], scalar=-1.0, in1=kaw[:P, :], op0=ALU.add, op1=ALU.mult), reads=[aa, kaw], writes=[t5])
            S.op("dve", lambda e: e.scalar_tensor_tensor(out=k2[:P, :], in0=t5[:P, :], scalar=1.0, in1=x[:P, 512:1024], op0=ALU.add, op1=ALU.mult), reads=[t5, x], writes=[k2])
            tt("pool", bb[:P, :], kk[:P, :], aa[:P, :], ALU.mult, [kk, aa], [bb])
            S.op("act", lambda e: e.mul(out=ain[:P, :], in_=kk[:P, :], mul=-1.0), reads=[kk], writes=[ain])
            S.dma("pool", dr["RWS"][0, r0:r0 + P, :], bb[:P, :], reads=[bb])
            S.dma("pool", dr["RWS"][1, r0:r0 + P, :], k2[:P, :], reads=[k2])
            S.dma("pool", dr["RWS"][2, r0:r0 + P, :], x[:P, 1024:1536], reads=[x])
            S.dma("pool", dr["RWS"][3, r0:r0 + P, :], x[:P, 0:512], reads=[x])
            S.dma("pool", dr["RWS"][4, r0:r0 + P, :], gg[:P, :], reads=[gg])
            A_, R_, W_ = fa[ti % 2], fr[ti % 2], fw[ti % 2]
            for h in range(8):
                transpose_to(C, ain, P, h * 64, 64, lambda: A_[:, :P, 32 + h], A_, ptp, 3 * h)
                transpose_to(C, x, P, h * 64, 64, lambda: R_[:, :P, h], R_, ptp, 3 * h + 1)
                transpose_to(C, dec, P, h * 64, 64, lambda: W_[:, :P, h], W_, ptp, 3 * h + 2)
            S.dma("pool", dr["FMA"][:, r0:r0 + P, :], A_[:, :P, :], reads=[A_])
            S.dma("pool", dr["FMR"][:, r0:r0 + P, :], R_[:, :P, :], reads=[R_])
            S.dma("pool", dr["FMW"][:, r0:r0 + P, :], W_[:, :P, :], reads=[W_])
    S.barrier()
    TC = 16
    with ExitStack() as st:
        ST = C.sb("nST", [64, 512], st=st)
        TMP = C.sb("nTMP", [64, 512], st=st)
        m40 = C.sb("nm40", [40, 512], st=st)
        S.op("dve", lambda e: e.memset(m40[:], 0.0), writes=[m40])
        S.dma("sp", m40[0:8, :], dr["mask8"], writes=[m40])
        S.dma("sp", m40[32:40, :], dr["mask8"], writes=[m40])
        BK = [C.sb(f"nBK{i}", [40, TC, 64], st=st) for i in range(2)]
        SAV = [C.sb(f"nSAV{i}", [40, TC, 512], st=st) for i in range(2)]
        AT = [C.sb(f"nAT{i}", [64, TC, 40], st=st) for i in range(2)]
        RT = [C.sb(f"nRT{i}", [64, TC, 8], st=st) for i in range(2)]
        WT = [C.sb(f"nWT{i}", [64, TC, 8], st=st) for i in range(2)]
        YB = [C.sb(f"nYB{i}", [8, TC, 512], st=st) for i in range(2)]
        psa = [C.ps(f"npsa{i}", [40, 512], st=st) for i in range(2)]
        psu = [C.ps(f"npsu{i}", [64, 512], st=st) for i in range(2)]
        psy = [C.ps(f"npsy{i}", [8, 512], st=st) for i in range(2)]
        for i in range(2):
            S.op("pool", lambda e: e.memset(BK[i][:], 0.0), writes=[BK[i]])
            S.op("pool", lambda e: e.memset(SAV[i][:], 0.0), writes=[SAV[i]])
        cc = [0]
        stp = [0]

        def seq(t0, n):
            steps = []
            for c0 in range(0, n, TC):
                L = min(TC, n - c0)
                for t in range(L):
                    steps.append((t0 + c0, L, t))
            cur = {}

            def loads(r0, L):
                k = cc[0] % 2
                cc[0] += 1
                bk, sav, at, rt, wt, yb = BK[k], SAV[k], AT[k], RT[k], WT[k], YB[k]
                S.dma("sp", bk[0:8, :L, :], dr["RWS"][1, r0:r0 + L, :].rearrange("t (h j) -> h t j", h=8), writes=[bk])
                S.dma("sp", bk[32:40, :L, :], dr["RWS"][0, r0:r0 + L, :].rearrange("t (h j) -> h t j", h=8), writes=[bk])
                S.dma("sp", sav[0:8, :L, :], dr["RWS"][2, r0:r0 + L, :].partition_broadcast(8), writes=[sav])
                S.op("pool", lambda e: e.tensor_tensor(out=sav[0:8, :L, :], in0=sav[0:8, :L, :],
                                                       in1=m40[0:8, :].unsqueeze(1).to_broadcast([8, L, 512]), op=ALU.mult), reads=[sav, m40], writes=[sav])
                S.dma("sp", at[:, :L, :], dr["FMA"][:, r0:r0 + L, :], writes=[at])
                S.dma("sp", rt[:, :L, :], dr["FMR"][:, r0:r0 + L, :], writes=[rt])
                S.dma("sp", wt[:, :L, :], dr["FMW"][:, r0:r0 + L, :], writes=[wt])
                return (bk, sav, at, rt, wt, yb)

            def mm1(si):
                r0, L, t = steps[si]
                if t == 0:
                    cur[r0] = loads(r0, L)
                at = cur[r0][2]
                pa = psa[si % 2]
                S.op("pe", lambda e: e.matmul(pa[:, :], lhsT=at[:, t, :], rhs=ST[:, :], start=True, stop=True), reads=[at, ST], writes=[pa])

            mm1(0)
            for si, (r0, L, t) in enumerate(steps):
                bk, sav, at, rt, wt, yb = cur[r0]
                pa, pu, py_ = psa[si % 2], psu[si % 2], psy[si % 2]
                S.op("dve", lambda e: e.tensor_tensor(out=sav[32:40, t, :], in0=pa[32:40, :], in1=m40[32:40, :], op=ALU.mult), reads=[pa, m40], writes=[sav])
                S.op("pool", lambda e: e.tensor_tensor(out=v3(TMP[:, :], 8), in0=v3(ST[:, :], 8),
                                                       in1=wt[:, t, :].unsqueeze(2).to_broadcast([64, 8, 64]), op=ALU.mult), reads=[ST, wt], writes=[TMP])
                S.op("pe", lambda e: e.matmul(pu[:, :], lhsT=bk[:, t, :], rhs=sav[:, t, :], start=True, stop=True), reads=[bk, sav], writes=[pu])
                S.op("dve", lambda e: e.tensor_tensor(out=ST[:, :], in0=TMP[:, :], in1=pu[:, :], op=ALU.add), reads=[TMP, pu], writes=[ST])
                if si + 1 < len(steps):
                    mm1(si + 1)
                S.op("pe", lambda e: e.matmul(py_[:, :], lhsT=rt[:, t, :], rhs=ST[:, :], start=True, stop=True), reads=[rt, ST], writes=[py_])
                S.op("act", lambda e: e.copy(out=yb[:, t, :], in_=py_[:, :]), reads=[py_], writes=[yb])
                if t == L - 1:
                    for h in range(8):
                        S.dma("pool", dr["RWS"][5, r0:r0 + L, h * 64:(h + 1) * 64], yb[h:h + 1, :L, h * 64:(h + 1) * 64], reads=[yb])
                    del cur[r0]

        S.op("dve", lambda e: e.memset(ST[:], 0.0), writes=[ST])
        seq(0, T)
        S.dma("pool", dr["o_rwkvT"][l, 0].rearrange("j h i -> j (h i)"), ST[:, :], reads=[ST])
        for b in range(NS):
            S.dma("sp", ST[:, :], dr["st_rwkvT"][l, b].rearrange("j h i -> j (h i)"), writes=[ST])
            seq(T + 8 * b, 8)
            S.dma("pool", dr["o_rwkvT"][l, 1 + b].rearrange("j h i -> j (h i)"), ST[:, :], reads=[ST])
    S.barrier()
    with ExitStack() as st:
        lw = bcast_load(C, st, "olw", dr["rwkv_ln_w"][l], 512)
        lb = bcast_load(C, st, "olb", dr["rwkv_ln_b"][l], 512)
        rk = bcast_load(C, st, "ork", dr["rwkv_rk"][l], 512)
        ins = [[C.sb(f"oi{j}_{i}", [128, 512], st=st) for j in range(5)] for i in range(2)]
        t1 = C.sb("ot1", [128, 512], st=st); t2 = C.sb("ot2", [128, 512], st=st)
        s8 = C.sb("os8", [128, 16], st=st)
        for ti, (r0, P) in enumerate(cfg.tiles):
            k2, v, r, g, y = ins[ti % 2]
            for j, tl in zip((1, 2, 3, 4, 5), (k2, v, r, g, y)):
                S.dma("sp", tl[:P, :], dr["RWS"][j, r0:r0 + P, :], writes=[tl])
            y3 = v3(y[:P, :], 8)
            S.op("dve", lambda e: e.tensor_reduce(out=s8[:P, 0:8], in_=y3, axis=AX.X, op=ALU.add), reads=[y], writes=[s8])
            S.op("dve", lambda e: e.tensor_scalar(out=s8[:P, 0:8], in0=s8[:P, 0:8], scalar1=1.0 / 64, scalar2=None, op0=ALU.mult), reads=[s8], writes=[s8])
            tt("dve", y3, y3, s8[:P, 0:8].unsqueeze(2).to_broadcast([P, 8, 64]), ALU.subtract, [y, s8], [y])
            tt("pool", t1[:P, :], y[:P, :], y[:P, :], ALU.mult, [y], [t1])
            S.op("dve", lambda e: e.tensor_reduce(out=s8[:P, 8:16], in_=v3(t1[:P, :], 8), axis=AX.X, op=ALU.add), reads=[t1], writes=[s8])
            S.op("dve", lambda e: e.tensor_scalar(out=s8[:P, 8:16], in0=s8[:P, 8:16], scalar1=1.0 / 64, scalar2=64e-5, op0=ALU.mult, op1=ALU.add), reads=[s8], writes=[s8])
            S.op("act", lambda e: e.sqrt(out=s8[:P, 8:16], in_=s8[:P, 8:16]), reads=[s8], writes=[s8])
            S.op("dve", lambda e: e.reciprocal(out=s8[:P, 8:16], in_=s8[:P, 8:16]), reads=[s8], writes=[s8])
            tt("dve", y3, y3, s8[:P, 8:16].unsqueeze(2).to_broadcast([P, 8, 64]), ALU.mult, [y, s8], [y])
            tt("dve", y[:P, :], y[:P, :], lw[:P, :], ALU.mult, [y, lw], [y])
            tt("dve", y[:P, :], y[:P, :], lb[:P, :], ALU.add, [y, lb], [y])
            tt("pool", t1[:P, :], r[:P, :], k2[:P, :], ALU.mult, [r, k2], [t1])
            tt("pool", t1[:P, :], t1[:P, :], rk[:P, :], ALU.mult, [t1, rk], [t1])
            S.op("dve", lambda e: e.tensor_reduce(out=s8[:P, 0:8], in_=v3(t1[:P, :], 8), axis=AX.X, op=ALU.add), reads=[t1], writes=[s8])
            tt("dve", v3(t2[:P, :], 8), v3(v[:P, :], 8), s8[:P, 0:8].unsqueeze(2).to_broadcast([P, 8, 64]), ALU.mult, [v, s8], [t2])
            tt("dve", y[:P, :], y[:P, :], t2[:P, :], ALU.add, [y, t2], [y])
            tt("dve", y[:P, :], y[:P, :], g[:P, :], ALU.mult, [y, g], [y])
            S.dma("pool", dr["OCAT"][r0:r0 + P, 1024:1536], y[:P, :], reads=[y])


def phase_diff(C, st0, l):
    from contextlib import ExitStack
    S, dr, cfg = C.S, C.dr, C.cfg
    T, NS, NTS, NPG = cfg.T, cfg.NS, cfg.NTS, cfg.NPG
    NQB = T // 128
    lam_init = 0.8 - 0.6 * math.exp(-0.3 * l)

    def tt(eng, out, a, b, op, rd, wr):
        S.op(eng, lambda e: e.tensor_tensor(out=out, in0=a, in1=b, op=op), reads=rd, writes=wr)
    with ExitStack() as st:
        dl = C.sb("dl", [128, 4, 64], st=st)
        S.dma("sp", dl[:], dr["diff_l"][l].partition_broadcast(128), writes=[dl])
        lt = C.sb("dlt", [128, 2, 64], st=st)
        lam = C.sb("dlam", [128, 4], st=st)
        tt("dve", lt[:, 0, :], dl[:, 0, :], dl[:, 1, :], ALU.mult, [dl], [lt])
        tt("dve", lt[:, 1, :], dl[:, 2, :], dl[:, 3, :], ALU.mult, [dl], [lt])
        S.op("dve", lambda e: e.tensor_reduce(out=lam[:, 0:2], in_=lt[:], axis=AX.X, op=ALU.add), reads=[lt], writes=[lam])
        S.op("act", lambda e: e.activation(out=lam[:, 0:2], in_=lam[:, 0:2], func=AF.Exp), reads=[lam], writes=[lam])
        tt("dve", lam[:, 2:3], lam[:, 0:1], lam[:, 1:2], ALU.subtract, [lam], [lam])
        S.op("dve", lambda e: e.tensor_scalar(out=lam[:, 2:3], in0=lam[:, 2:3], scalar1=lam_init, scalar2=-1.0, op0=ALU.add, op1=ALU.mult), reads=[lam], writes=[lam])
        sub = bcast_load(C, st, "dsub", dr["diff_subln"][l], 128)
        S.op("act", lambda e: e.mul(out=sub[:], in_=sub[:], mul=(1.0 - lam_init)), reads=[sub], writes=[sub])
        cm = C.sb("dcm", [128, 128], st=st); S.dma("sp", cm[:], dr["cmask"], writes=[cm])
        cm8 = C.sb("dcm8", [8, 8], st=st); S.dma("sp", cm8[:], dr["cmask8"], writes=[cm8])
        KW = max(T, NPG * 128 + 8)
        KT = C.sb("dKT", [128, KW], st=st)
        SC = [C.sb(f"dSC{m}", [128, KW], st=st) for m in range(2)]
        QT = C.sb("dQT", [128, 128], st=st)
        xq = [C.sb(f"dxq{i}", [128, 128], st=st) for i in range(2)]
        pTs = [C.sb(f"dpT{i}", [128, 128], st=st) for i in range(3)]
        oo = [C.sb(f"doo{i}", [128, 128], st=st) for i in range(2)]
        sq = C.sb("dsq", [128, 128], st=st)
        sm = C.sb("dsm", [128, 8], st=st)
        ss = C.sb("dss", [128, 2], st=st)
        vn = C.sb("dvn", [8, 128], st=st)
        ptp = [C.ps(f"dpt{i}", [128, 128], st=st) for i in range(2)]
        psc = [C.ps(f"dps{i}", [128, 512], st=st) for i in range(3)]
        po = [C.ps(f"dpo{i}", [128, 128], st=st) for i in range(2)]
        cnt = [0]

        def attn(P, nk, vblocks, mask, msz, r0, h):
            for m in range(2):
                for k0 in range(0, nk, 512):
                    n = min(512, nk - k0)
                    p_ = psc[cnt[0] % 3]
                    cnt[0] += 1
                    S.op("pe", lambda e: e.matmul(p_[:P, :n], lhsT=QT[m * 64:(m + 1) * 64, :P], rhs=KT[m * 64:(m + 1) * 64, k0:k0 + n],
                                                  start=True, stop=True), reads=[QT, KT], writes=[p_])
                    if cnt[0] % 2 == 0:
                        S.op("act", lambda e: e.mul(out=SC[m][:P, k0:k0 + n], in_=p_[:P, :n], mul=0.125), reads=[p_], writes=[SC[m]])
                    else:
                        S.op("dve", lambda e: e.tensor_scalar(out=SC[m][:P, k0:k0 + n], in0=p_[:P, :n], scalar1=0.125, scalar2=None, op0=ALU.mult), reads=[p_], writes=[SC[m]])
                tt("pool", SC[m][:P, nk - msz:nk], SC[m][:P, nk - msz:nk], mask[:P, :msz], ALU.add, [SC[m], mask], [SC[m]])
                S.op("dve", lambda e: e.tensor_reduce(out=sm[:P, m:m + 1], in_=SC[m][:P, :nk], axis=AX.X, op=ALU.max), reads=[SC[m]], writes=[sm])
                S.op("dve", lambda e: e.tensor_scalar(out=sm[:P, 2 + m:3 + m], in0=sm[:P, m:m + 1], scalar1=-1.0, scalar2=None, op0=ALU.mult), reads=[sm], writes=[sm])
                S.op("act", lambda e: e.activation(out=SC[m][:P, :nk], in_=SC[m][:P, :nk], func=AF.Exp, bias=sm[:P, 2 + m:3 + m], scale=1.0,
                                                   accum_out=sm[:P, 4 + m:5 + m]), reads=[SC[m], sm], writes=[SC[m], sm])
            S.op("dve", lambda e: e.reciprocal(out=sm[:P, 6:8], in_=sm[:P, 4:6]), reads=[sm], writes=[sm])
            tt("dve", sm[:P, 7:8], sm[:P, 7:8], lam[:P, 2:3], ALU.mult, [sm, lam], [sm])
            S.op("dve", lambda e: e.tensor_scalar(out=SC[0][:P, :nk], in0=SC[0][:P, :nk], scalar1=sm[:P, 6:7], scalar2=None, op0=ALU.mult), reads=[SC[0], sm], writes=[SC[0]])
            S.op("dve", lambda e: e.scalar_tensor_tensor(out=SC[0][:P, :nk], in0=SC[1][:P, :nk], scalar=sm[:P, 7:8], in1=SC[0][:P, :nk],
                                                         op0=ALU.mult, op1=ALU.add), reads=[SC[0], SC[1], sm], writes=[SC[0]])
            pO = po[cnt[0] % 2]
            for bi, (k0, ksz, v_ap, vt) in enumerate(vblocks):
                pt = ptp[bi % 2]
                pT = pTs[bi % 3]
                S.op("pe", lambda e: e.transpose(out=pt[:ksz, :P], in_=SC[0][:P, k0:k0 + ksz], identity=C.ident[:P, :P]), reads=[SC[0], C.ident], writes=[pt])
                if bi % 2 == 0:
                    S.op("act", lambda e: e.copy(out=pT[:ksz, :P], in_=pt[:ksz, :P]), reads=[pt], writes=[pT])
                else:
                    S.op("dve", lambda e: e.tensor_copy(out=pT[:ksz, :P], in_=pt[:ksz, :P]), reads=[pt], writes=[pT])
                S.op("pe", lambda e: e.matmul(pO[:P, :], lhsT=pT[:ksz, :P], rhs=v_ap, start=(bi == 0), stop=(bi == len(vblocks) - 1)), reads=[pT, vt], writes=[pO])
            o = oo[cnt[0] % 2]
            S.op("act", lambda e: e.copy(out=o[:P, :], in_=pO[:P, :]), reads=[pO], writes=[o])
            rms_rows(C, o, P, 128, sub, o, ss, sq)
            S.dma("pool", dr["OCAT"][r0:r0 + P, 512 + h * 128:512 + (h + 1) * 128], o[:P, :], reads=[o])

        n = 0
        st2 = ExitStack()
        st2.__enter__()
        Vh = C.sb("dVh", [128, NQB, 128], st=st2)
        for h in range(4):
            for kb in range(NQB):
                x = xq[n % 2]; n += 1
                S.dma("sp", x[:, :], dr["k_new"][l, kb * 128:(kb + 1) * 128, h * 128:(h + 1) * 128], writes=[x])
                transpose_to(C, x, 128, 0, 128, lambda: KT[:, kb * 128:(kb + 1) * 128], KT, ptp, kb)
            S.dma("sp", Vh[:, :NQB, :], dr["PROJ"][0:T, 3072 + h * 128:3072 + (h + 1) * 128].rearrange("(k p) c -> p k c", p=128), writes=[Vh])
            for qb in range(NQB):
                x = xq[n % 2]; n += 1
                S.dma("sp", x[:, :], dr["PROJ"][qb * 128:(qb + 1) * 128, 2048 + h * 128:2048 + (h + 1) * 128], writes=[x])
                transpose_to(C, x, 128, 0, 128, lambda: QT[:, :], QT, ptp, qb)
                vb = [(kb * 128, 128, Vh[:, kb, :], Vh) for kb in range(qb + 1)]
                attn(128, (qb + 1) * 128, vb, cm, 128, qb * 128, h)
        S.barrier()
        st2.__exit__(None, None, None)
        ptb = C.sb("dptb", [128, NS * NPG], I32, st=st)
        S.dma("sp", ptb[:], dr["pt"].partition_broadcast(128), writes=[ptb])
        ptf = C.sb("dptf", [128, NS * NPG], st=st)
        io = C.sb("dio", [128, 1], I32, st=st); S.dma("sp", io[:], dr["iota"], writes=[io])
        iof = C.sb("diof", [128, 1], st=st)
        S.op("dve", lambda e: e.tensor_copy(out=ptf[:], in_=ptb[:]), reads=[ptb], writes=[ptf])
        S.op("dve", lambda e: e.tensor_copy(out=iof[:], in_=io[:]), reads=[io], writes=[iof])
        S.op("dve", lambda e: e.tensor_scalar(out=ptf[:], in0=ptf[:], scalar1=128.0, scalar2=iof[:, 0:1], op0=ALU.mult, op1=ALU.add), reads=[ptf, iof], writes=[ptf])
        S.op("dve", lambda e: e.tensor_scalar(out=ptf[:], in0=ptf[:], scalar1=float(l * cfg.NPOOL * 128), scalar2=None, op0=ALU.add), reads=[ptf], writes=[ptf])
        idx = C.sb("didx", [128, NS * NPG], I32, st=st)
        S.op("dve", lambda e: e.tensor_copy(out=idx[:], in_=ptf[:]), reads=[ptf], writes=[idx])
        KP = [C.sb(f"dKP{i}", [128, 512], st=st) for i in range(2)]
        VP = [C.sb(f"dVP{i}", [128, 512], st=st) for i in range(NPG)]
        ckf = dr["ck"].rearrange("l r c -> (l r) c")
        cvf = dr["cv"].rearrange("l r c -> (l r) c")
        KTs = C.sb("dKTs", [128, 4, NPG * 128 + 8], st=st)
        QTs = C.sb("dQTs", [128, 4, 8], st=st)
        q8 = C.sb("dq8", [8, 512], st=st); k8 = C.sb("dk8", [8, 512], st=st); v8 = C.sb("dv8", [8, 512], st=st)
        for b in range(NS):
            r0 = T + 8 * b
            for pg in range(NPG):
                kp = KP[pg % 2]
                col = b * NPG + pg
                gather(C, kp, kp[:, :], ckf, idx, col)
                gather(C, VP[pg], VP[pg][:, :], cvf, idx, col)
                for h in range(4):
                    transpose_to(C, kp, 128, h * 128, 128, lambda: KTs[:, h, pg * 128:(pg + 1) * 128], KTs, ptp, h)
            S.dma("sp", q8[:, :], dr["PROJ"][r0:r0 + 8, 2048:2560], writes=[q8])
            S.dma("sp", k8[:, :], dr["k_new"][l, r0:r0 + 8, :], writes=[k8])
            S.dma("sp", v8[:, :], dr["PROJ"][r0:r0 + 8, 3072:3584], writes=[v8])
            for h in range(4):
                transpose_to(C, k8, 8, h * 128, 128, lambda: KTs[:, h, NPG * 128:NPG * 128 + 8], KTs, ptp, h)
                transpose_to(C, q8, 8, h * 128, 128, lambda: QTs[:, h, :], QTs, ptp, h + 1)
            for h in range(4):
                S.op("act", lambda e: e.copy(out=QT[:, 0:8], in_=QTs[:, h, :]), reads=[QTs], writes=[QT])
                S.op("pool", lambda e: e.tensor_copy(out=KT[:, 0:NPG * 128 + 8], in_=KTs[:, h, :]), reads=[KTs], writes=[KT])
                vb = [(pg * 128, 128, VP[pg][:, h * 128:(h + 1) * 128], VP[pg]) for pg in range(NPG)]
                vb.append((NPG * 128, 8, v8[:, h * 128:(h + 1) * 128], v8))
                attn(8, NPG * 128 + 8, vb, cm8, 8, r0, h)


def gather(C, tile, out_ap, table, idx, col):
    S = C.S
    q = "pool"
    S._deps(q, [idx], [tile])
    if q not in S.dsem or S.dcnt[q] >= S.DK * (S.SEM_LIMIT // 16):
        S.dsem[q] = [S.newsem() for _ in range(S.DK)]
        S.dcnt[q] = 0
        S.last.setdefault("dma", {})
    i = S.dcnt[q]
    sem = S.dsem[q][i % S.DK]
    prev = 16 * (i // S.DK)
    if prev > 0:
        S._wait(q, ("dma", sem, prev))
    ins = S.eng[q].indirect_dma_start(out=out_ap, out_offset=None, in_=table,
                                      in_offset=bass.IndirectOffsetOnAxis(ap=idx[:, col:col + 1], axis=0))
    ins.then_inc(sem, 16)
    S.dcnt[q] = i + 1
    ref = ("dma", sem, prev + 16)
    S.last.setdefault("dma", {})[id(sem)] = ref
    S._mark(ref, [idx], [tile])
    S.nins += 1
```

```python
import math
import numpy as np
import concourse.bass as bass
import concourse.mybir as mybir
from concourse.bass_utils import run_bass_kernel_spmd

F32 = mybir.dt.float32
I32 = mybir.dt.int32
F32R = mybir.dt.float32r


def R_(ap):
    return ap.bitcast(F32R)
AF = mybir.ActivationFunctionType
ALU = mybir.AluOpType
AX = mybir.AxisListType

D = 2048
GW = 512
IN_COLS = 5888
RW_COLS = 1792
DFF = 5632
NCORES = 8


class TT:
    __slots__ = ("t", "lw", "rd")

    def __init__(self, t):
        self.t = t
        self.lw = None
        self.rd = {}

    def __getitem__(self, k):
        return self.t[k]


class Sched:
    SEM_LIMIT = 30000
    DK = 8

    def __init__(self, nc):
        self.nc = nc
        self.eng = {"pe": nc.tensor, "act": nc.scalar, "dve": nc.vector, "pool": nc.gpsimd, "sp": nc.sync}
        self.csem = {}
        self.ccnt = {}
        self.waited = {e: {} for e in self.eng}
        self.dsem = {}
        self.dcnt = {}
        self.semid = 0
        self.last = {}
        self.nins = 0

    def newsem(self):
        self.semid += 1
        return self.nc.alloc_semaphore(f"s{self.semid}")

    def _need(self, e, ref, lst):
        pe, sem, val = ref
        if e == "pe" and pe == "pe":
            return
        w = self.waited[e]
        k = id(sem)
        if w.get(k, (None, 0))[1] >= val:
            return
        w[k] = (sem, val)
        lst.append((sem, val))

    def _wait(self, e, ref):
        lst = []
        self._need(e, ref, lst)
        for sem, val in lst:
            self.eng[e].wait_ge(sem, val)
            self.nins += 1

    def _deps(self, e, reads, writes, lst=None):
        own = lst is None
        if own:
            lst = []
        for t in reads:
            if t.lw is not None:
                self._need(e, t.lw, lst)
        for t in writes:
            if t.lw is not None and t.lw[0] != e:
                self._need(e, t.lw, lst)
            for r in t.rd.values():
                if r[0] != e:
                    self._need(e, r, lst)
        if own:
            for sem, val in lst:
                self.eng[e].wait_ge(sem, val)
                self.nins += 1
        return lst

    def _emit_waits(self, e, lst, ins):
        for sem, val in lst[:-1]:
            pass
        if lst:
            sem, val = lst[-1]
            ins._wait_ge(sem, val)

    def _mark(self, ref, reads, writes):
        for t in reads:
            t.rd[id(ref[1])] = ref
        for t in writes:
            t.lw = ref
            t.rd = {}

    def op(self, e, fn, reads=(), writes=()):
        lst = self._deps(e, reads, writes, [])
        for sem, val in lst[:-1]:
            self.eng[e].wait_ge(sem, val)
            self.nins += 1
        if e not in self.csem or self.ccnt[e] >= self.SEM_LIMIT:
            self.csem[e] = self.newsem()
            self.ccnt[e] = 0
        ins = fn(self.eng[e])
        if lst:
            ins._wait_ge(lst[-1][0], lst[-1][1])
        self.ccnt[e] += 1
        ins.then_inc(self.csem[e], 1)
        ref = (e, self.csem[e], self.ccnt[e])
        self.last[e] = ref
        self._mark(ref, reads, writes)
        self.nins += 1
        return ref

    def dma(self, q, out, in_, reads=(), writes=(), **kw):
        lst = self._deps(q, reads, writes, [])
        if q not in self.dsem or self.dcnt[q] >= self.DK * (self.SEM_LIMIT // 16):
            self.dsem[q] = [self.newsem() for _ in range(self.DK)]
            self.dcnt[q] = 0
            self.last.setdefault("dma", {})
        i = self.dcnt[q]
        sem = self.dsem[q][i % self.DK]
        prev = 16 * (i // self.DK)
        if prev > 0:
            self._need(q, ("dma", sem, prev), lst)
        for sm_, val in lst[:-1]:
            self.eng[q].wait_ge(sm_, val)
            self.nins += 1
        ins = self.eng[q].dma_start(out=out, in_=in_, **kw)
        if lst:
            ins._wait_ge(lst[-1][0], lst[-1][1])
        ins.then_inc(sem, 16)
        self.dcnt[q] = i + 1
        ref = ("dma", sem, prev + 16)
        self.last["dma"][id(sem)] = ref
        self._mark(ref, reads, writes)
        self.nins += 1
        return ref

    def barrier(self, engines=("pe", "act", "dve", "pool", "sp")):
        refs = [r for k, r in self.last.items() if k != "dma"]
        refs += list(self.last.get("dma", {}).values())
        for e in engines:
            for r in refs:
                if r[0] == e:
                    continue
                pe, sem, val = r
                w = self.waited[e]
                if w.get(id(sem), (None, 0))[1] >= val:
                    continue
                w[id(sem)] = (sem, val)
                self.eng[e].wait_ge(sem, val)
                self.nins += 1


class Cfg:
    def __init__(self, T=8192, NS=16, NPG=16, NPOOL=2560):
        self.T, self.NS, self.NPG, self.NPOOL = T, NS, NPG, NPOOL
        self.NTS = NS * 8
        self.NT = T + self.NTS
        self.tiles = [(i * 128, 128) for i in range(T // 128)] + [(T, self.NTS)]
        self.sts = [self.tiles[i:i + 4] for i in range(0, T // 128, 4)] + [[self.tiles[-1]]]
        self.sts2 = [self.tiles[i:i + 2] for i in range(0, T // 128, 2)] + [[self.tiles[-1]]]


class Ctx:
    pass


def build(cfg):
    from contextlib import ExitStack
    nc = bass.Bass("TRN2", target_bir_lowering=False)
    S = Sched(nc)
    T, NS, NT, NTS, NPG = cfg.T, cfg.NS, cfg.NT, cfg.NTS, cfg.NPG
    NSQ = 1 + NS
    dr = {}

    def din(name, shape, dt=F32):
        dr[name] = nc.dram_tensor(name, list(shape), dt, kind="ExternalInput").ap()

    def dout(name, shape):
        dr[name] = nc.dram_tensor(name, list(shape), F32, kind="ExternalOutput").ap()

    def dscr(name, shape):
        dr[name] = nc.dram_tensor(name, list(shape), F32).ap()

    din("x0", [NT, D]); din("p0", [2, NT, 256])
    din("ck", [2, cfg.NPOOL * 128, 512]); din("cv", [2, cfg.NPOOL * 128, 512])
    din("pt", [NS * NPG], I32)
    din("st_ret", [2, NS, 4, 128, 128]); din("st_rwkvT", [2, NS, 64, 8, 64]); din("st_shift", [2, NS, RW_COLS])
    din("st_s5re", [2, 128, NS, 16]); din("st_s5im", [2, 128, NS, 16]); din("st_convT", [2, 128, 88, NS, 2])
    din("norm_mix", [2, D]); din("w_in", [2, D, IN_COLS]); din("w_out", [2, D, D])
    din("ret_norm_w", [2, 512]); din("ret_norm_b", [2, 512])
    din("diff_l", [2, 4, 64]); din("diff_subln", [2, 128])
    din("rwkv_mu", [2, RW_COLS]); din("rwkv_w0", [2, 512]); din("rwkv_w2", [2, 64, 512]); din("rwkv_a0", [2, 512])
    din("rwkv_a2", [2, 64, 512]); din("rwkv_g2", [2, 128, 512]); din("rwkv_kk", [2, 512]); din("rwkv_ka", [2, 512])
    din("rwkv_rk", [2, 512]); din("rwkv_ln_w", [2, 512]); din("rwkv_ln_b", [2, 512])
    din("s5_lre", [2, 128, 16]); din("s5_lim", [2, 128, 16]); din("s5_ls", [2, 128, 16])
    din("s5_bre", [2, 128, 16, 128]); din("s5_bim", [2, 128, 16, 128])
    din("s5_cre", [2, 128, 16, 32]); din("s5_cim", [2, 128, 16, 32])
    din("s5_d", [2, 512]); din("s5_w_glu", [2, 512, 512]); din("s5_b_glu", [2, 512]); din("s5_norm", [2, 512])
    din("norm_ffn", [2, D]); din("ffn_w_up", [2, D, 2 * DFF]); din("ffn_cw", [2, 128, 88, 3]); din("ffn_cb", [2, 128, 88])
    din("ffn_w_down", [2, DFF, D]); din("norm_ple", [2, D]); din("ple_w_proj", [2, 256, D]); din("ple_norm_e", [2, D])
    din("ple_w_gate", [2, D, D]); din("norm_final", [D])
    din("ident", [128, 128]); din("rope_r", [NT, 2, 64]); din("rope_d", [NT, 2, 8])
    din("ret_dmT", [4, 128, 128]); din("ret_dmT8", [4, 8, 8]); din("ret_qd", [128, 4, 128]); din("ret_kd", [128, 4]);
    din("ret_qd8", [128, 4, 8]); din("ret_kd8", [8, 4]); din("cmask", [128, 128]); din("cmask8", [8, 8])
    din("mask8", [8, 512]); din("tidx", [128, 130]); din("iota", [128, 1], I32)
    dout("y", [NT, D]); dout("k_new", [2, NT, 512]); dout("v_new", [2, NT, 512])
    dout("o_ret", [2, NSQ, 4, 128, 128]); dout("o_rwkvT", [2, NSQ, 64, 8, 64]); dout("o_shift", [2, NSQ, RW_COLS])
    dout("o_s5re", [2, 128, NSQ, 16]); dout("o_s5im", [2, 128, NSQ, 16]); dout("o_convT", [2, 128, 88, NSQ, 2])
    dscr("X", [NT, D]); dscr("PROJ", [NT, IN_COLS]); dscr("OCAT", [NT, D]); dscr("QK", [NT, 1024])
    dscr("RWS", [6, NT, 512]); dscr("FMA", [64, NT, 40]); dscr("FMR", [64, NT, 8]); dscr("FMW", [64, NT, 8]); dscr("ERAW", [NT, D]); dscr("YS5", [NT, 512])

    C = Ctx()
    C.nc, C.S, C.cfg, C.dr = nc, S, cfg, dr
    C.uid = 0
    with ExitStack() as gs:
        def sb(name, shape, dt=F32, st=gs):
            C.uid += 1
            return TT(st.enter_context(nc.sbuf_tensor(f"t{C.uid}_" + name, list(shape), dt)))

        def ps(name, shape, st=gs):
            C.uid += 1
            return TT(st.enter_context(nc.psum_tensor(f"q{C.uid}_" + name, list(shape), F32)))
        C.sb, C.ps = sb, ps
        C.ident = sb("ident", [128, 128])
        S.dma("sp", C.ident[:], dr["ident"], writes=[C.ident])
        C.eps = sb("epsc", [128, 4])
        S.op("dve", lambda e: e.memset(C.eps[:, 0:1], 1e-6), writes=[C.eps])
        S.op("dve", lambda e: e.memset(C.eps[:, 1:2], -math.pi), writes=[C.eps])
        S.op("dve", lambda e: e.memset(C.eps[:, 2:3], 64e-5), writes=[C.eps])
        S.op("dve", lambda e: e.memset(C.eps[:, 3:4], 1.0), writes=[C.eps])
        for l in range(2):
            with ExitStack() as st:
                phase_proj(C, st, l)
            S.barrier()
            with ExitStack() as st:
                phase_prep(C, st, l)
            S.barrier()
            with ExitStack() as st:
                phase_ret(C, st, l)
            S.barrier()
            with ExitStack() as st:
                phase_diff(C, st, l)
            S.barrier()
            with ExitStack() as st:
                phase_rwkv(C, st, l)
            S.barrier()
            with ExitStack() as st:
                phase_s5(C, st, l)
            S.barrier()
            with ExitStack() as st:
                phase_wout(C, st, l)
            S.barrier()
            with ExitStack() as st:
                phase_ffn(C, st, l)
            S.barrier()
            with ExitStack() as st:
                phase_ple(C, st, l)
            S.barrier()
        S.barrier()
    return nc, S


def bcast_load(C, st, name, ap, n, q="sp"):
    t = C.sb(name, [128, n], st=st)
    C.S.dma(q, t[:], ap.partition_broadcast(128), writes=[t])
    return t


def rms_rows(C, x, P, n, g, out, ss, act_sq_out):
    S = C.S
    if isinstance(act_sq_out, tuple):
        jt, jap = act_sq_out
        jap = R_(jap)
    else:
        jt, jap = act_sq_out, act_sq_out
    S.op("act", lambda e: e.activation(out=jap[:P, :n], in_=x[:P, :n], func=AF.Square, accum_out=ss[:P, 0:1]),
         reads=[x], writes=[jt, ss])
    S.op("dve", lambda e: e.tensor_scalar(out=ss[:P, 1:2], in0=ss[:P, 0:1], scalar1=1.0 / n, scalar2=1e-6,
                                          op0=ALU.mult, op1=ALU.add), reads=[ss], writes=[ss])
    S.op("act", lambda e: e.sqrt(out=ss[:P, 1:2], in_=ss[:P, 1:2]), reads=[ss], writes=[ss])
    S.op("dve", lambda e: e.reciprocal(out=ss[:P, 1:2], in_=ss[:P, 1:2]), reads=[ss], writes=[ss])
    S.op("dve", lambda e: e.scalar_tensor_tensor(out=out[:P, :n], in0=x[:P, :n], scalar=ss[:P, 1:2], in1=g[:P, :n],
                                                 op0=ALU.mult, op1=ALU.mult), reads=[x, ss, g], writes=[out])


def transpose_to(C, src, P, c0, ncols, dst_ap_fn, dst, ptp, i, r=False):
    S = C.S
    pt = ptp[i % len(ptp)]
    S.op("pe", lambda e: e.transpose(out=pt[:ncols, :P], in_=src[:P, c0:c0 + ncols], identity=C.ident[:P, :P]),
         reads=[src, C.ident], writes=[pt])
    eng = "act" if i % 2 == 0 else "dve"
    cast = R_ if r else (lambda a: a)
    if eng == "act":
        S.op("act", lambda e: e.copy(out=cast(dst_ap_fn()), in_=pt[:ncols, :P]), reads=[pt], writes=[dst])
    else:
        S.op("dve", lambda e: e.tensor_copy(out=cast(dst_ap_fn()), in_=pt[:ncols, :P]), reads=[pt], writes=[dst])


def dense(C, hT, sizes, W, KC, N, wbufs, pbufs, epi, cbw=512, wst=None):
    S = C.S
    cb = 0
    cnt = 0
    for c0 in range(0, N, cbw):
        ncol = min(cbw, N - c0)
        wb = wbufs[cb % len(wbufs)]
        S.dma("sp", wst[:, :KC, :ncol], W[:, c0:c0 + ncol].rearrange("(k p) c -> p k c", p=128), writes=[wst])
        S.op("pool", lambda e: e.tensor_copy(out=R_(wb[:, :KC, :ncol]), in_=wst[:, :KC, :ncol]), reads=[wst], writes=[wb])
        for ti, (row0, P, off) in enumerate(sizes):
            po = pbufs[cnt % len(pbufs)]
            cnt += 1
            for k in range(KC):
                S.op("pe", lambda e: e.matmul(po[:P, :ncol], lhsT=R_(hT[:, k, off:off + P]), rhs=R_(wb[:, k, :ncol]),
                                              start=(k == 0), stop=(k == KC - 1)), reads=[hT, wb], writes=[po])
            epi(ti, row0, P, c0, ncol, po)
        cb += 1


def phase_proj(C, st, l):
    S, dr, cfg = C.S, C.dr, C.cfg
    g = bcast_load(C, st, "g_mix", dr["norm_mix"][l], D)
    xs = [C.sb(f"px{i}", [128, D], st=st) for i in range(2)]
    hs = C.sb("ph", [128, D], st=st)
    sq = C.sb("psq", [128, D], st=st)
    ss = C.sb("pss", [128, 2], st=st)
    hT = C.sb("phT", [128, 16, 512], st=st)
    wb = [C.sb(f"pw{i}", [128, 16, 512], st=st) for i in range(2)]
    wst = C.sb("pwst", [128, 16, 512], st=st)
    ob = [C.sb(f"pob{i}", [128, 512], st=st) for i in range(4)]
    ptp = [C.ps(f"ppt{i}", [128, 128], st=st) for i in range(2)]
    pb = [C.ps(f"ppo{i}", [128, 512], st=st) for i in range(4)]
    src = dr["x0"] if l == 0 else dr["X"]
    n = 0
    for stl in cfg.sts:
        sizes = []
        off = 0
        for (r0, P) in stl:
            x = xs[n % 2]
            n += 1
            S.dma("sp", x[:P, :], src[r0:r0 + P, :], writes=[x])
            rms_rows(C, x, P, D, g, hs, ss, sq)
            for k in range(16):
                transpose_to(C, hs, P, k * 128, 128, lambda: hT[:, k, off:off + P], hT, ptp, k, r=True)
            sizes.append((r0, P, off))
            off += P
        cnt = [0]

        def epi(ti, r0, P, c0, ncol, po):
            o = ob[cnt[0] % 4]
            if cnt[0] % 2 == 0:
                S.op("act", lambda e: e.copy(out=o[:P, :ncol], in_=po[:P, :ncol]), reads=[po], writes=[o])
            else:
                S.op("dve", lambda e: e.tensor_copy(out=o[:P, :ncol], in_=po[:P, :ncol]), reads=[po], writes=[o])
            cnt[0] += 1
            S.dma("pool", dr["PROJ"][r0:r0 + P, c0:c0 + ncol], o[:P, :ncol], reads=[o])
        dense(C, hT, sizes, dr["w_in"][l], 16, IN_COLS, wb, pb, epi, wst=wst)


def v3(ap, h):
    return ap.rearrange("p (h d) -> p h d", h=h)


def rope_apply(C, src, dst, P, c0, nh, hd, half, cs, tmp, scale=None, sc0=None):
    S = C.S
    if sc0 is None:
        sc0 = c0
    s3 = v3(src[:P, sc0:sc0 + nh * hd], nh)
    d3 = v3(dst[:P, c0:c0 + nh * hd], nh)
    t3 = v3(tmp[:P, 0:nh * hd], nh)
    cosb = cs[:P, 0:1, :].to_broadcast([P, nh, half])
    sinb = cs[:P, 1:2, :].to_broadcast([P, nh, half])
    x1, x2 = s3[:, :, 0:half], s3[:, :, half:2 * half]
    if 2 * half < hd:
        S.op("pool", lambda e: e.tensor_copy(out=d3[:, :, 2 * half:hd], in_=s3[:, :, 2 * half:hd]), reads=[src], writes=[dst])
    S.op("dve", lambda e: e.tensor_tensor(out=t3[:, :, 0:half], in0=x1, in1=cosb, op=ALU.mult), reads=[src, cs], writes=[tmp])
    S.op("dve", lambda e: e.tensor_tensor(out=t3[:, :, half:2 * half], in0=x2, in1=sinb, op=ALU.mult), reads=[src, cs], writes=[tmp])
    S.op("dve", lambda e: e.tensor_tensor(out=d3[:, :, 0:half], in0=t3[:, :, 0:half], in1=t3[:, :, half:2 * half], op=ALU.subtract),
         reads=[tmp], writes=[dst])
    S.op("dve", lambda e: e.tensor_tensor(out=t3[:, :, 0:half], in0=x1, in1=sinb, op=ALU.mult), reads=[src, cs], writes=[tmp])
    S.op("dve", lambda e: e.tensor_tensor(out=t3[:, :, half:2 * half], in0=x2, in1=cosb, op=ALU.mult), reads=[src, cs], writes=[tmp])
    S.op("dve", lambda e: e.tensor_tensor(out=d3[:, :, half:2 * half], in0=t3[:, :, 0:half], in1=t3[:, :, half:2 * half], op=ALU.add),
         reads=[tmp], writes=[dst])
    if scale is not None:
        S.op("act", lambda e: e.mul(out=dst[:P, c0:c0 + nh * hd], in_=dst[:P, c0:c0 + nh * hd], mul=scale), reads=[dst], writes=[dst])


def phase_prep(C, st, l):
    S, dr, cfg = C.S, C.dr, C.cfg
    T, NS = cfg.T, cfg.NS
    pj = [C.sb(f"rpj{i}", [128, 3072], st=st) for i in range(2)]
    oo = [C.sb(f"roo{i}", [128, 2048], st=st) for i in range(2)]
    tmp = C.sb("rtmp", [128, 512], st=st)
    csr = [C.sb(f"rcsr{i}", [128, 2, 64], st=st) for i in range(2)]
    csd = [C.sb(f"rcsd{i}", [128, 2, 8], st=st) for i in range(2)]
    S.dma("pool", dr["v_new"][l], dr["PROJ"][:, 3072:3584])
    S.dma("pool", dr["o_shift"][l, 0:1, :], dr["PROJ"][T - 1:T, 3584:3584 + RW_COLS])
    for b in range(NS):
        S.dma("pool", dr["o_shift"][l, 1 + b:2 + b, :], dr["PROJ"][T + 8 * b + 7:T + 8 * b + 8, 3584:3584 + RW_COLS])
    for i, (r0, P) in enumerate(cfg.tiles):
        x, o, cr, cd = pj[i % 2], oo[i % 2], csr[i % 2], csd[i % 2]
        S.dma("sp", x[:P, :], dr["PROJ"][r0:r0 + P, 0:3072], writes=[x])
        S.dma("sp", cr[:P], dr["rope_r"][r0:r0 + P], writes=[cr])
        S.dma("sp", cd[:P], dr["rope_d"][r0:r0 + P], writes=[cd])
        rope_apply(C, x, o, P, 0, 4, 128, 64, cr, tmp)
        rope_apply(C, x, o, P, 512, 4, 128, 64, cr, tmp, scale=128 ** -0.5)
        S.dma("pool", dr["QK"][r0:r0 + P, :], o[:P, 0:1024], reads=[o])
        rope_apply(C, x, o, P, 1024, 8, 64, 8, cd, tmp, sc0=2048)
        rope_apply(C, x, o, P, 1536, 8, 64, 8, cd, tmp, sc0=2560)
        S.dma("pool", dr["PROJ"][r0:r0 + P, 2048:2560], o[:P, 1024:1536], reads=[o])
        S.dma("pool", dr["k_new"][l, r0:r0 + P, :], o[:P, 1536:2048], reads=[o])


def phase_ret(C, st, l):
    S, dr, cfg = C.S, C.dr, C.cfg
    T, NS = cfg.T, cfg.NS
    gw = bcast_load(C, st, "rgw", dr["ret_norm_w"][l], 512)
    gb = bcast_load(C, st, "rgb", dr["ret_norm_b"][l], 512)
    dmT = C.sb("rdmT", [128, 4, 128], st=st)
    S.dma("sp", dmT[:], dr["ret_dmT"].rearrange("h m l -> m h l"), writes=[dmT])
    dmT8 = C.sb("rdmT8", [8, 4, 8], st=st)
    S.dma("sp", dmT8[:], dr["ret_dmT8"].rearrange("h m l -> m h l"), writes=[dmT8])
    qd = C.sb("rqd", [128, 4, 128], st=st); S.dma("sp", qd[:], dr["ret_qd"], writes=[qd])
    kd = C.sb("rkd", [128, 4], st=st); S.dma("sp", kd[:], dr["ret_kd"], writes=[kd])
    qd8 = C.sb("rqd8", [128, 4, 8], st=st); S.dma("sp", qd8[:], dr["ret_qd8"], writes=[qd8])
    kd8 = C.sb("rkd8", [8, 4], st=st); S.dma("sp", kd8[:], dr["ret_kd8"], writes=[kd8])
    Sst = C.sb("rS", [128, 4, 128], st=st)
    qk = [C.sb(f"rqk{i}", [128, 1024], st=st) for i in range(2)]
    vg = [C.sb(f"rvg{i}", [128, 1024], st=st) for i in range(2)]
    qT = C.sb("rqT", [128, 4, 128], st=st)
    kT = C.sb("rkT", [128, 4, 128], st=st)
    qdT = C.sb("rqdT", [128, 4, 128], st=st)
    kdt = C.sb("rkdt", [128, 512], st=st)
    am = C.sb("ram", [128, 128], st=st)
    oh = C.sb("roh", [128, 512], st=st)
    sg = C.sb("rsg", [128, 512], st=st)
    stt = C.sb("rstt", [128, 16], st=st)
    ptp = [C.ps(f"rpt{i}", [128, 128], st=st) for i in range(2)]
    pa = C.ps("rpa", [128, 128], st=st)
    po = C.ps("rpo", [128, 128], st=st)
    pS = C.ps("rpS", [128, 128], st=st)

    def chunk(i, r0, L, dm, qdd, kdd, cdec):
        q, v = qk[i % 2], vg[i % 2]
        S.dma("sp", q[:L, :], dr["QK"][r0:r0 + L, :], writes=[q])
        S.dma("sp", v[:L, :], dr["PROJ"][r0:r0 + L, 1024:2048], writes=[v])
        for h in range(4):
            transpose_to(C, q, L, h * 128, 128, lambda: qT[:, h, :L], qT, ptp, 2 * h)
            transpose_to(C, q, L, 512 + h * 128, 128, lambda: kT[:, h, :L], kT, ptp, 2 * h + 1)
        S.op("pool", lambda e: e.tensor_tensor(out=qdT[:, :, :L], in0=qT[:, :, :L], in1=qdd[:, :, :L], op=ALU.mult), reads=[qT, qdd], writes=[qdT])
        S.op("pool", lambda e: e.tensor_tensor(out=v3(kdt[:L, :], 4), in0=v3(q[:L, 512:1024], 4),
                                               in1=kdd[:L, :].unsqueeze(2).to_broadcast([L, 4, 128]), op=ALU.mult), reads=[q, kdd], writes=[kdt])
        for h in range(4):
            S.op("pe", lambda e: e.matmul(pa[:L, :L], lhsT=kT[:, h, :L], rhs=qT[:, h, :L], start=True, stop=True), reads=[kT, qT], writes=[pa])
            S.op("dve", lambda e: e.tensor_tensor(out=am[:L, :L], in0=pa[:L, :L], in1=dm[:L, h, :L], op=ALU.mult), reads=[pa, dm], writes=[am])
            S.op("pe", lambda e: e.matmul(po[:L, :], lhsT=am[:L, :L], rhs=v[:L, h * 128:(h + 1) * 128], start=True, stop=False), reads=[am, v], writes=[po])
            S.op("pe", lambda e: e.matmul(po[:L, :], lhsT=qdT[:, h, :L], rhs=Sst[:, h, :], start=False, stop=True), reads=[qdT, Sst], writes=[po])
            S.op("act", lambda e: e.copy(out=oh[:L, h * 128:(h + 1) * 128], in_=po[:L, :]), reads=[po], writes=[oh])
            S.op("pe", lambda e: e.matmul(pS[:, :], lhsT=kdt[:L, h * 128:(h + 1) * 128], rhs=v[:L, h * 128:(h + 1) * 128], start=True, stop=True), reads=[kdt, v], writes=[pS])
            S.op("dve", lambda e: e.scalar_tensor_tensor(out=Sst[:, h, :], in0=Sst[:, h, :], scalar=float(cdec[h]), in1=pS[:, :],
                                                         op0=ALU.mult, op1=ALU.add), reads=[Sst, pS], writes=[Sst])
        o3 = v3(oh[:L, :], 4)
        S.op("dve", lambda e: e.tensor_reduce(out=stt[:L, 0:4], in_=o3, axis=AX.X, op=ALU.add), reads=[oh], writes=[stt])
        S.op("dve", lambda e: e.tensor_scalar(out=stt[:L, 0:4], in0=stt[:L, 0:4], scalar1=1.0 / 128, scalar2=None, op0=ALU.mult), reads=[stt], writes=[stt])
        S.op("dve", lambda e: e.tensor_tensor(out=o3, in0=o3, in1=stt[:L, 0:4].unsqueeze(2).to_broadcast([L, 4, 128]), op=ALU.subtract), reads=[oh, stt], writes=[oh])
        S.op("pool", lambda e: e.tensor_tensor(out=sg[:L, :], in0=oh[:L, :], in1=oh[:L, :], op=ALU.mult), reads=[oh], writes=[sg])
        S.op("dve", lambda e: e.tensor_reduce(out=stt[:L, 4:8], in_=v3(sg[:L, :], 4), axis=AX.X, op=ALU.add), reads=[sg], writes=[stt])
        S.op("dve", lambda e: e.tensor_scalar(out=stt[:L, 4:8], in0=stt[:L, 4:8], scalar1=1.0 / 128, scalar2=1e-6, op0=ALU.mult, op1=ALU.add), reads=[stt], writes=[stt])
        S.op("act", lambda e: e.sqrt(out=stt[:L, 4:8], in_=stt[:L, 4:8]), reads=[stt], writes=[stt])
        S.op("dve", lambda e: e.reciprocal(out=stt[:L, 4:8], in_=stt[:L, 4:8]), reads=[stt], writes=[stt])
        S.op("dve", lambda e: e.tensor_tensor(out=o3, in0=o3, in1=stt[:L, 4:8].unsqueeze(2).to_broadcast([L, 4, 128]), op=ALU.mult), reads=[oh, stt], writes=[oh])
        S.op("dve", lambda e: e.tensor_tensor(out=oh[:L, :], in0=oh[:L, :], in1=gw[:L, :], op=ALU.mult), reads=[oh, gw], writes=[oh])
        S.op("dve", lambda e: e.tensor_tensor(out=oh[:L, :], in0=oh[:L, :], in1=gb[:L, :], op=ALU.add), reads=[oh, gb], writes=[oh])
        S.op("act", lambda e: e.activation(out=sg[:L, :], in_=v[:L, 512:1024], func=AF.Silu), reads=[v], writes=[sg])
        S.op("dve", lambda e: e.tensor_tensor(out=oh[:L, :], in0=oh[:L, :], in1=sg[:L, :], op=ALU.mult), reads=[oh, sg], writes=[oh])
        S.dma("pool", dr["OCAT"][r0:r0 + L, 0:512], oh[:L, :], reads=[oh])

    gam = [1.0 - 2.0 ** (-5.0 - h) for h in range(4)]
    S.op("dve", lambda e: e.memset(Sst[:], 0.0), writes=[Sst])
    for i in range(T // 128):
        chunk(i, i * 128, 128, dmT, qd, kd, [g ** 128 for g in gam])
    S.dma("pool", dr["o_ret"][l, 0].rearrange("h d e -> d h e"), Sst[:], reads=[Sst])
    for b in range(NS):
        S.dma("sp", Sst[:], dr["st_ret"][l, b].rearrange("h d e -> d h e"), writes=[Sst])
        chunk(b, T + 8 * b, 8, dmT8, qd8, kd8, [g ** 8 for g in gam])
        S.dma("pool", dr["o_ret"][l, 1 + b].rearrange("h d e -> d h e"), Sst[:], reads=[Sst])


def _stub(C, st, l):
    pass


def _tables(cfg, past_len):
    T, NS, NT = cfg.T, cfg.NS, cfg.NT
    f = np.float32
    pos = np.concatenate([np.arange(T), np.tile(past_len + np.arange(8), NS)]).astype(f)
    inv_r = np.power(f(10000.0), -np.arange(64, dtype=f) / f(64)).astype(f)
    ang = pos[:, None] * inv_r[None, :]
    rope_r = np.stack([np.cos(ang), np.sin(ang)], 1).astype(f)
    inv_d = np.power(f(500000.0), -np.arange(8, dtype=f) / f(8)).astype(f)
    angd = pos[:, None] * inv_d[None, :]
    rope_d = np.stack([np.cos(angd), np.sin(angd)], 1).astype(f)
    log_g = np.log1p(-np.exp2(-5.0 - np.arange(4))).astype(np.float64)

    def dm(L):
        idx = np.arange(L)
        rel = idx[:, None] - idx[None, :]
        d = np.where(rel >= 0, np.exp(log_g[:, None, None] * np.maximum(rel, 0)), 0.0)
        return np.ascontiguousarray(d.transpose(0, 2, 1)).astype(f)

    def qd(L):
        v = np.exp(log_g[:, None] * (np.arange(L) + 1.0))
        return np.ascontiguousarray(np.broadcast_to(v[None], (128, 4, L))).astype(f)

    def kd(L):
        v = np.exp(log_g[:, None] * (L - 1.0 - np.arange(L)))
        return np.ascontiguousarray(v.T).astype(f)

    def cm(L):
        idx = np.arange(L)
        return np.where(idx[None, :] <= idx[:, None], 0.0, -1e30).astype(f)
    mask8 = np.zeros((8, 512), f)
    for h in range(8):
        mask8[h, h * 64:(h + 1) * 64] = 1.0
    return dict(ident=np.eye(128, dtype=f), rope_r=rope_r, rope_d=rope_d, ret_dmT=dm(128), ret_dmT8=dm(8),
                ret_qd=qd(128), ret_kd=kd(128), ret_qd8=qd(8), ret_kd8=kd(8), cmask=cm(128), cmask8=cm(8),
                mask8=mask8, tidx=np.ascontiguousarray(np.broadcast_to(np.arange(130, dtype=f)[None], (128, 130))),
                iota=np.arange(128, dtype=np.int32).reshape(128, 1))


def _gn(a):
    return np.ascontiguousarray(a.reshape(2, 16, 2, 64).transpose(0, 2, 3, 1).reshape(2, 128, 16))


def kernel(cfg=None, **inp):
    f = np.float32
    if cfg is None:
        cfg = Cfg()
    T, NS, NPG = cfg.T, cfg.NS, cfg.NPG
    past_len = NPG * 128
    A = {k: np.asarray(v) for k, v in inp.items()}
    shared = {}
    for k in ["norm_mix", "w_in", "w_out", "diff_subln", "rwkv_mu", "rwkv_w0", "rwkv_w2", "rwkv_a0", "rwkv_a2", "rwkv_g2",
              "rwkv_kk", "rwkv_ka", "s5_d", "s5_w_glu", "s5_b_glu", "s5_norm", "norm_ffn", "ffn_w_up", "ffn_w_down",
              "norm_ple", "ple_w_proj", "ple_norm_e", "ple_w_gate", "norm_final"]:
        shared[k] = np.ascontiguousarray(A[k], dtype=f)
    for k in ["ret_norm_w", "ret_norm_b", "rwkv_rk", "rwkv_ln_w", "rwkv_ln_b"]:
        shared[k] = np.ascontiguousarray(A[k].reshape(2, 512), dtype=f)
    shared["diff_l"] = np.ascontiguousarray(np.stack([A["diff_lq1"], A["diff_lk1"], A["diff_lq2"], A["diff_lk2"]], 1), dtype=f)
    shared["ck"] = np.ascontiguousarray(A["cache_k"].reshape(2, -1, 512), dtype=f)
    shared["cv"] = np.ascontiguousarray(A["cache_v"].reshape(2, -1, 512), dtype=f)
    shared["s5_lre"] = _gn(A["s5_lam_re"]); shared["s5_lim"] = _gn(A["s5_lam_im"])
    shared["s5_ls"] = _gn(np.broadcast_to(A["s5_log_step"][:, :, None], (2, 32, 64)))
    for nm, src in (("s5_bre", "s5_b_re"), ("s5_bim", "s5_b_im")):
        b = A[src].reshape(2, 16, 2, 64, 16)
        e = np.zeros((2, 128, 16, 128), f)
        for i in range(16):
            for gl in range(2):
                c0 = (i % 4) * 32 + gl * 16
                e[:, gl * 64:(gl + 1) * 64, i, c0:c0 + 16] = b[:, i, gl]
        shared[nm] = e
    for nm, src in (("s5_cre", "s5_c_re"), ("s5_cim", "s5_c_im")):
        c = A[src].reshape(2, 16, 2, 16, 64)
        e = np.zeros((2, 128, 16, 32), f)
        for i in range(16):
            for gl in range(2):
                e[:, gl * 64:(gl + 1) * 64, i, gl * 16:(gl + 1) * 16] = c[:, i, gl].transpose(0, 2, 1)
        shared[nm] = e
    shared["ffn_cw"] = np.ascontiguousarray(A["ffn_conv_w"].reshape(2, 3, 88, 128).transpose(0, 3, 2, 1), dtype=f)
    shared["ffn_cb"] = np.ascontiguousarray(A["ffn_conv_b"].reshape(2, 88, 128).transpose(0, 2, 1), dtype=f)
    shared.update(_tables(cfg, past_len))
    in_maps = []
    for c in range(NCORES):
        sl = slice(c * NS, (c + 1) * NS)
        m = dict(shared)
        m["x0"] = np.ascontiguousarray(np.concatenate([A["x_prompt"][0], A["x_sample"][sl].reshape(NS * 8, D)], 0), dtype=f)
        m["p0"] = np.ascontiguousarray(np.concatenate([A["p_prompt"][:, 0], A["p_sample"][:, sl].reshape(2, NS * 8, 256)], 1), dtype=f)
        m["pt"] = np.ascontiguousarray(A["page_table"][sl].reshape(-1), dtype=np.int32)
        m["st_ret"] = np.ascontiguousarray(A["state_ret"][:, sl], dtype=f)
        m["st_rwkvT"] = np.ascontiguousarray(A["state_rwkv"][:, sl].transpose(0, 1, 4, 2, 3), dtype=f)
        m["st_shift"] = np.ascontiguousarray(A["state_rwkv_shift"][:, sl], dtype=f)
        for nm, src in (("st_s5re", "state_s5_re"), ("st_s5im", "state_s5_im")):
            s = A[src][:, sl].reshape(2, NS, 16, 2, 64)
            m[nm] = np.ascontiguousarray(s.transpose(0, 3, 4, 1, 2).reshape(2, 128, NS, 16), dtype=f)
        cs = A["state_ffn_conv"][:, sl].reshape(2, NS, 2, 88, 128)
        m["st_convT"] = np.ascontiguousarray(cs.transpose(0, 4, 3, 1, 2), dtype=f)
        in_maps.append(m)
    nc, S = build(cfg)
    res = run_bass_kernel_spmd(nc, in_maps, core_ids=list(range(NCORES)))
    R = res.results
    kernel.last_res = res

    def cat_seq(name, fn):
        return fn(R[0][name], True), np.concatenate([fn(R[c][name], False) for c in range(NCORES)], axis=1)
    y_p = R[0]["y"][:T][None]
    y_s = np.concatenate([R[c]["y"][T:].reshape(NS, 8, D) for c in range(NCORES)], 0)
    k_p = R[0]["k_new"][:, :T].reshape(2, 1, T, 4, 128)
    v_p = R[0]["v_new"][:, :T].reshape(2, 1, T, 4, 128)
    k_s = np.concatenate([R[c]["k_new"][:, T:].reshape(2, NS, 8, 4, 128) for c in range(NCORES)], 1)
    v_s = np.concatenate([R[c]["v_new"][:, T:].reshape(2, NS, 8, 4, 128) for c in range(NCORES)], 1)
    ret_p = R[0]["o_ret"][:, 0:1]
    ret_s = np.concatenate([R[c]["o_ret"][:, 1:] for c in range(NCORES)], 1)
    rw = lambda a: a.transpose(0, 1, 3, 4, 2)
    rw_p = rw(R[0]["o_rwkvT"][:, 0:1])
    rw_s = np.concatenate([rw(R[c]["o_rwkvT"][:, 1:]) for c in range(NCORES)], 1)
    sh_p = R[0]["o_shift"][:, 0:1]
    sh_s = np.concatenate([R[c]["o_shift"][:, 1:] for c in range(NCORES)], 1)

    def s5(a):
        l_, _, b_, _ = a.shape
        return a.reshape(l_, 2, 64, b_, 16).transpose(0, 3, 4, 1, 2).reshape(l_, b_, 32, 64)
    s5r_p = s5(R[0]["o_s5re"][:, :, 0:1]); s5i_p = s5(R[0]["o_s5im"][:, :, 0:1])
    s5r_s = np.concatenate([s5(R[c]["o_s5re"][:, :, 1:]) for c in range(NCORES)], 1)
    s5i_s = np.concatenate([s5(R[c]["o_s5im"][:, :, 1:]) for c in range(NCORES)], 1)

    def cv(a):
        l_, _, _, b_, _ = a.shape
        return a.transpose(0, 3, 4, 2, 1).reshape(l_, b_, 2, 2 * DFF)
    cv_p = cv(R[0]["o_convT"][:, :, :, 0:1])
    cv_s = np.concatenate([cv(R[c]["o_convT"][:, :, :, 1:]) for c in range(NCORES)], 1)
    outs = (y_p, y_s, k_p, v_p, k_s, v_s, ret_p, ret_s, rw_p, rw_s, sh_p, sh_s, s5r_p, s5i_p, s5r_s, s5i_s, cv_p, cv_s)
    return tuple(np.ascontiguousarray(o, dtype=f) for o in outs)


def load_T(C, stl, src_ap_fn, ncols, xs, hT, ptp, norm=None):
    S = C.S
    sizes = []
    off = 0
    for n, (r0, P) in enumerate(stl):
        x = xs[n % len(xs)]
        S.dma("sp", x[:P, :ncols], src_ap_fn(r0, P), writes=[x])
        src = x
        if norm is not None:
            g, hs, ss, sq = norm
            rms_rows(C, x, P, ncols, g, hs, ss, sq)
            src = hs
        for k in range(ncols // 128):
            transpose_to(C, src, P, k * 128, 128, lambda: hT[:, k, off:off + P], hT, ptp, k, r=True)
        sizes.append((r0, P, off))
        off += P
    return sizes


def phase_wout(C, st, l):
    S, dr, cfg = C.S, C.dr, C.cfg
    xs = [C.sb(f"wx{i}", [128, D], st=st) for i in range(2)]
    hT = C.sb("whT", [128, 16, 512], st=st)
    wb = [C.sb(f"ww{i}", [128, 16, 512], st=st) for i in range(2)]
    wst = C.sb("wwst", [128, 16, 512], st=st)
    xres = [C.sb(f"wxr{i}", [128, D], st=st) for i in range(4)]
    ptp = [C.ps(f"wpt{i}", [128, 128], st=st) for i in range(2)]
    pb = [C.ps(f"wpo{i}", [128, 512], st=st) for i in range(4)]
    xsrc = dr["x0"] if l == 0 else dr["X"]
    for stl in cfg.sts:
        sizes = load_T(C, stl, lambda r0, P: dr["OCAT"][r0:r0 + P, :], D, xs, hT, ptp)
        for ti, (r0, P, off) in enumerate(sizes):
            S.dma("sp", xres[ti][:P, :], xsrc[r0:r0 + P, :], writes=[xres[ti]])

        def epi(ti, r0, P, c0, ncol, po):
            xr = xres[ti]
            S.op("dve", lambda e: e.tensor_tensor(out=xr[:P, c0:c0 + ncol], in0=xr[:P, c0:c0 + ncol], in1=po[:P, :ncol], op=ALU.add),
                 reads=[xr, po], writes=[xr])
        dense(C, hT, sizes, dr["w_out"][l], 16, D, wb, pb, epi, wst=wst)
        for ti, (r0, P, off) in enumerate(sizes):
            S.dma("pool", dr["X"][r0:r0 + P, :], xres[ti][:P, :], reads=[xres[ti]])


def phase_ffn(C, st, l):
    S, dr, cfg = C.S, C.dr, C.cfg
    T, NS, NTS = cfg.T, cfg.NS, cfg.NTS
    g = bcast_load(C, st, "fg", dr["norm_ffn"][l], D)
    xs = [C.sb("fx0", [128, D], st=st)]
    ss = C.sb("fss", [128, 2], st=st)
    hT = C.sb("fhT", [128, 16, 512], st=st)
    actT = C.sb("factT", [128, 44, 512], st=st)
    wu = [C.sb(f"fwu{i}", [128, 16, 128], st=st) for i in range(2)]
    wd = [C.sb(f"fwd{i}", [128, 4, 512], st=st) for i in range(2)]
    halo = C.sb("fhalo", [128, 88, 2], st=st)
    cw = C.sb("fcw", [128, 88, 3], st=st); S.dma("sp", cw[:], dr["ffn_cw"][l], writes=[cw])
    cbb = C.sb("fcb", [128, 88], st=st); S.dma("sp", cbb[:], dr["ffn_cb"][l], writes=[cbb])
    ext = [C.sb(f"fext{i}", [128, 516], st=st) for i in range(2)]
    cv_ = [C.sb(f"fcv{i}", [128, 512], st=st) for i in range(2)]
    sgl = C.sb("fsgl", [128, 512], st=st)
    xr = [C.sb(f"fxr{i}", [128, 512], st=st) for i in range(2)]
    wus = C.sb("fwus", [128, 16, 128], st=st)
    wds = C.sb("fwds", [128, 4, 512], st=st)
    ptp = [C.ps(f"fpt{i}", [128, 128], st=st) for i in range(2)]
    pu = [C.ps(f"fpu{i}", [128, 512], st=st) for i in range(2)]
    pd = [C.ps(f"fpd{i}", [128, 512], st=st) for i in range(4)]
    S.op("dve", lambda e: e.memset(halo[:], 0.0), writes=[halo])
    wup = dr["ffn_w_up"][l].rearrange("(k p) c -> p k c", p=128)
    wdn = dr["ffn_w_down"][l].rearrange("(j p) c -> p j c", p=128)
    cnt = 0
    xcnt = 0
    for si, stl in enumerate(cfg.sts):
        sample = (stl[0][0] == T)
        sizes = load_T(C, stl, lambda r0, P: dr["X"][r0:r0 + P, :], D, xs, hT, ptp, norm=(g, xs[0], ss, (actT, actT.t[:, 0:4, :].rearrange("p a b -> p (a b)"))))
        n = sum(P for _, P, _ in sizes)
        for j in range(44):
            for half in range(2):
                ct = j + 44 * half
                w = wu[cnt % 2]
                p_ = pu[cnt % 2]
                ex = ext[cnt % 2]
                cvt = cv_[half]
                cnt += 1
                S.dma("sp", wus[:], wup[:, :, ct * 128:(ct + 1) * 128], writes=[wus])
                S.op("pool", lambda e: e.tensor_copy(out=R_(w[:]), in_=wus[:]), reads=[wus], writes=[w])
                for k in range(16):
                    S.op("pe", lambda e: e.matmul(p_[:, :n], lhsT=R_(w[:, k, :]), rhs=R_(hT[:, k, :n]), start=(k == 0), stop=(k == 15)),
                         reads=[w, hT], writes=[p_])
                if not sample:
                    e0 = lambda a, b: ex[:, a:b]
                    S.op("pool", lambda e: e.tensor_copy(out=ex[:, 0:2], in_=halo[:, ct, :]), reads=[halo], writes=[ex])
                    S.op("act", lambda e: e.copy(out=ex[:, 2:2 + n], in_=p_[:, :n]), reads=[p_], writes=[ex])
                    S.op("pool", lambda e: e.tensor_copy(out=halo[:, ct, :], in_=ex[:, n:n + 2]), reads=[ex], writes=[halo])
                    sl = [ex[:, 0:n], ex[:, 1:1 + n], ex[:, 2:2 + n]]
                    cvo = cvt[:, :n]
                else:
                    e3 = ex[:, 0:NS * 10].rearrange("p (b t) -> p b t", t=10)
                    S.dma("sp", e3[:, :, 0:2], dr["st_convT"][l, :, ct, :, :], writes=[ex])
                    S.op("act", lambda e: e.copy(out=e3[:, :, 2:10], in_=p_[:, :n].rearrange("p (b t) -> p b t", t=8)), reads=[p_], writes=[ex])
                    S.dma("pool", dr["o_convT"][l, :, ct, 1:, :], e3[:, :, 8:10], reads=[ex])
                    sl = [e3[:, :, 0:8], e3[:, :, 1:9], e3[:, :, 2:10]]
                    cvo = cvt[:, :n].rearrange("p (b t) -> p b t", t=8)
                S.op("act", lambda e: e.activation(out=cvo, in_=sl[2], func=AF.Identity, bias=cbb[:, ct:ct + 1], scale=cw[:, ct, 2:3]),
                     reads=[ex, cbb, cw], writes=[cvt])
                S.op("dve", lambda e: e.scalar_tensor_tensor(out=cvo, in0=sl[1], scalar=cw[:, ct, 1:2], in1=cvo, op0=ALU.mult, op1=ALU.add),
                     reads=[ex, cw, cvt], writes=[cvt])
                S.op("dve", lambda e: e.scalar_tensor_tensor(out=cvo, in0=sl[0], scalar=cw[:, ct, 0:1], in1=cvo, op0=ALU.mult, op1=ALU.add),
                     reads=[ex, cw, cvt], writes=[cvt])
            S.op("act", lambda e: e.activation(out=sgl[:, :n], in_=cv_[0][:, :n], func=AF.Silu), reads=[cv_[0]], writes=[sgl])
            S.op("dve", lambda e: e.tensor_tensor(out=R_(actT[:, j, :n]), in0=sgl[:, :n], in1=cv_[1][:, :n], op=ALU.mult),
                 reads=[sgl, cv_[1]], writes=[actT])
        if stl[-1][0] + stl[-1][1] == T:
            S.dma("pool", dr["o_convT"][l, :, :, 0, :], halo[:], reads=[halo])
        for c0 in range(0, D, 512):
            for jq in range(11):
                w = wd[xcnt % 2]
                xcnt += 1
                S.dma("sp", wds[:], wdn[:, jq * 4:(jq + 1) * 4, c0:c0 + 512], writes=[wds])
                S.op("pool", lambda e: e.tensor_copy(out=R_(w[:]), in_=wds[:]), reads=[wds], writes=[w])
                for ti, (r0, P, off) in enumerate(sizes):
                    for jj in range(4):
                        j = jq * 4 + jj
                        S.op("pe", lambda e: e.matmul(pd[ti][:P, :], lhsT=R_(actT[:, j, off:off + P]), rhs=R_(w[:, jj, :]),
                                                      start=(j == 0), stop=(j == 43)), reads=[actT, w], writes=[pd[ti]])
            for ti, (r0, P, off) in enumerate(sizes):
                x = xr[ti % 2]
                S.dma("sp", x[:P, :], dr["X"][r0:r0 + P, c0:c0 + 512], writes=[x])
                S.op("dve", lambda e: e.tensor_tensor(out=x[:P, :], in0=x[:P, :], in1=pd[ti][:P, :], op=ALU.add), reads=[x, pd[ti]], writes=[x])
                S.dma("pool", dr["X"][r0:r0 + P, c0:c0 + 512], x[:P, :], reads=[x])


def phase_ple(C, st0, l):
    from contextlib import ExitStack
    S, dr, cfg = C.S, C.dr, C.cfg
    with ExitStack() as st:
        xs = [C.sb(f"ep{i}", [128, 256], st=st) for i in range(2)]
        pT = C.sb("epT", [128, 2, 512], st=st)
        wb = [C.sb(f"ew{i}", [128, 2, 512], st=st) for i in range(2)]
        wst = C.sb("ewst", [128, 2, 512], st=st)
        ob = [C.sb(f"eob{i}", [128, 512], st=st) for i in range(4)]
        ptp = [C.ps(f"ept{i}", [128, 128], st=st) for i in range(2)]
        pb = [C.ps(f"epo{i}", [128, 512], st=st) for i in range(4)]
        cnt = [0]
        for stl in cfg.sts:
            sizes = load_T(C, stl, lambda r0, P: dr["p0"][l, r0:r0 + P, :], 256, xs, pT, ptp)

            def epi(ti, r0, P, c0, ncol, po):
                o = ob[cnt[0] % 4]
                cnt[0] += 1
                S.op("act", lambda e: e.copy(out=o[:P, :ncol], in_=po[:P, :ncol]), reads=[po], writes=[o])
                S.dma("pool", dr["ERAW"][r0:r0 + P, c0:c0 + ncol], o[:P, :ncol], reads=[o])
            dense(C, pT, sizes, dr["ple_w_proj"][l], 2, D, wb, pb, epi, wst=wst)
    S.barrier()
    with ExitStack() as st:
        g = bcast_load(C, st, "gg", dr["norm_ple"][l], D)
        ge = bcast_load(C, st, "gge", dr["ple_norm_e"][l], D)
        gf = bcast_load(C, st, "ggf", dr["norm_final"], D)
        xres = [C.sb(f"gx{i}", [128, D], st=st) for i in range(2)]
        et = [C.sb(f"ge{i}", [128, D], st=st) for i in range(2)]
        hs = C.sb("gh", [128, D], st=st)
        sq = C.sb("gsq", [128, D], st=st)
        ss = C.sb("gss", [128, 2], st=st)
        hT = C.sb("ghT", [128, 16, 256], st=st)
        wb = [C.sb(f"gw{i}", [128, 16, 512], st=st) for i in range(2)]
        wst = C.sb("gwst", [128, 16, 512], st=st)
        sg = [C.sb(f"gsg{i}", [128, 512], st=st) for i in range(2)]
        ptp = [C.ps(f"gpt{i}", [128, 128], st=st) for i in range(2)]
        pb = [C.ps(f"gpo{i}", [128, 512], st=st) for i in range(4)]
        cnt = [0]
        for stl in cfg.sts2:
            sizes = []
            off = 0
            for ti, (r0, P) in enumerate(stl):
                x = xres[ti]
                S.dma("sp", x[:P, :], dr["X"][r0:r0 + P, :], writes=[x])
                rms_rows(C, x, P, D, g, hs, ss, sq)
                for k in range(16):
                    transpose_to(C, hs, P, k * 128, 128, lambda: hT[:, k, off:off + P], hT, ptp, k, r=True)
                S.dma("sp", hs[:P, :], dr["ERAW"][r0:r0 + P, :], writes=[hs])
                rms_rows(C, hs, P, D, ge, et[ti], ss, sq)
                sizes.append((r0, P, off))
                off += P

            def epi(ti, r0, P, c0, ncol, po):
                s_ = sg[cnt[0] % 2]
                cnt[0] += 1
                S.op("act", lambda e: e.activation(out=s_[:P, :ncol], in_=po[:P, :ncol], func=AF.Sigmoid), reads=[po], writes=[s_])
                S.op("dve", lambda e: e.tensor_tensor(out=s_[:P, :ncol], in0=s_[:P, :ncol], in1=et[ti][:P, c0:c0 + ncol], op=ALU.mult),
                     reads=[s_, et[ti]], writes=[s_])
                S.op("pool", lambda e: e.tensor_tensor(out=xres[ti][:P, c0:c0 + ncol], in0=xres[ti][:P, c0:c0 + ncol], in1=s_[:P, :ncol], op=ALU.add),
                     reads=[s_, xres[ti]], writes=[xres[ti]])
            dense(C, hT, sizes, dr["ple_w_gate"][l], 16, D, wb, pb, epi, wst=wst)
            for ti, (r0, P, off) in enumerate(sizes):
                S.dma("pool", dr["X"][r0:r0 + P, :], xres[ti][:P, :], reads=[xres[ti]])
                if l == 1:
                    rms_rows(C, xres[ti], P, D, gf, hs, ss, sq)
                    S.dma("pool", dr["y"][r0:r0 + P, :], hs[:P, :], reads=[hs])


def phase_s5(C, st0, l):
    from contextlib import ExitStack
    S, dr, cfg = C.S, C.dr, C.cfg
    T, NS = cfg.T, cfg.NS
    PI = math.pi
    with ExitStack() as st:
        def ld(name, ap, shape):
            t = C.sb(name, shape, st=st)
            S.dma("sp", t[:], ap, writes=[t])
            return t
        lre = ld("slre", dr["s5_lre"][l], [128, 16]); lim = ld("slim", dr["s5_lim"][l], [128, 16]); ls = ld("sls", dr["s5_ls"][l], [128, 16])
        BRe = ld("sBRe", dr["s5_bre"][l], [128, 16, 128]); BIe = ld("sBIe", dr["s5_bim"][l], [128, 16, 128])
        CRe = ld("sCRe", dr["s5_cre"][l], [128, 16, 32]); CIm = ld("sCIm", dr["s5_cim"][l], [128, 16, 32])
        tidx = ld("stidx", dr["tidx"], [128, 130])
        sm = C.sb("ssm", [128, 16, 16], st=st)
        K = lambda k: sm[:, k, :]
        def tt(eng, out, a, b, op, rd, wr):
            S.op(eng, lambda e: e.tensor_tensor(out=out, in0=a, in1=b, op=op), reads=rd, writes=wr)
        def ts(eng, out, a, s1, s2, op0, op1, rd, wr):
            if op1 is None:
                S.op(eng, lambda e: e.tensor_scalar(out=out, in0=a, scalar1=s1, scalar2=None, op0=op0), reads=rd, writes=wr)
            else:
                S.op(eng, lambda e: e.tensor_scalar(out=out, in0=a, scalar1=s1, scalar2=s2, op0=op0, op1=op1), reads=rd, writes=wr)
        def act(out, a, fn, rd, wr, **kw):
            S.op("act", lambda e: e.activation(out=out, in_=a, func=fn, **kw), reads=rd, writes=wr)
        mpi = C.eps[:, 1:2]
        rti = C.sb("srti", [128, 129], I32, st=st); rtf = C.sb("srtf", [128, 129], st=st); rtx = C.sb("srtx", [128, 129], st=st)
        def sinr(out, x, n, shift, rd, wr):
            S.op("dve", lambda e: e.tensor_scalar(out=rtx[:, :n], in0=x, scalar1=shift, scalar2=None, op0=ALU.add), reads=rd, writes=[rtx])
            S.op("dve", lambda e: e.tensor_scalar(out=rti[:, :n], in0=rtx[:, :n], scalar1=1.0 / (2 * PI), scalar2=None, op0=ALU.mult), reads=[rtx], writes=[rti])
            S.op("dve", lambda e: e.tensor_copy(out=rtf[:, :n], in_=rti[:, :n]), reads=[rti], writes=[rtf])
            S.op("dve", lambda e: e.scalar_tensor_tensor(out=rtx[:, :n], in0=rtf[:, :n], scalar=-2 * PI, in1=rtx[:, :n], op0=ALU.mult, op1=ALU.add), reads=[rtf, rtx], writes=[rtx])
            S.op("dve", lambda e: e.tensor_scalar(out=rtf[:, :n], in0=rtx[:, :n], scalar1=PI, scalar2=2 * PI, op0=ALU.is_gt, op1=ALU.mult), reads=[rtx], writes=[rtf])
            S.op("dve", lambda e: e.tensor_tensor(out=rtx[:, :n], in0=rtx[:, :n], in1=rtf[:, :n], op=ALU.subtract), reads=[rtx, rtf], writes=[rtx])
            S.op("act", lambda e: e.activation(out=out, in_=rtx[:, :n], func=AF.Sin), reads=[rtx], writes=wr)
        act(K(0), ls[:], AF.Exp, [ls], [sm])
        tt("dve", K(1), lre[:], K(0), ALU.mult, [lre, sm], [sm])
        tt("dve", K(2), lim[:], K(0), ALU.mult, [lim, sm], [sm])
        ts("dve", K(3), K(1), -1.0, None, ALU.mult, None, [sm], [sm])
        act(K(9), K(1), AF.Exp, [sm], [sm])
        sinr(K(10), K(2), 16, 0.0, [sm], [sm])
        sinr(K(11), K(2), 16, 0.5 * PI, [sm], [sm])
        tt("dve", K(4), K(9), K(11), ALU.mult, [sm], [sm])
        tt("dve", K(5), K(9), K(10), ALU.mult, [sm], [sm])
        tt("dve", K(9), lre[:], lre[:], ALU.mult, [lre], [sm])
        tt("dve", K(10), lim[:], lim[:], ALU.mult, [lim], [sm])
        tt("dve", K(9), K(9), K(10), ALU.add, [sm], [sm])
        S.op("dve", lambda e: e.reciprocal(out=K(9), in_=K(9)), reads=[sm], writes=[sm])
        ts("dve", K(10), K(4), -1.0, None, ALU.add, None, [sm], [sm])
        tt("dve", K(11), K(10), lre[:], ALU.mult, [sm, lre], [sm])
        tt("dve", K(12), K(5), lim[:], ALU.mult, [sm, lim], [sm])
        tt("dve", K(11), K(11), K(12), ALU.add, [sm], [sm])
        tt("dve", K(6), K(11), K(9), ALU.mult, [sm], [sm])
        tt("dve", K(11), K(5), lre[:], ALU.mult, [sm, lre], [sm])
        tt("dve", K(12), K(10), lim[:], ALU.mult, [sm, lim], [sm])
        tt("dve", K(11), K(11), K(12), ALU.subtract, [sm], [sm])
        tt("dve", K(7), K(11), K(9), ALU.mult, [sm], [sm])
        ts("dve", K(8), K(7), -1.0, None, ALU.mult, None, [sm], [sm])
        S.op("dve", lambda e: e.tensor_scalar(out=CIm[:], in0=CIm[:], scalar1=-1.0, scalar2=None, op0=ALU.mult), reads=[CIm], writes=[CIm])
        BBTr = C.sb("sBBTr", [128, 16, 128], st=st); BBTi = C.sb("sBBTi", [128, 16, 128], st=st)
        t1 = C.sb("st1", [128, 128], st=st); t2 = C.sb("st2", [128, 128], st=st)
        ptp = [C.ps(f"spt{i}", [128, 128], st=st) for i in range(2)]
        for i in range(16):
            S.op("dve", lambda e: e.tensor_scalar(out=t1[:], in0=BRe[:, i, :], scalar1=sm[:, 6, i:i + 1], scalar2=None, op0=ALU.mult), reads=[BRe, sm], writes=[t1])
            S.op("dve", lambda e: e.scalar_tensor_tensor(out=t1[:], in0=BIe[:, i, :], scalar=sm[:, 8, i:i + 1], in1=t1[:], op0=ALU.mult, op1=ALU.add), reads=[BIe, sm, t1], writes=[t1])
            transpose_to(C, t1, 128, 0, 128, lambda: BBTr[:, i, :], BBTr, ptp, 2 * i)
            S.op("dve", lambda e: e.tensor_scalar(out=t2[:], in0=BIe[:, i, :], scalar1=sm[:, 6, i:i + 1], scalar2=None, op0=ALU.mult), reads=[BIe, sm], writes=[t2])
            S.op("dve", lambda e: e.scalar_tensor_tensor(out=t2[:], in0=BRe[:, i, :], scalar=sm[:, 7, i:i + 1], in1=t2[:], op0=ALU.mult, op1=ALU.add), reads=[BRe, sm, t2], writes=[t2])
            transpose_to(C, t2, 128, 0, 128, lambda: BBTi[:, i, :], BBTi, ptp, 2 * i + 1)
        PWr = C.sb("sPWr", [128, 16, 129], st=st); PWi = C.sb("sPWi", [128, 16, 129], st=st)
        PIr = C.sb("sPIr", [128, 16, 129], st=st); PIi = C.sb("sPIi", [128, 16, 129], st=st)
        ta = C.sb("sta", [128, 129], st=st); tb = C.sb("stb", [128, 129], st=st); tc = C.sb("stc", [128, 129], st=st); td = C.sb("std", [128, 129], st=st)
        for i in range(16):
            S.op("dve", lambda e: e.tensor_scalar(out=ta[:], in0=tidx[:, 0:129], scalar1=sm[:, 2, i:i + 1], scalar2=None, op0=ALU.mult), reads=[tidx, sm], writes=[ta])
            sinr(tb[:], ta[:], 129, 0.0, [ta], [tb])
            sinr(tc[:], ta[:], 129, 0.5 * PI, [ta], [tc])
            act(td[:], tidx[:, 0:129], AF.Exp, [tidx, sm], [td], scale=sm[:, 1, i:i + 1])
            tt("dve", PWr[:, i, :], td[:], tc[:], ALU.mult, [td, tc], [PWr])
            tt("dve", PWi[:, i, :], td[:], tb[:], ALU.mult, [td, tb], [PWi])
            act(td[:], tidx[:, 0:129], AF.Exp, [tidx, sm], [td], scale=sm[:, 3, i:i + 1])
            tt("dve", PIr[:, i, :], td[:], tc[:], ALU.mult, [td, tc], [PIr])
            S.op("dve", lambda e: e.scalar_tensor_tensor(out=PIi[:, i, :], in0=td[:], scalar=-1.0, in1=tb[:], op0=ALU.mult, op1=ALU.mult), reads=[td, tb], writes=[PIi])
        ones = C.sb("sones", [128, 128], st=st)
        S.op("dve", lambda e: e.memset(ones[:], 1.0), writes=[ones])
        ut = [C.sb(f"su{i}", [128, 512], st=st) for i in range(2)]
        uT = C.sb("suT", [128, 4, 128], st=st)
        SR = C.sb("sSR", [128, 16, 128], st=st); SI = C.sb("sSI", [128, 16, 128], st=st)
        br = C.sb("sbr", [128, 16, 128], st=st); bi = C.sb("sbi", [128, 16, 128], st=st)
        w1 = C.sb("sw1", [128, 16, 128], st=st); w2 = C.sb("sw2", [128, 16, 128], st=st); w3 = C.sb("sw3", [128, 16, 128], st=st); w4 = C.sb("sw4", [128, 16, 128], st=st)
        zr = C.sb("szr", [128, 16, 128], st=st); zi = C.sb("szi", [128, 16, 128], st=st)
        Z0 = C.sb("sZ0", [128, 2, 16], st=st); ZL = C.sb("sZL", [128, 2, 16], st=st); zt = C.sb("szt", [128, 4, 16], st=st)
        sin_ = C.sb("ssin", [128, 2, 16], st=st)
        SL = C.sb("sSL", [128, 2, 16], st=st)
        yo = [C.sb(f"syo{i}", [128, 512], st=st) for i in range(2)]
        pbr = [C.ps(f"spbr{i}", [128, 128], st=st) for i in range(2)]; pbi = [C.ps(f"spbi{i}", [128, 128], st=st) for i in range(2)]
        py = [C.ps(f"spy{i}", [128, 512], st=st) for i in range(2)]
        ycnt = [0]

        def chunk(r0, c0, L, last_seq_idx):
            for i in range(16):
                q = i // 4
                pr_, pi_ = pbr[i % 2], pbi[i % 2]
                S.op("pe", lambda e: e.matmul(pr_[:, :L], lhsT=BBTr[:, i, :], rhs=uT[:, q, c0:c0 + L], start=True, stop=True), reads=[BBTr, uT], writes=[pr_])
                S.op("pe", lambda e: e.matmul(pi_[:, :L], lhsT=BBTi[:, i, :], rhs=uT[:, q, c0:c0 + L], start=True, stop=True), reads=[BBTi, uT], writes=[pi_])
                S.op("act", lambda e: e.copy(out=br[:, i, :L], in_=pr_[:, :L]), reads=[pr_], writes=[br])
                S.op("dve", lambda e: e.tensor_copy(out=bi[:, i, :L], in_=pi_[:, :L]), reads=[pi_], writes=[bi])
            tt("dve", w1[:, :, :L], PIr[:, :, :L], br[:, :, :L], ALU.mult, [PIr, br], [w1])
            tt("pool", w2[:, :, :L], PIi[:, :, :L], bi[:, :, :L], ALU.mult, [PIi, bi], [w2])
            tt("dve", w1[:, :, :L], w1[:, :, :L], w2[:, :, :L], ALU.subtract, [w1, w2], [w1])
            tt("pool", w3[:, :, :L], PIr[:, :, :L], bi[:, :, :L], ALU.mult, [PIr, bi], [w3])
            tt("dve", w4[:, :, :L], PIi[:, :, :L], br[:, :, :L], ALU.mult, [PIi, br], [w4])
            tt("dve", w3[:, :, :L], w3[:, :, :L], w4[:, :, :L], ALU.add, [w3, w4], [w3])
            for i in range(16):
                S.op("dve", lambda e: e.tensor_tensor_scan(out=zr[:, i, :L], data0=ones[:, :L], data1=w1[:, i, :L], initial=Z0[:, 0, i:i + 1], op0=ALU.mult, op1=ALU.add), reads=[ones, w1, Z0], writes=[zr])
                S.op("dve", lambda e: e.tensor_tensor_scan(out=zi[:, i, :L], data0=ones[:, :L], data1=w3[:, i, :L], initial=Z0[:, 1, i:i + 1], op0=ALU.mult, op1=ALU.add), reads=[ones, w3, Z0], writes=[zi])
            S.op("act", lambda e: e.copy(out=ZL[:, 0, :], in_=zr[:, :, L - 1]), reads=[zr], writes=[ZL])
            S.op("act", lambda e: e.copy(out=ZL[:, 1, :], in_=zi[:, :, L - 1]), reads=[zi], writes=[ZL])
            tt("dve", w1[:, :, :L], PWr[:, :, :L], zr[:, :, :L], ALU.mult, [PWr, zr], [w1])
            tt("pool", w2[:, :, :L], PWi[:, :, :L], zi[:, :, :L], ALU.mult, [PWi, zi], [w2])
            tt("dve", SR[:, :, :L], w1[:, :, :L], w2[:, :, :L], ALU.subtract, [w1, w2], [SR])
            tt("pool", w3[:, :, :L], PWr[:, :, :L], zi[:, :, :L], ALU.mult, [PWr, zi], [w3])
            tt("dve", w4[:, :, :L], PWi[:, :, :L], zr[:, :, :L], ALU.mult, [PWi, zr], [w4])
            tt("dve", SI[:, :, :L], w3[:, :, :L], w4[:, :, :L], ALU.add, [w3, w4], [SI])
            p = py[ycnt[0] % 2]; o = yo[ycnt[0] % 2]; ycnt[0] += 1
            for i in range(16):
                S.op("pe", lambda e: e.matmul(p[:L, i * 32:(i + 1) * 32], lhsT=SR[:, i, :L], rhs=CRe[:, i, :], start=True, stop=False), reads=[SR, CRe], writes=[p])
                S.op("pe", lambda e: e.matmul(p[:L, i * 32:(i + 1) * 32], lhsT=SI[:, i, :L], rhs=CIm[:, i, :], start=False, stop=True), reads=[SI, CIm], writes=[p])
            S.op("act", lambda e: e.copy(out=o[:L, :], in_=p[:L, :]), reads=[p], writes=[o])
            S.dma("pool", dr["YS5"][r0:r0 + L, :], o[:L, :], reads=[o])
            tt("dve", zt[:, 0, :], PWr[:, :, L], ZL[:, 0, :], ALU.mult, [PWr, ZL], [zt])
            tt("dve", zt[:, 1, :], PWi[:, :, L], ZL[:, 1, :], ALU.mult, [PWi, ZL], [zt])
            tt("dve", zt[:, 2, :], PWr[:, :, L], ZL[:, 1, :], ALU.mult, [PWr, ZL], [zt])
            tt("dve", zt[:, 3, :], PWi[:, :, L], ZL[:, 0, :], ALU.mult, [PWi, ZL], [zt])
            tt("dve", Z0[:, 0, :], zt[:, 0, :], zt[:, 1, :], ALU.subtract, [zt], [Z0])
            tt("dve", Z0[:, 1, :], zt[:, 2, :], zt[:, 3, :], ALU.add, [zt], [Z0])
            if last_seq_idx is not None:
                S.op("act", lambda e: e.copy(out=SL[:, 0, :], in_=SR[:, :, L - 1]), reads=[SR], writes=[SL])
                S.op("act", lambda e: e.copy(out=SL[:, 1, :], in_=SI[:, :, L - 1]), reads=[SI], writes=[SL])
                S.dma("pool", dr["o_s5re"][l, :, last_seq_idx, :], SL[:, 0, :], reads=[SL])
                S.dma("pool", dr["o_s5im"][l, :, last_seq_idx, :], SL[:, 1, :], reads=[SL])

        S.op("dve", lambda e: e.memset(Z0[:], 0.0), writes=[Z0])
        for ti, (r0, P) in enumerate(cfg.tiles):
            u = ut[ti % 2]
            S.dma("sp", u[:P, :], dr["PROJ"][r0:r0 + P, 5376:5888], writes=[u])
            for k in range(4):
                transpose_to(C, u, P, k * 128, 128, lambda: uT[:, k, :P], uT, ptp, k)
            if r0 < T:
                chunk(r0, 0, 128, 0 if r0 + 128 == T else None)
            else:
                for b in range(NS):
                    S.dma("sp", sin_[:, 0, :], dr["st_s5re"][l, :, b, :], writes=[sin_])
                    S.dma("sp", sin_[:, 1, :], dr["st_s5im"][l, :, b, :], writes=[sin_])
                    tt("dve", zt[:, 0, :], K(4), sin_[:, 0, :], ALU.mult, [sm, sin_], [zt])
                    tt("dve", zt[:, 1, :], K(5), sin_[:, 1, :], ALU.mult, [sm, sin_], [zt])
                    tt("dve", zt[:, 2, :], K(4), sin_[:, 1, :], ALU.mult, [sm, sin_], [zt])
                    tt("dve", zt[:, 3, :], K(5), sin_[:, 0, :], ALU.mult, [sm, sin_], [zt])
                    tt("dve", Z0[:, 0, :], zt[:, 0, :], zt[:, 1, :], ALU.subtract, [zt], [Z0])
                    tt("dve", Z0[:, 1, :], zt[:, 2, :], zt[:, 3, :], ALU.add, [zt], [Z0])
                    chunk(r0 + 8 * b, 8 * b, 8, 1 + b)
    S.barrier()
    with ExitStack() as st:
        dsk = bcast_load(C, st, "pd", dr["s5_d"][l], 512)
        bgl = bcast_load(C, st, "pbg", dr["s5_b_glu"][l], 512)
        gn = bcast_load(C, st, "pgn", dr["s5_norm"][l], 512)
        wg = C.sb("pwg", [128, 4, 512], st=st)
        S.dma("sp", wg[:], dr["s5_w_glu"][l].rearrange("(k p) c -> p k c", p=128), writes=[wg])
        yt = [C.sb(f"py{i}", [128, 512], st=st) for i in range(2)]
        ut = [C.sb(f"pu{i}", [128, 512], st=st) for i in range(2)]
        a1 = C.sb("pa1", [128, 512], st=st); a2 = C.sb("pa2", [128, 512], st=st); a3 = C.sb("pa3", [128, 512], st=st)
        yT = C.sb("pyT", [128, 4, 128], st=st)
        ss = C.sb("pss", [128, 2], st=st)
        ptp = [C.ps(f"ppt{i}", [128, 128], st=st) for i in range(2)]
        pg = C.ps("ppg", [128, 512], st=st)
        for ti, (r0, P) in enumerate(cfg.tiles):
            y, u = yt[ti % 2], ut[ti % 2]
            S.dma("sp", y[:P, :], dr["YS5"][r0:r0 + P, :], writes=[y])
            S.dma("sp", u[:P, :], dr["PROJ"][r0:r0 + P, 5376:5888], writes=[u])
            S.op("dve", lambda e: e.tensor_tensor(out=u[:P, :], in0=u[:P, :], in1=dsk[:P, :], op=ALU.mult), reads=[u, dsk], writes=[u])
            S.op("dve", lambda e: e.tensor_tensor(out=y[:P, :], in0=y[:P, :], in1=u[:P, :], op=ALU.add), reads=[y, u], writes=[y])
            S.op("pool", lambda e: e.tensor_tensor(out=a1[:P, :], in0=y[:P, :], in1=y[:P, :], op=ALU.mult), reads=[y], writes=[a1])
            S.op("dve", lambda e: e.tensor_scalar(out=a1[:P, :], in0=a1[:P, :], scalar1=0.044715, scalar2=1.0, op0=ALU.mult, op1=ALU.add), reads=[a1], writes=[a1])
            S.op("dve", lambda e: e.tensor_tensor(out=a1[:P, :], in0=a1[:P, :], in1=y[:P, :], op=ALU.mult), reads=[a1, y], writes=[a1])
            S.op("act", lambda e: e.activation(out=a1[:P, :], in_=a1[:P, :], func=AF.Sigmoid, scale=2.0 * math.sqrt(2.0 / math.pi)), reads=[a1], writes=[a1])
            S.op("dve", lambda e: e.tensor_tensor(out=a2[:P, :], in0=a1[:P, :], in1=y[:P, :], op=ALU.mult), reads=[a1, y], writes=[a2])
            for k in range(4):
                transpose_to(C, a2, P, k * 128, 128, lambda: yT[:, k, :P], yT, ptp, k)
            for k in range(4):
                S.op("pe", lambda e: e.matmul(pg[:P, :], lhsT=yT[:, k, :P], rhs=wg[:, k, :], start=(k == 0), stop=(k == 3)), reads=[yT, wg], writes=[pg])
            S.op("dve", lambda e: e.tensor_tensor(out=a3[:P, :], in0=pg[:P, :], in1=bgl[:P, :], op=ALU.add), reads=[pg, bgl], writes=[a3])
            S.op("act", lambda e: e.activation(out=a3[:P, :], in_=a3[:P, :], func=AF.Sigmoid), reads=[a3], writes=[a3])
            S.op("dve", lambda e: e.tensor_tensor(out=a2[:P, :], in0=a2[:P, :], in1=a3[:P, :], op=ALU.mult), reads=[a2, a3], writes=[a2])
            rms_rows(C, a2, P, 512, gn, a3, ss, a1)
            S.dma("pool", dr["OCAT"][r0:r0 + P, 1536:2048], a3[:P, :], reads=[a3])


def phase_rwkv(C, st0, l):
    from contextlib import ExitStack
    S, dr, cfg = C.S, C.dr, C.cfg
    T, NS, NTS = cfg.T, cfg.NS, cfg.NTS
    R0 = 3584

    def tt(eng, out, a, b, op, rd, wr):
        S.op(eng, lambda e: e.tensor_tensor(out=out, in0=a, in1=b, op=op), reads=rd, writes=wr)
    with ExitStack() as st:
        mu = bcast_load(C, st, "kmu", dr["rwkv_mu"][l], RW_COLS)
        w0 = bcast_load(C, st, "kw0", dr["rwkv_w0"][l], 512)
        a0 = bcast_load(C, st, "ka0", dr["rwkv_a0"][l], 512)
        kkw = bcast_load(C, st, "kkk", dr["rwkv_kk"][l], 512)
        kaw = bcast_load(C, st, "kka", dr["rwkv_ka"][l], 512)
        w2 = C.sb("kw2", [64, 512], st=st); S.dma("sp", w2[:], dr["rwkv_w2"][l], writes=[w2])
        a2 = C.sb("ka2", [64, 512], st=st); S.dma("sp", a2[:], dr["rwkv_a2"][l], writes=[a2])
        g2 = C.sb("kg2", [128, 512], st=st); S.dma("sp", g2[:], dr["rwkv_g2"][l], writes=[g2])
        cur = [C.sb(f"kc{i}", [128, RW_COLS], st=st) for i in range(2)]
        prv = [C.sb(f"kp{i}", [128, RW_COLS], st=st) for i in range(2)]
        lo = C.sb("klo", [128, 256], st=st)
        loT = C.sb("kloT", [128, 3, 128], st=st)
        dec = C.sb("kdec", [128, 512], st=st); aa = C.sb("kaa", [128, 512], st=st); gg = C.sb("kgg", [128, 512], st=st)
        kk = C.sb("kkkv", [128, 512], st=st); k2 = C.sb("kk2", [128, 512], st=st); bb = C.sb("kbb", [128, 512], st=st)
        ain = C.sb("kain", [128, 512], st=st); t5 = C.sb("kt5", [128, 512], st=st)
        s8 = C.sb("ks8", [128, 8], st=st)
        fa = [C.sb(f"kfa{i}", [64, 128, 40], st=st) for i in range(2)]
        fr = [C.sb(f"kfr{i}", [64, 128, 8], st=st) for i in range(2)]
        fw = [C.sb(f"kfw{i}", [64, 128, 8], st=st) for i in range(2)]
        ptp = [C.ps(f"kpt{i}", [128, 128], st=st) for i in range(2)]
        pl = [C.ps(f"kpl{i}", [128, 512], st=st) for i in range(3)]
        for i in range(2):
            S.op("pool", lambda e: e.memset(fa[i][:], 0.0), writes=[fa[i]])
        for ti, (r0, P) in enumerate(cfg.tiles):
            c, p = cur[ti % 2], prv[ti % 2]
            S.dma("sp", c[:P, :], dr["PROJ"][r0:r0 + P, R0:R0 + RW_COLS], writes=[c])
            if r0 == 0:
                S.op("pool", lambda e: e.memset(p[0:32, :], 0.0), writes=[p])
                S.dma("sp", p[1:P, :], dr["PROJ"][0:P - 1, R0:R0 + RW_COLS], writes=[p])
            else:
                S.dma("sp", p[:P, :], dr["PROJ"][r0 - 1:r0 + P - 1, R0:R0 + RW_COLS], writes=[p])
                if r0 >= T:
                    for b in range(NS):
                        S.dma("sp", p[8 * b:8 * b + 1, :], dr["st_shift"][l, b:b + 1, :], writes=[p])
            tt("dve", p[:P, :], p[:P, :], c[:P, :], ALU.subtract, [p, c], [p])
            tt("pool", p[:P, :], p[:P, :], mu[:P, :], ALU.mult, [p, mu], [p])
            tt("dve", p[:P, :], p[:P, :], c[:P, :], ALU.add, [p, c], [p])
            x = p
            S.op("act", lambda e: e.activation(out=lo[:P, 0:64], in_=x[:P, 1536:1600], func=AF.Tanh), reads=[x], writes=[lo])
            S.op("act", lambda e: e.copy(out=lo[:P, 64:128], in_=x[:P, 1600:1664]), reads=[x], writes=[lo])
            S.op("act", lambda e: e.activation(out=lo[:P, 128:256], in_=x[:P, 1664:1792], func=AF.Sigmoid), reads=[x], writes=[lo])
            transpose_to(C, lo, P, 0, 64, lambda: loT[0:64, 0, :P], loT, ptp, 0)
            transpose_to(C, lo, P, 64, 64, lambda: loT[0:64, 1, :P], loT, ptp, 1)
            transpose_to(C, lo, P, 128, 128, lambda: loT[:, 2, :P], loT, ptp, 2)
            S.op("pe", lambda e: e.matmul(pl[0][:P, :], lhsT=loT[0:64, 0, :P], rhs=w2[:, :], start=True, stop=True), reads=[loT, w2], writes=[pl[0]])
            S.op("pe", lambda e: e.matmul(pl[1][:P, :], lhsT=loT[0:64, 1, :P], rhs=a2[:, :], start=True, stop=True), reads=[loT, a2], writes=[pl[1]])
            S.op("pe", lambda e: e.matmul(pl[2][:P, :], lhsT=loT[:, 2, :P], rhs=g2[:, :], start=True, stop=True), reads=[loT, g2], writes=[pl[2]])
            tt("dve", dec[:P, :], pl[0][:P, :], w0[:P, :], ALU.add, [pl[0], w0], [dec])
            S.op("act", lambda e: e.activation(out=dec[:P, :], in_=dec[:P, :], func=AF.Sigmoid), reads=[dec], writes=[dec])
            S.op("act", lambda e: e.activation(out=dec[:P, :], in_=dec[:P, :], func=AF.Exp, scale=-math.exp(-0.5)), reads=[dec], writes=[dec])
            tt("dve", aa[:P, :], pl[1][:P, :], a0[:P, :], ALU.add, [pl[1], a0], [aa])
            S.op("act", lambda e: e.activation(out=aa[:P, :], in_=aa[:P, :], func=AF.Sigmoid), reads=[aa], writes=[aa])
            S.op("act", lambda e: e.copy(out=gg[:P, :], in_=pl[2][:P, :]), reads=[pl[2]], writes=[gg])
            tt("dve", kk[:P, :], x[:P, 512:1024], kkw[:P, :], ALU.mult, [x, kkw], [kk])
            tt("pool", t5[:P, :], kk[:P, :], kk[:P, :], ALU.mult, [kk], [t5])
            S.op("dve", lambda e: e.tensor_reduce(out=s8[:P, :], in_=v3(t5[:P, :], 8), axis=AX.X, op=ALU.add), reads=[t5], writes=[s8])
            S.op("dve", lambda e: e.tensor_scalar(out=s8[:P, :], in0=s8[:P, :], scalar1=1e-24, scalar2=None, op0=ALU.max), reads=[s8], writes=[s8])
            S.op("act", lambda e: e.sqrt(out=s8[:P, :], in_=s8[:P, :]), reads=[s8], writes=[s8])
            S.op("dve", lambda e: e.reciprocal(out=s8[:P, :], in_=s8[:P, :]), reads=[s8], writes=[s8])
            tt("dve", v3(kk[:P, :], 8), v3(kk[:P, :], 8), s8[:P, :].unsqueeze(2).to_broadcast([P, 8, 64]), ALU.mult, [kk, s8], [kk])
            S.op("dve", lambda e: e.scalar_tensor_tensor(out=t5[:P, :], in0=aa[:P, :], scalar=-1.0, in1=kaw[:P, :], op0=ALU.add, op1=ALU.mult), reads=[aa, kaw], writes=[t5])
            S.op("dve", lambda e: e.scalar_tensor_tensor(out=k2[:P, :], in0=t5[:P, :], scalar=1.0, in1=x[:P, 512:1024], op0=ALU.add, op1=ALU.mult), reads=[t5, x], writes=[k2])
            tt("pool", bb[:P, :], kk[:P, :], aa[:P, :], ALU.mult, [kk, aa], [bb])
            S.op("act", lambda e: e.mul(out=ain[:P, :], in_=kk[:P, :], mul=-1.0), reads=[kk], writes=[ain])
            S.dma("pool", dr["RWS"][0, r0:r0 + P, :], bb[:P, :], reads=[bb])
            S.dma("pool", dr["RWS"][1, r0:r0 + P, :], k2[:P, :], reads=[k2])
            S.dma("pool", dr["RWS"][2, r0:r0 + P, :], x[:P, 1024:1536], reads=[x])
            S.dma("pool", dr["RWS"][3, r0:r0 + P, :], x[:P, 0:512], reads=[x])
            S.dma("pool", dr["RWS"][4, r0:r0 + P, :], gg[:P, :], reads=[gg])
            A_, R_, W_ = fa[ti % 2], fr[ti % 2], fw[ti % 2]
            for h in range(8):
                transpose_to(C, ain, P, h * 64, 64, lambda: A_[:, :P, 32 + h], A_, ptp, 3 * h)
                transpose_to(C, x, P, h * 64, 64, lambda: R_[:, :P, h], R_, ptp, 3 * h + 1)
                transpose_to(C, dec, P, h * 64, 64, lambda: W_[:, :P, h], W_, ptp, 3 * h + 2)
            S.dma("pool", dr["FMA"][:, r0:r0 + P, :], A_[:, :P, :], reads=[A_])
            S.dma("pool", dr["FMR"][:, r0:r0 + P, :], R_[:, :P, :], reads=[R_])
            S.dma("pool", dr["FMW"][:, r0:r0 + P, :], W_[:, :P, :], reads=[W_])
    S.barrier()
    TC = 16
    with ExitStack() as st:
        ST = C.sb("nST", [64, 512], st=st)
        TMP = C.sb("nTMP", [64, 512], st=st)
        m40 = C.sb("nm40", [40, 512], st=st)
        S.op("dve", lambda e: e.memset(m40[:], 0.0), writes=[m40])
        S.dma("sp", m40[0:8, :], dr["mask8"], writes=[m40])
        S.dma("sp", m40[32:40, :], dr["mask8"], writes=[m40])
        BK = [C.sb(f"nBK{i}", [40, TC, 64], st=st) for i in range(2)]
        SAV = [C.sb(f"nSAV{i}", [40, TC, 512], st=st) for i in range(2)]
        AT = [C.sb(f"nAT{i}", [64, TC, 40], st=st) for i in range(2)]
        RT = [C.sb(f"nRT{i}", [64, TC, 8], st=st) for i in range(2)]
        WT = [C.sb(f"nWT{i}", [64, TC, 8], st=st) for i in range(2)]
        YB = [C.sb(f"nYB{i}", [8, TC, 512], st=st) for i in range(2)]
        psa = [C.ps(f"npsa{i}", [40, 512], st=st) for i in range(2)]
        psu = [C.ps(f"npsu{i}", [64, 512], st=st) for i in range(2)]
        psy = [C.ps(f"npsy{i}", [8, 512], st=st) for i in range(2)]
        for i in range(2):
            S.op("pool", lambda e: e.memset(BK[i][:], 0.0), writes=[BK[i]])
            S.op("pool", lambda e: e.memset(SAV[i][:], 0.0), writes=[SAV[i]])
        cc = [0]
        stp = [0]

        def seq(t0, n):
            steps = []
            for c0 in range(0, n, TC):
                L = min(TC, n - c0)
                for t in range(L):
                    steps.append((t0 + c0, L, t))
            cur = {}

            def loads(r0, L):
                k = cc[0] % 2
                cc[0] += 1
                bk, sav, at, rt, wt, yb = BK[k], SAV[k], AT[k], RT[k], WT[k], YB[k]
                S.dma("sp", bk[0:8, :L, :], dr["RWS"][1, r0:r0 + L, :].rearrange("t (h j) -> h t j", h=8), writes=[bk])
                S.dma("sp", bk[32:40, :L, :], dr["RWS"][0, r0:r0 + L, :].rearrange("t (h j) -> h t j", h=8), writes=[bk])
                S.dma("sp", sav[0:8, :L, :], dr["RWS"][2, r0:r0 + L, :].partition_broadcast(8), writes=[sav])
                S.op("pool", lambda e: e.tensor_tensor(out=sav[0:8, :L, :], in0=sav[0:8, :L, :],
                                                       in1=m40[0:8, :].unsqueeze(1).to_broadcast([8, L, 512]), op=ALU.mult), reads=[sav, m40], writes=[sav])
                S.dma("sp", at[:, :L, :], dr["FMA"][:, r0:r0 + L, :], writes=[at])
                S.dma("sp", rt[:, :L, :], dr["FMR"][:, r0:r0 + L, :], writes=[rt])
                S.dma("sp", wt[:, :L, :], dr["FMW"][:, r0:r0 + L, :], writes=[wt])
                return (bk, sav, at, rt, wt, yb)

            def mm1(si):
                r0, L, t = steps[si]
                if t == 0:
                    cur[r0] = loads(r0, L)
                at = cur[r0][2]
                pa = psa[si % 2]
                S.op("pe", lambda e: e.matmul(pa[:, :], lhsT=at[:, t, :], rhs=ST[:, :], start=True, stop=True), reads=[at, ST], writes=[pa])

            mm1(0)
            for si, (r0, L, t) in enumerate(steps):
                bk, sav, at, rt, wt, yb = cur[r0]
                pa, pu, py_ = psa[si % 2], psu[si % 2], psy[si % 2]
                S.op("dve", lambda e: e.tensor_tensor(out=sav[32:40, t, :], in0=pa[32:40, :], in1=m40[32:40, :], op=ALU.mult), reads=[pa, m40], writes=[sav])
                S.op("pool", lambda e: e.tensor_tensor(out=v3(TMP[:, :], 8), in0=v3(ST[:, :], 8),
                                                       in1=wt[:, t, :].unsqueeze(2).to_broadcast([64, 8, 64]), op=ALU.mult), reads=[ST, wt], writes=[TMP])
                S.op("pe", lambda e: e.matmul(pu[:, :], lhsT=bk[:, t, :], rhs=sav[:, t, :], start=True, stop=True), reads=[bk, sav], writes=[pu])
                S.op("dve", lambda e: e.tensor_tensor(out=ST[:, :], in0=TMP[:, :], in1=pu[:, :], op=ALU.add), reads=[TMP, pu], writes=[ST])
                if si + 1 < len(steps):
                    mm1(si + 1)
                S.op("pe", lambda e: e.matmul(py_[:, :], lhsT=rt[:, t, :], rhs=ST[:, :], start=True, stop=True), reads=[rt, ST], writes=[py_])
                S.op("act", lambda e: e.copy(out=yb[:, t, :], in_=py_[:, :]), reads=[py_], writes=[yb])
                if t == L - 1:
                    for h in range(8):
                        S.dma("pool", dr["RWS"][5, r0:r0 + L, h * 64:(h + 1) * 64], yb[h:h + 1, :L, h * 64:(h + 1) * 64], reads=[yb])
                    del cur[r0]

        S.op("dve", lambda e: e.memset(ST[:], 0.0), writes=[ST])
        seq(0, T)
        S.dma("pool", dr["o_rwkvT"][l, 0].rearrange("j h i -> j (h i)"), ST[:, :], reads=[ST])
        for b in range(NS):
            S.dma("sp", ST[:, :], dr["st_rwkvT"][l, b].rearrange("j h i -> j (h i)"), writes=[ST])
            seq(T + 8 * b, 8)
            S.dma("pool", dr["o_rwkvT"][l, 1 + b].rearrange("j h i -> j (h i)"), ST[:, :], reads=[ST])
    S.barrier()
    with ExitStack() as st:
        lw = bcast_load(C, st, "olw", dr["rwkv_ln_w"][l], 512)
        lb = bcast_load(C, st, "olb", dr["rwkv_ln_b"][l], 512)
        rk = bcast_load(C, st, "ork", dr["rwkv_rk"][l], 512)
        ins = [[C.sb(f"oi{j}_{i}", [128, 512], st=st) for j in range(5)] for i in range(2)]
        t1 = C.sb("ot1", [128, 512], st=st); t2 = C.sb("ot2", [128, 512], st=st)
        s8 = C.sb("os8", [128, 16], st=st)
        for ti, (r0, P) in enumerate(cfg.tiles):
            k2, v, r, g, y = ins[ti % 2]
            for j, tl in zip((1, 2, 3, 4, 5), (k2, v, r, g, y)):
                S.dma("sp", tl[:P, :], dr["RWS"][j, r0:r0 + P, :], writes=[tl])
            y3 = v3(y[:P, :], 8)
            S.op("dve", lambda e: e.tensor_reduce(out=s8[:P, 0:8], in_=y3, axis=AX.X, op=ALU.add), reads=[y], writes=[s8])
            S.op("dve", lambda e: e.tensor_scalar(out=s8[:P, 0:8], in0=s8[:P, 0:8], scalar1=1.0 / 64, scalar2=None, op0=ALU.mult), reads=[s8], writes=[s8])
            tt("dve", y3, y3, s8[:P, 0:8].unsqueeze(2).to_broadcast([P, 8, 64]), ALU.subtract, [y, s8], [y])
            tt("pool", t1[:P, :], y[:P, :], y[:P, :], ALU.mult, [y], [t1])
            S.op("dve", lambda e: e.tensor_reduce(out=s8[:P, 8:16], in_=v3(t1[:P, :], 8), axis=AX.X, op=ALU.add), reads=[t1], writes=[s8])
            S.op("dve", lambda e: e.tensor_scalar(out=s8[:P, 8:16], in0=s8[:P, 8:16], scalar1=1.0 / 64, scalar2=64e-5, op0=ALU.mult, op1=ALU.add), reads=[s8], writes=[s8])
            S.op("act", lambda e: e.sqrt(out=s8[:P, 8:16], in_=s8[:P, 8:16]), reads=[s8], writes=[s8])
            S.op("dve", lambda e: e.reciprocal(out=s8[:P, 8:16], in_=s8[:P, 8:16]), reads=[s8], writes=[s8])
            tt("dve", y3, y3, s8[:P, 8:16].unsqueeze(2).to_broadcast([P, 8, 64]), ALU.mult, [y, s8], [y])
            tt("dve", y[:P, :], y[:P, :], lw[:P, :], ALU.mult, [y, lw], [y])
            tt("dve", y[:P, :], y[:P, :], lb[:P, :], ALU.add, [y, lb], [y])
            tt("pool", t1[:P, :], r[:P, :], k2[:P, :], ALU.mult, [r, k2], [t1])
            tt("pool", t1[:P, :], t1[:P, :], rk[:P, :], ALU.mult, [t1, rk], [t1])
            S.op("dve", lambda e: e.tensor_reduce(out=s8[:P, 0:8], in_=v3(t1[:P, :], 8), axis=AX.X, op=ALU.add), reads=[t1], writes=[s8])
            tt("dve", v3(t2[:P, :], 8), v3(v[:P, :], 8), s8[:P, 0:8].unsqueeze(2).to_broadcast([P, 8, 64]), ALU.mult, [v, s8], [t2])
            tt("dve", y[:P, :], y[:P, :], t2[:P, :], ALU.add, [y, t2], [y])
            tt("dve", y[:P, :], y[:P, :], g[:P, :], ALU.mult, [y, g], [y])
            S.dma("pool", dr["OCAT"][r0:r0 + P, 1024:1536], y[:P, :], reads=[y])


def phase_diff(C, st0, l):
    from contextlib import ExitStack
    S, dr, cfg = C.S, C.dr, C.cfg
    T, NS, NTS, NPG = cfg.T, cfg.NS, cfg.NTS, cfg.NPG
    NQB = T // 128
    lam_init = 0.8 - 0.6 * math.exp(-0.3 * l)

    def tt(eng, out, a, b, op, rd, wr):
        S.op(eng, lambda e: e.tensor_tensor(out=out, in0=a, in1=b, op=op), reads=rd, writes=wr)
    with ExitStack() as st:
        dl = C.sb("dl", [128, 4, 64], st=st)
        S.dma("sp", dl[:], dr["diff_l"][l].partition_broadcast(128), writes=[dl])
        lt = C.sb("dlt", [128, 2, 64], st=st)
        lam = C.sb("dlam", [128, 4], st=st)
        tt("dve", lt[:, 0, :], dl[:, 0, :], dl[:, 1, :], ALU.mult, [dl], [lt])
        tt("dve", lt[:, 1, :], dl[:, 2, :], dl[:, 3, :], ALU.mult, [dl], [lt])
        S.op("dve", lambda e: e.tensor_reduce(out=lam[:, 0:2], in_=lt[:], axis=AX.X, op=ALU.add), reads=[lt], writes=[lam])
        S.op("act", lambda e: e.activation(out=lam[:, 0:2], in_=lam[:, 0:2], func=AF.Exp), reads=[lam], writes=[lam])
        tt("dve", lam[:, 2:3], lam[:, 0:1], lam[:, 1:2], ALU.subtract, [lam], [lam])
        S.op("dve", lambda e: e.tensor_scalar(out=lam[:, 2:3], in0=lam[:, 2:3], scalar1=lam_init, scalar2=-1.0, op0=ALU.add, op1=ALU.mult), reads=[lam], writes=[lam])
        sub = bcast_load(C, st, "dsub", dr["diff_subln"][l], 128)
        S.op("act", lambda e: e.mul(out=sub[:], in_=sub[:], mul=(1.0 - lam_init)), reads=[sub], writes=[sub])
        cm = C.sb("dcm", [128, 128], st=st); S.dma("sp", cm[:], dr["cmask"], writes=[cm])
        cm8 = C.sb("dcm8", [8, 8], st=st); S.dma("sp", cm8[:], dr["cmask8"], writes=[cm8])
        KW = max(T, NPG * 128 + 8)
        KT = C.sb("dKT", [128, KW], st=st)
        SC = [C.sb(f"dSC{m}", [128, KW], st=st) for m in range(2)]
        QT = C.sb("dQT", [128, 128], st=st)
        xq = [C.sb(f"dxq{i}", [128, 128], st=st) for i in range(2)]
        pTs = [C.sb(f"dpT{i}", [128, 128], st=st) for i in range(3)]
        oo = [C.sb(f"doo{i}", [128, 128], st=st) for i in range(2)]
        sq = C.sb("dsq", [128, 128], st=st)
        sm = C.sb("dsm", [128, 8], st=st)
        ss = C.sb("dss", [128, 2], st=st)
        vn = C.sb("dvn", [8, 128], st=st)
        ptp = [C.ps(f"dpt{i}", [128, 128], st=st) for i in range(2)]
        psc = [C.ps(f"dps{i}", [128, 512], st=st) for i in range(3)]
        po = [C.ps(f"dpo{i}", [128, 128], st=st) for i in range(2)]
        cnt = [0]

        def attn(P, nk, vblocks, mask, msz, r0, h):
            for m in range(2):
                for k0 in range(0, nk, 512):
                    n = min(512, nk - k0)
                    p_ = psc[cnt[0] % 3]
                    cnt[0] += 1
                    S.op("pe", lambda e: e.matmul(p_[:P, :n], lhsT=R_(QT[m * 64:(m + 1) * 64, :P]), rhs=R_(KT[m * 64:(m + 1) * 64, k0:k0 + n]),
                                                  start=True, stop=True), reads=[QT, KT], writes=[p_])
                    if cnt[0] % 2 == 0:
                        S.op("act", lambda e: e.mul(out=SC[m][:P, k0:k0 + n], in_=p_[:P, :n], mul=0.125), reads=[p_], writes=[SC[m]])
                    else:
                        S.op("dve", lambda e: e.tensor_scalar(out=SC[m][:P, k0:k0 + n], in0=p_[:P, :n], scalar1=0.125, scalar2=None, op0=ALU.mult), reads=[p_], writes=[SC[m]])
                tt("pool", SC[m][:P, nk - msz:nk], SC[m][:P, nk - msz:nk], mask[:P, :msz], ALU.add, [SC[m], mask], [SC[m]])
                S.op("dve", lambda e: e.tensor_reduce(out=sm[:P, m:m + 1], in_=SC[m][:P, :nk], axis=AX.X, op=ALU.max), reads=[SC[m]], writes=[sm])
                S.op("dve", lambda e: e.tensor_scalar(out=sm[:P, 2 + m:3 + m], in0=sm[:P, m:m + 1], scalar1=-1.0, scalar2=None, op0=ALU.mult), reads=[sm], writes=[sm])
                S.op("act", lambda e: e.activation(out=SC[m][:P, :nk], in_=SC[m][:P, :nk], func=AF.Exp, bias=sm[:P, 2 + m:3 + m], scale=1.0,
                                                   accum_out=sm[:P, 4 + m:5 + m]), reads=[SC[m], sm], writes=[SC[m], sm])
            S.op("dve", lambda e: e.reciprocal(out=sm[:P, 6:8], in_=sm[:P, 4:6]), reads=[sm], writes=[sm])
            tt("dve", sm[:P, 7:8], sm[:P, 7:8], lam[:P, 2:3], ALU.mult, [sm, lam], [sm])
            S.op("dve", lambda e: e.tensor_scalar(out=SC[0][:P, :nk], in0=SC[0][:P, :nk], scalar1=sm[:P, 6:7], scalar2=None, op0=ALU.mult), reads=[SC[0], sm], writes=[SC[0]])
            S.op("dve", lambda e: e.scalar_tensor_tensor(out=SC[0][:P, :nk], in0=SC[1][:P, :nk], scalar=sm[:P, 7:8], in1=SC[0][:P, :nk],
                                                         op0=ALU.mult, op1=ALU.add), reads=[SC[0], SC[1], sm], writes=[SC[0]])
            pO = po[cnt[0] % 2]
            for bi, (k0, ksz, v_ap, vt) in enumerate(vblocks):
                pt = ptp[bi % 2]
                pT = pTs[bi % 3]
                S.op("pe", lambda e: e.transpose(out=pt[:ksz, :P], in_=SC[0][:P, k0:k0 + ksz], identity=C.ident[:P, :P]), reads=[SC[0], C.ident], writes=[pt])
                if bi % 2 == 0:
                    S.op("act", lambda e: e.copy(out=pT[:ksz, :P], in_=pt[:ksz, :P]), reads=[pt], writes=[pT])
                else:
                    S.op("dve", lambda e: e.tensor_copy(out=pT[:ksz, :P], in_=pt[:ksz, :P]), reads=[pt], writes=[pT])
                S.op("pe", lambda e: e.matmul(pO[:P, :], lhsT=pT[:ksz, :P], rhs=v_ap, start=(bi == 0), stop=(bi == len(vblocks) - 1)), reads=[pT, vt], writes=[pO])
            o = oo[cnt[0] % 2]
            S.op("act", lambda e: e.copy(out=o[:P, :], in_=pO[:P, :]), reads=[pO], writes=[o])
            rms_rows(C, o, P, 128, sub, o, ss, sq)
            S.dma("pool", dr["OCAT"][r0:r0 + P, 512 + h * 128:512 + (h + 1) * 128], o[:P, :], reads=[o])

        n = 0
        st2 = ExitStack()
        st2.__enter__()
        Vh = C.sb("dVh", [128, NQB, 128], st=st2)
        for h in range(4):
            for kb in range(NQB):
                x = xq[n % 2]; n += 1
                S.dma("sp", x[:, :], dr["k_new"][l, kb * 128:(kb + 1) * 128, h * 128:(h + 1) * 128], writes=[x])
                transpose_to(C, x, 128, 0, 128, lambda: KT[:, kb * 128:(kb + 1) * 128], KT, ptp, kb, r=True)
            S.dma("sp", Vh[:, :NQB, :], dr["PROJ"][0:T, 3072 + h * 128:3072 + (h + 1) * 128].rearrange("(k p) c -> p k c", p=128), writes=[Vh])
            for qb in range(NQB):
                x = xq[n % 2]; n += 1
                S.dma("sp", x[:, :], dr["PROJ"][qb * 128:(qb + 1) * 128, 2048 + h * 128:2048 + (h + 1) * 128], writes=[x])
                transpose_to(C, x, 128, 0, 128, lambda: QT[:, :], QT, ptp, qb, r=True)
                vb = [(kb * 128, 128, Vh[:, kb, :], Vh) for kb in range(qb + 1)]
                attn(128, (qb + 1) * 128, vb, cm, 128, qb * 128, h)
        S.barrier()
        st2.__exit__(None, None, None)
        ptb = C.sb("dptb", [128, NS * NPG], I32, st=st)
        S.dma("sp", ptb[:], dr["pt"].partition_broadcast(128), writes=[ptb])
        ptf = C.sb("dptf", [128, NS * NPG], st=st)
        io = C.sb("dio", [128, 1], I32, st=st); S.dma("sp", io[:], dr["iota"], writes=[io])
        iof = C.sb("diof", [128, 1], st=st)
        S.op("dve", lambda e: e.tensor_copy(out=ptf[:], in_=ptb[:]), reads=[ptb], writes=[ptf])
        S.op("dve", lambda e: e.tensor_copy(out=iof[:], in_=io[:]), reads=[io], writes=[iof])
        S.op("dve", lambda e: e.tensor_scalar(out=ptf[:], in0=ptf[:], scalar1=128.0, scalar2=iof[:, 0:1], op0=ALU.mult, op1=ALU.add), reads=[ptf, iof], writes=[ptf])
        S.op("dve", lambda e: e.tensor_scalar(out=ptf[:], in0=ptf[:], scalar1=float(l * cfg.NPOOL * 128), scalar2=None, op0=ALU.add), reads=[ptf], writes=[ptf])
        idx = C.sb("didx", [128, NS * NPG], I32, st=st)
        S.op("dve", lambda e: e.tensor_copy(out=idx[:], in_=ptf[:]), reads=[ptf], writes=[idx])
        KP = [C.sb(f"dKP{i}", [128, 512], st=st) for i in range(2)]
        VP = [C.sb(f"dVP{i}", [128, 512], st=st) for i in range(NPG)]
        ckf = dr["ck"].rearrange("l r c -> (l r) c")
        cvf = dr["cv"].rearrange("l r c -> (l r) c")
        KTs = C.sb("dKTs", [128, 4, NPG * 128 + 8], st=st)
        QTs = C.sb("dQTs", [128, 4, 8], st=st)
        q8 = C.sb("dq8", [8, 512], st=st); k8 = C.sb("dk8", [8, 512], st=st); v8 = C.sb("dv8", [8, 512], st=st)
        for b in range(NS):
            r0 = T + 8 * b
            for pg in range(NPG):
                kp = KP[pg % 2]
                col = b * NPG + pg
                gather(C, kp, kp[:, :], ckf, idx, col)
                gather(C, VP[pg], VP[pg][:, :], cvf, idx, col)
                for h in range(4):
                    transpose_to(C, kp, 128, h * 128, 128, lambda: KTs[:, h, pg * 128:(pg + 1) * 128], KTs, ptp, h)
            S.dma("sp", q8[:, :], dr["PROJ"][r0:r0 + 8, 2048:2560], writes=[q8])
            S.dma("sp", k8[:, :], dr["k_new"][l, r0:r0 + 8, :], writes=[k8])
            S.dma("sp", v8[:, :], dr["PROJ"][r0:r0 + 8, 3072:3584], writes=[v8])
            for h in range(4):
                transpose_to(C, k8, 8, h * 128, 128, lambda: KTs[:, h, NPG * 128:NPG * 128 + 8], KTs, ptp, h)
                transpose_to(C, q8, 8, h * 128, 128, lambda: QTs[:, h, :], QTs, ptp, h + 1)
            for h in range(4):
                S.op("act", lambda e: e.copy(out=R_(QT[:, 0:8]), in_=QTs[:, h, :]), reads=[QTs], writes=[QT])
                S.op("pool", lambda e: e.tensor_copy(out=R_(KT[:, 0:NPG * 128 + 8]), in_=KTs[:, h, :]), reads=[KTs], writes=[KT])
                vb = [(pg * 128, 128, VP[pg][:, h * 128:(h + 1) * 128], VP[pg]) for pg in range(NPG)]
                vb.append((NPG * 128, 8, v8[:, h * 128:(h + 1) * 128], v8))
                attn(8, NPG * 128 + 8, vb, cm8, 8, r0, h)


def gather(C, tile, out_ap, table, idx, col):
    S = C.S
    q = "pool"
    S._deps(q, [idx], [tile])
    if q not in S.dsem or S.dcnt[q] >= S.DK * (S.SEM_LIMIT // 16):
        S.dsem[q] = [S.newsem() for _ in range(S.DK)]
        S.dcnt[q] = 0
        S.last.setdefault("dma", {})
    i = S.dcnt[q]
    sem = S.dsem[q][i % S.DK]
    prev = 16 * (i // S.DK)
    if prev > 0:
        S._wait(q, ("dma", sem, prev))
    ins = S.eng[q].indirect_dma_start(out=out_ap, out_offset=None, in_=table,
                                      in_offset=bass.IndirectOffsetOnAxis(ap=idx[:, col:col + 1], axis=0))
    ins.then_inc(sem, 16)
    S.dcnt[q] = i + 1
    ref = ("dma", sem, prev + 16)
    S.last.setdefault("dma", {})[id(sem)] = ref
    S._mark(ref, [idx], [tile])
    S.nins += 1
```
